# Optimizing a Trainium2 kernel written in Bass

```python
import math, functools
import jax, jax.numpy as jnp
from jax import lax
import numpy as np

D_MODEL = 1024
BATCH = 4
SEQ = 4096
DEPTH = 1
DEC_BATCH = 128
DEC_SEQ = 4
PAST_LEN = 16384
PAGE_SIZE = 128

GDN_HEADS = 8
GDN_DK = 64
GDN_DV = 64
GDN_CONV = 4
GDN_CHUNK = 64
GDN_QK_DIM = GDN_HEADS * GDN_DK
GDN_V_DIM = GDN_HEADS * GDN_DV
GDN_CONV_DIM = 2 * GDN_QK_DIM + GDN_V_DIM
MLA_HEADS = 8
MLA_NOPE = 64
MLA_ROPE = 32
MLA_V = 64
MLA_KV_RANK = 128
MLA_Q_DIM = MLA_HEADS * (MLA_NOPE + MLA_ROPE)
ROPE_THETA = 10000.0
ATTN_Q_BLOCK = 128
MIX_WIDTH = GDN_V_DIM + MLA_HEADS * MLA_V
IN_SPLITS = (GDN_CONV_DIM, GDN_V_DIM, GDN_HEADS, GDN_HEADS, MLA_Q_DIM, MLA_KV_RANK, MLA_ROPE)
IN_DIM = sum(IN_SPLITS)
D_FF = 2816
N_MOD = 9
NORM_EPS = 1e-6

kernel_name = "hybrid_gdn_mla_macaron_adaln_step"


def split_cols(y, sizes):
    out, start = [], 0
    for s in sizes:
        out.append(y[..., start:start + s])
        start += s
    return out


def rms_norm(x, g):
    xf = x.astype(jnp.float32)
    y = xf * lax.rsqrt(jnp.mean(xf * xf, axis=-1, keepdims=True) + NORM_EPS)
    return (y * g.astype(jnp.float32)).astype(x.dtype)


def l2_norm(x):
    xf = x.astype(jnp.float32)
    return xf * lax.rsqrt(jnp.sum(xf * xf, axis=-1, keepdims=True) + NORM_EPS)


def rope_angles(pos):
    half = MLA_ROPE // 2
    inv = ROPE_THETA ** (-jnp.arange(half, dtype=jnp.float32) / half)
    ang = pos[:, None] * inv[None, :]
    return jnp.cos(ang), jnp.sin(ang)


def apply_rope(x, cos, sin):
    half = MLA_ROPE // 2
    xf = x.astype(jnp.float32)
    x1, x2 = xf[..., :half], xf[..., half:]
    return jnp.concatenate([x1 * cos - x2 * sin, x1 * sin + x2 * cos], axis=-1).astype(x.dtype)


def swiglu_ffn(h, wi, wo):
    gate, up = jnp.split(h @ wi, 2, axis=-1)
    return (jax.nn.silu(gate) * up) @ wo


def causal_conv(prev, x, w):
    T = x.shape[1]
    xin = jnp.concatenate([prev.astype(x.dtype), x], axis=1)
    y = sum(xin[:, j:j + T] * w[j] for j in range(GDN_CONV))
    return jax.nn.silu(y), xin[:, xin.shape[1] - (GDN_CONV - 1):]


def gated_delta_chunked(q, k, v, beta, g, s0):
    f32 = jnp.float32
    B, T, H, DK = q.shape
    DV = v.shape[-1]
    C = min(GDN_CHUNK, T)
    pad = (-T) % C

    def prep(a):
        a = a.astype(f32)
        if pad:
            a = jnp.pad(a, [(0, 0), (0, pad)] + [(0, 0)] * (a.ndim - 2))
        n = a.shape[1] // C
        a = a.reshape((B, n, C) + a.shape[2:])
        return a.transpose((1, 0, 3, 2) + tuple(range(4, a.ndim)))

    q, k, v, beta, g = prep(q), prep(k), prep(v), prep(beta), prep(g)
    gc = jnp.cumsum(g, axis=-1)
    causal = jnp.tril(jnp.ones((C, C), dtype=bool))
    strict = jnp.tril(jnp.ones((C, C), dtype=bool), k=-1)
    decay = jnp.exp(jnp.where(causal, gc[..., :, None] - gc[..., None, :], -jnp.inf))
    kb = k * beta[..., None]
    m = jnp.where(strict, jnp.einsum('nbhid,nbhjd->nbhij', kb, k) * decay, 0.0)
    eye = jnp.eye(C, dtype=f32)
    t_inv = lax.linalg.triangular_solve(eye + m, jnp.broadcast_to(eye, m.shape),
                                        left_side=True, lower=True, unit_diagonal=True)
    u = t_inv @ (v * beta[..., None])
    w = t_inv @ (kb * jnp.exp(gc)[..., None])
    qk = jnp.einsum('nbhid,nbhjd->nbhij', q, k) * decay
    q_dec = q * jnp.exp(gc)[..., None]
    k_dec = k * jnp.exp(gc[..., -1:] - gc)[..., None]
    g_last = jnp.exp(gc[..., -1])

    def step(s, xs):
        q_i, k_i, u_i, w_i, qk_i, gl = xs
        v_new = u_i - w_i @ s
        o = q_i @ s + qk_i @ v_new
        s = s * gl[..., None, None] + jnp.einsum('bhck,bhcv->bhkv', k_i, v_new)
        return s, o

    s, o = lax.scan(step, s0.astype(f32), (q_dec, k_dec, u, w, qk, g_last))
    o = o.transpose(1, 0, 3, 2, 4).reshape(B, -1, H, DV)[:, :T]
    return o, s


def gdn_mixer(qkv, z, b_raw, a_raw, conv_prev, s0, conv_w, a_log, dt_bias, norm_g):
    B, T, _ = qkv.shape
    y, new_conv = causal_conv(conv_prev, qkv, conv_w)
    q, k, v = split_cols(y, (GDN_QK_DIM, GDN_QK_DIM, GDN_V_DIM))
    q = l2_norm(q.reshape(B, T, GDN_HEADS, GDN_DK)) * (GDN_DK ** -0.5)
    k = l2_norm(k.reshape(B, T, GDN_HEADS, GDN_DK))
    v = v.reshape(B, T, GDN_HEADS, GDN_DV)
    beta = jax.nn.sigmoid(b_raw.astype(jnp.float32))
    g = -jnp.exp(a_log.astype(jnp.float32)) * jax.nn.softplus(a_raw.astype(jnp.float32) + dt_bias.astype(jnp.float32))
    o, s = gated_delta_chunked(q, k, v, beta, g, s0)
    o = rms_norm(o, norm_g) * jax.nn.silu(z.reshape(B, T, GDN_HEADS, GDN_DV).astype(jnp.float32))
    return o.reshape(B, T, GDN_V_DIM).astype(qkv.dtype), new_conv, s


def mla_project(q_raw, ckv_raw, kr_raw, pos, qn_g, qr_g, ckv_g, kr_g):
    B, T, _ = q_raw.shape
    q = q_raw.reshape(B, T, MLA_HEADS, MLA_NOPE + MLA_ROPE)
    cos, sin = rope_angles(pos)
    qn = rms_norm(q[..., :MLA_NOPE], qn_g)
    qr = apply_rope(rms_norm(q[..., MLA_NOPE:], qr_g), cos[:, None, :], sin[:, None, :])
    c = rms_norm(ckv_raw, ckv_g)
    kr = apply_rope(rms_norm(kr_raw, kr_g), cos, sin)
    return qn, qr, c, kr


def mla_keys(c, w_uk, kn_g):
    return rms_norm(jnp.einsum('...tr,rhd->...thd', c, w_uk), kn_g)


def mla_core(qn, qr, kn, kr, c, q_pos, k_pos):
    f32 = jnp.float32
    scale = (MLA_NOPE + MLA_ROPE) ** -0.5
    s = (jnp.einsum('qhd,khd->hqk', qn.astype(f32), kn.astype(f32))
         + jnp.einsum('qhd,kd->hqk', qr.astype(f32), kr.astype(f32))) * scale
    s = jnp.where(k_pos[None, None, :] <= q_pos[None, :, None], s, -1e30)
    p = jax.nn.softmax(s, axis=-1)
    return jnp.einsum('hqk,kr->qhr', p.astype(c.dtype), c)


def prompt_mla(qn, qr, c, kr, w_uk, kn_g):
    B, T = c.shape[:2]
    qb = min(ATTN_Q_BLOCK, T)
    nb = T // qb
    pos = jnp.arange(T, dtype=jnp.int32)
    kn = mla_keys(c, w_uk, kn_g)
    core = jax.vmap(mla_core, in_axes=(0, 0, 0, 0, 0, None, None))

    def block(xs):
        qn_b, qr_b, qpos_b = xs
        return core(qn_b, qr_b, kn, kr, c, qpos_b, pos)

    qn_blocks = qn.reshape(B, nb, qb, MLA_HEADS, MLA_NOPE).swapaxes(0, 1)
    qr_blocks = qr.reshape(B, nb, qb, MLA_HEADS, MLA_ROPE).swapaxes(0, 1)
    ctx = lax.map(block, (qn_blocks, qr_blocks, pos.reshape(nb, qb)))
    return ctx.swapaxes(0, 1).reshape(B, T, MLA_HEADS, MLA_KV_RANK)


def sample_mla(qn, qr, c, kr, w_uk, kn_g, cache_c, cache_kr, page_table):
    S = c.shape[1]
    past = page_table.shape[1] * PAGE_SIZE
    k_pos = jnp.arange(past + S, dtype=jnp.int32)
    q_pos = past + jnp.arange(S, dtype=jnp.int32)

    def one(xs):
        pt, qn_s, qr_s, c_s, kr_s = xs
        c_all = jnp.concatenate([cache_c[pt].reshape(past, MLA_KV_RANK).astype(c_s.dtype), c_s], axis=0)
        kr_all = jnp.concatenate([cache_kr[pt].reshape(past, MLA_ROPE).astype(kr_s.dtype), kr_s], axis=0)
        kn = mla_keys(c_all, w_uk, kn_g)
        return mla_core(qn_s, qr_s, kn, kr_all, c_all, q_pos, k_pos)

    return lax.map(one, (page_table, qn, qr, c, kr))


def decoder_layer(x, cond, pos, conv_prev, s0, mla_attend, lp):
    B, T, _ = x.shape
    mods = (jax.nn.silu(cond) @ lp['ada_w'] + lp['ada_b']).reshape(B, N_MOD, 1, D_MODEL)
    h = rms_norm(x, lp['norm_ffn1']) * (1 + mods[:, 1]) + mods[:, 0]
    x = x + 0.5 * mods[:, 2] * swiglu_ffn(h, lp['ffn1_wi'], lp['ffn1_wo'])
    h = rms_norm(x, lp['norm_mix']) * (1 + mods[:, 4]) + mods[:, 3]
    qkv, z, b_raw, a_raw, q_raw, ckv_raw, kr_raw = split_cols(h @ lp['w_in'], IN_SPLITS)
    gdn_out, new_conv, new_s = gdn_mixer(qkv, z, b_raw, a_raw, conv_prev, s0, lp['gdn_conv_w'],
                                         lp['gdn_a_log'], lp['gdn_dt_bias'], lp['gdn_norm'])
    qn, qr, c, kr = mla_project(q_raw, ckv_raw, kr_raw, pos, lp['mla_qn_norm'], lp['mla_qr_norm'],
                                lp['mla_ckv_norm'], lp['mla_kr_norm'])
    ctx = mla_attend(qn, qr, c, kr, lp['mla_w_uk'], lp['mla_kn_norm'])
    mla_out = jnp.einsum('bthr,rhv->bthv', ctx, lp['mla_w_uv']).reshape(B, T, MLA_HEADS * MLA_V)
    mix = jnp.concatenate([gdn_out, mla_out.astype(gdn_out.dtype)], axis=-1) @ lp['w_out']
    x = x + mods[:, 5] * mix
    h = rms_norm(x, lp['norm_ffn2']) * (1 + mods[:, 7]) + mods[:, 6]
    x = x + 0.5 * mods[:, 8] * swiglu_ffn(h, lp['ffn2_wi'], lp['ffn2_wo'])
    return x, c, kr, new_conv, new_s


def setup_inputs(seed: int = 0) -> dict:
    key = jax.random.key(seed)
    ks = iter(jax.random.split(key, 48))
    f32 = jnp.float32
    n_pages = PAST_LEN // PAGE_SIZE
    n_pool = (DEC_BATCH * n_pages * 5) // 4

    def normal(shape, scale=1.0):
        return jax.random.normal(next(ks), shape, f32) * scale

    def gain(shape):
        return 1.0 + 0.02 * normal(shape)

    page_table = jax.random.permutation(next(ks), n_pool)[:DEC_BATCH * n_pages]
    page_table = page_table.reshape(DEC_BATCH, n_pages).astype(jnp.int32)
    dt = jnp.exp(jax.random.uniform(next(ks), (DEPTH, GDN_HEADS), f32, math.log(1e-3), math.log(1e-1)))
    a_log = jnp.log(jax.random.uniform(next(ks), (DEPTH, GDN_HEADS), f32, 1.0, 16.0))
    return {
        'x_prompt': normal((BATCH, SEQ, D_MODEL)),
        'x_sample': normal((DEC_BATCH, DEC_SEQ, D_MODEL)),
        'cache_ckv': normal((DEPTH, n_pool, PAGE_SIZE, MLA_KV_RANK)),
        'cache_krope': normal((DEPTH, n_pool, PAGE_SIZE, MLA_ROPE)),
        'state_conv': normal((DEPTH, DEC_BATCH, GDN_CONV - 1, GDN_CONV_DIM)),
        'state_gdn': normal((DEPTH, DEC_BATCH, GDN_HEADS, GDN_DK, GDN_DV), 0.1),
        'page_table': page_table,
        'c_prompt': normal((BATCH, D_MODEL)),
        'c_sample': normal((DEC_BATCH, D_MODEL)),
        'ada_w': normal((DEPTH, D_MODEL, N_MOD * D_MODEL), D_MODEL ** -0.5),
        'ada_b': normal((DEPTH, N_MOD * D_MODEL), 0.02),
        'norm_ffn1': gain((DEPTH, D_MODEL)),
        'ffn1_wi': normal((DEPTH, D_MODEL, 2 * D_FF), D_MODEL ** -0.5),
        'ffn1_wo': normal((DEPTH, D_FF, D_MODEL), D_FF ** -0.5),
        'norm_mix': gain((DEPTH, D_MODEL)),
        'w_in': normal((DEPTH, D_MODEL, IN_DIM), D_MODEL ** -0.5),
        'gdn_conv_w': normal((DEPTH, GDN_CONV, GDN_CONV_DIM), GDN_CONV ** -0.5),
        'gdn_a_log': a_log,
        'gdn_dt_bias': dt + jnp.log(-jnp.expm1(-dt)),
        'gdn_norm': gain((DEPTH, GDN_DV)),
        'mla_qn_norm': gain((DEPTH, MLA_NOPE)),
        'mla_qr_norm': gain((DEPTH, MLA_ROPE)),
        'mla_ckv_norm': gain((DEPTH, MLA_KV_RANK)),
        'mla_kr_norm': gain((DEPTH, MLA_ROPE)),
        'mla_kn_norm': gain((DEPTH, MLA_NOPE)),
        'mla_w_uk': normal((DEPTH, MLA_KV_RANK, MLA_HEADS, MLA_NOPE), MLA_KV_RANK ** -0.5),
        'mla_w_uv': normal((DEPTH, MLA_KV_RANK, MLA_HEADS, MLA_V), MLA_KV_RANK ** -0.5),
        'w_out': normal((DEPTH, MIX_WIDTH, D_MODEL), MIX_WIDTH ** -0.5),
        'norm_ffn2': gain((DEPTH, D_MODEL)),
        'ffn2_wi': normal((DEPTH, D_MODEL, 2 * D_FF), D_MODEL ** -0.5),
        'ffn2_wo': normal((DEPTH, D_FF, D_MODEL), D_FF ** -0.5),
    }


def reference(x_prompt, x_sample, cache_ckv, cache_krope, state_conv, state_gdn, page_table,
              c_prompt, c_sample, ada_w, ada_b, norm_ffn1, ffn1_wi, ffn1_wo, norm_mix, w_in,
              gdn_conv_w, gdn_a_log, gdn_dt_bias, gdn_norm, mla_qn_norm, mla_qr_norm, mla_ckv_norm,
              mla_kr_norm, mla_kn_norm, mla_w_uk, mla_w_uv, w_out, norm_ffn2, ffn2_wi, ffn2_wo):
    bp, tp = x_prompt.shape[:2]
    ts = x_sample.shape[1]
    pos_p = jnp.arange(tp, dtype=jnp.float32)
    pos_s = PAST_LEN + jnp.arange(ts, dtype=jnp.float32)
    yp, ys = x_prompt, x_sample
    ckv_p, kr_p, conv_p, gdn_p = [], [], [], []
    ckv_s, kr_s, conv_s, gdn_s = [], [], [], []
    for l in range(DEPTH):
        lp = {
            'ada_w': ada_w[l], 'ada_b': ada_b[l],
            'norm_ffn1': norm_ffn1[l], 'ffn1_wi': ffn1_wi[l], 'ffn1_wo': ffn1_wo[l],
            'norm_mix': norm_mix[l], 'w_in': w_in[l],
            'gdn_conv_w': gdn_conv_w[l], 'gdn_a_log': gdn_a_log[l], 'gdn_dt_bias': gdn_dt_bias[l],
            'gdn_norm': gdn_norm[l],
            'mla_qn_norm': mla_qn_norm[l], 'mla_qr_norm': mla_qr_norm[l], 'mla_ckv_norm': mla_ckv_norm[l],
            'mla_kr_norm': mla_kr_norm[l], 'mla_kn_norm': mla_kn_norm[l],
            'mla_w_uk': mla_w_uk[l], 'mla_w_uv': mla_w_uv[l], 'w_out': w_out[l],
            'norm_ffn2': norm_ffn2[l], 'ffn2_wi': ffn2_wi[l], 'ffn2_wo': ffn2_wo[l],
        }
        conv0 = jnp.zeros((bp, GDN_CONV - 1, GDN_CONV_DIM), dtype=yp.dtype)
        s0 = jnp.zeros((bp, GDN_HEADS, GDN_DK, GDN_DV), dtype=jnp.float32)
        yp, c_new, kr_new, cv_new, s_new = decoder_layer(yp, c_prompt, pos_p, conv0, s0, prompt_mla, lp)
        ckv_p.append(c_new); kr_p.append(kr_new); conv_p.append(cv_new); gdn_p.append(s_new)
        attend_s = functools.partial(sample_mla, cache_c=cache_ckv[l], cache_kr=cache_krope[l],
                                     page_table=page_table)
        ys, c_new, kr_new, cv_new, s_new = decoder_layer(ys, c_sample, pos_s, state_conv[l], state_gdn[l],
                                                         attend_s, lp)
        ckv_s.append(c_new); kr_s.append(kr_new); conv_s.append(cv_new); gdn_s.append(s_new)
    new_ckv_prompt = jnp.stack(ckv_p)
    new_krope_prompt = jnp.stack(kr_p)
    new_conv_prompt = jnp.stack(conv_p)
    new_gdn_prompt = jnp.stack(gdn_p)
    new_ckv_sample = jnp.stack(ckv_s)
    new_krope_sample = jnp.stack(kr_s)
    new_conv_sample = jnp.stack(conv_s)
    new_gdn_sample = jnp.stack(gdn_s)
    return (yp, ys, new_ckv_prompt, new_krope_prompt, new_conv_prompt, new_gdn_prompt,
            new_ckv_sample, new_krope_sample, new_conv_sample, new_gdn_sample)
```

```python
import numpy as np
from contextlib import ExitStack
import concourse.bass as bass
import concourse.mybir as mybir
from concourse.bass_utils import run_bass_kernel_spmd

F32 = mybir.dt.float32
BF16 = mybir.dt.bfloat16
I32 = mybir.dt.int32
U32 = mybir.dt.uint32
AF = mybir.ActivationFunctionType
ALU = mybir.AluOpType
AX = mybir.AxisListType

D = 1024
DFF = 2816
NFC = DFF // 128
EPS = 1e-6

class Res:
    __slots__ = ("name", "w", "r")

    def __init__(self, name=""):
        self.name = name
        self.w = None
        self.r = {}


class DSem:
    __slots__ = ("sem", "cnt", "name")

    def __init__(self, sem, name):
        self.sem = sem
        self.cnt = 0
        self.name = name


class Sched:
    ENGS = ("pe", "act", "dve", "pool", "sp")

    def __init__(self, nc, stack):
        self.nc = nc
        self.stack = stack
        self.q = {k: [] for k in self.ENGS}
        self.esem = {k: stack.enter_context(nc.semaphore("es_" + k)) for k in self.ENGS}
        self.ecnt = {k: 0 for k in self.ENGS}
        self.known = {k: {} for k in self.ENGS}
        self.dsems = []
        self.nwait = 0
        self.nop = 0

    def dsem(self, name):
        s = self.stack.enter_context(self.nc.semaphore("ds_" + name))
        d = DSem(s, name)
        self.dsems.append(d)
        return d

    def _waits(self, eng, reads, writes):
        need = {}

        def add(ev):
            sem_id, sem, val, src = ev
            if src == "pe" and eng == "pe":
                return
            if self.known[eng].get(sem_id, 0) >= val:
                return
            if sem_id not in need or need[sem_id][1] < val:
                need[sem_id] = (sem, val)

        for r in reads:
            if r.w is not None:
                add(r.w)
        for w in writes:
            if w.w is not None:
                add(w.w)
            for ev in w.r.values():
                add(ev)
        for sem_id, (sem, val) in need.items():
            self.q[eng].append(("wait", sem, val))
            self.known[eng][sem_id] = val
            self.nwait += 1

    def _record(self, ev, reads, writes):
        for r in reads:
            old = r.r.get(ev[0])
            if old is None or old[2] < ev[2]:
                r.r[ev[0]] = ev
        for w in writes:
            w.w = ev
            w.r = {}

    def op(self, eng, meth, kw, reads=(), writes=(), inc=True):
        self._waits(eng, reads, writes)
        if inc:
            self.ecnt[eng] += 1
            ev = (eng, self.esem[eng], self.ecnt[eng], eng)
        else:
            ev = (eng, self.esem[eng], self.ecnt[eng] + 1, eng)
        self.q[eng].append(("op", meth, kw, inc))
        self.nop += 1
        self._record(ev, reads, writes)

    def dma(self, q, ds, out, in_, reads=(), writes=(), **kw):
        self._waits(q, reads, writes)
        ds.cnt += 16
        ev = (id(ds), ds.sem, ds.cnt, "dma")
        self.q[q].append(("dma", out, in_, ds.sem, kw))
        self.nop += 1
        self._record(ev, reads, writes)

    def idma(self, ds, reads=(), writes=(), **kw):
        self._waits("pool", reads, writes)
        ds.cnt += 16
        ev = (id(ds), ds.sem, ds.cnt, "dma")
        self.q["pool"].append(("idma", kw, ds.sem))
        self.nop += 1
        self._record(ev, reads, writes)

    def barrier(self):
        for e in self.ENGS:
            for e2 in self.ENGS:
                if e2 == e:
                    continue
                v = self.ecnt[e2]
                if v > 0 and self.known[e].get(e2, 0) < v:
                    self.q[e].append(("wait", self.esem[e2], v))
                    self.known[e][e2] = v
            for d in self.dsems:
                if d.cnt > 0 and self.known[e].get(id(d), 0) < d.cnt:
                    self.q[e].append(("wait", d.sem, d.cnt))
                    self.known[e][id(d)] = d.cnt

    def emit(self):
        import os
        if os.environ.get("EMITLOG"):
            print("EMIT", {k: len(v) for k, v in self.q.items()}, "cnt", dict(self.ecnt), "ndsem", len(self.dsems))
        nc = self.nc
        engs = {"pe": "tensor", "act": "scalar", "dve": "vector", "pool": "gpsimd", "sp": "sync"}
        with nc.Block() as block:
            for k, attr in engs.items():
                items = self.q[k]
                esem = self.esem[k]

                def body(e, items=items, esem=esem):
                    for it in items:
                        if it[0] == "wait":
                            e.wait_ge(it[1], it[2])
                        elif it[0] == "op":
                            ins = getattr(e, it[1])(**it[2])
                            if it[3]:
                                ins.then_inc(esem, 1)
                        elif it[0] == "idma":
                            e.indirect_dma_start(**it[1]).then_inc(it[2], 16)
                        else:
                            _, out, in_, sem, kw = it
                            e.dma_start(out=out, in_=in_, **kw).then_inc(sem, 16)

                getattr(block, attr)(body)
        self.q = {k: [] for k in self.ENGS}


class Ctx:
    pass


def sb(cx, stack, name, shape, dt):
    return stack.enter_context(cx.nc.sbuf_tensor(name, list(shape), dt))


def row_bcast(ap_row, n):
    t = ap_row.tensor
    F = ap_row.shape[-1]
    return bass.AP(t, ap_row.offset, [[0, n], [1, F]])


def load_weight_bf16(cx, S, name, dst, dst_res, w_ap, kchunks, cols, col_piece):
    with ExitStack() as st:
        stg = [sb(cx, st, name + "stg%d" % i, [128, col_piece], F32) for i in range(3)]
        r_stg = [Res() for _ in range(3)]
        d_stg = [S.dsem(name + "stg%d" % i) for i in range(3)]
        engs = (("dve", "tensor_copy"), ("pool", "tensor_copy"), ("act", "activation"))
        i = 0
        for c in range(kchunks):
            for c0 in range(0, cols, col_piece):
                c1 = min(cols, c0 + col_piece)
                k = i % 3
                i += 1
                S.dma("sp", d_stg[k], stg[k][:, :c1 - c0], w_ap[c * 128:(c + 1) * 128, c0:c1], writes=(r_stg[k],))
                eng, meth = engs[k]
                kw = dict(out=dst[:, c, c0:c1], in_=stg[k][:, :c1 - c0])
                if meth == "activation":
                    kw["func"] = AF.Copy
                S.op(eng, meth, kw, reads=(r_stg[k],), writes=(dst_res,))
        S.barrier()
        S.emit()


def norm_mod_transpose(cx, S, bufs, src_ap, src_res, m, hT_dst, pt_k):
    (xs, tt, junk, hb, stat, GG, Bt, r_xs, r_tt, r_junk, r_hb, r_stat, r_GG, r_Bt, r_hT, d_xs) = bufs
    S.dma("sp", d_xs, xs[:m, :], src_ap, reads=(src_res,), writes=(r_xs,))
    S.op("act", "activation", dict(out=junk[:m, :], in_=xs[:m, :], func=AF.Square, accum_out=stat[:m, 0:1]),
         reads=(r_xs,), writes=(r_junk, r_stat))
    S.op("act", "activation", dict(out=stat[:m, 1:2], in_=stat[:m, 0:1], func=AF.Sqrt, scale=1.0 / D,
                                   bias=cx.eps_t[:m, 0:1]), reads=(r_stat,), writes=(r_stat,))
    S.op("dve", "reciprocal", dict(out=stat[:m, 2:3], in_=stat[:m, 1:2]), reads=(r_stat,), writes=(r_stat,))
    S.op("dve", "scalar_tensor_tensor", dict(out=tt[:m, :], in0=xs[:m, :], scalar=stat[:m, 2:3], in1=GG[:m, :],
                                             op0=ALU.mult, op1=ALU.mult),
         reads=(r_xs, r_stat, r_GG), writes=(r_tt,))
    S.op("pool", "tensor_tensor", dict(out=hb[:m, :], in0=tt[:m, :], in1=Bt[:m, :], op=ALU.add),
         reads=(r_tt, r_Bt), writes=(r_hb,))
    ptb = cx.ps[pt_k].bitcast(BF16)
    for c in range(8):
        S.op("pe", "transpose", dict(out=ptb[:, c * 128:c * 128 + m], in_=hb[:m, c * 128:(c + 1) * 128],
                                     identity=cx.ident_bf[:m, :m]),
             reads=(r_hb,), writes=(cx.rps[pt_k],), inc=(c == 7))
    S.op("act", "activation", dict(out=hT_dst, in_=ptb.rearrange("p (c t) -> p c t", c=8)[:, :, :m], func=AF.Copy),
         reads=(cx.rps[pt_k],), writes=(r_hT,))


def load_mod_tiles(cx, S, g_ap, mods, mod_base, n, gscale, GG, Bt, Gt, tt, r_GG, r_Bt, r_Gt, r_tt, d_m):
    S.dma("sp", d_m, tt[:n, :], row_bcast(g_ap, n), writes=(r_tt,))
    S.dma("sp", d_m, GG[:n, :], mods[mod_base + 1, :n, :], writes=(r_GG,))
    S.dma("sp", d_m, Bt[:n, :], mods[mod_base + 0, :n, :], writes=(r_Bt,))
    if Gt is not None:
        S.dma("sp", d_m, Gt[:n, :], mods[mod_base + 2, :n, :], writes=(r_Gt,))
    S.barrier()
    S.op("dve", "scalar_tensor_tensor", dict(out=GG[:n, :], in0=GG[:n, :], scalar=1.0, in1=tt[:n, :],
                                             op0=ALU.add, op1=ALU.mult), reads=(r_tt,), writes=(r_GG,))
    if Gt is not None and gscale != 1.0:
        S.op("dve", "tensor_scalar", dict(out=Gt[:n, :], in0=Gt[:n, :], scalar1=float(gscale), scalar2=None,
                                          op0=ALU.mult), writes=(r_Gt,))


def phase_ffn(cx, S, name, groups, wi_ap, wo_ap, g_ap, mod_base):
    with ExitStack() as st:
        wi = sb(cx, st, name + "wi", [128, 8, 2 * DFF], BF16)
        wo = sb(cx, st, name + "wo", [128, NFC, D], BF16)
        r_wi, r_wo = Res("wi"), Res("wo")
        load_weight_bf16(cx, S, name + "wi", wi, r_wi, wi_ap, 8, 2 * DFF, DFF)
        load_weight_bf16(cx, S, name + "wo", wo, r_wo, wo_ap, NFC, D, D)
        GG = sb(cx, st, name + "GG", [128, D], F32)
        Bt = sb(cx, st, name + "Bt", [128, D], F32)
        Gt = sb(cx, st, name + "Gt", [128, D], F32)
        xs = sb(cx, st, name + "xs", [128, D], F32)
        tt = sb(cx, st, name + "tt", [128, D], F32)
        junk = sb(cx, st, name + "junk", [128, D], BF16)
        hb = sb(cx, st, name + "hb", [128, D], BF16)
        hT = sb(cx, st, name + "hT", [128, 8, 512], BF16)
        sg = [sb(cx, st, name + "sg%d" % i, [128, 512], BF16) for i in range(2)]
        actT = sb(cx, st, name + "actT", [128, NFC, 512], BF16)
        xr = [sb(cx, st, name + "xr%d" % i, [128, D], F32) for i in range(2)]
        tmp = sb(cx, st, name + "tmp", [128, 512], F32)
        stat = sb(cx, st, name + "stat", [128, 4], F32)

        r_GG, r_Bt, r_Gt, r_xs, r_tt, r_junk, r_hb, r_hT = (Res(n_) for n_ in
                                                              ("GG", "Bt", "Gt", "xs", "tt", "junk", "hb", "hT"))
        r_sg = [Res("sg0"), Res("sg1")]
        r_actT = [Res("actT%d" % j) for j in range(NFC)]
        r_xr = [Res("xr0"), Res("xr1")]
        r_tmp, r_stat = Res("tmp"), Res("stat")
        d_m, d_xs = S.dsem(name + "m"), S.dsem(name + "xs")
        d_xr = [S.dsem(name + "xr0"), S.dsem(name + "xr1")]
        d_xst = [S.dsem(name + "xst0"), S.dsem(name + "xst1")]
        bufs = (xs, tt, junk, hb, stat, GG, Bt, r_xs, r_tt, r_junk, r_hb, r_stat, r_GG, r_Bt, r_hT, d_xs)


        pg, pu, po = cx.ps[0:2], cx.ps[2:4], cx.ps[4:6]
        r_pg, r_pu, r_po = cx.rps[0:2], cx.rps[2:4], cx.rps[4:6]
        cnt = {"up": 0, "po": 0, "pt": 0, "xr": 0}

        for g in groups:
            n = g["n"]
            load_mod_tiles(cx, S, g_ap, g["mods"], mod_base, n, 0.5, GG, Bt, Gt, tt, r_GG, r_Bt, r_Gt, r_tt, d_m)
            T = g["T"]
            nblk = (T + 511) // 512

            def stage_norm(b):
                t0 = b * 512
                tb = min(512, T - t0)
                for s_ in range((tb + 127) // 128):
                    r0 = t0 + s_ * 128
                    m = min(128, T - r0)
                    k = 6 + cnt["pt"] % 2
                    cnt["pt"] += 1
                    norm_mod_transpose(cx, S, bufs, g["src"][r0:r0 + m, :], g["src_res"][r0 // 128], m,
                                       hT[:, :, s_ * 128:s_ * 128 + m], k)

            def stage_up(b):
                t0 = b * 512
                tb = min(512, T - t0)
                for j in range(NFC):
                    k = cnt["up"] % 2
                    cnt["up"] += 1
                    for kc in range(8):
                        S.op("pe", "matmul", dict(out=pg[k][:, :tb], lhsT=wi[:, kc, j * 128:(j + 1) * 128],
                                                  rhs=hT[:, kc, :tb], start=(kc == 0), stop=(kc == 7)),
                             reads=(r_wi, r_hT), writes=(r_pg[k],), inc=(kc == 7))
                    for kc in range(8):
                        S.op("pe", "matmul", dict(out=pu[k][:, :tb], lhsT=wi[:, kc, DFF + j * 128:DFF + (j + 1) * 128],
                                                  rhs=hT[:, kc, :tb], start=(kc == 0), stop=(kc == 7)),
                             reads=(r_wi, r_hT), writes=(r_pu[k],), inc=(kc == 7))
                    S.op("act", "activation", dict(out=sg[k][:, :tb], in_=pg[k][:, :tb], func=AF.Silu),
                         reads=(r_pg[k],), writes=(r_sg[k],))
                    S.op("dve", "tensor_tensor", dict(out=actT[:, j, :tb], in0=sg[k][:, :tb], in1=pu[k][:, :tb],
                                                      op=ALU.mult),
                         reads=(r_sg[k], r_pu[k]), writes=(r_actT[j],))

            def stage_down(b):
                t0 = b * 512
                tb = min(512, T - t0)
                for s_ in range((tb + 127) // 128):
                    r0 = t0 + s_ * 128
                    m = min(128, T - r0)
                    kx = cnt["xr"] % 2
                    cnt["xr"] += 1
                    S.dma("sp", d_xr[kx], xr[kx][:m, :], g["src"][r0:r0 + m, :], reads=(g["src_res"][r0 // 128],),
                          writes=(r_xr[kx],))
                    for half in range(2):
                        k = cnt["po"] % 2
                        cnt["po"] += 1
                        for j in range(NFC):
                            S.op("pe", "matmul", dict(out=po[k][:m, :], lhsT=actT[:, j, s_ * 128:s_ * 128 + m],
                                                      rhs=wo[:, j, half * 512:(half + 1) * 512],
                                                      start=(j == 0), stop=(j == NFC - 1)),
                                 reads=(r_wo, r_actT[j]), writes=(r_po[k],), inc=(j == NFC - 1))
                        S.op("dve", "tensor_tensor", dict(out=tmp[:m, :], in0=po[k][:m, :],
                                                          in1=Gt[:m, half * 512:(half + 1) * 512], op=ALU.mult),
                             reads=(r_po[k], r_Gt), writes=(r_tmp,))
                        S.op("pool", "tensor_tensor", dict(out=xr[kx][:m, half * 512:(half + 1) * 512],
                                                           in0=xr[kx][:m, half * 512:(half + 1) * 512],
                                                           in1=tmp[:m, :], op=ALU.add),
                             reads=(r_tmp,), writes=(r_xr[kx],))
                    S.dma("pool", d_xst[kx], g["dst"][r0:r0 + m, :], xr[kx][:m, :], reads=(r_xr[kx],),
                          writes=(g["dst_res"][r0 // 128],))

            stage_norm(0)
            for b in range(nblk):
                stage_up(b)
                if b + 1 < nblk:
                    stage_norm(b + 1)
                stage_down(b)
        S.barrier()
        S.emit()


def phase_adaln(cx, S, c_rep, ada_w, ada_b, mods_p, mods_s, r_mods):
    with ExitStack() as st:
        ct = sb(cx, st, "ad_ct", [128, D], F32)
        cb_ = sb(cx, st, "ad_cb", [128, D], BF16)
        cT = sb(cx, st, "ad_cT", [128, 8, 192], BF16)
        w = [sb(cx, st, "ad_w%d" % i, [128, 8, D], BF16) for i in range(2)]
        wst = [sb(cx, st, "ad_wst%d" % i, [128, 8, D], F32) for i in range(2)]
        bb = [sb(cx, st, "ad_b%d" % i, [128, D], F32) for i in range(2)]
        o = [sb(cx, st, "ad_o%d" % i, [128, D], F32) for i in range(2)]
        r_ct, r_cb, r_cT = Res(), Res(), Res()
        r_w, r_wst, r_bb, r_o = [Res(), Res()], [Res(), Res()], [Res(), Res()], [Res(), Res()]
        d_ct = S.dsem("ad_ct")
        d_w = [S.dsem("ad_w0"), S.dsem("ad_w1")]
        d_b = [S.dsem("ad_b0"), S.dsem("ad_b1")]
        d_o = [S.dsem("ad_o0"), S.dsem("ad_o1")]

        def load(blk):
            k = blk % 2
            for hf in range(2):
                S.dma("sp", d_w[k], wst[k][:, hf * 4:(hf + 1) * 4, :],
                      ada_w[hf * 512:(hf + 1) * 512, blk * D:(blk + 1) * D].rearrange("(c p) f -> p c f", p=128),
                      writes=(r_wst[k],))
            S.dma("sp", d_b[k], bb[k][:, :], row_bcast(ada_b[0:1, blk * D:(blk + 1) * D], 128), writes=(r_bb[k],))

        load(0)
        for gi, (r0, m) in enumerate(((0, 128), (128, 64))):
            S.dma("sp", d_ct, ct[:m, :], c_rep[r0:r0 + m, :], writes=(r_ct,))
            S.op("act", "activation", dict(out=cb_[:m, :], in_=ct[:m, :], func=AF.Silu), reads=(r_ct,), writes=(r_cb,))
            ptb = cx.ps[6 + gi].bitcast(BF16)
            for c in range(8):
                S.op("pe", "transpose", dict(out=ptb[:, c * 128:c * 128 + m], in_=cb_[:m, c * 128:(c + 1) * 128],
                                             identity=cx.ident_bf[:m, :m]),
                     reads=(r_cb,), writes=(cx.rps[6 + gi],), inc=(c == 7))
            S.op("act", "activation", dict(out=cT[:, :, r0:r0 + m],
                                           in_=ptb.rearrange("p (c t) -> p c t", c=8)[:, :, :m], func=AF.Copy),
                 reads=(cx.rps[6 + gi],), writes=(r_cT,))
        for blk in range(9):
            k = blk % 2
            if blk + 1 < 9:
                load(blk + 1)
            S.op("pool", "tensor_copy", dict(out=w[k][:, 0:4, :], in_=wst[k][:, 0:4, :]),
                 reads=(r_wst[k],), writes=(r_w[k],))
            S.op("dve", "tensor_copy", dict(out=w[k][:, 4:8, :], in_=wst[k][:, 4:8, :]),
                 reads=(r_wst[k],), writes=(r_w[k],))
            for gi, (r0, m, dst) in enumerate(((0, 128, mods_p), (128, 64, mods_s))):
                ko = gi
                for half in range(2):
                    pk = (blk * 4 + gi * 2 + half) % 4
                    ps, rp = cx.ps[pk], cx.rps[pk]
                    for c in range(8):
                        S.op("pe", "matmul", dict(out=ps[:m, :], lhsT=cT[:, c, r0:r0 + m],
                                                  rhs=w[k][:, c, half * 512:(half + 1) * 512], start=(c == 0),
                                                  stop=(c == 7)), reads=(r_cT, r_w[k]), writes=(rp,), inc=(c == 7))
                    S.op("dve", "tensor_tensor", dict(out=o[ko][:m, half * 512:(half + 1) * 512], in0=ps[:m, :],
                                                      in1=bb[k][:m, half * 512:(half + 1) * 512], op=ALU.add),
                         reads=(rp, r_bb[k]), writes=(r_o[ko],))
                S.dma("act", d_o[ko], dst[blk, :m, :], o[ko][:m, :], reads=(r_o[ko],), writes=(r_mods,))
        S.barrier()
        S.emit()


NLEV = 5
QK_SCALE = 64 ** -0.5
ATT_SCALE = 96 ** -0.5


def bc(ap2, n):
    return ap2.unsqueeze(2).broadcast_to([ap2.shape[0], ap2.shape[1], n])


def bc_mid(ap2, n):
    return ap2.unsqueeze(1).broadcast_to([ap2.shape[0], n, ap2.shape[1]])


class MixBufs:
    pass


def mixer_alloc(cx, S, st, name, P):
    B = MixBufs()

    def t(nm, shape, dt):
        tt_ = sb(cx, st, name + nm, shape, dt)
        setattr(B, nm, tt_)
        setattr(B, "r_" + nm, Res(nm))
        return tt_

    t("win", [128, 8, 2992], BF16)
    load_weight_bf16(cx, S, "mxwin", B.win, B.r_win, P["w_in"], 8, 2992, 1496)
    t("GG", [128, D], F32); t("Bt", [128, D], F32)
    t("xs", [128, D], F32); t("tt", [128, D], F32); t("junk", [128, D], BF16); t("hb", [128, D], BF16)
    t("stat", [128, 4], F32)
    t("hT", [128, 8, 256], BF16)
    t("xq", [128, 12, 259], F32)
    t("yc", [128, 12, 256], F32)
    t("sqb", [128, 512], BF16)
    t("rinv", [128, 512], F32)
    t("qkT", [128, 8, 256], BF16)
    t("tm", [128, 1456], F32)
    t("sc", [128, 96], F32)
    t("rows", [8, 2, 128], F32)
    t("glb", [128, 8], F32)
    t("vtok", [128, 512], BF16); t("ktok", [128, 512], BF16); t("kdec", [128, 512], BF16)
    t("decq", [128, 512], F32); t("decb", [128, 512], F32)
    for g in range(2):
        for nm in ("A", "M", "P"):
            for k in range(2):
                t("%s%d%d" % (nm, g, k), [128, 512], F32)
        t("Pf%d" % g, [128, 512], BF16)
    t("qk", [128, 1024], BF16)
    t("u", [128, 512], F32)
    t("wT", [128, 4, 128], BF16)
    t("vnew", [128, 512], BF16)
    t("o1", [128, 512], F32)
    t("osb", [128, 512], F32)
    t("ost", [128, 16], F32)
    t("zs", [128, 512], F32)
    t("gout", [128, 512], BF16)
    t("S32", [128, 512], F32); t("Sbf", [128, 512], BF16)
    t("mq", [128, 768], F32); t("mqs", [128, 32], F32); t("Qb", [128, 768], BF16)
    t("cn", [128, 128], F32); t("krn", [128, 32], F32); t("krr", [128, 32], F32); t("qrr", [128, 8, 32], F32)
    t("cs", [128, 32], F32)
    t("cw", [128, 12, 4], F32)
    t("g8", [128, 16], F32)
    t("gng", [128, 64], F32); t("qng", [128, 64], F32); t("qrg", [128, 32], F32); t("ckg", [128, 128], F32)
    t("krg", [128, 32], F32); t("kng", [128, 64], F32)
    t("nc3", [128, 1536], F32)
    B.d = {}
    return B


def mixer_group(cx, S, B, name, grp, P):
    def dsem(k):
        if k not in B.d:
            B.d[k] = S.dsem(name + k)
        return B.d[k]

    def op(eng, meth, reads, writes, inc=True, **kw):
        S.op(eng, meth, kw, reads=reads, writes=writes, inc=inc)

    ps, rps = cx.ps, cx.rps
    T, nseq = grp["T"], grp["nseq"]
    Ts = T // nseq
    prompt = (nseq == 1)
    n = grp["n"]
    load_mod_tiles(cx, S, P["norm_mix"], grp["mods"], 3, n, 1.0, B.GG, B.Bt, None, B.tt, B.r_GG, B.r_Bt, None,
                   B.r_tt, dsem("m"))
    bufs = (B.xs, B.tt, B.junk, B.hb, B.stat, B.GG, B.Bt, B.r_xs, B.r_tt, B.r_junk, B.r_hb, B.r_stat, B.r_GG,
            B.r_Bt, B.r_hT, dsem("xs"))
    blk_T = 256 if prompt else T
    nblk = (T + blk_T - 1) // blk_T
    if prompt:
        op("dve", "memset", (), (B.r_S32,), ap=B.S32[:, :], constant=0.0)
        op("dve", "memset", (), (B.r_Sbf,), ap=B.Sbf[:, :], constant=0.0)
        op("pool", "memset", (), (B.r_xq,), ap=B.xq[:, :, 0:3], constant=0.0)
    ptk = [0]

    for b in range(nblk):
        t0 = b * blk_T
        tb = min(blk_T, T - t0)
        for s_ in range((tb + 127) // 128):
            r0 = t0 + s_ * 128
            m = min(128, T - r0)
            k = 6 + ptk[0] % 2
            ptk[0] += 1
            norm_mod_transpose(cx, S, bufs, grp["src"][r0:r0 + m, :], grp["src_res"][r0 // 128], m,
                               B.hT[:, :, s_ * 128:s_ * 128 + m], k)
        if prompt:
            xq_new = B.xq[:, :, 3:3 + tb]
            xqv = None
        else:
            xqv = B.xq[:, :, 0:nseq * 7].rearrange("p c (s t) -> p c s t", t=7)
            op("sp", "dma_start", (), (), ) if False else None
            S.dma("sp", dsem("nc3"), B.nc3[:nseq * 3, :], grp["conv0"].rearrange("s t c -> (s t) c"), writes=(B.r_nc3,))
            for c in range(12):
                kq = c % 2
                op("pe", "transpose", (B.r_nc3,), (rps[kq],), out=ps[kq][:, :nseq * 3],
                   in_=B.nc3[:nseq * 3, c * 128:(c + 1) * 128], identity=cx.ident_f[:nseq * 3, :nseq * 3])
                op("act" if c % 2 else "dve", "activation" if c % 2 else "tensor_copy", (rps[kq],), (B.r_xq,),
                   out=xqv[:, c, :, 0:3], in_=ps[kq][:, :nseq * 3].rearrange("p (s t) -> p s t", t=3),
                   **({"func": AF.Copy} if c % 2 else {}))
        for c in range(12):
            kq = c % 2
            for kc in range(8):
                op("pe", "matmul", (B.r_win, B.r_hT), (rps[kq],), inc=(kc == 7), out=ps[kq][:, :tb],
                   lhsT=B.win[:, kc, c * 128:(c + 1) * 128], rhs=B.hT[:, kc, :tb], start=(kc == 0), stop=(kc == 7))
            if prompt:
                dst, srcp = B.xq[:, c, 3:3 + tb], ps[kq][:, :tb]
            else:
                dst, srcp = xqv[:, c, :, 3:7], ps[kq][:, :tb].rearrange("p (s t) -> p s t", t=Ts)
            if c % 2:
                op("act", "activation", (rps[kq],), (B.r_xq,), out=dst, in_=srcp, func=AF.Copy)
            else:
                op("dve", "tensor_copy", (rps[kq],), (B.r_xq,), out=dst, in_=srcp)
        for c in range(12):
            if prompt:
                ydst = B.yc[:, c, :tb]
                xin = [B.xq[:, c, j:j + tb] for j in range(4)]
            else:
                ydst = B.yc[:, c, :tb].rearrange("p (s t) -> p s t", t=Ts)
                xin = [xqv[:, c, :, j:j + Ts] for j in range(4)]
            eng = "dve"
            op(eng, "tensor_scalar", (B.r_xq, B.r_cw), (B.r_yc,), out=ydst, in0=xin[0], scalar1=B.cw[:, c, 0:1],
               scalar2=None, op0=ALU.mult)
            for j in range(1, 4):
                op(eng, "scalar_tensor_tensor", (B.r_xq, B.r_cw), (B.r_yc,), out=ydst, in0=xin[j],
                   scalar=B.cw[:, c, j:j + 1], in1=ydst, op0=ALU.mult, op1=ALU.add)
        for c in range(12):
            op("act", "activation", (B.r_yc,), (B.r_yc,), out=B.yc[:, c, :tb], in_=B.yc[:, c, :tb], func=AF.Silu)
        if prompt and b + 1 < nblk:
            op("pool", "tensor_copy", (B.r_xq,), (B.r_xq,), out=B.xq[:, :, 0:3], in_=B.xq[:, :, tb:tb + 3])
        for c in range(8):
            kq = c % 2
            op("pool", "tensor_tensor", (B.r_yc,), (B.r_sqb,), out=B.sqb[:, :tb], in0=B.yc[:, c, :tb], in1=B.yc[:, c, :tb],
               op=ALU.mult)
            op("pe", "matmul", (B.r_sqb,), (rps[kq],), out=ps[kq][:, :tb], lhsT=cx.bones[:, :], rhs=B.sqb[:, :tb],
               start=True, stop=True)
            op("act", "activation", (rps[kq],), (B.r_rinv,), out=B.rinv[:, :tb], in_=ps[kq][:, :tb], func=AF.Sqrt,
               bias=cx.eps_t[:, 0:1], scale=1.0)
            op("dve", "reciprocal", (B.r_rinv,), (B.r_rinv,), out=B.rinv[:, :tb], in_=B.rinv[:, :tb])
            if c < 4:
                op("dve", "scalar_tensor_tensor", (B.r_rinv, B.r_yc), (B.r_qkT,), out=B.qkT[:, c, :tb], in0=B.yc[:, c, :tb],
                   scalar=QK_SCALE, in1=B.rinv[:, :tb], op0=ALU.mult, op1=ALU.mult)
            else:
                op("dve", "tensor_tensor", (B.r_rinv, B.r_yc), (B.r_qkT,), out=B.qkT[:, c, :tb], in0=B.yc[:, c, :tb],
                   in1=B.rinv[:, :tb], op=ALU.mult)
        ntile = (tb + 63) // 64 if prompt else nseq
        if cx.stop <= 1:
            ntile = 0
        for ti in range(ntile):
            if prompt:
                c0 = ti * 64
                m = min(64, tb - c0)
                seq = 0
            else:
                c0 = ti * Ts
                m = Ts
                seq = ti
            tok0 = t0 + c0
            mixer_tile(cx, S, B, name, grp, P, m, c0, tok0, seq, prompt, dsem, op,
                       first=(tok0 == 0) if prompt else True, last=(tok0 + m == T) if prompt else True)


def mixer_tile(cx, S, B, name, grp, P, m, c0, tok0, seq, prompt, dsem, op, first, last):
    ps, rps = cx.ps, cx.rps
    sc, rsc = B.sc, B.r_sc
    for gi, (cA, cB) in enumerate(((1536, 2048), (2048, 2560), (2560, 2992))):
        kq = gi % 2
        for kc in range(8):
            op("pe", "matmul", (B.r_win, B.r_hT), (rps[kq],), inc=(kc == 7), out=ps[kq][:m, :cB - cA],
               lhsT=B.hT[:, kc, c0:c0 + m], rhs=B.win[:, kc, cA:cB], start=(kc == 0), stop=(kc == 7))
        if gi % 2:
            op("act", "activation", (rps[kq],), (B.r_tm,), out=B.tm[:m, cA - 1536:cB - 1536], in_=ps[kq][:m, :cB - cA],
               func=AF.Copy)
        else:
            op("dve", "tensor_copy", (rps[kq],), (B.r_tm,), out=B.tm[:m, cA - 1536:cB - 1536], in_=ps[kq][:m, :cB - cA])
    if last:
        for gi in range(3):
            kq = gi % 2
            for kc in range(8):
                op("pe", "matmul", (B.r_win, B.r_hT), (rps[kq],), inc=(kc == 7), out=ps[kq][:m, :512],
                   lhsT=B.hT[:, kc, c0:c0 + m], rhs=B.win[:, kc, gi * 512:(gi + 1) * 512], start=(kc == 0), stop=(kc == 7))
            op("dve", "tensor_copy", (rps[kq],), (B.r_nc3,), out=B.nc3[:m, gi * 512:(gi + 1) * 512], in_=ps[kq][:m, :512])
        S.dma("sp", dsem("nc3"), grp["new_conv"][seq, :, :], B.nc3[m - 3:m, :], reads=(B.r_nc3,), writes=(grp["r_out"],))
    if cx.stop <= 2:
        return
    zr, br, ar = B.tm[:m, 0:512], B.tm[:m, 512:520], B.tm[:m, 520:528]
    qraw, ckvr, krr_ = B.tm[:m, 528:1296], B.tm[:m, 1296:1424], B.tm[:m, 1424:1456]
    op("act", "activation", (B.r_tm,), (rsc,), out=sc[:m, 64:72], in_=br, func=AF.Exp, scale=-1.0)
    op("dve", "tensor_scalar", (rsc,), (rsc,), out=sc[:m, 64:72], in0=sc[:m, 64:72], scalar1=1.0, scalar2=None,
       op0=ALU.add)
    op("dve", "reciprocal", (rsc,), (rsc,), out=sc[:m, 0:8], in_=sc[:m, 64:72])
    op("act", "activation", (rsc,), (rsc,), out=sc[:m, 8:16], in_=sc[:m, 0:8], func=AF.Ln)
    op("dve", "tensor_tensor", (B.r_tm, B.r_g8), (rsc,), out=sc[:m, 64:72], in0=ar, in1=B.g8[:m, 8:16], op=ALU.add)
    op("act", "activation", (rsc,), (rsc,), out=sc[:m, 64:72], in_=sc[:m, 64:72], func=AF.Exp)
    op("act", "activation", (rsc,), (rsc,), out=sc[:m, 64:72], in_=sc[:m, 64:72], func=AF.Ln, bias=cx.one_t[:m, 0:1],
       scale=1.0)
    op("dve", "scalar_tensor_tensor", (rsc, B.r_g8), (rsc,), out=sc[:m, 16:24], in0=sc[:m, 64:72], scalar=-1.0,
       in1=B.g8[:m, 0:8], op0=ALU.mult, op1=ALU.mult)
    op("pe", "matmul", (rsc,), (rps[2],), out=ps[2][:m, 0:8], lhsT=cx.utri[:m, :m], rhs=sc[:m, 16:24], start=True, stop=True)
    op("dve", "tensor_copy", (rps[2],), (rsc,), out=sc[:m, 24:32], in_=ps[2][:m, 0:8])
    op("pe", "matmul", (rsc,), (rps[2],), out=ps[2][:m, 8:16], lhsT=cx.onesf[:m, :m], rhs=sc[:m, 16:24], start=True, stop=True)
    op("dve", "tensor_tensor", (rps[2], rsc), (rsc,), out=sc[:m, 48:56], in0=ps[2][:m, 8:16], in1=sc[:m, 24:32],
       op=ALU.subtract)
    op("act", "activation", (rsc,), (rsc,), out=sc[:m, 48:56], in_=sc[:m, 48:56], func=AF.Exp)
    op("act", "activation", (rsc,), (rsc,), out=sc[:m, 32:40], in_=sc[:m, 24:32], func=AF.Exp)
    op("dve", "tensor_tensor", (rsc,), (rsc,), out=sc[:m, 40:48], in0=sc[:m, 32:40], in1=sc[:m, 0:8], op=ALU.mult)
    op("dve", "tensor_scalar", (rsc,), (rsc,), out=sc[:m, 56:64], in0=sc[:m, 24:32], scalar1=-1.0, scalar2=None,
       op0=ALU.mult)
    op("dve", "tensor_tensor", (rsc,), (rsc,), out=sc[:m, 72:80], in0=sc[:m, 24:32], in1=sc[:m, 8:16], op=ALU.add)
    assert m <= 64
    g2 = sc[:m, 16:24].rearrange("p (c two) -> p two c", two=2)
    for par in range(2):
        op("pe", "matmul", (rsc,), (rps[2],), out=ps[2][par * 64:(par + 1) * 64, 16:20], lhsT=cx.onesf[:m, :64],
           rhs=g2[:, par, :], start=True, stop=True, skip_group_check=True)
    op("act", "activation", (rps[2],), (B.r_glb,), out=B.glb[:, 0:4], in_=ps[2][:, 16:20], func=AF.Exp)
    op("pe", "transpose", (rsc,), (rps[2],), out=ps[2][:8, 32:32 + m], in_=sc[:m, 24:32], identity=cx.ident_f[:m, :m])
    op("pe", "transpose", (rsc,), (rps[2],), out=ps[2][:8, 160:160 + m], in_=sc[:m, 72:80], identity=cx.ident_f[:m, :m])
    op("dve", "tensor_copy", (rps[2],), (B.r_rows,), out=B.rows[:, 0, :m], in_=ps[2][:8, 32:32 + m])
    op("dve", "tensor_copy", (rps[2],), (B.r_rows,), out=B.rows[:, 1, :m], in_=ps[2][:8, 160:160 + m])
    if cx.stop <= 3:
        return
    for c in range(4):
        op("pe", "transpose", (B.r_yc,), (rps[3],), out=ps[3][:m, c * 128:(c + 1) * 128], in_=B.yc[:, 8 + c, c0:c0 + m],
           identity=cx.ident_f[:, :], inc=(c == 3))
    op("dve", "tensor_tensor", (rps[3], rsc), (B.r_vtok,), out=B.vtok[:m, :].rearrange("p (h d) -> p h d", d=64),
       in0=ps[3][:m, :].rearrange("p (h d) -> p h d", d=64), in1=bc(sc[:m, 0:8], 64), op=ALU.mult)
    pkb = ps[2].bitcast(BF16)
    for c in range(4):
        op("pe", "transpose", (B.r_qkT,), (rps[2],), out=pkb[:m, 512 + c * 128:512 + (c + 1) * 128],
           in_=B.qkT[:, 4 + c, c0:c0 + m], identity=cx.ident_bf[:, :], inc=(c == 3))
    kh3 = pkb[:m, 512:1024].rearrange("p (h d) -> p h d", d=64)
    op("dve", "tensor_tensor", (rps[2], rsc), (B.r_ktok,), out=B.ktok[:m, :].rearrange("p (h d) -> p h d", d=64),
       in0=kh3, in1=bc(sc[:m, 40:48], 64), op=ALU.mult)
    op("dve", "tensor_tensor", (rps[2], rsc), (B.r_kdec,), out=B.kdec[:m, :].rearrange("p (h d) -> p h d", d=64),
       in0=kh3, in1=bc(sc[:m, 48:56], 64), op=ALU.mult)
    if cx.stop <= 4:
        return
    W4 = 4 * m
    mle, mlt, id4 = cx.mconst[m]
    for g in range(2):
        Ab = [getattr(B, "A%d%d" % (g, k)) for k in range(2)]
        Mb = [getattr(B, "M%d%d" % (g, k)) for k in range(2)]
        Pb = [getattr(B, "P%d%d" % (g, k)) for k in range(2)]
        rA = [getattr(B, "r_A%d%d" % (g, k)) for k in range(2)]
        rM = [getattr(B, "r_M%d%d" % (g, k)) for k in range(2)]
        rP = [getattr(B, "r_P%d%d" % (g, k)) for k in range(2)]
        pP, rpP = ps[6 + g], rps[6 + g]
        for kind, dec, rdec, mask in ((0, B.decq, B.r_decq, mle), (1, B.decb, B.r_decb, mlt)):
            op("pe", "matmul", (), (rps[3],), inc=False, out=ps[3][:m, 0:W4], lhsT=cx.ident_f[:m, :m], rhs=mask[:m, 0:W4],
               start=True, stop=False)
            for hh in range(4):
                h = 2 * hh + g
                op("pe", "matmul", (B.r_rows,), (rps[3],), inc=(hh == 3), out=ps[3][:m, hh * m:(hh + 1) * m],
                   lhsT=cx.ohsel[:, h, :m], rhs=B.rows[:, kind, :m], start=False, stop=(hh == 3))
            for hh in range(4):
                h = 2 * hh + g
                op("act", "activation", (rps[3], rsc), (rdec,), out=dec[:m, hh * m:(hh + 1) * m],
                   in_=ps[3][:m, hh * m:(hh + 1) * m], func=AF.Exp, bias=sc[:m, 56 + h:57 + h], scale=1.0)
        if cx.stop <= 4.2:
            continue
        for hh in range(4):
            h = 2 * hh + g
            c, po = h // 2, (h % 2) * 64
            op("pe", "matmul", (B.r_qkT,), (rps[4],), inc=(hh == 3), out=ps[4][:m, hh * m:(hh + 1) * m],
               lhsT=B.qkT[po:po + 64, 4 + c, c0:c0 + m], rhs=B.qkT[po:po + 64, 4 + c, c0:c0 + m], start=True, stop=True,
               skip_group_check=True)
        op("dve", "tensor_tensor", (rps[4], B.r_decb), (rA[0],), out=Ab[0][:m, 0:W4], in0=ps[4][:m, 0:W4],
           in1=B.decb[:m, 0:W4], op=ALU.mult)
        for hh in range(4):
            h = 2 * hh + g
            c, po = h // 2, (h % 2) * 64
            op("pe", "matmul", (B.r_qkT,), (rps[5],), inc=(hh == 3), out=ps[5][:m, hh * m:(hh + 1) * m],
               lhsT=B.qkT[po:po + 64, 4 + c, c0:c0 + m], rhs=B.qkT[po:po + 64, c, c0:c0 + m], start=True, stop=True,
               skip_group_check=True)
        op("dve", "tensor_tensor", (rps[5], B.r_decq), (B.r_qk,), out=B.qk[:m, 0:8 * m].rearrange("p (c two i) -> p two c i", two=2, i=m)[:, g],
           in0=ps[5][:m, 0:W4].rearrange("p (c i) -> p c i", i=m),
           in1=B.decq[:m, 0:W4].rearrange("p (c i) -> p c i", i=m), op=ALU.mult)
        if cx.stop <= 4.4:
            continue
        for hh in range(4):
            op("pe", "transpose", (rA[0],), (rps[4],), inc=(hh == 3), out=ps[4][:m, hh * m:(hh + 1) * m],
               in_=Ab[0][:m, hh * m:(hh + 1) * m], identity=cx.ident_f[:m, :m])
        op("act", "activation", (rps[4],), (rM[0],), out=Mb[0][:m, 0:W4], in_=ps[4][:m, 0:W4], func=AF.Copy)
        op("pe", "matmul", (), (rpP,), inc=False, out=pP[:m, 0:W4], lhsT=cx.ident_f[:m, :m], rhs=id4[:m, 0:W4],
           start=True, stop=False, skip_group_check=True)
        for hh in range(4):
            op("pe", "matmul", (rA[0],), (rpP,), inc=(hh == 3), out=pP[:m, hh * m:(hh + 1) * m], lhsT=cx.nident_f[:m, :m],
               rhs=Ab[0][:m, hh * m:(hh + 1) * m], start=False, stop=False, skip_group_check=True)
        op("act", "activation", (rpP,), (rP[0],), out=Pb[0][:m, 0:W4], in_=pP[:m, 0:W4], func=AF.Copy)
        if cx.stop <= 4.6:
            continue
        nlev = NLEV if m > 64 else (5 if m > 32 else (4 if m > 16 else (3 if m > 8 else (2 if m > 4 else 1))))
        for lv in range(1, nlev + 1):
            a, bq = (lv - 1) % 2, lv % 2
            for hh in range(4):
                sl = slice(hh * m, (hh + 1) * m)
                op("pe", "matmul", (rA[a], rM[a]), (rps[4],), inc=(hh == 3), out=ps[4][:m, sl], lhsT=Ab[a][:m, sl],
                   rhs=Mb[a][:m, sl], start=True, stop=True, skip_group_check=True)
            op("act", "activation", (rps[4],), (rM[bq],), out=Mb[bq][:m, 0:W4], in_=ps[4][:m, 0:W4], func=AF.Copy)
            if lv < nlev:
                for hh in range(4):
                    sl = slice(hh * m, (hh + 1) * m)
                    op("pe", "matmul", (rA[a], rM[a]), (rps[5],), inc=(hh == 3), out=ps[5][:m, sl], lhsT=Mb[a][:m, sl],
                       rhs=Ab[a][:m, sl], start=True, stop=True, skip_group_check=True)
                op("dve", "tensor_copy", (rps[5],), (rA[bq],), out=Ab[bq][:m, 0:W4], in_=ps[5][:m, 0:W4])
            for hh in range(4):
                sl = slice(hh * m, (hh + 1) * m)
                op("pe", "matmul", (rM[bq], rP[a]), (rpP,), inc=(hh == 3), out=pP[:m, sl], lhsT=Mb[bq][:m, sl],
                   rhs=Pb[a][:m, sl], start=False, stop=(lv == nlev), skip_group_check=True)
            if lv % 2:
                op("dve", "tensor_copy", (rpP,), (rP[bq],), out=Pb[bq][:m, 0:W4], in_=pP[:m, 0:W4])
            else:
                op("act", "activation", (rpP,), (rP[bq],), out=Pb[bq][:m, 0:W4], in_=pP[:m, 0:W4], func=AF.Copy)
        if cx.stop <= 4.8:
            continue
        Pf, rPf = getattr(B, "Pf%d" % g), getattr(B, "r_Pf%d" % g)
        op("dve", "tensor_copy", (rpP,), (rPf,), out=Pf[:m, 0:W4], in_=pP[:m, 0:W4])
        if cx.stop <= 4.85:
            continue
        for hh in range(4):
            h = 2 * hh + g
            sl = slice(hh * m, (hh + 1) * m)
            op("pe", "matmul", (rPf, B.r_vtok), (rps[0],), inc=(hh == 3), out=ps[0][:m, h * 64:(h + 1) * 64], lhsT=Pf[:m, sl],
               rhs=B.vtok[:m, h * 64:(h + 1) * 64], start=True, stop=True, skip_group_check=True)
        if cx.stop <= 4.9:
            continue
        for hh in range(4):
            h = 2 * hh + g
            sl = slice(hh * m, (hh + 1) * m)
            po = (h % 2) * 64
            op("pe", "matmul", (rPf, B.r_ktok), (rps[1],), inc=(hh == 3),
               out=ps[1][po:po + 64, hh * 128:hh * 128 + m],
               lhsT=B.ktok[:m, h * 64:(h + 1) * 64], rhs=Pf[:m, sl], start=True, stop=True, skip_group_check=True)
        op("act", "activation", (rps[1],), (B.r_wT,), out=B.wT[g * 64:(g + 1) * 64, :, :m],
           in_=ps[1][g * 64:(g + 1) * 64, :].rearrange("p (h i) -> p h i", i=128)[:, :, :m], func=AF.Identity)
    if cx.stop <= 5:
        return
    op("dve", "tensor_copy", (rps[0],), (B.r_u,), out=B.u[:m, :], in_=ps[0][:m, :])
    def sdiag(t_, par):
        return t_[par * 64:(par + 1) * 64, :].rearrange("k (c x) -> k c x", x=128)[:, :, par * 64:(par + 1) * 64]

    if not prompt:
        op("dve", "memset", (), (B.r_S32,), ap=B.S32[:, :], constant=0.0)
        for par in range(2):
            S.dma("sp", dsem("s0"), sdiag(B.S32, par), grp["s0"][seq].rearrange("(c par) k v -> par k c v", par=2)[par],
                  writes=(B.r_S32,))
        op("act", "activation", (B.r_S32,), (B.r_Sbf,), out=B.Sbf[:, :], in_=B.S32[:, :], func=AF.Copy)
    for c in range(4):
        op("pe", "matmul", (B.r_wT, B.r_Sbf), (rps[0],), inc=(c == 3), out=ps[0][:m, c * 128:(c + 1) * 128],
           lhsT=B.wT[:, c, :m], rhs=B.Sbf[:, c * 128:(c + 1) * 128], start=True, stop=True, skip_group_check=True)
    op("dve", "tensor_tensor", (rps[0], B.r_u), (B.r_vnew,), out=B.vnew[:m, :], in0=B.u[:m, :], in1=ps[0][:m, :],
       op=ALU.subtract)
    for c in range(4):
        op("pe", "matmul", (B.r_qkT, B.r_Sbf), (rps[1],), inc=(c == 3), out=ps[1][:m, c * 128:(c + 1) * 128],
           lhsT=B.qkT[:, c, c0:c0 + m], rhs=B.Sbf[:, c * 128:(c + 1) * 128], start=True, stop=True, skip_group_check=True)
    op("dve", "tensor_tensor", (rps[1], rsc), (B.r_o1,), out=B.o1[:m, :].rearrange("p (h d) -> p h d", d=64),
       in0=ps[1][:m, :].rearrange("p (h d) -> p h d", d=64), in1=bc(sc[:m, 32:40], 64), op=ALU.mult)
    for h in range(8):
        op("pe", "matmul", (B.r_qk, B.r_vnew), (rps[0],), inc=(h == 7), out=ps[0][:m, h * 64:(h + 1) * 64],
           lhsT=B.qk[:m, h * m:(h + 1) * m], rhs=B.vnew[:m, h * 64:(h + 1) * 64], start=True, stop=True,
           skip_group_check=True)
    op("dve", "tensor_tensor", (rps[0], B.r_o1), (B.r_osb,), out=B.osb[:m, :], in0=ps[0][:m, :], in1=B.o1[:m, :], op=ALU.add)
    for c in range(4):
        op("pe", "matmul", (B.r_kdec, B.r_vnew), (rps[3],), inc=(c == 3), out=ps[3][:, c * 128:(c + 1) * 128],
           lhsT=B.kdec[:m, c * 128:(c + 1) * 128], rhs=B.vnew[:m, c * 128:(c + 1) * 128], start=True, stop=True,
           skip_group_check=True)
    op("dve", "tensor_tensor", (rps[3],), (B.r_decq,), out=B.decq[:, :].rearrange("p (c x) -> p c x", x=128),
       in0=ps[3][:, :].rearrange("p (c x) -> p c x", x=128), in1=bc_mid(cx.bones[:, :], 4), op=ALU.mult)
    op("dve", "tensor_tensor", (B.r_glb,), (B.r_S32,), out=B.S32[:, :].rearrange("p (c x) -> p c x", x=128),
       in0=B.S32[:, :].rearrange("p (c x) -> p c x", x=128), in1=bc(B.glb[:, 0:4], 128), op=ALU.mult)
    op("pool", "tensor_tensor", (B.r_decq,), (B.r_S32,), out=B.S32[:, :], in0=B.S32[:, :], in1=B.decq[:, :], op=ALU.add)
    if last:
        for par in range(2):
            S.dma("sp", dsem("s0"), grp["new_gdn"][seq].rearrange("(c par) k v -> par k c v", par=2)[par],
                  sdiag(B.S32, par), reads=(B.r_S32,), writes=(grp["r_out"],))
    else:
        op("act", "activation", (B.r_S32,), (B.r_Sbf,), out=B.Sbf[:, :], in_=B.S32[:, :], func=AF.Copy)
    if cx.stop <= 6:
        return
    op("pool", "tensor_tensor", (B.r_osb,), (B.r_o1,), out=B.o1[:m, :], in0=B.osb[:m, :], in1=B.osb[:m, :], op=ALU.mult)
    op("dve", "tensor_reduce", (B.r_o1,), (B.r_ost,), out=B.ost[:m, 0:8], in_=B.o1[:m, :].rearrange("p (h d) -> p h d", d=64),
       axis=AX.X, op=ALU.add)
    op("act", "activation", (B.r_ost,), (B.r_ost,), out=B.ost[:m, 8:16], in_=B.ost[:m, 0:8], func=AF.Sqrt, scale=1.0 / 64,
       bias=cx.eps_t[:m, 0:1])
    op("dve", "reciprocal", (B.r_ost,), (B.r_ost,), out=B.ost[:m, 8:16], in_=B.ost[:m, 8:16])
    op("act", "activation", (B.r_tm,), (B.r_zs,), out=B.zs[:m, :], in_=zr, func=AF.Silu)
    op("dve", "tensor_tensor", (B.r_osb, B.r_ost), (B.r_osb,), out=B.osb[:m, :].rearrange("p (h d) -> p h d", d=64),
       in0=B.osb[:m, :].rearrange("p (h d) -> p h d", d=64), in1=bc(B.ost[:m, 8:16], 64), op=ALU.mult)
    op("pool", "tensor_tensor", (B.r_osb, B.r_gng), (B.r_osb,), out=B.osb[:m, :].rearrange("p (h d) -> p h d", d=64),
       in0=B.osb[:m, :].rearrange("p (h d) -> p h d", d=64), in1=bc_mid(B.gng[:m, :], 8), op=ALU.mult)
    op("dve", "tensor_tensor", (B.r_osb, B.r_zs), (B.r_gout,), out=B.gout[:m, :], in0=B.osb[:m, :], in1=B.zs[:m, :], op=ALU.mult)
    S.dma("sp", dsem("gout"), grp["mix"][tok0:tok0 + m, 0:512], B.gout[:m, :], reads=(B.r_gout,), writes=(grp["r_mix"],))
    if cx.stop <= 7:
        return
    S.dma("sp", dsem("cs"), B.cs[:m, :], grp["cs"][(tok0 if prompt else 0):(tok0 if prompt else 0) + m, :], writes=(B.r_cs,))
    q3 = qraw.rearrange("p (h d) -> p h d", d=96)
    mq3 = B.mq[:m, :].rearrange("p (h d) -> p h d", d=96)
    op("pool", "tensor_tensor", (B.r_tm,), (B.r_mq,), out=B.mq[:m, :], in0=qraw, in1=qraw, op=ALU.mult)
    op("dve", "tensor_reduce", (B.r_mq,), (B.r_mqs,), out=B.mqs[:m, 0:8], in_=mq3[:, :, 0:64], axis=AX.X, op=ALU.add)
    op("dve", "tensor_reduce", (B.r_mq,), (B.r_mqs,), out=B.mqs[:m, 8:16], in_=mq3[:, :, 64:96], axis=AX.X, op=ALU.add)
    op("act", "activation", (B.r_mqs,), (B.r_mqs,), out=B.mqs[:m, 16:24], in_=B.mqs[:m, 0:8], func=AF.Sqrt, scale=1.0 / 64,
       bias=cx.eps_t[:m, 0:1])
    op("act", "activation", (B.r_mqs,), (B.r_mqs,), out=B.mqs[:m, 24:32], in_=B.mqs[:m, 8:16], func=AF.Sqrt, scale=1.0 / 32,
       bias=cx.eps_t[:m, 0:1])
    op("dve", "reciprocal", (B.r_mqs,), (B.r_mqs,), out=B.mqs[:m, 16:32], in_=B.mqs[:m, 16:32])
    op("dve", "tensor_tensor", (B.r_tm, B.r_mqs), (B.r_mq,), out=mq3[:, :, 0:64], in0=q3[:, :, 0:64], in1=bc(B.mqs[:m, 16:24], 64),
       op=ALU.mult)
    op("pool", "tensor_tensor", (B.r_mq, B.r_qng), (B.r_Qb,), out=B.Qb[:m, :].rearrange("p (h d) -> p h d", d=96)[:, :, 0:64],
       in0=mq3[:, :, 0:64], in1=bc_mid(B.qng[:m, :], 8), op=ALU.mult)
    op("dve", "tensor_tensor", (B.r_tm, B.r_mqs), (B.r_mq,), out=mq3[:, :, 64:96], in0=q3[:, :, 64:96], in1=bc(B.mqs[:m, 24:32], 32),
       op=ALU.mult)
    op("pool", "tensor_tensor", (B.r_mq, B.r_qrg), (B.r_mq,), out=mq3[:, :, 64:96], in0=mq3[:, :, 64:96],
       in1=bc_mid(B.qrg[:m, :], 8), op=ALU.mult)
    cosb, sinb = bc_mid(B.cs[:m, 0:16], 8), bc_mid(B.cs[:m, 16:32], 8)
    x1, x2 = mq3[:, :, 64:80], mq3[:, :, 80:96]
    Q3 = B.Qb[:m, :].rearrange("p (h d) -> p h d", d=96)
    op("dve", "tensor_tensor", (B.r_mq, B.r_cs), (B.r_qrr,), out=B.qrr[:m, :, 0:16], in0=x1, in1=cosb, op=ALU.mult)
    op("dve", "tensor_tensor", (B.r_mq, B.r_cs), (B.r_qrr,), out=B.qrr[:m, :, 16:32], in0=x2, in1=sinb, op=ALU.mult)
    op("dve", "tensor_tensor", (B.r_qrr,), (B.r_Qb,), out=Q3[:, :, 64:80], in0=B.qrr[:m, :, 0:16], in1=B.qrr[:m, :, 16:32],
       op=ALU.subtract)
    op("dve", "tensor_tensor", (B.r_mq, B.r_cs), (B.r_qrr,), out=B.qrr[:m, :, 0:16], in0=x1, in1=sinb, op=ALU.mult)
    op("dve", "tensor_tensor", (B.r_mq, B.r_cs), (B.r_qrr,), out=B.qrr[:m, :, 16:32], in0=x2, in1=cosb, op=ALU.mult)
    op("dve", "tensor_tensor", (B.r_qrr,), (B.r_Qb,), out=Q3[:, :, 80:96], in0=B.qrr[:m, :, 0:16], in1=B.qrr[:m, :, 16:32],
       op=ALU.add)
    S.dma("sp", dsem("Qb"), grp["Q"][tok0:tok0 + m, :], B.Qb[:m, :], reads=(B.r_Qb,), writes=(grp["r_Q"],))
    op("pool", "tensor_tensor", (B.r_tm,), (B.r_cn,), out=B.cn[:m, :], in0=ckvr, in1=ckvr, op=ALU.mult)
    op("dve", "tensor_reduce", (B.r_cn,), (B.r_mqs,), out=B.mqs[:m, 0:1], in_=B.cn[:m, :], axis=AX.X, op=ALU.add)
    op("act", "activation", (B.r_mqs,), (B.r_mqs,), out=B.mqs[:m, 1:2], in_=B.mqs[:m, 0:1], func=AF.Sqrt, scale=1.0 / 128,
       bias=cx.eps_t[:m, 0:1])
    op("dve", "reciprocal", (B.r_mqs,), (B.r_mqs,), out=B.mqs[:m, 1:2], in_=B.mqs[:m, 1:2])
    op("dve", "scalar_tensor_tensor", (B.r_tm, B.r_mqs, B.r_ckg), (B.r_cn,), out=B.cn[:m, :], in0=ckvr, scalar=B.mqs[:m, 1:2],
       in1=B.ckg[:m, :], op0=ALU.mult, op1=ALU.mult)
    S.dma("sp", dsem("cn"), grp["new_ckv"][tok0:tok0 + m, :], B.cn[:m, :], reads=(B.r_cn,), writes=(grp["r_ckv"],))
    op("pool", "tensor_tensor", (B.r_tm,), (B.r_krn,), out=B.krn[:m, :], in0=krr_, in1=krr_, op=ALU.mult)
    op("dve", "tensor_reduce", (B.r_krn,), (B.r_mqs,), out=B.mqs[:m, 2:3], in_=B.krn[:m, :], axis=AX.X, op=ALU.add)
    op("act", "activation", (B.r_mqs,), (B.r_mqs,), out=B.mqs[:m, 3:4], in_=B.mqs[:m, 2:3], func=AF.Sqrt, scale=1.0 / 32,
       bias=cx.eps_t[:m, 0:1])
    op("dve", "reciprocal", (B.r_mqs,), (B.r_mqs,), out=B.mqs[:m, 3:4], in_=B.mqs[:m, 3:4])
    op("dve", "scalar_tensor_tensor", (B.r_tm, B.r_mqs, B.r_krg), (B.r_krn,), out=B.krn[:m, :], in0=krr_, scalar=B.mqs[:m, 3:4],
       in1=B.krg[:m, :], op0=ALU.mult, op1=ALU.mult)
    k1, k2, cs1, sn1 = B.krn[:m, 0:16], B.krn[:m, 16:32], B.cs[:m, 0:16], B.cs[:m, 16:32]
    op("dve", "tensor_tensor", (B.r_krn, B.r_cs), (B.r_krr,), out=B.krr[:m, 0:16], in0=k1, in1=cs1, op=ALU.mult)
    op("dve", "tensor_tensor", (B.r_krn, B.r_cs), (B.r_krr,), out=B.krr[:m, 16:32], in0=k2, in1=sn1, op=ALU.mult)
    op("dve", "tensor_tensor", (B.r_krr,), (B.r_qrr,), out=B.qrr[:m, 0, 0:16], in0=B.krr[:m, 0:16], in1=B.krr[:m, 16:32],
       op=ALU.subtract)
    op("dve", "tensor_tensor", (B.r_krn, B.r_cs), (B.r_krr,), out=B.krr[:m, 0:16], in0=k1, in1=sn1, op=ALU.mult)
    op("dve", "tensor_tensor", (B.r_krn, B.r_cs), (B.r_krr,), out=B.krr[:m, 16:32], in0=k2, in1=cs1, op=ALU.mult)
    op("dve", "tensor_tensor", (B.r_krr,), (B.r_qrr,), out=B.qrr[:m, 0, 16:32], in0=B.krr[:m, 0:16], in1=B.krr[:m, 16:32],
       op=ALU.add)
    S.dma("sp", dsem("kr"), grp["new_kr"][tok0:tok0 + m, :], B.qrr[:m, 0, :], reads=(B.r_qrr,), writes=(grp["r_kr"],))


def phase_mixer(cx, S, groups, P):
    with ExitStack() as st:
        B = mixer_alloc(cx, S, st, "mx", P)
        mc = sb(cx, st, "mx_cst", [128, CST_COLS - 128], F32)
        r_mc = Res("mxcst")
        S.dma("sp", S.dsem("mxcst"), mc[:, :], cx.cst[:, 128:CST_COLS], writes=(r_mc,))
        cx.utri = mc[:, 0:128]
        cx.onesf = mc[:, 128:256]
        cx.ohsel = mc[0:8, 384:1408].rearrange("p (h i) -> p h i", i=128)
        cx.mconst = {}
        off = 1408
        for m_ in (64, 4):
            cx.mconst[m_] = (mc[:, off + 4 * m_:off + 8 * m_], mc[:, off + 8 * m_:off + 12 * m_], mc[:, off:off + 4 * m_])
            off += 12 * m_
        idf = sb(cx, st, "mx_idf", [128, 128], F32)
        S.op("dve", "tensor_scalar", dict(out=idf[:, :], in0=cx.ident_f, scalar1=-1.0, scalar2=None, op0=ALU.mult),
             writes=(r_mc,))
        cx.nident_f = idf[:, :]
        dc = S.dsem("mxc")
        r_c = Res("mxconst")
        S.dma("sp", dc, B.cw[:, :, :], P["conv_w_fm"], writes=(B.r_cw,))
        S.dma("sp", dc, B.g8[:, 0:8], row_bcast(P["a_log"], 128), writes=(B.r_g8,))
        S.dma("sp", dc, B.g8[:, 8:16], row_bcast(P["dt_bias"], 128), writes=(B.r_g8,))
        S.dma("sp", dc, B.gng[:, :], row_bcast(P["gdn_norm"], 128), writes=(B.r_gng,))
        S.dma("sp", dc, B.qng[:, :], row_bcast(P["qn_g"], 128), writes=(B.r_qng,))
        S.dma("sp", dc, B.kng[:, :], row_bcast(P["kn_g"], 128), writes=(B.r_kng,))
        S.dma("sp", dc, B.qrg[:, :], row_bcast(P["qr_g"], 128), writes=(B.r_qrg,))
        S.dma("sp", dc, B.ckg[:, :], row_bcast(P["ckv_g"], 128), writes=(B.r_ckg,))
        S.dma("sp", dc, B.krg[:, :], row_bcast(P["kr_g"], 128), writes=(B.r_krg,))
        S.barrier()
        S.op("act", "activation", dict(out=B.g8[:, 0:8], in_=B.g8[:, 0:8], func=AF.Exp), writes=(B.r_g8,))
        S.op("dve", "scalar_tensor_tensor", dict(out=B.qng[:, :], in0=B.qng[:, :], scalar=ATT_SCALE, in1=B.kng[:, :],
                                                 op0=ALU.mult, op1=ALU.mult), writes=(B.r_qng,))
        S.op("dve", "tensor_scalar", dict(out=B.qrg[:, :], in0=B.qrg[:, :], scalar1=ATT_SCALE, scalar2=None, op0=ALU.mult),
             writes=(B.r_qrg,))
        S.barrier()
        for gi, grp in enumerate(groups):
            mixer_group(cx, S, B, "mx%d" % gi, grp, P)
            S.barrier()
            S.emit()


def phase_wout(cx, S, groups, w_out_ap):
    with ExitStack() as st:
        wo = sb(cx, st, "wo_w", [128, 8, D], BF16)
        r_wo = Res()
        load_weight_bf16(cx, S, "wow", wo, r_wo, w_out_ap, 8, D, D)
        Gt = sb(cx, st, "wo_Gt", [128, D], F32)
        mx = [sb(cx, st, "wo_mx%d" % i, [128, D], BF16) for i in range(2)]
        mT = sb(cx, st, "wo_mT", [128, 8, 128], BF16)
        xr = [sb(cx, st, "wo_xr%d" % i, [128, D], F32) for i in range(2)]
        tmp = sb(cx, st, "wo_tmp", [128, 512], F32)
        r_Gt, r_mT, r_tmp = Res(), Res(), Res()
        r_mx, r_xr = [Res(), Res()], [Res(), Res()]
        d_g = S.dsem("wo_g")
        d_mx = [S.dsem("wo_mx0"), S.dsem("wo_mx1")]
        d_xr = [S.dsem("wo_xr0"), S.dsem("wo_xr1")]
        d_xst = [S.dsem("wo_xst0"), S.dsem("wo_xst1")]
        i = 0
        for g in groups:
            n, T = g["n"], g["T"]
            S.dma("sp", d_g, Gt[:n, :], g["mods"][5, :n, :], writes=(r_Gt,))
            for r0 in range(0, T, 128):
                m = min(128, T - r0)
                k = i % 2
                i += 1
                S.dma("sp", d_mx[k], mx[k][:m, :], g["mix"][r0:r0 + m, :], reads=(g["r_mix"],), writes=(r_mx[k],))
                S.dma("sp", d_xr[k], xr[k][:m, :], g["src"][r0:r0 + m, :], reads=(g["src_res"][r0 // 128],),
                      writes=(r_xr[k],))
                ptb = cx.ps[6 + k].bitcast(BF16)
                for c in range(8):
                    S.op("pe", "transpose", dict(out=ptb[:, c * 128:c * 128 + m], in_=mx[k][:m, c * 128:(c + 1) * 128],
                                                 identity=cx.ident_bf[:m, :m]), reads=(r_mx[k],), writes=(cx.rps[6 + k],),
                         inc=(c == 7))
                S.op("act", "activation", dict(out=mT[:, :, :m], in_=ptb.rearrange("p (c t) -> p c t", c=8)[:, :, :m],
                                               func=AF.Copy), reads=(cx.rps[6 + k],), writes=(r_mT,))
                for half in range(2):
                    pk = (i * 2 + half) % 4
                    for c in range(8):
                        S.op("pe", "matmul", dict(out=cx.ps[pk][:m, :], lhsT=mT[:, c, :m],
                                                  rhs=wo[:, c, half * 512:(half + 1) * 512], start=(c == 0), stop=(c == 7)),
                             reads=(r_mT, r_wo), writes=(cx.rps[pk],), inc=(c == 7))
                    S.op("dve", "tensor_tensor", dict(out=tmp[:m, :], in0=cx.ps[pk][:m, :],
                                                      in1=Gt[:m, half * 512:(half + 1) * 512], op=ALU.mult),
                         reads=(cx.rps[pk], r_Gt), writes=(r_tmp,))
                    S.op("pool", "tensor_tensor", dict(out=xr[k][:m, half * 512:(half + 1) * 512],
                                                       in0=xr[k][:m, half * 512:(half + 1) * 512], in1=tmp[:m, :],
                                                       op=ALU.add), reads=(r_tmp,), writes=(r_xr[k],))
                S.dma("pool", d_xst[k], g["dst"][r0:r0 + m, :], xr[k][:m, :], reads=(r_xr[k],),
                      writes=(g["dst_res"][r0 // 128],))
        S.barrier()
        S.emit()


class AttBufs:
    pass


def att_alloc(cx, S, st, name, P):
    A = AttBufs()

    def t(nm, shape, dt):
        tt_ = sb(cx, st, name + nm, shape, dt)
        setattr(A, nm, tt_)
        setattr(A, "r_" + nm, Res(nm))
        return tt_

    t("wuk", [128, 512], BF16)
    t("wuv", [128, 512], BF16)
    t("wst", [128, 512], F32)
    d = S.dsem(name + "w")
    S.dma("sp", d, A.wst[:, :], P["w_uk"], writes=(A.r_wst,))
    S.op("dve", "tensor_copy", dict(out=A.wuk[:, :], in_=A.wst[:, :]), reads=(A.r_wst,), writes=(A.r_wuk,))
    S.dma("sp", d, A.wst[:, :], P["w_uv"], writes=(A.r_wst,))
    S.op("dve", "tensor_copy", dict(out=A.wuv[:, :], in_=A.wst[:, :]), reads=(A.r_wst,), writes=(A.r_wuv,))
    t("cTb", [128, 128], BF16)
    t("sq", [128, 512], F32)
    t("kst", [128, 16], F32)
    t("Kf", [128, 8, 96], BF16)
    t("KT", [96, 8, 128], BF16)
    t("QT", [96, 8, 128], BF16)
    t("Qb", [128, 768], BF16)
    t("PT", [128, 512], BF16)
    t("ctx", [128, 8, 128], BF16)
    t("rden", [128, 8], F32)
    t("ctxT", [128, 8, 128], BF16)
    t("mo", [128, 512], BF16)
    t("m01", [128, 128], BF16)
    A.d = {}
    return A


def att_kside(cx, S, A, c_blk, kr_blk, r_src, n, KT_dst, r_KT):
    ps, rps = cx.ps, cx.rps

    def op(eng, meth, reads, writes, inc=True, **kw):
        S.op(eng, meth, kw, reads=reads, writes=writes, inc=inc)

    op("pe", "transpose", (r_src,), (rps[0],), out=ps[0][:, :n], in_=c_blk, identity=cx.ident_f[:n, :n])
    op("act", "activation", (rps[0],), (A.r_cTb,), out=A.cTb[:, :n], in_=ps[0][:, :n], func=AF.Identity)
    op("pe", "matmul", (A.r_cTb, A.r_wuk), (rps[1],), out=ps[1][:n, :], lhsT=A.cTb[:, :n], rhs=A.wuk[:, :], start=True, stop=True)
    op("act", "activation", (rps[1],), (A.r_sq,), out=A.sq[:n, :], in_=ps[1][:n, :], func=AF.Square)
    op("dve", "tensor_reduce", (A.r_sq,), (A.r_kst,), out=A.kst[:n, 0:8], in_=A.sq[:n, :].rearrange("p (h d) -> p h d", d=64),
       axis=AX.X, op=ALU.add)
    op("act", "activation", (A.r_kst,), (A.r_kst,), out=A.kst[:n, 8:16], in_=A.kst[:n, 0:8], func=AF.Sqrt, scale=1.0 / 64,
       bias=cx.eps_t[:n, 0:1])
    op("dve", "reciprocal", (A.r_kst,), (A.r_kst,), out=A.kst[:n, 8:16], in_=A.kst[:n, 8:16])
    op("dve", "tensor_tensor", (rps[1], A.r_kst), (A.r_Kf,), out=A.Kf[:n, :, 0:64],
       in0=ps[1][:n, :].rearrange("p (h d) -> p h d", d=64), in1=bc(A.kst[:n, 8:16], 64), op=ALU.mult)
    op("pool", "tensor_copy", (r_src,), (A.r_Kf,), out=A.Kf[:n, :, 64:96], in_=bc_mid(kr_blk, 8))
    ptb = ps[2].bitcast(BF16)
    for h in range(8):
        op("pe", "transpose", (A.r_Kf,), (rps[2],), inc=(h == 7), out=ptb[0:96, h * 128:h * 128 + n], in_=A.Kf[:n, h, :],
           identity=cx.ident_bf[:n, :n])
    op("act", "activation", (rps[2],), (r_KT,), out=KT_dst, in_=ptb[0:96, :].rearrange("p (h t) -> p h t", t=128)[:, :, :n],
       func=AF.Copy)


def att_out(cx, S, A, nq, r_ctx_in, mix_dst, r_mix, dsem_):
    ps, rps = cx.ps, cx.rps
    ptb = ps[2].bitcast(BF16)
    for h in range(8):
        S.op("pe", "transpose", dict(out=ptb[:, h * 128:h * 128 + nq], in_=A.ctx[:nq, h, :], identity=cx.ident_bf[:nq, :nq]),
             reads=(A.r_ctx,), writes=(rps[2],), inc=(h == 7))
    S.op("act", "activation", dict(out=A.ctxT[:, :, :nq], in_=ptb.rearrange("p (h t) -> p h t", t=128)[:, :, :nq], func=AF.Copy),
         reads=(rps[2],), writes=(A.r_ctxT,))
    for h in range(8):
        S.op("pe", "matmul", dict(out=ps[3][:nq, h * 64:(h + 1) * 64], lhsT=A.ctxT[:, h, :nq], rhs=A.wuv[:, h * 64:(h + 1) * 64],
                                  start=True, stop=True, skip_group_check=True), reads=(A.r_ctxT, A.r_wuv), writes=(rps[3],),
             inc=(h == 7))
    S.op("dve", "tensor_copy", dict(out=A.mo[:nq, :], in_=ps[3][:nq, :]), reads=(rps[3],), writes=(A.r_mo,))
    S.dma("sp", dsem_, mix_dst, A.mo[:nq, :], reads=(A.r_mo,), writes=(r_mix,))


def phase_attn_prompt(cx, S, grp, P):
    T = grp["T"]
    nb = T // 128
    with ExitStack() as st:
        A = att_alloc(cx, S, st, "ap", P)
        KTall = sb(cx, st, "ap_KTall", [96, 8, T], BF16)
        caug = sb(cx, st, "ap_caug", [128, nb, 132], BF16)
        cst_ = [sb(cx, st, "ap_cst%d" % i, [128, 160], F32) for i in range(2)]
        r_KTall, r_caug = Res(), Res()
        r_cst = [Res(), Res()]
        d_cst = [S.dsem("ap_c0"), S.dsem("ap_c1")]
        d_q, d_o = S.dsem("ap_q"), S.dsem("ap_o")
        ps, rps = cx.ps, cx.rps
        S.op("pool", "memset", dict(ap=caug[:, :, 128:132], constant=1.0), writes=(r_caug,))
        S.op("pool", "memset", dict(ap=A.m01[:, :], constant=1.0), writes=(A.r_m01,))
        S.op("pool", "affine_select", dict(out=A.m01[:, :], in_=A.m01[:, :], pattern=[[1, 128]], compare_op=ALU.is_ge, fill=0.0,
                                           base=0, channel_multiplier=-1), writes=(A.r_m01,))
        for kb in range(nb):
            k = kb % 2
            S.dma("sp", d_cst[k], cst_[k][:, 0:128], grp["new_ckv"][kb * 128:(kb + 1) * 128, :], reads=(grp["r_ckv"],),
                  writes=(r_cst[k],))
            S.dma("sp", d_cst[k], cst_[k][:, 128:160], grp["new_kr"][kb * 128:(kb + 1) * 128, :], reads=(grp["r_kr"],),
                  writes=(r_cst[k],))
            S.op("dve", "tensor_copy", dict(out=caug[:, kb, 0:128], in_=cst_[k][:, 0:128]), reads=(r_cst[k],), writes=(r_caug,))
            att_kside(cx, S, A, cst_[k][:, 0:128], cst_[k][:, 128:160], r_cst[k], 128, KTall[:, :, kb * 128:(kb + 1) * 128], r_KTall)
        for qb in range(nb):
            S.dma("sp", d_q, A.Qb[:, :], grp["Q"][qb * 128:(qb + 1) * 128, :], reads=(grp["r_Q"],), writes=(A.r_Qb,))
            ptb = ps[2].bitcast(BF16)
            for h in range(8):
                S.op("pe", "transpose", dict(out=ptb[0:96, h * 128:(h + 1) * 128], in_=A.Qb[:, h * 96:(h + 1) * 96],
                                             identity=cx.ident_bf[:, :]), reads=(A.r_Qb,), writes=(rps[2],), inc=(h == 7))
            S.op("act", "activation", dict(out=A.QT[:, :, :], in_=ptb[0:96, :].rearrange("p (h t) -> p h t", t=128), func=AF.Copy),
                 reads=(rps[2],), writes=(A.r_QT,))
            gi = 0
            for h in range(8):
                pc = 4 + h % 2
                for kb0 in range(0, qb + 1, 4):
                    ng = min(4, qb + 1 - kb0)
                    pk = gi % 2
                    gi += 1
                    for j in range(ng):
                        kb = kb0 + j
                        S.op("pe", "matmul", dict(out=ps[pk][:, j * 128:(j + 1) * 128], lhsT=KTall[:, h, kb * 128:(kb + 1) * 128],
                                                  rhs=A.QT[:, h, :], start=True, stop=True, skip_group_check=True),
                             reads=(r_KTall, A.r_QT), writes=(rps[pk],), inc=(j == ng - 1))
                    S.op("act", "activation", dict(out=A.PT[:, 0:ng * 128], in_=ps[pk][:, 0:ng * 128], func=AF.Exp),
                         reads=(rps[pk],), writes=(A.r_PT,))
                    if kb0 + ng - 1 == qb:
                        j = ng - 1
                        S.op("dve", "tensor_tensor", dict(out=A.PT[:, j * 128:(j + 1) * 128], in0=A.PT[:, j * 128:(j + 1) * 128],
                                                          in1=A.m01[:, :], op=ALU.mult), reads=(A.r_m01,), writes=(A.r_PT,))
                    for j in range(ng):
                        kb = kb0 + j
                        S.op("pe", "matmul", dict(out=ps[pc][:, 0:129], lhsT=A.PT[:, j * 128:(j + 1) * 128], rhs=caug[:, kb, 0:129],
                                                  start=(kb == 0), stop=(kb == qb), skip_group_check=True),
                             reads=(A.r_PT, r_caug), writes=(rps[pc],), inc=(j == ng - 1))
                S.op("dve", "reciprocal", dict(out=A.rden[:, h:h + 1], in_=ps[pc][:, 128:129]), reads=(rps[pc],), writes=(A.r_rden,))
                S.op("dve", "tensor_scalar", dict(out=A.ctx[:, h, :], in0=ps[pc][:, 0:128], scalar1=A.rden[:, h:h + 1], scalar2=None,
                                                  op0=ALU.mult), reads=(rps[pc], A.r_rden), writes=(A.r_ctx,))
            att_out(cx, S, A, 128, A.r_ctx, grp["mix"][qb * 128:(qb + 1) * 128, 512:1024], grp["r_mix"], d_o)
        S.barrier()
        S.emit()


def phase_attn_sample(cx, S, grp, P, cache_c, cache_k, ptab, nseq, npages):
    with ExitStack() as st:
        A = att_alloc(cx, S, st, "as", P)
        cg = sb(cx, st, "as_cg", [128, 128 * 128], F32)
        kg = sb(cx, st, "as_kg", [128, 128 * 32], F32)
        idx = sb(cx, st, "as_idx", [128, 2], I32)
        cb = sb(cx, st, "as_cb", [128, 132], BF16)
        cn = sb(cx, st, "as_cn", [4, 160], F32)
        qs = sb(cx, st, "as_qs", [4, 768], BF16)
        QTs = sb(cx, st, "as_QTs", [96, 8, 4], BF16)
        PTs = sb(cx, st, "as_PTs", [128, 32], BF16)
        ms = sb(cx, st, "as_ms", [4, 32], BF16)
        c32 = sb(cx, st, "as_c32", [32, 128], BF16)
        cT32 = sb(cx, st, "as_cT32", [128, 32], BF16)
        rd = sb(cx, st, "as_rd", [32, 1], F32)
        mo = sb(cx, st, "as_mo", [4, 512], BF16)
        r_cg, r_kg, r_idx, r_cb, r_cn, r_qs, r_QTs, r_PTs, r_ms, r_c32, r_cT32, r_rd, r_mo = (Res() for _ in range(13))
        d_idx, d_cg, d_kg, d_cn, d_qs, d_mo = (S.dsem("as%d" % i) for i in range(6))
        ps, rps = cx.ps, cx.rps
        S.op("pool", "memset", dict(ap=cb[:, 128:132], constant=1.0), writes=(r_cb,))
        S.op("pool", "memset", dict(ap=ms[:, :], constant=1.0), writes=(r_ms,))
        S.op("pool", "affine_select", dict(out=ms[:, :], in_=ms[:, :], pattern=[[0, 8], [1, 4]], compare_op=ALU.is_ge, fill=0.0,
                                           base=0, channel_multiplier=-1), writes=(r_ms,))
        for s in range(nseq):
            S.dma("sp", d_idx, idx[:npages, 0:1], ptab[s:s + 1, :].rearrange("o p -> p o"), writes=(r_idx,),
                  allow_slow_non_contiguous=True)
            S.idma(d_cg, reads=(r_idx,), writes=(r_cg,), out=cg[:npages, :], out_offset=None, in_=cache_c,
                   in_offset=bass.IndirectOffsetOnAxis(ap=idx[:npages, 0:1], axis=0))
            S.idma(d_cg, reads=(r_idx,), writes=(r_cg, r_kg), out=kg[:npages, :], out_offset=None, in_=cache_k,
                   in_offset=bass.IndirectOffsetOnAxis(ap=idx[:npages, 0:1], axis=0))
            S.dma("sp", d_qs, qs[:, :], grp["Q"][4 * s:4 * s + 4, :], reads=(grp["r_Q"],), writes=(r_qs,))
            ptb = ps[2].bitcast(BF16)
            for h in range(8):
                S.op("pe", "transpose", dict(out=ptb[0:96, h * 128:h * 128 + 4], in_=qs[:, h * 96:(h + 1) * 96],
                                             identity=cx.ident_bf[:4, :4]), reads=(r_qs,), writes=(rps[2],), inc=(h == 7))
            S.op("act", "activation", dict(out=QTs[:, :, :], in_=ptb[0:96, :].rearrange("p (h t) -> p h t", t=128)[:, :, 0:4],
                                           func=AF.Copy), reads=(rps[2],), writes=(r_QTs,))
            S.dma("sp", d_cn, cn[:, 0:128], grp["new_ckv"][4 * s:4 * s + 4, :], reads=(grp["r_ckv"],), writes=(r_cn,))
            S.dma("sp", d_cn, cn[:, 128:160], grp["new_kr"][4 * s:4 * s + 4, :], reads=(grp["r_kr"],), writes=(r_cn,))
            nblk = 128 + 1
            for t in range(nblk):
                if t < 128:
                    n = npages
                    c_blk, k_blk, r_src = cg[:n, t * 128:(t + 1) * 128], kg[:n, t * 32:(t + 1) * 32], r_cg
                    rs = (r_cg, r_kg)
                else:
                    n = 4
                    c_blk, k_blk, r_src = cn[:, 0:128], cn[:, 128:160], r_cn
                    rs = (r_cn,)
                S.op("dve", "tensor_copy", dict(out=cb[:n, 0:128], in_=c_blk), reads=rs, writes=(r_cb,))
                att_kside(cx, S, A, c_blk, k_blk, Res() if False else r_src, n, A.KT[:, :, :n], A.r_KT)
                for h in range(8):
                    S.op("pe", "matmul", dict(out=ps[4][:n, h * 4:(h + 1) * 4], lhsT=A.KT[:, h, :n], rhs=QTs[:, h, :], start=True,
                                              stop=True, skip_group_check=True), reads=(A.r_KT, r_QTs), writes=(rps[4],),
                         inc=(h == 7))
                S.op("act", "activation", dict(out=PTs[:n, :], in_=ps[4][:n, 0:32], func=AF.Exp), reads=(rps[4],), writes=(r_PTs,))
                if t == 128:
                    S.op("dve", "tensor_tensor", dict(out=PTs[:n, :], in0=PTs[:n, :], in1=ms[:n, :], op=ALU.mult),
                         reads=(r_ms,), writes=(r_PTs,))
                S.op("pe", "matmul", dict(out=ps[5][0:32, 0:129], lhsT=PTs[:n, :], rhs=cb[:n, 0:129], start=(t == 0),
                                          stop=(t == nblk - 1), skip_group_check=True), reads=(r_PTs, r_cb), writes=(rps[5],))
            S.op("dve", "reciprocal", dict(out=rd[:, :], in_=ps[5][0:32, 128:129]), reads=(rps[5],), writes=(r_rd,))
            S.op("dve", "tensor_scalar", dict(out=c32[:, :], in0=ps[5][0:32, 0:128], scalar1=rd[:, 0:1], scalar2=None,
                                              op0=ALU.mult), reads=(rps[5], r_rd), writes=(r_c32,))
            S.op("pe", "transpose", dict(out=ptb[:, 0:32], in_=c32[:, :], identity=cx.ident_bf[:32, :32]), reads=(r_c32,),
                 writes=(rps[2],))
            S.op("act", "activation", dict(out=cT32[:, :], in_=ptb[:, 0:32], func=AF.Copy), reads=(rps[2],), writes=(r_cT32,))
            for h in range(8):
                S.op("pe", "matmul", dict(out=ps[3][0:4, h * 64:(h + 1) * 64], lhsT=cT32[:, h * 4:(h + 1) * 4],
                                          rhs=A.wuv[:, h * 64:(h + 1) * 64], start=True, stop=True, skip_group_check=True),
                     reads=(r_cT32, A.r_wuv), writes=(rps[3],), inc=(h == 7))
            S.op("dve", "tensor_copy", dict(out=mo[:, :], in_=ps[3][0:4, :]), reads=(rps[3],), writes=(r_mo,))
            S.dma("sp", d_mo, grp["mix"][4 * s:4 * s + 4, 512:1024], mo[:, :], reads=(r_mo,), writes=(grp["r_mix"],))
            if s % 4 == 3:
                S.barrier()
                S.emit()
        S.barrier()
        S.emit()


CST_COLS = 128 + 128 + 128 + 128 + 1024 + 3 * 256 + 3 * 16


def host_consts():
    c = np.zeros((128, CST_COLS), np.float32)
    j = np.arange(128)[:, None]
    i = np.arange(128)[None, :]
    same = (j // 64 == i // 64)
    c[:, 0:128] = np.eye(128)
    c[:, 128:256] = (j <= i) & same
    c[:, 256:384] = same
    c[:, 384:512] = same
    oh = np.zeros((8, 8, 128), np.float32)
    for h in range(8):
        oh[h, h, :] = 1.0
    c[0:8, 512:1536] = oh.reshape(8, 1024)
    off = 1536
    for m in (64, 4):
        jj = np.arange(m)[:, None]
        ii = np.arange(m)[None, :]
        c[0:m, off:off + 4 * m] = np.tile(np.eye(m, dtype=np.float32), (1, 4))
        c[0:m, off + 4 * m:off + 8 * m] = np.tile(np.where(jj <= ii, 0.0, -1e30).astype(np.float32), (1, 4))
        c[0:m, off + 8 * m:off + 12 * m] = np.tile(np.where(jj < ii, 0.0, -1e30).astype(np.float32), (1, 4))
        off += 12 * m
    return c


def host_rope_table(pos):
    half = 16
    inv = (np.float32(10000.0) ** (-(np.arange(half, dtype=np.float32) / np.float32(half)))).astype(np.float32)
    ang = (pos.astype(np.float32)[:, None] * inv[None, :]).astype(np.float32)
    return np.concatenate([np.cos(ang), np.sin(ang)], axis=1).astype(np.float32)


def build(cfg):
    TP = cfg["TP"]
    NS = cfg["NS"]
    NSEQ = NS // 4
    phases = cfg.get("phases", ("adaln", "ffn1"))
    nc = bass.Bass("TRN2", target_bir_lowering=False)
    cx = Ctx()
    cx.nc = nc
    cx.stop = cfg.get("stop", 99)

    def din(name, shape, dt=F32):
        return nc.dram_tensor(name, list(shape), dt, kind="ExternalInput").ap()

    def dout(name, shape, dt=F32):
        return nc.dram_tensor(name, list(shape), dt, kind="ExternalOutput").ap()

    def dscr(name, shape, dt=F32):
        return nc.dram_tensor(name, list(shape), dt, kind="Internal").ap()

    xp = din("xp", [TP, D])
    xs_in = din("xs", [NS, D])
    c_rep = din("c_rep", [192, D])
    cst = din("cst", [128, CST_COLS])
    ada_w = din("ada_w", [D, 9 * D])
    ada_b = din("ada_b", [1, 9 * D])
    norm_ffn1 = din("norm_ffn1", [1, D])
    ffn1_wi = din("ffn1_wi", [D, 2 * DFF])
    ffn1_wo = din("ffn1_wo", [DFF, D])
    P = dict(norm_mix=din("norm_mix", [1, D]), w_in=din("w_in", [D, 2992]), conv_w_fm=din("conv_w_fm", [128, 12, 4]),
             a_log=din("a_log", [1, 8]), dt_bias=din("dt_bias", [1, 8]), gdn_norm=din("gdn_norm", [1, 64]),
             qn_g=din("qn_g", [1, 64]), qr_g=din("qr_g", [1, 32]), ckv_g=din("ckv_g", [1, 128]), kr_g=din("kr_g", [1, 32]),
             kn_g=din("kn_g", [1, 64]))
    P["w_uk"] = din("w_uk", [128, 512])
    P["w_uv"] = din("w_uv", [128, 512])
    w_out = din("w_out", [D, D])
    norm_ffn2 = din("norm_ffn2", [1, D])
    ffn2_wi = din("ffn2_wi", [D, 2 * DFF])
    ffn2_wo = din("ffn2_wo", [DFF, D])
    NPOOL = cfg.get("npool", 20480)
    cache_c = din("cache_c", [NPOOL, 128 * 128])
    cache_k = din("cache_k", [NPOOL, 128 * 32])
    ptab = din("ptab", [NSEQ, cfg.get("npages", 128)], I32)
    cs_p = din("cs_p", [TP, 32])
    cs_s = din("cs_s", [4, 32])
    st_conv = din("st_conv", [NSEQ, 3, 1536])
    st_gdn = din("st_gdn", [NSEQ, 8, 64, 64])

    yp = dout("yp", [TP, D])
    ys = dout("ys", [NS, D])
    o_ckv_p, o_kr_p = dout("ckv_p", [TP, 128]), dout("kr_p", [TP, 32])
    o_conv_p, o_gdn_p = dout("conv_p", [1, 3, 1536]), dout("gdn_p", [1, 8, 64, 64])
    o_ckv_s, o_kr_s = dout("ckv_s", [NS, 128]), dout("kr_s", [NS, 32])
    o_conv_s, o_gdn_s = dout("conv_s", [NSEQ, 3, 1536]), dout("gdn_s", [NSEQ, 8, 64, 64])
    mods_p = dscr("mods_p", [9, 128, D])
    mods_s = dscr("mods_s", [9, 64, D])
    x1p, x1s = dscr("x1p", [TP, D]), dscr("x1s", [NS, D])
    x2p, x2s = dscr("x2p", [TP, D]), dscr("x2s", [NS, D])
    dbg = dout if cfg.get("debug") else dscr
    mix_p, mix_s = dbg("mix_p", [TP, D], BF16), dbg("mix_s", [NS, D], BF16)
    Q_p, Q_s = dbg("Q_p", [TP, 768], BF16), dbg("Q_s", [NS, 768], BF16)

    with ExitStack() as stack:
        S = Sched(nc, stack)
        cx.S = S
        cx.ps = [stack.enter_context(nc.psum_tensor("ps%d" % i, [128, 512], F32))[:, :] for i in range(8)]
        cx.rps = [Res("ps%d" % i) for i in range(8)]
        cst_f = sb(cx, stack, "cst_f", [128, 128], F32)
        cst_b = sb(cx, stack, "cst_b", [128, 128 + 128 + 512 + 128], BF16)
        cx.cst = cst
        cx.eps_t = sb(cx, stack, "eps_t", [128, 1], F32)
        cx.one_t = sb(cx, stack, "one_t", [128, 1], F32)
        cx.ident_f = cst_f[:, 0:128]
        cx.ident_bf = cst_b[:, 0:128]
        cx.nident_bf = cst_b[:, 128:256]
        cx.ident4_bf = cst_b[:, 256:768]
        cx.bones = cst_b[:, 768:896]
        r_c = Res("consts")
        d_c = S.dsem("consts")
        S.dma("sp", d_c, cst_f[:, :], cst[:, 0:128], writes=(r_c,))
        bon = sb(cx, stack, "bon_tmp", [128, 128], F32)
        S.dma("sp", d_c, bon[:, :], cst[:, 384:512], writes=(r_c,))
        S.op("dve", "tensor_copy", dict(out=cst_b[:, 0:128], in_=cst_f[:, 0:128]), reads=(r_c,), writes=(r_c,))
        S.op("dve", "tensor_scalar", dict(out=cst_b[:, 128:256], in0=cst_f[:, 0:128], scalar1=-1.0, scalar2=None,
                                          op0=ALU.mult), reads=(r_c,), writes=(r_c,))
        for h in range(4):
            S.op("dve", "tensor_copy", dict(out=cst_b[:, 256 + h * 128:256 + (h + 1) * 128], in_=cst_f[:, 0:128]),
                 reads=(r_c,), writes=(r_c,))
        S.op("dve", "tensor_copy", dict(out=cst_b[:, 768:896], in_=bon[:, :]), reads=(r_c,), writes=(r_c,))
        S.op("dve", "memset", dict(ap=cx.eps_t[:, :], constant=EPS), writes=(r_c,))
        S.op("dve", "memset", dict(ap=cx.one_t[:, :], constant=1.0), writes=(r_c,))
        S.barrier()
        S.emit()

        r_mods = Res("mods")
        if "adaln" in phases:
            phase_adaln(cx, S, c_rep, ada_w, ada_b, mods_p, mods_s, r_mods)

        nblk_p = (TP + 127) // 128
        r_xp = [Res() for _ in range(nblk_p)]
        r_xs = [Res()]
        r_x1p = [Res() for _ in range(nblk_p)]
        r_x1s = [Res()]
        f1_dst_p, f1_dst_s = (x1p, x1s) if "mixer" in phases else (yp, ys)
        if "ffn1" in phases:
            groups = [dict(src=xp, dst=f1_dst_p, T=TP, mods=mods_p, n=128, src_res=r_xp, dst_res=r_x1p),
                      dict(src=xs_in, dst=f1_dst_s, T=NS, mods=mods_s, n=64, src_res=r_xs, dst_res=r_x1s)]
            phase_ffn(cx, S, "f1", groups, ffn1_wi, ffn1_wo, norm_ffn1, 0)
        r_out = Res("outs")
        gp = gs = None
        if "mixer" in phases:
            msrc_p, msrc_s = (x1p, x1s) if "ffn1" in phases else (xp, xs_in)
            gp = dict(src=msrc_p, src_res=r_x1p, T=TP, nseq=1, mods=mods_p, n=128, s0=None, conv0=None, cs=cs_p,
                      new_conv=o_conv_p, new_gdn=o_gdn_p, new_ckv=o_ckv_p, new_kr=o_kr_p, mix=mix_p, Q=Q_p,
                      r_out=r_out, r_mix=Res(), r_Q=Res(), r_ckv=Res(), r_kr=Res())
            gs = dict(src=msrc_s, src_res=r_x1s, T=NS, nseq=NSEQ, mods=mods_s, n=64, s0=st_gdn, conv0=st_conv, cs=cs_s,
                      new_conv=o_conv_s, new_gdn=o_gdn_s, new_ckv=o_ckv_s, new_kr=o_kr_s, mix=mix_s, Q=Q_s,
                      r_out=r_out, r_mix=Res(), r_Q=Res(), r_ckv=Res(), r_kr=Res())
            phase_mixer(cx, S, [gp, gs], P)
        if "attn_p" in phases:
            phase_attn_prompt(cx, S, gp, P)
        if "attn_s" in phases:
            phase_attn_sample(cx, S, gs, P, cache_c, cache_k, ptab, NSEQ, cfg.get("npages", 128))
        if "wout" in phases:
            r_x2p = [Res() for _ in range(nblk_p)]
            r_x2s = [Res()]
            gp.update(dst=x2p, dst_res=r_x2p)
            gs.update(dst=x2s, dst_res=r_x2s)
            phase_wout(cx, S, [gp, gs], w_out)
            if "ffn2" in phases:
                groups = [dict(src=x2p, dst=yp, T=TP, mods=mods_p, n=128, src_res=r_x2p, dst_res=[Res() for _ in range(nblk_p)]),
                          dict(src=x2s, dst=ys, T=NS, mods=mods_s, n=64, src_res=r_x2s, dst_res=[Res()])]
                phase_ffn(cx, S, "f2", groups, ffn2_wi, ffn2_wo, norm_ffn2, 6)
        S.barrier()
        S.emit()
    return nc


ALL_PHASES = ("adaln", "ffn1", "mixer", "attn_p", "attn_s", "wout", "ffn2")


def make_inputs(core, inputs, TP=4096):
    f = lambda a: np.ascontiguousarray(np.asarray(a, dtype=np.float32))
    b = core % 4
    sl = slice(16 * core, 16 * core + 16)
    conv_w = f(inputs["gdn_conv_w"][0])
    m = dict(
        xp=f(inputs["x_prompt"][b][:TP]), xs=f(inputs["x_sample"][sl]).reshape(64, D),
        c_rep=np.concatenate([np.repeat(f(inputs["c_prompt"][b:b + 1]), 128, 0), np.repeat(f(inputs["c_sample"][sl]), 4, 0)], 0),
        cst=host_consts(), ada_w=f(inputs["ada_w"][0]), ada_b=f(inputs["ada_b"][0:1]),
        norm_ffn1=f(inputs["norm_ffn1"][0:1]), ffn1_wi=f(inputs["ffn1_wi"][0]), ffn1_wo=f(inputs["ffn1_wo"][0]),
        norm_mix=f(inputs["norm_mix"][0:1]), w_in=f(inputs["w_in"][0]),
        conv_w_fm=np.ascontiguousarray(conv_w.reshape(4, 12, 128).transpose(2, 1, 0)),
        a_log=f(inputs["gdn_a_log"][0:1]), dt_bias=f(inputs["gdn_dt_bias"][0:1]), gdn_norm=f(inputs["gdn_norm"][0:1]),
        qn_g=f(inputs["mla_qn_norm"][0:1]), qr_g=f(inputs["mla_qr_norm"][0:1]), ckv_g=f(inputs["mla_ckv_norm"][0:1]),
        kr_g=f(inputs["mla_kr_norm"][0:1]), kn_g=f(inputs["mla_kn_norm"][0:1]),
        w_uk=f(inputs["mla_w_uk"][0]).reshape(128, 512), w_uv=f(inputs["mla_w_uv"][0]).reshape(128, 512),
        w_out=f(inputs["w_out"][0]), norm_ffn2=f(inputs["norm_ffn2"][0:1]), ffn2_wi=f(inputs["ffn2_wi"][0]),
        ffn2_wo=f(inputs["ffn2_wo"][0]),
        cache_c=f(inputs["cache_ckv"][0]).reshape(-1, 128 * 128), cache_k=f(inputs["cache_krope"][0]).reshape(-1, 128 * 32),
        ptab=np.ascontiguousarray(np.asarray(inputs["page_table"][sl], dtype=np.int32)),
        cs_p=host_rope_table(np.arange(TP)), cs_s=host_rope_table(16384 + np.arange(4)),
        st_conv=f(inputs["state_conv"][0][sl]), st_gdn=f(inputs["state_gdn"][0][sl]),
    )
    return m


def kernel(**inputs):
    nc = build(dict(TP=4096, NS=64, phases=ALL_PHASES))
    in_maps = [make_inputs(c, inputs) for c in range(8)]
    res = run_bass_kernel_spmd(nc, in_maps, core_ids=list(range(8))).results
    f32 = np.float32
    yp = np.stack([res[b]["yp"] for b in range(4)]).astype(f32)
    ys = np.concatenate([res[c]["ys"].reshape(16, 4, D) for c in range(8)]).astype(f32)
    ckv_p = np.stack([res[b]["ckv_p"] for b in range(4)])[None].astype(f32)
    kr_p = np.stack([res[b]["kr_p"] for b in range(4)])[None].astype(f32)
    conv_p = np.concatenate([res[b]["conv_p"] for b in range(4)])[None].astype(f32)
    gdn_p = np.concatenate([res[b]["gdn_p"] for b in range(4)])[None].astype(f32)
    ckv_s = np.concatenate([res[c]["ckv_s"].reshape(16, 4, 128) for c in range(8)])[None].astype(f32)
    kr_s = np.concatenate([res[c]["kr_s"].reshape(16, 4, 32) for c in range(8)])[None].astype(f32)
    conv_s = np.concatenate([res[c]["conv_s"] for c in range(8)])[None].astype(f32)
    gdn_s = np.concatenate([res[c]["gdn_s"] for c in range(8)])[None].astype(f32)
    return (yp, ys, ckv_p, kr_p, conv_p, gdn_p, ckv_s, kr_s, conv_s, gdn_s)
```

```python
import numpy as np
from contextlib import ExitStack
import concourse.bass as bass
import concourse.mybir as mybir
from concourse.bass_utils import run_bass_kernel_spmd

F32 = mybir.dt.float32
BF16 = mybir.dt.bfloat16
I32 = mybir.dt.int32
U32 = mybir.dt.uint32
AF = mybir.ActivationFunctionType
ALU = mybir.AluOpType
AX = mybir.AxisListType

D = 1024
DFF = 2816
NFC = DFF // 128
EPS = 1e-6

class Res:
    __slots__ = ("name", "w", "r")

    def __init__(self, name=""):
        self.name = name
        self.w = None
        self.r = {}


class DSem:
    __slots__ = ("sem", "cnt", "name")

    def __init__(self, sem, name):
        self.sem = sem
        self.cnt = 0
        self.name = name


class Sched:
    ENGS = ("pe", "act", "dve", "pool", "sp")

    def __init__(self, nc, stack):
        self.nc = nc
        self.stack = stack
        self.q = {k: [] for k in self.ENGS}
        self.esem = {k: stack.enter_context(nc.semaphore("es_" + k)) for k in self.ENGS}
        self.ecnt = {k: 0 for k in self.ENGS}
        self.known = {k: {} for k in self.ENGS}
        self.dsems = []
        self.nwait = 0
        self.nop = 0

    def dsem(self, name):
        s = self.stack.enter_context(self.nc.semaphore("ds_" + name))
        d = DSem(s, name)
        self.dsems.append(d)
        return d

    def _waits(self, eng, reads, writes):
        need = {}

        def add(ev):
            sem_id, sem, val, src = ev
            if src == "pe" and eng == "pe":
                return
            if self.known[eng].get(sem_id, 0) >= val:
                return
            if sem_id not in need or need[sem_id][1] < val:
                need[sem_id] = (sem, val)

        for r in reads:
            if r.w is not None:
                add(r.w)
        for w in writes:
            if w.w is not None:
                add(w.w)
            for ev in w.r.values():
                add(ev)
        for sem_id, (sem, val) in need.items():
            self.q[eng].append(("wait", sem, val))
            self.known[eng][sem_id] = val
            self.nwait += 1

    def _record(self, ev, reads, writes):
        for r in reads:
            old = r.r.get(ev[0])
            if old is None or old[2] < ev[2]:
                r.r[ev[0]] = ev
        for w in writes:
            w.w = ev
            w.r = {}

    def op(self, eng, meth, kw, reads=(), writes=(), inc=True):
        self._waits(eng, reads, writes)
        if inc:
            self.ecnt[eng] += 1
            ev = (eng, self.esem[eng], self.ecnt[eng], eng)
        else:
            ev = (eng, self.esem[eng], self.ecnt[eng] + 1, eng)
        self.q[eng].append(("op", meth, kw, inc))
        self.nop += 1
        self._record(ev, reads, writes)

    def dma(self, q, ds, out, in_, reads=(), writes=(), **kw):
        self._waits(q, reads, writes)
        ds.cnt += 16
        ev = (id(ds), ds.sem, ds.cnt, "dma")
        self.q[q].append(("dma", out, in_, ds.sem, kw))
        self.nop += 1
        self._record(ev, reads, writes)

    def idma(self, ds, reads=(), writes=(), **kw):
        self._waits("pool", reads, writes)
        ds.cnt += 16
        ev = (id(ds), ds.sem, ds.cnt, "dma")
        self.q["pool"].append(("idma", kw, ds.sem))
        self.nop += 1
        self._record(ev, reads, writes)

    def barrier(self):
        for e in self.ENGS:
            for e2 in self.ENGS:
                if e2 == e:
                    continue
                v = self.ecnt[e2]
                if v > 0 and self.known[e].get(e2, 0) < v:
                    self.q[e].append(("wait", self.esem[e2], v))
                    self.known[e][e2] = v
            for d in self.dsems:
                if d.cnt > 0 and self.known[e].get(id(d), 0) < d.cnt:
                    self.q[e].append(("wait", d.sem, d.cnt))
                    self.known[e][id(d)] = d.cnt

    def emit(self):
        import os
        if os.environ.get("EMITLOG"):
            print("EMIT", {k: len(v) for k, v in self.q.items()}, "cnt", dict(self.ecnt), "ndsem", len(self.dsems))
        nc = self.nc
        engs = {"pe": "tensor", "act": "scalar", "dve": "vector", "pool": "gpsimd", "sp": "sync"}
        with nc.Block() as block:
            for k, attr in engs.items():
                items = self.q[k]
                esem = self.esem[k]

                def body(e, items=items, esem=esem):
                    for it in items:
                        if it[0] == "wait":
                            e.wait_ge(it[1], it[2])
                        elif it[0] == "op":
                            ins = getattr(e, it[1])(**it[2])
                            if it[3]:
                                ins.then_inc(esem, 1)
                        elif it[0] == "idma":
                            e.indirect_dma_start(**it[1]).then_inc(it[2], 16)
                        else:
                            _, out, in_, sem, kw = it
                            e.dma_start(out=out, in_=in_, **kw).then_inc(sem, 16)

                getattr(block, attr)(body)
        self.q = {k: [] for k in self.ENGS}


class Ctx:
    pass


def sb(cx, stack, name, shape, dt):
    return stack.enter_context(cx.nc.sbuf_tensor(name, list(shape), dt))


def row_bcast(ap_row, n):
    t = ap_row.tensor
    F = ap_row.shape[-1]
    return bass.AP(t, ap_row.offset, [[0, n], [1, F]])


def load_weight_bf16(cx, S, name, dst, dst_res, w_ap, kchunks, cols, col_piece):
    with ExitStack() as st:
        stg = [sb(cx, st, name + "stg%d" % i, [128, col_piece], F32) for i in range(3)]
        r_stg = [Res() for _ in range(3)]
        d_stg = [S.dsem(name + "stg%d" % i) for i in range(3)]
        engs = (("dve", "tensor_copy"), ("pool", "tensor_copy"), ("act", "activation"))
        i = 0
        for c in range(kchunks):
            for c0 in range(0, cols, col_piece):
                c1 = min(cols, c0 + col_piece)
                k = i % 3
                i += 1
                S.dma("sp", d_stg[k], stg[k][:, :c1 - c0], w_ap[c * 128:(c + 1) * 128, c0:c1], writes=(r_stg[k],))
                eng, meth = engs[k]
                kw = dict(out=dst[:, c, c0:c1], in_=stg[k][:, :c1 - c0])
                if meth == "activation":
                    kw["func"] = AF.Copy
                S.op(eng, meth, kw, reads=(r_stg[k],), writes=(dst_res,))
        S.barrier()
        S.emit()


def norm_mod_transpose(cx, S, bufs, src_ap, src_res, m, hT_dst, pt_k):
    (xs, tt, junk, hb, stat, GG, Bt, r_xs, r_tt, r_junk, r_hb, r_stat, r_GG, r_Bt, r_hT, d_xs) = bufs
    S.dma("sp", d_xs, xs[:m, :], src_ap, reads=(src_res,), writes=(r_xs,))
    S.op("act", "activation", dict(out=junk[:m, :], in_=xs[:m, :], func=AF.Square, accum_out=stat[:m, 0:1]),
         reads=(r_xs,), writes=(r_junk, r_stat))
    S.op("act", "activation", dict(out=stat[:m, 1:2], in_=stat[:m, 0:1], func=AF.Sqrt, scale=1.0 / D,
                                   bias=cx.eps_t[:m, 0:1]), reads=(r_stat,), writes=(r_stat,))
    S.op("dve", "reciprocal", dict(out=stat[:m, 2:3], in_=stat[:m, 1:2]), reads=(r_stat,), writes=(r_stat,))
    S.op("dve", "scalar_tensor_tensor", dict(out=tt[:m, :], in0=xs[:m, :], scalar=stat[:m, 2:3], in1=GG[:m, :],
                                             op0=ALU.mult, op1=ALU.mult),
         reads=(r_xs, r_stat, r_GG), writes=(r_tt,))
    S.op("pool", "tensor_tensor", dict(out=hb[:m, :], in0=tt[:m, :], in1=Bt[:m, :], op=ALU.add),
         reads=(r_tt, r_Bt), writes=(r_hb,))
    ptb = cx.ps[pt_k].bitcast(BF16)
    for c in range(8):
        S.op("pe", "transpose", dict(out=ptb[:, c * 128:c * 128 + m], in_=hb[:m, c * 128:(c + 1) * 128],
                                     identity=cx.ident_bf[:m, :m]),
             reads=(r_hb,), writes=(cx.rps[pt_k],), inc=(c == 7))
    S.op("act", "activation", dict(out=hT_dst, in_=ptb.rearrange("p (c t) -> p c t", c=8)[:, :, :m], func=AF.Copy),
         reads=(cx.rps[pt_k],), writes=(r_hT,))


def load_mod_tiles(cx, S, g_ap, mods, mod_base, n, gscale, GG, Bt, Gt, tt, r_GG, r_Bt, r_Gt, r_tt, d_m):
    S.dma("sp", d_m, tt[:n, :], row_bcast(g_ap, n), writes=(r_tt,))
    S.dma("sp", d_m, GG[:n, :], mods[mod_base + 1, :n, :], writes=(r_GG,))
    S.dma("sp", d_m, Bt[:n, :], mods[mod_base + 0, :n, :], writes=(r_Bt,))
    if Gt is not None:
        S.dma("sp", d_m, Gt[:n, :], mods[mod_base + 2, :n, :], writes=(r_Gt,))
    S.barrier()
    S.op("dve", "scalar_tensor_tensor", dict(out=GG[:n, :], in0=GG[:n, :], scalar=1.0, in1=tt[:n, :],
                                             op0=ALU.add, op1=ALU.mult), reads=(r_tt,), writes=(r_GG,))
    if Gt is not None and gscale != 1.0:
        S.op("dve", "tensor_scalar", dict(out=Gt[:n, :], in0=Gt[:n, :], scalar1=float(gscale), scalar2=None,
                                          op0=ALU.mult), writes=(r_Gt,))


def phase_ffn(cx, S, name, groups, wi_ap, wo_ap, g_ap, mod_base):
    with ExitStack() as st:
        wi = sb(cx, st, name + "wi", [128, 8, 2 * DFF], BF16)
        wo = sb(cx, st, name + "wo", [128, NFC, D], BF16)
        r_wi, r_wo = Res("wi"), Res("wo")
        load_weight_bf16(cx, S, name + "wi", wi, r_wi, wi_ap, 8, 2 * DFF, DFF)
        load_weight_bf16(cx, S, name + "wo", wo, r_wo, wo_ap, NFC, D, D)
        GG = sb(cx, st, name + "GG", [128, D], F32)
        Bt = sb(cx, st, name + "Bt", [128, D], F32)
        Gt = sb(cx, st, name + "Gt", [128, D], F32)
        xs = sb(cx, st, name + "xs", [128, D], F32)
        tt = sb(cx, st, name + "tt", [128, D], F32)
        junk = sb(cx, st, name + "junk", [128, D], BF16)
        hb = sb(cx, st, name + "hb", [128, D], BF16)
        hT = sb(cx, st, name + "hT", [128, 8, 512], BF16)
        sg = [sb(cx, st, name + "sg%d" % i, [128, 512], BF16) for i in range(2)]
        actT = sb(cx, st, name + "actT", [128, NFC, 512], BF16)
        xr = [sb(cx, st, name + "xr%d" % i, [128, D], F32) for i in range(2)]
        tmp = sb(cx, st, name + "tmp", [128, 512], F32)
        stat = sb(cx, st, name + "stat", [128, 4], F32)

        r_GG, r_Bt, r_Gt, r_xs, r_tt, r_junk, r_hb, r_hT = (Res(n_) for n_ in
                                                              ("GG", "Bt", "Gt", "xs", "tt", "junk", "hb", "hT"))
        r_sg = [Res("sg0"), Res("sg1")]
        r_actT = [Res("actT%d" % j) for j in range(NFC)]
        r_xr = [Res("xr0"), Res("xr1")]
        r_tmp, r_stat = Res("tmp"), Res("stat")
        d_m, d_xs = S.dsem(name + "m"), S.dsem(name + "xs")
        d_xr = [S.dsem(name + "xr0"), S.dsem(name + "xr1")]
        d_xst = [S.dsem(name + "xst0"), S.dsem(name + "xst1")]
        bufs = (xs, tt, junk, hb, stat, GG, Bt, r_xs, r_tt, r_junk, r_hb, r_stat, r_GG, r_Bt, r_hT, d_xs)


        pg, pu, po = cx.ps[0:2], cx.ps[2:4], cx.ps[4:6]
        r_pg, r_pu, r_po = cx.rps[0:2], cx.rps[2:4], cx.rps[4:6]
        cnt = {"up": 0, "po": 0, "pt": 0, "xr": 0}

        for g in groups:
            n = g["n"]
            load_mod_tiles(cx, S, g_ap, g["mods"], mod_base, n, 0.5, GG, Bt, Gt, tt, r_GG, r_Bt, r_Gt, r_tt, d_m)
            T = g["T"]
            nblk = (T + 511) // 512

            def stage_norm(b):
                t0 = b * 512
                tb = min(512, T - t0)
                for s_ in range((tb + 127) // 128):
                    r0 = t0 + s_ * 128
                    m = min(128, T - r0)
                    k = 6 + cnt["pt"] % 2
                    cnt["pt"] += 1
                    norm_mod_transpose(cx, S, bufs, g["src"][r0:r0 + m, :], g["src_res"][r0 // 128], m,
                                       hT[:, :, s_ * 128:s_ * 128 + m], k)

            def stage_up(b):
                t0 = b * 512
                tb = min(512, T - t0)
                for j in range(NFC):
                    k = cnt["up"] % 2
                    cnt["up"] += 1
                    for kc in range(8):
                        S.op("pe", "matmul", dict(out=pg[k][:, :tb], lhsT=wi[:, kc, j * 128:(j + 1) * 128],
                                                  rhs=hT[:, kc, :tb], start=(kc == 0), stop=(kc == 7)),
                             reads=(r_wi, r_hT), writes=(r_pg[k],), inc=(kc == 7))
                    for kc in range(8):
                        S.op("pe", "matmul", dict(out=pu[k][:, :tb], lhsT=wi[:, kc, DFF + j * 128:DFF + (j + 1) * 128],
                                                  rhs=hT[:, kc, :tb], start=(kc == 0), stop=(kc == 7)),
                             reads=(r_wi, r_hT), writes=(r_pu[k],), inc=(kc == 7))
                    S.op("act", "activation", dict(out=sg[k][:, :tb], in_=pg[k][:, :tb], func=AF.Silu),
                         reads=(r_pg[k],), writes=(r_sg[k],))
                    S.op("dve", "tensor_tensor", dict(out=actT[:, j, :tb], in0=sg[k][:, :tb], in1=pu[k][:, :tb],
                                                      op=ALU.mult),
                         reads=(r_sg[k], r_pu[k]), writes=(r_actT[j],))

            def stage_down(b):
                t0 = b * 512
                tb = min(512, T - t0)
                for s_ in range((tb + 127) // 128):
                    r0 = t0 + s_ * 128
                    m = min(128, T - r0)
                    kx = cnt["xr"] % 2
                    cnt["xr"] += 1
                    S.dma("sp", d_xr[kx], xr[kx][:m, :], g["src"][r0:r0 + m, :], reads=(g["src_res"][r0 // 128],),
                          writes=(r_xr[kx],))
                    for half in range(2):
                        k = cnt["po"] % 2
                        cnt["po"] += 1
                        for j in range(NFC):
                            S.op("pe", "matmul", dict(out=po[k][:m, :], lhsT=actT[:, j, s_ * 128:s_ * 128 + m],
                                                      rhs=wo[:, j, half * 512:(half + 1) * 512],
                                                      start=(j == 0), stop=(j == NFC - 1)),
                                 reads=(r_wo, r_actT[j]), writes=(r_po[k],), inc=(j == NFC - 1))
                        S.op("dve", "tensor_tensor", dict(out=tmp[:m, :], in0=po[k][:m, :],
                                                          in1=Gt[:m, half * 512:(half + 1) * 512], op=ALU.mult),
                             reads=(r_po[k], r_Gt), writes=(r_tmp,))
                        S.op("pool", "tensor_tensor", dict(out=xr[kx][:m, half * 512:(half + 1) * 512],
                                                           in0=xr[kx][:m, half * 512:(half + 1) * 512],
                                                           in1=tmp[:m, :], op=ALU.add),
                             reads=(r_tmp,), writes=(r_xr[kx],))
                    S.dma("pool", d_xst[kx], g["dst"][r0:r0 + m, :], xr[kx][:m, :], reads=(r_xr[kx],),
                          writes=(g["dst_res"][r0 // 128],))

            stage_norm(0)
            for b in range(nblk):
                stage_up(b)
                if b + 1 < nblk:
                    stage_norm(b + 1)
                stage_down(b)
        S.barrier()
        S.emit()


def phase_adaln(cx, S, c_rep, ada_w, ada_b, mods_p, mods_s, r_mods):
    with ExitStack() as st:
        ct = sb(cx, st, "ad_ct", [128, D], F32)
        cb_ = sb(cx, st, "ad_cb", [128, D], BF16)
        cT = sb(cx, st, "ad_cT", [128, 8, 192], BF16)
        w = [sb(cx, st, "ad_w%d" % i, [128, 8, D], BF16) for i in range(2)]
        wst = [sb(cx, st, "ad_wst%d" % i, [128, 8, D], F32) for i in range(2)]
        bb = [sb(cx, st, "ad_b%d" % i, [128, D], F32) for i in range(2)]
        o = [sb(cx, st, "ad_o%d" % i, [128, D], F32) for i in range(2)]
        r_ct, r_cb, r_cT = Res(), Res(), Res()
        r_w, r_wst, r_bb, r_o = [Res(), Res()], [Res(), Res()], [Res(), Res()], [Res(), Res()]
        d_ct = S.dsem("ad_ct")
        d_w = [S.dsem("ad_w0"), S.dsem("ad_w1")]
        d_b = [S.dsem("ad_b0"), S.dsem("ad_b1")]
        d_o = [S.dsem("ad_o0"), S.dsem("ad_o1")]

        def load(blk):
            k = blk % 2
            for hf in range(2):
                S.dma("sp", d_w[k], wst[k][:, hf * 4:(hf + 1) * 4, :],
                      ada_w[hf * 512:(hf + 1) * 512, blk * D:(blk + 1) * D].rearrange("(c p) f -> p c f", p=128),
                      writes=(r_wst[k],))
            S.dma("sp", d_b[k], bb[k][:, :], row_bcast(ada_b[0:1, blk * D:(blk + 1) * D], 128), writes=(r_bb[k],))

        load(0)
        for gi, (r0, m) in enumerate(((0, 128), (128, 64))):
            S.dma("sp", d_ct, ct[:m, :], c_rep[r0:r0 + m, :], writes=(r_ct,))
            S.op("act", "activation", dict(out=cb_[:m, :], in_=ct[:m, :], func=AF.Silu), reads=(r_ct,), writes=(r_cb,))
            ptb = cx.ps[6 + gi].bitcast(BF16)
            for c in range(8):
                S.op("pe", "transpose", dict(out=ptb[:, c * 128:c * 128 + m], in_=cb_[:m, c * 128:(c + 1) * 128],
                                             identity=cx.ident_bf[:m, :m]),
                     reads=(r_cb,), writes=(cx.rps[6 + gi],), inc=(c == 7))
            S.op("act", "activation", dict(out=cT[:, :, r0:r0 + m],
                                           in_=ptb.rearrange("p (c t) -> p c t", c=8)[:, :, :m], func=AF.Copy),
                 reads=(cx.rps[6 + gi],), writes=(r_cT,))
        for blk in range(9):
            k = blk % 2
            if blk + 1 < 9:
                load(blk + 1)
            S.op("pool", "tensor_copy", dict(out=w[k][:, 0:4, :], in_=wst[k][:, 0:4, :]),
                 reads=(r_wst[k],), writes=(r_w[k],))
            S.op("dve", "tensor_copy", dict(out=w[k][:, 4:8, :], in_=wst[k][:, 4:8, :]),
                 reads=(r_wst[k],), writes=(r_w[k],))
            for gi, (r0, m, dst) in enumerate(((0, 128, mods_p), (128, 64, mods_s))):
                ko = gi
                for half in range(2):
                    pk = (blk * 4 + gi * 2 + half) % 4
                    ps, rp = cx.ps[pk], cx.rps[pk]
                    for c in range(8):
                        S.op("pe", "matmul", dict(out=ps[:m, :], lhsT=cT[:, c, r0:r0 + m],
                                                  rhs=w[k][:, c, half * 512:(half + 1) * 512], start=(c == 0),
                                                  stop=(c == 7)), reads=(r_cT, r_w[k]), writes=(rp,), inc=(c == 7))
                    S.op("dve", "tensor_tensor", dict(out=o[ko][:m, half * 512:(half + 1) * 512], in0=ps[:m, :],
                                                      in1=bb[k][:m, half * 512:(half + 1) * 512], op=ALU.add),
                         reads=(rp, r_bb[k]), writes=(r_o[ko],))
                S.dma("act", d_o[ko], dst[blk, :m, :], o[ko][:m, :], reads=(r_o[ko],), writes=(r_mods,))
        S.barrier()
        S.emit()


NLEV = 5
QK_SCALE = 64 ** -0.5
ATT_SCALE = 96 ** -0.5


def bc(ap2, n):
    return ap2.unsqueeze(2).broadcast_to([ap2.shape[0], ap2.shape[1], n])


def bc_mid(ap2, n):
    return ap2.unsqueeze(1).broadcast_to([ap2.shape[0], n, ap2.shape[1]])


class MixBufs:
    pass


def mixer_alloc(cx, S, st, name, P):
    B = MixBufs()

    def t(nm, shape, dt):
        tt_ = sb(cx, st, name + nm, shape, dt)
        setattr(B, nm, tt_)
        setattr(B, "r_" + nm, Res(nm))
        return tt_

    t("win", [128, 8, 2992], BF16)
    load_weight_bf16(cx, S, "mxwin", B.win, B.r_win, P["w_in"], 8, 2992, 1496)
    t("GG", [128, D], F32); t("Bt", [128, D], F32)
    t("xs", [128, D], F32); t("tt", [128, D], F32); t("junk", [128, D], BF16); t("hb", [128, D], BF16)
    t("stat", [128, 4], F32)
    t("hT", [128, 8, 256], BF16)
    t("xq", [128, 12, 259], F32)
    t("yc", [128, 12, 256], F32)
    t("sqb", [128, 512], BF16)
    t("rinv", [128, 512], F32)
    t("qkT", [128, 8, 256], BF16)
    t("tm", [128, 1456], F32)
    t("sc", [128, 96], F32)
    t("rows", [8, 2, 128], F32)
    t("glb", [128, 8], F32)
    t("vtok", [128, 512], BF16); t("ktok", [128, 512], BF16); t("kdec", [128, 512], BF16)
    t("decq", [128, 512], F32); t("decb", [128, 512], F32)
    for g in range(2):
        for nm in ("A", "M", "P"):
            for k in range(2):
                t("%s%d%d" % (nm, g, k), [128, 512], F32)
        t("Pf%d" % g, [128, 512], BF16)
    t("qk", [128, 1024], BF16)
    t("u", [128, 512], F32)
    t("wT", [128, 4, 128], BF16)
    t("vnew", [128, 512], BF16)
    t("o1", [128, 512], F32)
    t("osb", [128, 512], F32)
    t("ost", [128, 16], F32)
    t("zs", [128, 512], F32)
    t("gout", [128, 512], BF16)
    t("S32", [128, 512], F32); t("Sbf", [128, 512], BF16)
    t("mq", [128, 768], F32); t("mqs", [128, 32], F32); t("Qb", [128, 768], BF16)
    t("cn", [128, 128], F32); t("krn", [128, 32], F32); t("krr", [128, 32], F32); t("qrr", [128, 8, 32], F32)
    t("cs", [128, 32], F32)
    t("cw", [128, 12, 4], F32)
    t("g8", [128, 16], F32)
    t("gng", [128, 64], F32); t("qng", [128, 64], F32); t("qrg", [128, 32], F32); t("ckg", [128, 128], F32)
    t("krg", [128, 32], F32); t("kng", [128, 64], F32)
    t("nc3", [128, 1536], F32)
    B.d = {}
    return B


def mixer_group(cx, S, B, name, grp, P):
    def dsem(k):
        if k not in B.d:
            B.d[k] = S.dsem(name + k)
        return B.d[k]

    def op(eng, meth, reads, writes, inc=True, **kw):
        S.op(eng, meth, kw, reads=reads, writes=writes, inc=inc)

    ps, rps = cx.ps, cx.rps
    T, nseq = grp["T"], grp["nseq"]
    Ts = T // nseq
    prompt = (nseq == 1)
    n = grp["n"]
    load_mod_tiles(cx, S, P["norm_mix"], grp["mods"], 3, n, 1.0, B.GG, B.Bt, None, B.tt, B.r_GG, B.r_Bt, None,
                   B.r_tt, dsem("m"))
    bufs = (B.xs, B.tt, B.junk, B.hb, B.stat, B.GG, B.Bt, B.r_xs, B.r_tt, B.r_junk, B.r_hb, B.r_stat, B.r_GG,
            B.r_Bt, B.r_hT, dsem("xs"))
    blk_T = 256 if prompt else T
    nblk = (T + blk_T - 1) // blk_T
    if prompt:
        op("dve", "memset", (), (B.r_S32,), ap=B.S32[:, :], constant=0.0)
        op("dve", "memset", (), (B.r_Sbf,), ap=B.Sbf[:, :], constant=0.0)
        op("pool", "memset", (), (B.r_xq,), ap=B.xq[:, :, 0:3], constant=0.0)
    ptk = [0]

    for b in range(nblk):
        t0 = b * blk_T
        tb = min(blk_T, T - t0)
        for s_ in range((tb + 127) // 128):
            r0 = t0 + s_ * 128
            m = min(128, T - r0)
            k = 6 + ptk[0] % 2
            ptk[0] += 1
            norm_mod_transpose(cx, S, bufs, grp["src"][r0:r0 + m, :], grp["src_res"][r0 // 128], m,
                               B.hT[:, :, s_ * 128:s_ * 128 + m], k)
        if prompt:
            xq_new = B.xq[:, :, 3:3 + tb]
            xqv = None
        else:
            xqv = B.xq[:, :, 0:nseq * 7].rearrange("p c (s t) -> p c s t", t=7)
            op("sp", "dma_start", (), (), ) if False else None
            S.dma("sp", dsem("nc3"), B.nc3[:nseq * 3, :], grp["conv0"].rearrange("s t c -> (s t) c"), writes=(B.r_nc3,))
            for c in range(12):
                kq = c % 2
                op("pe", "transpose", (B.r_nc3,), (rps[kq],), out=ps[kq][:, :nseq * 3],
                   in_=B.nc3[:nseq * 3, c * 128:(c + 1) * 128], identity=cx.ident_f[:nseq * 3, :nseq * 3])
                op("act" if c % 2 else "dve", "activation" if c % 2 else "tensor_copy", (rps[kq],), (B.r_xq,),
                   out=xqv[:, c, :, 0:3], in_=ps[kq][:, :nseq * 3].rearrange("p (s t) -> p s t", t=3),
                   **({"func": AF.Copy} if c % 2 else {}))
        for c in range(12):
            kq = c % 2
            for kc in range(8):
                op("pe", "matmul", (B.r_win, B.r_hT), (rps[kq],), inc=(kc == 7), out=ps[kq][:, :tb],
                   lhsT=B.win[:, kc, c * 128:(c + 1) * 128], rhs=B.hT[:, kc, :tb], start=(kc == 0), stop=(kc == 7))
            if prompt:
                dst, srcp = B.xq[:, c, 3:3 + tb], ps[kq][:, :tb]
            else:
                dst, srcp = xqv[:, c, :, 3:7], ps[kq][:, :tb].rearrange("p (s t) -> p s t", t=Ts)
            if c % 2:
                op("act", "activation", (rps[kq],), (B.r_xq,), out=dst, in_=srcp, func=AF.Copy)
            else:
                op("dve", "tensor_copy", (rps[kq],), (B.r_xq,), out=dst, in_=srcp)
        for c in range(12):
            if prompt:
                ydst = B.yc[:, c, :tb]
                xin = [B.xq[:, c, j:j + tb] for j in range(4)]
            else:
                ydst = B.yc[:, c, :tb].rearrange("p (s t) -> p s t", t=Ts)
                xin = [xqv[:, c, :, j:j + Ts] for j in range(4)]
            eng = "dve"
            op(eng, "tensor_scalar", (B.r_xq, B.r_cw), (B.r_yc,), out=ydst, in0=xin[0], scalar1=B.cw[:, c, 0:1],
               scalar2=None, op0=ALU.mult)
            for j in range(1, 4):
                op(eng, "scalar_tensor_tensor", (B.r_xq, B.r_cw), (B.r_yc,), out=ydst, in0=xin[j],
                   scalar=B.cw[:, c, j:j + 1], in1=ydst, op0=ALU.mult, op1=ALU.add)
        for c in range(12):
            op("act", "activation", (B.r_yc,), (B.r_yc,), out=B.yc[:, c, :tb], in_=B.yc[:, c, :tb], func=AF.Silu)
        if prompt and b + 1 < nblk:
            op("pool", "tensor_copy", (B.r_xq,), (B.r_xq,), out=B.xq[:, :, 0:3], in_=B.xq[:, :, tb:tb + 3])
        for c in range(8):
            kq = c % 2
            op("pool", "tensor_tensor", (B.r_yc,), (B.r_sqb,), out=B.sqb[:, :tb], in0=B.yc[:, c, :tb], in1=B.yc[:, c, :tb],
               op=ALU.mult)
            op("pe", "matmul", (B.r_sqb,), (rps[kq],), out=ps[kq][:, :tb], lhsT=cx.bones[:, :], rhs=B.sqb[:, :tb],
               start=True, stop=True)
            op("act", "activation", (rps[kq],), (B.r_rinv,), out=B.rinv[:, :tb], in_=ps[kq][:, :tb], func=AF.Sqrt,
               bias=cx.eps_t[:, 0:1], scale=1.0)
            op("dve", "reciprocal", (B.r_rinv,), (B.r_rinv,), out=B.rinv[:, :tb], in_=B.rinv[:, :tb])
            if c < 4:
                op("dve", "scalar_tensor_tensor", (B.r_rinv, B.r_yc), (B.r_qkT,), out=B.qkT[:, c, :tb], in0=B.yc[:, c, :tb],
                   scalar=QK_SCALE, in1=B.rinv[:, :tb], op0=ALU.mult, op1=ALU.mult)
            else:
                op("dve", "tensor_tensor", (B.r_rinv, B.r_yc), (B.r_qkT,), out=B.qkT[:, c, :tb], in0=B.yc[:, c, :tb],
                   in1=B.rinv[:, :tb], op=ALU.mult)
        ntile = (tb + 63) // 64 if prompt else nseq
        if cx.stop <= 1:
            ntile = 0
        for ti in range(ntile):
            if prompt:
                c0 = ti * 64
                m = min(64, tb - c0)
                seq = 0
            else:
                c0 = ti * Ts
                m = Ts
                seq = ti
            tok0 = t0 + c0
            mixer_tile(cx, S, B, name, grp, P, m, c0, tok0, seq, prompt, dsem, op,
                       first=(tok0 == 0) if prompt else True, last=(tok0 + m == T) if prompt else True)


def mixer_tile(cx, S, B, name, grp, P, m, c0, tok0, seq, prompt, dsem, op, first, last):
    ps, rps = cx.ps, cx.rps
    sc, rsc = B.sc, B.r_sc
    for gi, (cA, cB) in enumerate(((1536, 2048), (2048, 2560), (2560, 2992))):
        kq = gi % 2
        for kc in range(8):
            op("pe", "matmul", (B.r_win, B.r_hT), (rps[kq],), inc=(kc == 7), out=ps[kq][:m, :cB - cA],
               lhsT=B.hT[:, kc, c0:c0 + m], rhs=B.win[:, kc, cA:cB], start=(kc == 0), stop=(kc == 7))
        if gi % 2:
            op("act", "activation", (rps[kq],), (B.r_tm,), out=B.tm[:m, cA - 1536:cB - 1536], in_=ps[kq][:m, :cB - cA],
               func=AF.Copy)
        else:
            op("dve", "tensor_copy", (rps[kq],), (B.r_tm,), out=B.tm[:m, cA - 1536:cB - 1536], in_=ps[kq][:m, :cB - cA])
    if last:
        for gi in range(3):
            kq = gi % 2
            for kc in range(8):
                op("pe", "matmul", (B.r_win, B.r_hT), (rps[kq],), inc=(kc == 7), out=ps[kq][:m, :512],
                   lhsT=B.hT[:, kc, c0:c0 + m], rhs=B.win[:, kc, gi * 512:(gi + 1) * 512], start=(kc == 0), stop=(kc == 7))
            op("dve", "tensor_copy", (rps[kq],), (B.r_nc3,), out=B.nc3[:m, gi * 512:(gi + 1) * 512], in_=ps[kq][:m, :512])
        S.dma("sp", dsem("nc3"), grp["new_conv"][seq, :, :], B.nc3[m - 3:m, :], reads=(B.r_nc3,), writes=(grp["r_out"],))
    if cx.stop <= 2:
        return
    zr, br, ar = B.tm[:m, 0:512], B.tm[:m, 512:520], B.tm[:m, 520:528]
    qraw, ckvr, krr_ = B.tm[:m, 528:1296], B.tm[:m, 1296:1424], B.tm[:m, 1424:1456]
    op("act", "activation", (B.r_tm,), (rsc,), out=sc[:m, 64:72], in_=br, func=AF.Exp, scale=-1.0)
    op("dve", "tensor_scalar", (rsc,), (rsc,), out=sc[:m, 64:72], in0=sc[:m, 64:72], scalar1=1.0, scalar2=None,
       op0=ALU.add)
    op("dve", "reciprocal", (rsc,), (rsc,), out=sc[:m, 0:8], in_=sc[:m, 64:72])
    op("act", "activation", (rsc,), (rsc,), out=sc[:m, 8:16], in_=sc[:m, 0:8], func=AF.Ln)
    op("dve", "tensor_tensor", (B.r_tm, B.r_g8), (rsc,), out=sc[:m, 64:72], in0=ar, in1=B.g8[:m, 8:16], op=ALU.add)
    op("act", "activation", (rsc,), (rsc,), out=sc[:m, 64:72], in_=sc[:m, 64:72], func=AF.Exp)
    op("act", "activation", (rsc,), (rsc,), out=sc[:m, 64:72], in_=sc[:m, 64:72], func=AF.Ln, bias=cx.one_t[:m, 0:1],
       scale=1.0)
    op("dve", "scalar_tensor_tensor", (rsc, B.r_g8), (rsc,), out=sc[:m, 16:24], in0=sc[:m, 64:72], scalar=-1.0,
       in1=B.g8[:m, 0:8], op0=ALU.mult, op1=ALU.mult)
    op("pe", "matmul", (rsc,), (rps[2],), out=ps[2][:m, 0:8], lhsT=cx.utri[:m, :m], rhs=sc[:m, 16:24], start=True, stop=True)
    op("dve", "tensor_copy", (rps[2],), (rsc,), out=sc[:m, 24:32], in_=ps[2][:m, 0:8])
    op("pe", "matmul", (rsc,), (rps[2],), out=ps[2][:m, 8:16], lhsT=cx.onesf[:m, :m], rhs=sc[:m, 16:24], start=True, stop=True)
    op("dve", "tensor_tensor", (rps[2], rsc), (rsc,), out=sc[:m, 48:56], in0=ps[2][:m, 8:16], in1=sc[:m, 24:32],
       op=ALU.subtract)
    op("act", "activation", (rsc,), (rsc,), out=sc[:m, 48:56], in_=sc[:m, 48:56], func=AF.Exp)
    op("act", "activation", (rsc,), (rsc,), out=sc[:m, 32:40], in_=sc[:m, 24:32], func=AF.Exp)
    op("dve", "tensor_tensor", (rsc,), (rsc,), out=sc[:m, 40:48], in0=sc[:m, 32:40], in1=sc[:m, 0:8], op=ALU.mult)
    op("dve", "tensor_scalar", (rsc,), (rsc,), out=sc[:m, 56:64], in0=sc[:m, 24:32], scalar1=-1.0, scalar2=None,
       op0=ALU.mult)
    op("dve", "tensor_tensor", (rsc,), (rsc,), out=sc[:m, 72:80], in0=sc[:m, 24:32], in1=sc[:m, 8:16], op=ALU.add)
    assert m <= 64
    g2 = sc[:m, 16:24].rearrange("p (c two) -> p two c", two=2)
    for par in range(2):
        op("pe", "matmul", (rsc,), (rps[2],), out=ps[2][par * 64:(par + 1) * 64, 16:20], lhsT=cx.onesf[:m, :64],
           rhs=g2[:, par, :], start=True, stop=True, skip_group_check=True)
    op("act", "activation", (rps[2],), (B.r_glb,), out=B.glb[:, 0:4], in_=ps[2][:, 16:20], func=AF.Exp)
    op("pe", "transpose", (rsc,), (rps[2],), out=ps[2][:8, 32:32 + m], in_=sc[:m, 24:32], identity=cx.ident_f[:m, :m])
    op("pe", "transpose", (rsc,), (rps[2],), out=ps[2][:8, 160:160 + m], in_=sc[:m, 72:80], identity=cx.ident_f[:m, :m])
    op("dve", "tensor_copy", (rps[2],), (B.r_rows,), out=B.rows[:, 0, :m], in_=ps[2][:8, 32:32 + m])
    op("dve", "tensor_copy", (rps[2],), (B.r_rows,), out=B.rows[:, 1, :m], in_=ps[2][:8, 160:160 + m])
    if cx.stop <= 3:
        return
    for c in range(4):
        op("pe", "transpose", (B.r_yc,), (rps[3],), out=ps[3][:m, c * 128:(c + 1) * 128], in_=B.yc[:, 8 + c, c0:c0 + m],
           identity=cx.ident_f[:, :], inc=(c == 3))
    op("dve", "tensor_tensor", (rps[3], rsc), (B.r_vtok,), out=B.vtok[:m, :].rearrange("p (h d) -> p h d", d=64),
       in0=ps[3][:m, :].rearrange("p (h d) -> p h d", d=64), in1=bc(sc[:m, 0:8], 64), op=ALU.mult)
    pkb = ps[2].bitcast(BF16)
    for c in range(4):
        op("pe", "transpose", (B.r_qkT,), (rps[2],), out=pkb[:m, 512 + c * 128:512 + (c + 1) * 128],
           in_=B.qkT[:, 4 + c, c0:c0 + m], identity=cx.ident_bf[:, :], inc=(c == 3))
    kh3 = pkb[:m, 512:1024].rearrange("p (h d) -> p h d", d=64)
    op("dve", "tensor_tensor", (rps[2], rsc), (B.r_ktok,), out=B.ktok[:m, :].rearrange("p (h d) -> p h d", d=64),
       in0=kh3, in1=bc(sc[:m, 40:48], 64), op=ALU.mult)
    op("dve", "tensor_tensor", (rps[2], rsc), (B.r_kdec,), out=B.kdec[:m, :].rearrange("p (h d) -> p h d", d=64),
       in0=kh3, in1=bc(sc[:m, 48:56], 64), op=ALU.mult)
    if cx.stop <= 4:
        return
    W4 = 4 * m
    mle, mlt, id4 = cx.mconst[m]
    for g in range(2):
        Ab = [getattr(B, "A%d%d" % (g, k)) for k in range(2)]
        Mb = [getattr(B, "M%d%d" % (g, k)) for k in range(2)]
        Pb = [getattr(B, "P%d%d" % (g, k)) for k in range(2)]
        rA = [getattr(B, "r_A%d%d" % (g, k)) for k in range(2)]
        rM = [getattr(B, "r_M%d%d" % (g, k)) for k in range(2)]
        rP = [getattr(B, "r_P%d%d" % (g, k)) for k in range(2)]
        pP, rpP = ps[6 + g], rps[6 + g]
        for kind, dec, rdec, mask in ((0, B.decq, B.r_decq, mle), (1, B.decb, B.r_decb, mlt)):
            op("pe", "matmul", (), (rps[3],), inc=False, out=ps[3][:m, 0:W4], lhsT=cx.ident_f[:m, :m], rhs=mask[:m, 0:W4],
               start=True, stop=False)
            for hh in range(4):
                h = 2 * hh + g
                op("pe", "matmul", (B.r_rows,), (rps[3],), inc=(hh == 3), out=ps[3][:m, hh * m:(hh + 1) * m],
                   lhsT=cx.ohsel[:, h, :m], rhs=B.rows[:, kind, :m], start=False, stop=(hh == 3))
            for hh in range(4):
                h = 2 * hh + g
                op("act", "activation", (rps[3], rsc), (rdec,), out=dec[:m, hh * m:(hh + 1) * m],
                   in_=ps[3][:m, hh * m:(hh + 1) * m], func=AF.Exp, bias=sc[:m, 56 + h:57 + h], scale=1.0)
        if cx.stop <= 4.2:
            continue
        for hh in range(4):
            h = 2 * hh + g
            c, po = h // 2, (h % 2) * 64
            op("pe", "matmul", (B.r_qkT,), (rps[4],), inc=(hh == 3), out=ps[4][:m, hh * m:(hh + 1) * m],
               lhsT=B.qkT[po:po + 64, 4 + c, c0:c0 + m], rhs=B.qkT[po:po + 64, 4 + c, c0:c0 + m], start=True, stop=True,
               skip_group_check=True)
        op("dve", "tensor_tensor", (rps[4], B.r_decb), (rA[0],), out=Ab[0][:m, 0:W4], in0=ps[4][:m, 0:W4],
           in1=B.decb[:m, 0:W4], op=ALU.mult)
        for hh in range(4):
            h = 2 * hh + g
            c, po = h // 2, (h % 2) * 64
            op("pe", "matmul", (B.r_qkT,), (rps[5],), inc=(hh == 3), out=ps[5][:m, hh * m:(hh + 1) * m],
               lhsT=B.qkT[po:po + 64, 4 + c, c0:c0 + m], rhs=B.qkT[po:po + 64, c, c0:c0 + m], start=True, stop=True,
               skip_group_check=True)
        op("dve", "tensor_tensor", (rps[5], B.r_decq), (B.r_qk,), out=B.qk[:m, 0:8 * m].rearrange("p (c two i) -> p two c i", two=2, i=m)[:, g],
           in0=ps[5][:m, 0:W4].rearrange("p (c i) -> p c i", i=m),
           in1=B.decq[:m, 0:W4].rearrange("p (c i) -> p c i", i=m), op=ALU.mult)
        if cx.stop <= 4.4:
            continue
        for hh in range(4):
            op("pe", "transpose", (rA[0],), (rps[4],), inc=(hh == 3), out=ps[4][:m, hh * m:(hh + 1) * m],
               in_=Ab[0][:m, hh * m:(hh + 1) * m], identity=cx.ident_f[:m, :m])
        op("act", "activation", (rps[4],), (rM[0],), out=Mb[0][:m, 0:W4], in_=ps[4][:m, 0:W4], func=AF.Copy)
        op("pe", "matmul", (), (rpP,), inc=False, out=pP[:m, 0:W4], lhsT=cx.ident_f[:m, :m], rhs=id4[:m, 0:W4],
           start=True, stop=False, skip_group_check=True)
        for hh in range(4):
            op("pe", "matmul", (rA[0],), (rpP,), inc=(hh == 3), out=pP[:m, hh * m:(hh + 1) * m], lhsT=cx.nident_f[:m, :m],
               rhs=Ab[0][:m, hh * m:(hh + 1) * m], start=False, stop=False, skip_group_check=True)
        op("act", "activation", (rpP,), (rP[0],), out=Pb[0][:m, 0:W4], in_=pP[:m, 0:W4], func=AF.Copy)
        if cx.stop <= 4.6:
            continue
        nlev = NLEV if m > 64 else (5 if m > 32 else (4 if m > 16 else (3 if m > 8 else (2 if m > 4 else 1))))
        for lv in range(1, nlev + 1):
            a, bq = (lv - 1) % 2, lv % 2
            for hh in range(4):
                sl = slice(hh * m, (hh + 1) * m)
                op("pe", "matmul", (rA[a], rM[a]), (rps[4],), inc=(hh == 3), out=ps[4][:m, sl], lhsT=Ab[a][:m, sl],
                   rhs=Mb[a][:m, sl], start=True, stop=True, skip_group_check=True)
            op("act", "activation", (rps[4],), (rM[bq],), out=Mb[bq][:m, 0:W4], in_=ps[4][:m, 0:W4], func=AF.Copy)
            if lv < nlev:
                for hh in range(4):
                    sl = slice(hh * m, (hh + 1) * m)
                    op("pe", "matmul", (rA[a], rM[a]), (rps[5],), inc=(hh == 3), out=ps[5][:m, sl], lhsT=Mb[a][:m, sl],
                       rhs=Ab[a][:m, sl], start=True, stop=True, skip_group_check=True)
                op("dve", "tensor_copy", (rps[5],), (rA[bq],), out=Ab[bq][:m, 0:W4], in_=ps[5][:m, 0:W4])
            for hh in range(4):
                sl = slice(hh * m, (hh + 1) * m)
                op("pe", "matmul", (rM[bq], rP[a]), (rpP,), inc=(hh == 3), out=pP[:m, sl], lhsT=Mb[bq][:m, sl],
                   rhs=Pb[a][:m, sl], start=False, stop=(lv == nlev), skip_group_check=True)
            if lv % 2:
                op("dve", "tensor_copy", (rpP,), (rP[bq],), out=Pb[bq][:m, 0:W4], in_=pP[:m, 0:W4])
            else:
                op("act", "activation", (rpP,), (rP[bq],), out=Pb[bq][:m, 0:W4], in_=pP[:m, 0:W4], func=AF.Copy)
        if cx.stop <= 4.8:
            continue
        Pf, rPf = getattr(B, "Pf%d" % g), getattr(B, "r_Pf%d" % g)
        op("dve", "tensor_copy", (rpP,), (rPf,), out=Pf[:m, 0:W4], in_=pP[:m, 0:W4])
        if cx.stop <= 4.85:
            continue
        for hh in range(4):
            h = 2 * hh + g
            sl = slice(hh * m, (hh + 1) * m)
            op("pe", "matmul", (rPf, B.r_vtok), (rps[0],), inc=(hh == 3), out=ps[0][:m, h * 64:(h + 1) * 64], lhsT=Pf[:m, sl],
               rhs=B.vtok[:m, h * 64:(h + 1) * 64], start=True, stop=True, skip_group_check=True)
        if cx.stop <= 4.9:
            continue
        for hh in range(4):
            h = 2 * hh + g
            sl = slice(hh * m, (hh + 1) * m)
            po = (h % 2) * 64
            op("pe", "matmul", (rPf, B.r_ktok), (rps[1],), inc=(hh == 3),
               out=ps[1][po:po + 64, hh * 128:hh * 128 + m],
               lhsT=B.ktok[:m, h * 64:(h + 1) * 64], rhs=Pf[:m, sl], start=True, stop=True, skip_group_check=True)
        op("act", "activation", (rps[1],), (B.r_wT,), out=B.wT[g * 64:(g + 1) * 64, :, :m],
           in_=ps[1][g * 64:(g + 1) * 64, :].rearrange("p (h i) -> p h i", i=128)[:, :, :m], func=AF.Identity)
    if cx.stop <= 5:
        return
    op("dve", "tensor_copy", (rps[0],), (B.r_u,), out=B.u[:m, :], in_=ps[0][:m, :])
    def sdiag(t_, par):
        return t_[par * 64:(par + 1) * 64, :].rearrange("k (c x) -> k c x", x=128)[:, :, par * 64:(par + 1) * 64]

    if not prompt:
        op("dve", "memset", (), (B.r_S32,), ap=B.S32[:, :], constant=0.0)
        for par in range(2):
            S.dma("sp", dsem("s0"), sdiag(B.S32, par), grp["s0"][seq].rearrange("(c par) k v -> par k c v", par=2)[par],
                  writes=(B.r_S32,))
        op("act", "activation", (B.r_S32,), (B.r_Sbf,), out=B.Sbf[:, :], in_=B.S32[:, :], func=AF.Copy)
    for c in range(4):
        op("pe", "matmul", (B.r_wT, B.r_Sbf), (rps[0],), inc=(c == 3), out=ps[0][:m, c * 128:(c + 1) * 128],
           lhsT=B.wT[:, c, :m], rhs=B.Sbf[:, c * 128:(c + 1) * 128], start=True, stop=True, skip_group_check=True)
    op("dve", "tensor_tensor", (rps[0], B.r_u), (B.r_vnew,), out=B.vnew[:m, :], in0=B.u[:m, :], in1=ps[0][:m, :],
       op=ALU.subtract)
    for c in range(4):
        op("pe", "matmul", (B.r_qkT, B.r_Sbf), (rps[1],), inc=(c == 3), out=ps[1][:m, c * 128:(c + 1) * 128],
           lhsT=B.qkT[:, c, c0:c0 + m], rhs=B.Sbf[:, c * 128:(c + 1) * 128], start=True, stop=True, skip_group_check=True)
    op("dve", "tensor_tensor", (rps[1], rsc), (B.r_o1,), out=B.o1[:m, :].rearrange("p (h d) -> p h d", d=64),
       in0=ps[1][:m, :].rearrange("p (h d) -> p h d", d=64), in1=bc(sc[:m, 32:40], 64), op=ALU.mult)
    for h in range(8):
        op("pe", "matmul", (B.r_qk, B.r_vnew), (rps[0],), inc=(h == 7), out=ps[0][:m, h * 64:(h + 1) * 64],
           lhsT=B.qk[:m, h * m:(h + 1) * m], rhs=B.vnew[:m, h * 64:(h + 1) * 64], start=True, stop=True,
           skip_group_check=True)
    op("dve", "tensor_tensor", (rps[0], B.r_o1), (B.r_osb,), out=B.osb[:m, :], in0=ps[0][:m, :], in1=B.o1[:m, :], op=ALU.add)
    for c in range(4):
        op("pe", "matmul", (B.r_kdec, B.r_vnew), (rps[3],), inc=(c == 3), out=ps[3][:, c * 128:(c + 1) * 128],
           lhsT=B.kdec[:m, c * 128:(c + 1) * 128], rhs=B.vnew[:m, c * 128:(c + 1) * 128], start=True, stop=True,
           skip_group_check=True)
    op("dve", "tensor_tensor", (rps[3],), (B.r_decq,), out=B.decq[:, :].rearrange("p (c x) -> p c x", x=128),
       in0=ps[3][:, :].rearrange("p (c x) -> p c x", x=128), in1=bc_mid(cx.bones[:, :], 4), op=ALU.mult)
    op("dve", "tensor_tensor", (B.r_glb,), (B.r_S32,), out=B.S32[:, :].rearrange("p (c x) -> p c x", x=128),
       in0=B.S32[:, :].rearrange("p (c x) -> p c x", x=128), in1=bc(B.glb[:, 0:4], 128), op=ALU.mult)
    op("pool", "tensor_tensor", (B.r_decq,), (B.r_S32,), out=B.S32[:, :], in0=B.S32[:, :], in1=B.decq[:, :], op=ALU.add)
    if last:
        for par in range(2):
            S.dma("sp", dsem("s0"), grp["new_gdn"][seq].rearrange("(c par) k v -> par k c v", par=2)[par],
                  sdiag(B.S32, par), reads=(B.r_S32,), writes=(grp["r_out"],))
    else:
        op("act", "activation", (B.r_S32,), (B.r_Sbf,), out=B.Sbf[:, :], in_=B.S32[:, :], func=AF.Copy)
    if cx.stop <= 6:
        return
    op("pool", "tensor_tensor", (B.r_osb,), (B.r_o1,), out=B.o1[:m, :], in0=B.osb[:m, :], in1=B.osb[:m, :], op=ALU.mult)
    op("dve", "tensor_reduce", (B.r_o1,), (B.r_ost,), out=B.ost[:m, 0:8], in_=B.o1[:m, :].rearrange("p (h d) -> p h d", d=64),
       axis=AX.X, op=ALU.add)
    op("act", "activation", (B.r_ost,), (B.r_ost,), out=B.ost[:m, 8:16], in_=B.ost[:m, 0:8], func=AF.Sqrt, scale=1.0 / 64,
       bias=cx.eps_t[:m, 0:1])
    op("dve", "reciprocal", (B.r_ost,), (B.r_ost,), out=B.ost[:m, 8:16], in_=B.ost[:m, 8:16])
    op("act", "activation", (B.r_tm,), (B.r_zs,), out=B.zs[:m, :], in_=zr, func=AF.Silu)
    op("dve", "tensor_tensor", (B.r_osb, B.r_ost), (B.r_osb,), out=B.osb[:m, :].rearrange("p (h d) -> p h d", d=64),
       in0=B.osb[:m, :].rearrange("p (h d) -> p h d", d=64), in1=bc(B.ost[:m, 8:16], 64), op=ALU.mult)
    op("pool", "tensor_tensor", (B.r_osb, B.r_gng), (B.r_osb,), out=B.osb[:m, :].rearrange("p (h d) -> p h d", d=64),
       in0=B.osb[:m, :].rearrange("p (h d) -> p h d", d=64), in1=bc_mid(B.gng[:m, :], 8), op=ALU.mult)
    op("dve", "tensor_tensor", (B.r_osb, B.r_zs), (B.r_gout,), out=B.gout[:m, :], in0=B.osb[:m, :], in1=B.zs[:m, :], op=ALU.mult)
    S.dma("sp", dsem("gout"), grp["mix"][tok0:tok0 + m, 0:512], B.gout[:m, :], reads=(B.r_gout,), writes=(grp["r_mix"],))
    if cx.stop <= 7:
        return
    S.dma("sp", dsem("cs"), B.cs[:m, :], grp["cs"][(tok0 if prompt else 0):(tok0 if prompt else 0) + m, :], writes=(B.r_cs,))
    q3 = qraw.rearrange("p (h d) -> p h d", d=96)
    mq3 = B.mq[:m, :].rearrange("p (h d) -> p h d", d=96)
    op("pool", "tensor_tensor", (B.r_tm,), (B.r_mq,), out=B.mq[:m, :], in0=qraw, in1=qraw, op=ALU.mult)
    op("dve", "tensor_reduce", (B.r_mq,), (B.r_mqs,), out=B.mqs[:m, 0:8], in_=mq3[:, :, 0:64], axis=AX.X, op=ALU.add)
    op("dve", "tensor_reduce", (B.r_mq,), (B.r_mqs,), out=B.mqs[:m, 8:16], in_=mq3[:, :, 64:96], axis=AX.X, op=ALU.add)
    op("act", "activation", (B.r_mqs,), (B.r_mqs,), out=B.mqs[:m, 16:24], in_=B.mqs[:m, 0:8], func=AF.Sqrt, scale=1.0 / 64,
       bias=cx.eps_t[:m, 0:1])
    op("act", "activation", (B.r_mqs,), (B.r_mqs,), out=B.mqs[:m, 24:32], in_=B.mqs[:m, 8:16], func=AF.Sqrt, scale=1.0 / 32,
       bias=cx.eps_t[:m, 0:1])
    op("dve", "reciprocal", (B.r_mqs,), (B.r_mqs,), out=B.mqs[:m, 16:32], in_=B.mqs[:m, 16:32])
    op("dve", "tensor_tensor", (B.r_tm, B.r_mqs), (B.r_mq,), out=mq3[:, :, 0:64], in0=q3[:, :, 0:64], in1=bc(B.mqs[:m, 16:24], 64),
       op=ALU.mult)
    op("pool", "tensor_tensor", (B.r_mq, B.r_qng), (B.r_Qb,), out=B.Qb[:m, :].rearrange("p (h d) -> p h d", d=96)[:, :, 0:64],
       in0=mq3[:, :, 0:64], in1=bc_mid(B.qng[:m, :], 8), op=ALU.mult)
    op("dve", "tensor_tensor", (B.r_tm, B.r_mqs), (B.r_mq,), out=mq3[:, :, 64:96], in0=q3[:, :, 64:96], in1=bc(B.mqs[:m, 24:32], 32),
       op=ALU.mult)
    op("pool", "tensor_tensor", (B.r_mq, B.r_qrg), (B.r_mq,), out=mq3[:, :, 64:96], in0=mq3[:, :, 64:96],
       in1=bc_mid(B.qrg[:m, :], 8), op=ALU.mult)
    cosb, sinb = bc_mid(B.cs[:m, 0:16], 8), bc_mid(B.cs[:m, 16:32], 8)
    x1, x2 = mq3[:, :, 64:80], mq3[:, :, 80:96]
    Q3 = B.Qb[:m, :].rearrange("p (h d) -> p h d", d=96)
    op("dve", "tensor_tensor", (B.r_mq, B.r_cs), (B.r_qrr,), out=B.qrr[:m, :, 0:16], in0=x1, in1=cosb, op=ALU.mult)
    op("dve", "tensor_tensor", (B.r_mq, B.r_cs), (B.r_qrr,), out=B.qrr[:m, :, 16:32], in0=x2, in1=sinb, op=ALU.mult)
    op("dve", "tensor_tensor", (B.r_qrr,), (B.r_Qb,), out=Q3[:, :, 64:80], in0=B.qrr[:m, :, 0:16], in1=B.qrr[:m, :, 16:32],
       op=ALU.subtract)
    op("dve", "tensor_tensor", (B.r_mq, B.r_cs), (B.r_qrr,), out=B.qrr[:m, :, 0:16], in0=x1, in1=sinb, op=ALU.mult)
    op("dve", "tensor_tensor", (B.r_mq, B.r_cs), (B.r_qrr,), out=B.qrr[:m, :, 16:32], in0=x2, in1=cosb, op=ALU.mult)
    op("dve", "tensor_tensor", (B.r_qrr,), (B.r_Qb,), out=Q3[:, :, 80:96], in0=B.qrr[:m, :, 0:16], in1=B.qrr[:m, :, 16:32],
       op=ALU.add)
    S.dma("sp", dsem("Qb"), grp["Q"][tok0:tok0 + m, :], B.Qb[:m, :], reads=(B.r_Qb,), writes=(grp["r_Q"],))
    op("pool", "tensor_tensor", (B.r_tm,), (B.r_cn,), out=B.cn[:m, :], in0=ckvr, in1=ckvr, op=ALU.mult)
    op("dve", "tensor_reduce", (B.r_cn,), (B.r_mqs,), out=B.mqs[:m, 0:1], in_=B.cn[:m, :], axis=AX.X, op=ALU.add)
    op("act", "activation", (B.r_mqs,), (B.r_mqs,), out=B.mqs[:m, 1:2], in_=B.mqs[:m, 0:1], func=AF.Sqrt, scale=1.0 / 128,
       bias=cx.eps_t[:m, 0:1])
    op("dve", "reciprocal", (B.r_mqs,), (B.r_mqs,), out=B.mqs[:m, 1:2], in_=B.mqs[:m, 1:2])
    op("dve", "scalar_tensor_tensor", (B.r_tm, B.r_mqs, B.r_ckg), (B.r_cn,), out=B.cn[:m, :], in0=ckvr, scalar=B.mqs[:m, 1:2],
       in1=B.ckg[:m, :], op0=ALU.mult, op1=ALU.mult)
    S.dma("sp", dsem("cn"), grp["new_ckv"][tok0:tok0 + m, :], B.cn[:m, :], reads=(B.r_cn,), writes=(grp["r_ckv"],))
    op("pool", "tensor_tensor", (B.r_tm,), (B.r_krn,), out=B.krn[:m, :], in0=krr_, in1=krr_, op=ALU.mult)
    op("dve", "tensor_reduce", (B.r_krn,), (B.r_mqs,), out=B.mqs[:m, 2:3], in_=B.krn[:m, :], axis=AX.X, op=ALU.add)
    op("act", "activation", (B.r_mqs,), (B.r_mqs,), out=B.mqs[:m, 3:4], in_=B.mqs[:m, 2:3], func=AF.Sqrt, scale=1.0 / 32,
       bias=cx.eps_t[:m, 0:1])
    op("dve", "reciprocal", (B.r_mqs,), (B.r_mqs,), out=B.mqs[:m, 3:4], in_=B.mqs[:m, 3:4])
    op("dve", "scalar_tensor_tensor", (B.r_tm, B.r_mqs, B.r_krg), (B.r_krn,), out=B.krn[:m, :], in0=krr_, scalar=B.mqs[:m, 3:4],
       in1=B.krg[:m, :], op0=ALU.mult, op1=ALU.mult)
    k1, k2, cs1, sn1 = B.krn[:m, 0:16], B.krn[:m, 16:32], B.cs[:m, 0:16], B.cs[:m, 16:32]
    op("dve", "tensor_tensor", (B.r_krn, B.r_cs), (B.r_krr,), out=B.krr[:m, 0:16], in0=k1, in1=cs1, op=ALU.mult)
    op("dve", "tensor_tensor", (B.r_krn, B.r_cs), (B.r_krr,), out=B.krr[:m, 16:32], in0=k2, in1=sn1, op=ALU.mult)
    op("dve", "tensor_tensor", (B.r_krr,), (B.r_qrr,), out=B.qrr[:m, 0, 0:16], in0=B.krr[:m, 0:16], in1=B.krr[:m, 16:32],
       op=ALU.subtract)
    op("dve", "tensor_tensor", (B.r_krn, B.r_cs), (B.r_krr,), out=B.krr[:m, 0:16], in0=k1, in1=sn1, op=ALU.mult)
    op("dve", "tensor_tensor", (B.r_krn, B.r_cs), (B.r_krr,), out=B.krr[:m, 16:32], in0=k2, in1=cs1, op=ALU.mult)
    op("dve", "tensor_tensor", (B.r_krr,), (B.r_qrr,), out=B.qrr[:m, 0, 16:32], in0=B.krr[:m, 0:16], in1=B.krr[:m, 16:32],
       op=ALU.add)
    S.dma("sp", dsem("kr"), grp["new_kr"][tok0:tok0 + m, :], B.qrr[:m, 0, :], reads=(B.r_qrr,), writes=(grp["r_kr"],))


def phase_mixer(cx, S, groups, P):
    with ExitStack() as st:
        B = mixer_alloc(cx, S, st, "mx", P)
        mc = sb(cx, st, "mx_cst", [128, CST_COLS - 128], F32)
        r_mc = Res("mxcst")
        S.dma("sp", S.dsem("mxcst"), mc[:, :], cx.cst[:, 128:CST_COLS], writes=(r_mc,))
        cx.utri = mc[:, 0:128]
        cx.onesf = mc[:, 128:256]
        cx.ohsel = mc[0:8, 384:1408].rearrange("p (h i) -> p h i", i=128)
        cx.mconst = {}
        off = 1408
        for m_ in (64, 4):
            cx.mconst[m_] = (mc[:, off + 4 * m_:off + 8 * m_], mc[:, off + 8 * m_:off + 12 * m_], mc[:, off:off + 4 * m_])
            off += 12 * m_
        idf = sb(cx, st, "mx_idf", [128, 128], F32)
        S.op("dve", "tensor_scalar", dict(out=idf[:, :], in0=cx.ident_f, scalar1=-1.0, scalar2=None, op0=ALU.mult),
             writes=(r_mc,))
        cx.nident_f = idf[:, :]
        dc = S.dsem("mxc")
        r_c = Res("mxconst")
        S.dma("sp", dc, B.cw[:, :, :], P["conv_w_fm"], writes=(B.r_cw,))
        S.dma("sp", dc, B.g8[:, 0:8], row_bcast(P["a_log"], 128), writes=(B.r_g8,))
        S.dma("sp", dc, B.g8[:, 8:16], row_bcast(P["dt_bias"], 128), writes=(B.r_g8,))
        S.dma("sp", dc, B.gng[:, :], row_bcast(P["gdn_norm"], 128), writes=(B.r_gng,))
        S.dma("sp", dc, B.qng[:, :], row_bcast(P["qn_g"], 128), writes=(B.r_qng,))
        S.dma("sp", dc, B.kng[:, :], row_bcast(P["kn_g"], 128), writes=(B.r_kng,))
        S.dma("sp", dc, B.qrg[:, :], row_bcast(P["qr_g"], 128), writes=(B.r_qrg,))
        S.dma("sp", dc, B.ckg[:, :], row_bcast(P["ckv_g"], 128), writes=(B.r_ckg,))
        S.dma("sp", dc, B.krg[:, :], row_bcast(P["kr_g"], 128), writes=(B.r_krg,))
        S.barrier()
        S.op("act", "activation", dict(out=B.g8[:, 0:8], in_=B.g8[:, 0:8], func=AF.Exp), writes=(B.r_g8,))
        S.op("dve", "scalar_tensor_tensor", dict(out=B.qng[:, :], in0=B.qng[:, :], scalar=ATT_SCALE, in1=B.kng[:, :],
                                                 op0=ALU.mult, op1=ALU.mult), writes=(B.r_qng,))
        S.op("dve", "tensor_scalar", dict(out=B.qrg[:, :], in0=B.qrg[:, :], scalar1=ATT_SCALE, scalar2=None, op0=ALU.mult),
             writes=(B.r_qrg,))
        S.barrier()
        for gi, grp in enumerate(groups):
            mixer_group(cx, S, B, "mx%d" % gi, grp, P)
            S.barrier()
            S.emit()


def phase_wout(cx, S, groups, w_out_ap):
    with ExitStack() as st:
        wo = sb(cx, st, "wo_w", [128, 8, D], BF16)
        r_wo = Res()
        load_weight_bf16(cx, S, "wow", wo, r_wo, w_out_ap, 8, D, D)
        Gt = sb(cx, st, "wo_Gt", [128, D], F32)
        mx = [sb(cx, st, "wo_mx%d" % i, [128, D], BF16) for i in range(2)]
        mT = sb(cx, st, "wo_mT", [128, 8, 128], BF16)
        xr = [sb(cx, st, "wo_xr%d" % i, [128, D], F32) for i in range(2)]
        tmp = sb(cx, st, "wo_tmp", [128, 512], F32)
        r_Gt, r_mT, r_tmp = Res(), Res(), Res()
        r_mx, r_xr = [Res(), Res()], [Res(), Res()]
        d_g = S.dsem("wo_g")
        d_mx = [S.dsem("wo_mx0"), S.dsem("wo_mx1")]
        d_xr = [S.dsem("wo_xr0"), S.dsem("wo_xr1")]
        d_xst = [S.dsem("wo_xst0"), S.dsem("wo_xst1")]
        i = 0
        for g in groups:
            n, T = g["n"], g["T"]
            S.dma("sp", d_g, Gt[:n, :], g["mods"][5, :n, :], writes=(r_Gt,))
            for r0 in range(0, T, 128):
                m = min(128, T - r0)
                k = i % 2
                i += 1
                S.dma("sp", d_mx[k], mx[k][:m, :], g["mix"][r0:r0 + m, :], reads=(g["r_mix"],), writes=(r_mx[k],))
                S.dma("sp", d_xr[k], xr[k][:m, :], g["src"][r0:r0 + m, :], reads=(g["src_res"][r0 // 128],),
                      writes=(r_xr[k],))
                ptb = cx.ps[6 + k].bitcast(BF16)
                for c in range(8):
                    S.op("pe", "transpose", dict(out=ptb[:, c * 128:c * 128 + m], in_=mx[k][:m, c * 128:(c + 1) * 128],
                                                 identity=cx.ident_bf[:m, :m]), reads=(r_mx[k],), writes=(cx.rps[6 + k],),
                         inc=(c == 7))
                S.op("act", "activation", dict(out=mT[:, :, :m], in_=ptb.rearrange("p (c t) -> p c t", c=8)[:, :, :m],
                                               func=AF.Copy), reads=(cx.rps[6 + k],), writes=(r_mT,))
                for half in range(2):
                    pk = (i * 2 + half) % 4
                    for c in range(8):
                        S.op("pe", "matmul", dict(out=cx.ps[pk][:m, :], lhsT=mT[:, c, :m],
                                                  rhs=wo[:, c, half * 512:(half + 1) * 512], start=(c == 0), stop=(c == 7)),
                             reads=(r_mT, r_wo), writes=(cx.rps[pk],), inc=(c == 7))
                    S.op("dve", "tensor_tensor", dict(out=tmp[:m, :], in0=cx.ps[pk][:m, :],
                                                      in1=Gt[:m, half * 512:(half + 1) * 512], op=ALU.mult),
                         reads=(cx.rps[pk], r_Gt), writes=(r_tmp,))
                    S.op("pool", "tensor_tensor", dict(out=xr[k][:m, half * 512:(half + 1) * 512],
                                                       in0=xr[k][:m, half * 512:(half + 1) * 512], in1=tmp[:m, :],
                                                       op=ALU.add), reads=(r_tmp,), writes=(r_xr[k],))
                S.dma("pool", d_xst[k], g["dst"][r0:r0 + m, :], xr[k][:m, :], reads=(r_xr[k],),
                      writes=(g["dst_res"][r0 // 128],))
        S.barrier()
        S.emit()


class AttBufs:
    pass


def att_alloc(cx, S, st, name, P):
    A = AttBufs()

    def t(nm, shape, dt):
        tt_ = sb(cx, st, name + nm, shape, dt)
        setattr(A, nm, tt_)
        setattr(A, "r_" + nm, Res(nm))
        return tt_

    t("wuk", [128, 512], BF16)
    t("wuv", [128, 512], BF16)
    t("wst", [128, 512], F32)
    d = S.dsem(name + "w")
    S.dma("sp", d, A.wst[:, :], P["w_uk"], writes=(A.r_wst,))
    S.op("dve", "tensor_copy", dict(out=A.wuk[:, :], in_=A.wst[:, :]), reads=(A.r_wst,), writes=(A.r_wuk,))
    S.dma("sp", d, A.wst[:, :], P["w_uv"], writes=(A.r_wst,))
    S.op("dve", "tensor_copy", dict(out=A.wuv[:, :], in_=A.wst[:, :]), reads=(A.r_wst,), writes=(A.r_wuv,))
    for k_ in range(2):
        t("cTb%d" % k_, [128, 128], BF16)
        t("sq%d" % k_, [128, 512], F32)
        t("kst%d" % k_, [128, 16], F32)
        t("Kf%d" % k_, [128, 8, 96], BF16)
        t("KT%d" % k_, [96, 8, 128], BF16)
    t("QT", [96, 8, 128], BF16)
    t("Qb", [128, 768], BF16)
    t("PT", [128, 512], BF16)
    t("ctx", [128, 8, 128], BF16)
    t("rden", [128, 8], F32)
    t("ctxT", [128, 8, 128], BF16)
    t("mo", [128, 512], BF16)
    t("m01", [128, 128], BF16)
    A.d = {}
    return A


def att_kside(cx, S, A, c_blk, kr_blk, r_src, n, KT_dst, r_KT, k=0):
    ps, rps = cx.ps, cx.rps
    cTb, sq, kst, Kf = (getattr(A, nm + str(k)) for nm in ("cTb", "sq", "kst", "Kf"))
    r_cTb, r_sq, r_kst, r_Kf = (getattr(A, "r_" + nm + str(k)) for nm in ("cTb", "sq", "kst", "Kf"))
    p0, p1, p2 = (0, 1, 2) if k == 0 else (6, 7, 3)

    def op(eng, meth, reads, writes, inc=True, **kw):
        S.op(eng, meth, kw, reads=reads, writes=writes, inc=inc)

    op("pe", "transpose", (r_src,), (rps[p0],), out=ps[p0][:, :n], in_=c_blk, identity=cx.ident_f[:n, :n])
    op("act", "activation", (rps[p0],), (r_cTb,), out=cTb[:, :n], in_=ps[p0][:, :n], func=AF.Identity)
    op("pe", "matmul", (r_cTb, A.r_wuk), (rps[p1],), out=ps[p1][:n, :], lhsT=cTb[:, :n], rhs=A.wuk[:, :], start=True, stop=True)
    op("act", "activation", (rps[p1],), (r_sq,), out=sq[:n, :], in_=ps[p1][:n, :], func=AF.Square)
    op("dve", "tensor_reduce", (r_sq,), (r_kst,), out=kst[:n, 0:8], in_=sq[:n, :].rearrange("p (h d) -> p h d", d=64),
       axis=AX.X, op=ALU.add)
    op("act", "activation", (r_kst,), (r_kst,), out=kst[:n, 8:16], in_=kst[:n, 0:8], func=AF.Sqrt, scale=1.0 / 64,
       bias=cx.eps_t[:n, 0:1])
    op("dve", "reciprocal", (r_kst,), (r_kst,), out=kst[:n, 8:16], in_=kst[:n, 8:16])
    op("dve", "tensor_tensor", (rps[p1], r_kst), (r_Kf,), out=Kf[:n, :, 0:64],
       in0=ps[p1][:n, :].rearrange("p (h d) -> p h d", d=64), in1=bc(kst[:n, 8:16], 64), op=ALU.mult)
    op("pool", "tensor_copy", (r_src,), (r_Kf,), out=Kf[:n, :, 64:96], in_=bc_mid(kr_blk, 8))
    ptb = ps[p2].bitcast(BF16)
    for h in range(8):
        op("pe", "transpose", (r_Kf,), (rps[p2],), inc=(h == 7), out=ptb[0:96, h * 128:h * 128 + n], in_=Kf[:n, h, :],
           identity=cx.ident_bf[:n, :n])
    op("act", "activation", (rps[p2],), (r_KT,), out=KT_dst, in_=ptb[0:96, :].rearrange("p (h t) -> p h t", t=128)[:, :, :n],
       func=AF.Copy)


def att_out(cx, S, A, nq, r_ctx_in, mix_dst, r_mix, dsem_):
    ps, rps = cx.ps, cx.rps
    ptb = ps[2].bitcast(BF16)
    for h in range(8):
        S.op("pe", "transpose", dict(out=ptb[:, h * 128:h * 128 + nq], in_=A.ctx[:nq, h, :], identity=cx.ident_bf[:nq, :nq]),
             reads=(A.r_ctx,), writes=(rps[2],), inc=(h == 7))
    S.op("act", "activation", dict(out=A.ctxT[:, :, :nq], in_=ptb.rearrange("p (h t) -> p h t", t=128)[:, :, :nq], func=AF.Copy),
         reads=(rps[2],), writes=(A.r_ctxT,))
    for h in range(8):
        S.op("pe", "matmul", dict(out=ps[3][:nq, h * 64:(h + 1) * 64], lhsT=A.ctxT[:, h, :nq], rhs=A.wuv[:, h * 64:(h + 1) * 64],
                                  start=True, stop=True, skip_group_check=True), reads=(A.r_ctxT, A.r_wuv), writes=(rps[3],),
             inc=(h == 7))
    S.op("dve", "tensor_copy", dict(out=A.mo[:nq, :], in_=ps[3][:nq, :]), reads=(rps[3],), writes=(A.r_mo,))
    S.dma("sp", dsem_, mix_dst, A.mo[:nq, :], reads=(A.r_mo,), writes=(r_mix,))


def phase_attn_prompt(cx, S, grp, P):
    T = grp["T"]
    nb = T // 128
    with ExitStack() as st:
        A = att_alloc(cx, S, st, "ap", P)
        KTall = sb(cx, st, "ap_KTall", [96, 8, T], BF16)
        caug = sb(cx, st, "ap_caug", [128, nb, 132], BF16)
        cst_ = [sb(cx, st, "ap_cst%d" % i, [128, 160], F32) for i in range(2)]
        r_KTall, r_caug = Res(), Res()
        r_cst = [Res(), Res()]
        d_cst = [S.dsem("ap_c0"), S.dsem("ap_c1")]
        d_q, d_o = S.dsem("ap_q"), S.dsem("ap_o")
        ps, rps = cx.ps, cx.rps
        S.op("pool", "memset", dict(ap=caug[:, :, 128:132], constant=1.0), writes=(r_caug,))
        S.op("pool", "memset", dict(ap=A.m01[:, :], constant=1.0), writes=(A.r_m01,))
        S.op("pool", "affine_select", dict(out=A.m01[:, :], in_=A.m01[:, :], pattern=[[1, 128]], compare_op=ALU.is_ge, fill=0.0,
                                           base=0, channel_multiplier=-1), writes=(A.r_m01,))
        for kb in range(nb):
            k = kb % 2
            S.dma("sp", d_cst[k], cst_[k][:, 0:128], grp["new_ckv"][kb * 128:(kb + 1) * 128, :], reads=(grp["r_ckv"],),
                  writes=(r_cst[k],))
            S.dma("sp", d_cst[k], cst_[k][:, 128:160], grp["new_kr"][kb * 128:(kb + 1) * 128, :], reads=(grp["r_kr"],),
                  writes=(r_cst[k],))
            S.op("dve", "tensor_copy", dict(out=caug[:, kb, 0:128], in_=cst_[k][:, 0:128]), reads=(r_cst[k],), writes=(r_caug,))
            att_kside(cx, S, A, cst_[k][:, 0:128], cst_[k][:, 128:160], r_cst[k], 128, KTall[:, :, kb * 128:(kb + 1) * 128], r_KTall, k=k)
        for qb in range(nb):
            S.dma("sp", d_q, A.Qb[:, :], grp["Q"][qb * 128:(qb + 1) * 128, :], reads=(grp["r_Q"],), writes=(A.r_Qb,))
            ptb = ps[2].bitcast(BF16)
            for h in range(8):
                S.op("pe", "transpose", dict(out=ptb[0:96, h * 128:(h + 1) * 128], in_=A.Qb[:, h * 96:(h + 1) * 96],
                                             identity=cx.ident_bf[:, :]), reads=(A.r_Qb,), writes=(rps[2],), inc=(h == 7))
            S.op("act", "activation", dict(out=A.QT[:, :, :], in_=ptb[0:96, :].rearrange("p (h t) -> p h t", t=128), func=AF.Copy),
                 reads=(rps[2],), writes=(A.r_QT,))
            gi = 0
            for h in range(8):
                pc = 4 + h % 2
                for kb0 in range(0, qb + 1, 4):
                    ng = min(4, qb + 1 - kb0)
                    pk = gi % 2
                    gi += 1
                    for j in range(ng):
                        kb = kb0 + j
                        S.op("pe", "matmul", dict(out=ps[pk][:, j * 128:(j + 1) * 128], lhsT=KTall[:, h, kb * 128:(kb + 1) * 128],
                                                  rhs=A.QT[:, h, :], start=True, stop=True, skip_group_check=True),
                             reads=(r_KTall, A.r_QT), writes=(rps[pk],), inc=(j == ng - 1))
                    S.op("act", "activation", dict(out=A.PT[:, 0:ng * 128], in_=ps[pk][:, 0:ng * 128], func=AF.Exp),
                         reads=(rps[pk],), writes=(A.r_PT,))
                    if kb0 + ng - 1 == qb:
                        j = ng - 1
                        S.op("dve", "tensor_tensor", dict(out=A.PT[:, j * 128:(j + 1) * 128], in0=A.PT[:, j * 128:(j + 1) * 128],
                                                          in1=A.m01[:, :], op=ALU.mult), reads=(A.r_m01,), writes=(A.r_PT,))
                    for j in range(ng):
                        kb = kb0 + j
                        S.op("pe", "matmul", dict(out=ps[pc][:, 0:129], lhsT=A.PT[:, j * 128:(j + 1) * 128], rhs=caug[:, kb, 0:129],
                                                  start=(kb == 0), stop=(kb == qb), skip_group_check=True),
                             reads=(A.r_PT, r_caug), writes=(rps[pc],), inc=(j == ng - 1))
                S.op("dve", "reciprocal", dict(out=A.rden[:, h:h + 1], in_=ps[pc][:, 128:129]), reads=(rps[pc],), writes=(A.r_rden,))
                S.op("dve", "tensor_scalar", dict(out=A.ctx[:, h, :], in0=ps[pc][:, 0:128], scalar1=A.rden[:, h:h + 1], scalar2=None,
                                                  op0=ALU.mult), reads=(rps[pc], A.r_rden), writes=(A.r_ctx,))
            att_out(cx, S, A, 128, A.r_ctx, grp["mix"][qb * 128:(qb + 1) * 128, 512:1024], grp["r_mix"], d_o)
        S.barrier()
        S.emit()


def phase_attn_sample(cx, S, grp, P, cache_c, cache_k, ptab, nseq, npages):
    with ExitStack() as st:
        A = att_alloc(cx, S, st, "as", P)
        cg = sb(cx, st, "as_cg", [128, 128 * 128], F32)
        kg = sb(cx, st, "as_kg", [128, 128 * 32], F32)
        idx = sb(cx, st, "as_idx", [128, 2], I32)
        cb2 = [sb(cx, st, "as_cb%d" % i, [128, 132], BF16) for i in range(2)]
        cn = sb(cx, st, "as_cn", [4, 160], F32)
        qs = sb(cx, st, "as_qs", [4, 768], BF16)
        QTs = sb(cx, st, "as_QTs", [96, 8, 4], BF16)
        PTs2 = [sb(cx, st, "as_PTs%d" % i, [128, 32], BF16) for i in range(2)]
        ms = sb(cx, st, "as_ms", [4, 32], BF16)
        c32 = sb(cx, st, "as_c32", [32, 128], BF16)
        cT32 = sb(cx, st, "as_cT32", [128, 32], BF16)
        rd = sb(cx, st, "as_rd", [32, 1], F32)
        mo = sb(cx, st, "as_mo", [4, 512], BF16)
        r_cg, r_kg, r_idx, r_cb, r_cn, r_qs, r_QTs, r_PTs, r_ms, r_c32, r_cT32, r_rd, r_mo = (Res() for _ in range(13))
        d_idx, d_cg, d_kg, d_cn, d_qs, d_mo = (S.dsem("as%d" % i) for i in range(6))
        ps, rps = cx.ps, cx.rps
        r_cb2, r_PTs2, r_sc = [Res(), Res()], [Res(), Res()], [Res(), Res()]
        for i_ in range(2):
            S.op("pool", "memset", dict(ap=cb2[i_][:, 128:132], constant=1.0), writes=(r_cb2[i_],))
        S.op("pool", "memset", dict(ap=ms[:, :], constant=1.0), writes=(r_ms,))
        S.op("pool", "affine_select", dict(out=ms[:, :], in_=ms[:, :], pattern=[[0, 8], [1, 4]], compare_op=ALU.is_ge, fill=0.0,
                                           base=0, channel_multiplier=-1), writes=(r_ms,))
        for s in range(nseq):
            S.dma("sp", d_idx, idx[:npages, 0:1], ptab[s:s + 1, :].rearrange("o p -> p o"), writes=(r_idx,),
                  allow_slow_non_contiguous=True)
            S.idma(d_cg, reads=(r_idx,), writes=(r_cg,), out=cg[:npages, :], out_offset=None, in_=cache_c,
                   in_offset=bass.IndirectOffsetOnAxis(ap=idx[:npages, 0:1], axis=0))
            S.idma(d_cg, reads=(r_idx,), writes=(r_cg, r_kg), out=kg[:npages, :], out_offset=None, in_=cache_k,
                   in_offset=bass.IndirectOffsetOnAxis(ap=idx[:npages, 0:1], axis=0))
            S.dma("sp", d_qs, qs[:, :], grp["Q"][4 * s:4 * s + 4, :], reads=(grp["r_Q"],), writes=(r_qs,))
            ptb = ps[2].bitcast(BF16)
            for h in range(8):
                S.op("pe", "transpose", dict(out=ptb[0:96, h * 128:h * 128 + 4], in_=qs[:, h * 96:(h + 1) * 96],
                                             identity=cx.ident_bf[:4, :4]), reads=(r_qs,), writes=(rps[2],), inc=(h == 7))
            S.op("act", "activation", dict(out=QTs[:, :, :], in_=ptb[0:96, :].rearrange("p (h t) -> p h t", t=128)[:, :, 0:4],
                                           func=AF.Copy), reads=(rps[2],), writes=(r_QTs,))
            S.dma("sp", d_cn, cn[:, 0:128], grp["new_ckv"][4 * s:4 * s + 4, :], reads=(grp["r_ckv"],), writes=(r_cn,))
            S.dma("sp", d_cn, cn[:, 128:160], grp["new_kr"][4 * s:4 * s + 4, :], reads=(grp["r_kr"],), writes=(r_cn,))
            nblk = 128 + 1
            for t in range(nblk):
                if t < 128:
                    n = npages
                    c_blk, k_blk, r_src = cg[:n, t * 128:(t + 1) * 128], kg[:n, t * 32:(t + 1) * 32], r_cg
                    rs = (r_cg, r_kg)
                else:
                    n = 4
                    c_blk, k_blk, r_src = cn[:, 0:128], cn[:, 128:160], r_cn
                    rs = (r_cn,)
                kk = t % 2
                cb, r_cb, PTs, r_PTs = cb2[kk], r_cb2[kk], PTs2[kk], r_PTs2[kk]
                KT, r_KT = (A.KT0, A.r_KT0) if kk == 0 else (A.KT1, A.r_KT1)
                S.op("pool", "tensor_copy", dict(out=cb[:n, 0:128], in_=c_blk), reads=rs, writes=(r_cb,))
                att_kside(cx, S, A, c_blk, k_blk, r_src, n, KT[:, :, :n], r_KT, k=kk)
                sc0 = kk * 32
                for h in range(8):
                    S.op("pe", "matmul", dict(out=ps[4][:n, sc0 + h * 4:sc0 + (h + 1) * 4], lhsT=KT[:, h, :n], rhs=QTs[:, h, :],
                                              start=True, stop=True, skip_group_check=True), reads=(r_KT, r_QTs),
                         writes=(r_sc[kk],), inc=(h == 7))
                S.op("act", "activation", dict(out=PTs[:n, :], in_=ps[4][:n, sc0:sc0 + 32], func=AF.Exp), reads=(r_sc[kk],),
                     writes=(r_PTs,))
                if t == 128:
                    S.op("dve", "tensor_tensor", dict(out=PTs[:n, :], in0=PTs[:n, :], in1=ms[:n, :], op=ALU.mult),
                         reads=(r_ms,), writes=(r_PTs,))
                S.op("pe", "matmul", dict(out=ps[5][0:32, 0:129], lhsT=PTs[:n, :], rhs=cb[:n, 0:129], start=(t == 0),
                                          stop=(t == nblk - 1), skip_group_check=True), reads=(r_PTs, r_cb), writes=(rps[5],))
            S.op("dve", "reciprocal", dict(out=rd[:, :], in_=ps[5][0:32, 128:129]), reads=(rps[5],), writes=(r_rd,))
            S.op("dve", "tensor_scalar", dict(out=c32[:, :], in0=ps[5][0:32, 0:128], scalar1=rd[:, 0:1], scalar2=None,
                                              op0=ALU.mult), reads=(rps[5], r_rd), writes=(r_c32,))
            S.op("pe", "transpose", dict(out=ptb[:, 0:32], in_=c32[:, :], identity=cx.ident_bf[:32, :32]), reads=(r_c32,),
                 writes=(rps[2],))
            S.op("act", "activation", dict(out=cT32[:, :], in_=ptb[:, 0:32], func=AF.Copy), reads=(rps[2],), writes=(r_cT32,))
            for h in range(8):
                S.op("pe", "matmul", dict(out=ps[3][0:4, h * 64:(h + 1) * 64], lhsT=cT32[:, h * 4:(h + 1) * 4],
                                          rhs=A.wuv[:, h * 64:(h + 1) * 64], start=True, stop=True, skip_group_check=True),
                     reads=(r_cT32, A.r_wuv), writes=(rps[3],), inc=(h == 7))
            S.op("dve", "tensor_copy", dict(out=mo[:, :], in_=ps[3][0:4, :]), reads=(rps[3],), writes=(r_mo,))
            S.dma("sp", d_mo, grp["mix"][4 * s:4 * s + 4, 512:1024], mo[:, :], reads=(r_mo,), writes=(grp["r_mix"],))
            if s % 4 == 3:
                S.barrier()
                S.emit()
        S.barrier()
        S.emit()


CST_COLS = 128 + 128 + 128 + 128 + 1024 + 3 * 256 + 3 * 16


def host_consts():
    c = np.zeros((128, CST_COLS), np.float32)
    j = np.arange(128)[:, None]
    i = np.arange(128)[None, :]
    same = (j // 64 == i // 64)
    c[:, 0:128] = np.eye(128)
    c[:, 128:256] = (j <= i) & same
    c[:, 256:384] = same
    c[:, 384:512] = same
    oh = np.zeros((8, 8, 128), np.float32)
    for h in range(8):
        oh[h, h, :] = 1.0
    c[0:8, 512:1536] = oh.reshape(8, 1024)
    off = 1536
    for m in (64, 4):
        jj = np.arange(m)[:, None]
        ii = np.arange(m)[None, :]
        c[0:m, off:off + 4 * m] = np.tile(np.eye(m, dtype=np.float32), (1, 4))
        c[0:m, off + 4 * m:off + 8 * m] = np.tile(np.where(jj <= ii, 0.0, -1e30).astype(np.float32), (1, 4))
        c[0:m, off + 8 * m:off + 12 * m] = np.tile(np.where(jj < ii, 0.0, -1e30).astype(np.float32), (1, 4))
        off += 12 * m
    return c


def host_rope_table(pos):
    half = 16
    inv = (np.float32(10000.0) ** (-(np.arange(half, dtype=np.float32) / np.float32(half)))).astype(np.float32)
    ang = (pos.astype(np.float32)[:, None] * inv[None, :]).astype(np.float32)
    return np.concatenate([np.cos(ang), np.sin(ang)], axis=1).astype(np.float32)


def build(cfg):
    TP = cfg["TP"]
    NS = cfg["NS"]
    NSEQ = NS // 4
    phases = cfg.get("phases", ("adaln", "ffn1"))
    nc = bass.Bass("TRN2", target_bir_lowering=False)
    cx = Ctx()
    cx.nc = nc
    cx.stop = cfg.get("stop", 99)

    def din(name, shape, dt=F32):
        return nc.dram_tensor(name, list(shape), dt, kind="ExternalInput").ap()

    def dout(name, shape, dt=F32):
        return nc.dram_tensor(name, list(shape), dt, kind="ExternalOutput").ap()

    def dscr(name, shape, dt=F32):
        return nc.dram_tensor(name, list(shape), dt, kind="Internal").ap()

    xp = din("xp", [TP, D])
    xs_in = din("xs", [NS, D])
    c_rep = din("c_rep", [192, D])
    cst = din("cst", [128, CST_COLS])
    ada_w = din("ada_w", [D, 9 * D])
    ada_b = din("ada_b", [1, 9 * D])
    norm_ffn1 = din("norm_ffn1", [1, D])
    ffn1_wi = din("ffn1_wi", [D, 2 * DFF])
    ffn1_wo = din("ffn1_wo", [DFF, D])
    P = dict(norm_mix=din("norm_mix", [1, D]), w_in=din("w_in", [D, 2992]), conv_w_fm=din("conv_w_fm", [128, 12, 4]),
             a_log=din("a_log", [1, 8]), dt_bias=din("dt_bias", [1, 8]), gdn_norm=din("gdn_norm", [1, 64]),
             qn_g=din("qn_g", [1, 64]), qr_g=din("qr_g", [1, 32]), ckv_g=din("ckv_g", [1, 128]), kr_g=din("kr_g", [1, 32]),
             kn_g=din("kn_g", [1, 64]))
    P["w_uk"] = din("w_uk", [128, 512])
    P["w_uv"] = din("w_uv", [128, 512])
    w_out = din("w_out", [D, D])
    norm_ffn2 = din("norm_ffn2", [1, D])
    ffn2_wi = din("ffn2_wi", [D, 2 * DFF])
    ffn2_wo = din("ffn2_wo", [DFF, D])
    NPOOL = cfg.get("npool", 20480)
    cache_c = din("cache_c", [NPOOL, 128 * 128])
    cache_k = din("cache_k", [NPOOL, 128 * 32])
    ptab = din("ptab", [NSEQ, cfg.get("npages", 128)], I32)
    cs_p = din("cs_p", [TP, 32])
    cs_s = din("cs_s", [4, 32])
    st_conv = din("st_conv", [NSEQ, 3, 1536])
    st_gdn = din("st_gdn", [NSEQ, 8, 64, 64])

    yp = dout("yp", [TP, D])
    ys = dout("ys", [NS, D])
    o_ckv_p, o_kr_p = dout("ckv_p", [TP, 128]), dout("kr_p", [TP, 32])
    o_conv_p, o_gdn_p = dout("conv_p", [1, 3, 1536]), dout("gdn_p", [1, 8, 64, 64])
    o_ckv_s, o_kr_s = dout("ckv_s", [NS, 128]), dout("kr_s", [NS, 32])
    o_conv_s, o_gdn_s = dout("conv_s", [NSEQ, 3, 1536]), dout("gdn_s", [NSEQ, 8, 64, 64])
    mods_p = dscr("mods_p", [9, 128, D])
    mods_s = dscr("mods_s", [9, 64, D])
    x1p, x1s = dscr("x1p", [TP, D]), dscr("x1s", [NS, D])
    x2p, x2s = dscr("x2p", [TP, D]), dscr("x2s", [NS, D])
    dbg = dout if cfg.get("debug") else dscr
    mix_p, mix_s = dbg("mix_p", [TP, D], BF16), dbg("mix_s", [NS, D], BF16)
    Q_p, Q_s = dbg("Q_p", [TP, 768], BF16), dbg("Q_s", [NS, 768], BF16)

    with ExitStack() as stack:
        S = Sched(nc, stack)
        cx.S = S
        cx.ps = [stack.enter_context(nc.psum_tensor("ps%d" % i, [128, 512], F32))[:, :] for i in range(8)]
        cx.rps = [Res("ps%d" % i) for i in range(8)]
        cst_f = sb(cx, stack, "cst_f", [128, 128], F32)
        cst_b = sb(cx, stack, "cst_b", [128, 128 + 128 + 512 + 128], BF16)
        cx.cst = cst
        cx.eps_t = sb(cx, stack, "eps_t", [128, 1], F32)
        cx.one_t = sb(cx, stack, "one_t", [128, 1], F32)
        cx.ident_f = cst_f[:, 0:128]
        cx.ident_bf = cst_b[:, 0:128]
        cx.nident_bf = cst_b[:, 128:256]
        cx.ident4_bf = cst_b[:, 256:768]
        cx.bones = cst_b[:, 768:896]
        r_c = Res("consts")
        d_c = S.dsem("consts")
        S.dma("sp", d_c, cst_f[:, :], cst[:, 0:128], writes=(r_c,))
        bon = sb(cx, stack, "bon_tmp", [128, 128], F32)
        S.dma("sp", d_c, bon[:, :], cst[:, 384:512], writes=(r_c,))
        S.op("dve", "tensor_copy", dict(out=cst_b[:, 0:128], in_=cst_f[:, 0:128]), reads=(r_c,), writes=(r_c,))
        S.op("dve", "tensor_scalar", dict(out=cst_b[:, 128:256], in0=cst_f[:, 0:128], scalar1=-1.0, scalar2=None,
                                          op0=ALU.mult), reads=(r_c,), writes=(r_c,))
        for h in range(4):
            S.op("dve", "tensor_copy", dict(out=cst_b[:, 256 + h * 128:256 + (h + 1) * 128], in_=cst_f[:, 0:128]),
                 reads=(r_c,), writes=(r_c,))
        S.op("dve", "tensor_copy", dict(out=cst_b[:, 768:896], in_=bon[:, :]), reads=(r_c,), writes=(r_c,))
        S.op("dve", "memset", dict(ap=cx.eps_t[:, :], constant=EPS), writes=(r_c,))
        S.op("dve", "memset", dict(ap=cx.one_t[:, :], constant=1.0), writes=(r_c,))
        S.barrier()
        S.emit()

        r_mods = Res("mods")
        if "adaln" in phases:
            phase_adaln(cx, S, c_rep, ada_w, ada_b, mods_p, mods_s, r_mods)

        nblk_p = (TP + 127) // 128
        r_xp = [Res() for _ in range(nblk_p)]
        r_xs = [Res()]
        r_x1p = [Res() for _ in range(nblk_p)]
        r_x1s = [Res()]
        f1_dst_p, f1_dst_s = (x1p, x1s) if "mixer" in phases else (yp, ys)
        if "ffn1" in phases:
            groups = [dict(src=xp, dst=f1_dst_p, T=TP, mods=mods_p, n=128, src_res=r_xp, dst_res=r_x1p),
                      dict(src=xs_in, dst=f1_dst_s, T=NS, mods=mods_s, n=64, src_res=r_xs, dst_res=r_x1s)]
            phase_ffn(cx, S, "f1", groups, ffn1_wi, ffn1_wo, norm_ffn1, 0)
        r_out = Res("outs")
        gp = gs = None
        if "mixer" in phases:
            msrc_p, msrc_s = (x1p, x1s) if "ffn1" in phases else (xp, xs_in)
            gp = dict(src=msrc_p, src_res=r_x1p, T=TP, nseq=1, mods=mods_p, n=128, s0=None, conv0=None, cs=cs_p,
                      new_conv=o_conv_p, new_gdn=o_gdn_p, new_ckv=o_ckv_p, new_kr=o_kr_p, mix=mix_p, Q=Q_p,
                      r_out=r_out, r_mix=Res(), r_Q=Res(), r_ckv=Res(), r_kr=Res())
            gs = dict(src=msrc_s, src_res=r_x1s, T=NS, nseq=NSEQ, mods=mods_s, n=64, s0=st_gdn, conv0=st_conv, cs=cs_s,
                      new_conv=o_conv_s, new_gdn=o_gdn_s, new_ckv=o_ckv_s, new_kr=o_kr_s, mix=mix_s, Q=Q_s,
                      r_out=r_out, r_mix=Res(), r_Q=Res(), r_ckv=Res(), r_kr=Res())
            phase_mixer(cx, S, [gp, gs], P)
        if "attn_p" in phases:
            phase_attn_prompt(cx, S, gp, P)
        if "attn_s" in phases:
            phase_attn_sample(cx, S, gs, P, cache_c, cache_k, ptab, NSEQ, cfg.get("npages", 128))
        if "wout" in phases:
            r_x2p = [Res() for _ in range(nblk_p)]
            r_x2s = [Res()]
            gp.update(dst=x2p, dst_res=r_x2p)
            gs.update(dst=x2s, dst_res=r_x2s)
            phase_wout(cx, S, [gp, gs], w_out)
            if "ffn2" in phases:
                groups = [dict(src=x2p, dst=yp, T=TP, mods=mods_p, n=128, src_res=r_x2p, dst_res=[Res() for _ in range(nblk_p)]),
                          dict(src=x2s, dst=ys, T=NS, mods=mods_s, n=64, src_res=r_x2s, dst_res=[Res()])]
                phase_ffn(cx, S, "f2", groups, ffn2_wi, ffn2_wo, norm_ffn2, 6)
        S.barrier()
        S.emit()
    return nc


ALL_PHASES = ("adaln", "ffn1", "mixer", "attn_p", "attn_s", "wout", "ffn2")


def make_inputs(core, inputs, TP=4096):
    f = lambda a: np.ascontiguousarray(np.asarray(a, dtype=np.float32))
    b = core % 4
    sl = slice(16 * core, 16 * core + 16)
    conv_w = f(inputs["gdn_conv_w"][0])
    m = dict(
        xp=f(inputs["x_prompt"][b][:TP]), xs=f(inputs["x_sample"][sl]).reshape(64, D),
        c_rep=np.concatenate([np.repeat(f(inputs["c_prompt"][b:b + 1]), 128, 0), np.repeat(f(inputs["c_sample"][sl]), 4, 0)], 0),
        cst=host_consts(), ada_w=f(inputs["ada_w"][0]), ada_b=f(inputs["ada_b"][0:1]),
        norm_ffn1=f(inputs["norm_ffn1"][0:1]), ffn1_wi=f(inputs["ffn1_wi"][0]), ffn1_wo=f(inputs["ffn1_wo"][0]),
        norm_mix=f(inputs["norm_mix"][0:1]), w_in=f(inputs["w_in"][0]),
        conv_w_fm=np.ascontiguousarray(conv_w.reshape(4, 12, 128).transpose(2, 1, 0)),
        a_log=f(inputs["gdn_a_log"][0:1]), dt_bias=f(inputs["gdn_dt_bias"][0:1]), gdn_norm=f(inputs["gdn_norm"][0:1]),
        qn_g=f(inputs["mla_qn_norm"][0:1]), qr_g=f(inputs["mla_qr_norm"][0:1]), ckv_g=f(inputs["mla_ckv_norm"][0:1]),
        kr_g=f(inputs["mla_kr_norm"][0:1]), kn_g=f(inputs["mla_kn_norm"][0:1]),
        w_uk=f(inputs["mla_w_uk"][0]).reshape(128, 512), w_uv=f(inputs["mla_w_uv"][0]).reshape(128, 512),
        w_out=f(inputs["w_out"][0]), norm_ffn2=f(inputs["norm_ffn2"][0:1]), ffn2_wi=f(inputs["ffn2_wi"][0]),
        ffn2_wo=f(inputs["ffn2_wo"][0]),
        cache_c=f(inputs["cache_ckv"][0]).reshape(-1, 128 * 128), cache_k=f(inputs["cache_krope"][0]).reshape(-1, 128 * 32),
        ptab=np.ascontiguousarray(np.asarray(inputs["page_table"][sl], dtype=np.int32)),
        cs_p=host_rope_table(np.arange(TP)), cs_s=host_rope_table(16384 + np.arange(4)),
        st_conv=f(inputs["state_conv"][0][sl]), st_gdn=f(inputs["state_gdn"][0][sl]),
    )
    return m


def kernel(**inputs):
    nc = build(dict(TP=4096, NS=64, phases=ALL_PHASES))
    in_maps = [make_inputs(c, inputs) for c in range(8)]
    res = run_bass_kernel_spmd(nc, in_maps, core_ids=list(range(8))).results
    f32 = np.float32
    yp = np.stack([res[b]["yp"] for b in range(4)]).astype(f32)
    ys = np.concatenate([res[c]["ys"].reshape(16, 4, D) for c in range(8)]).astype(f32)
    ckv_p = np.stack([res[b]["ckv_p"] for b in range(4)])[None].astype(f32)
    kr_p = np.stack([res[b]["kr_p"] for b in range(4)])[None].astype(f32)
    conv_p = np.concatenate([res[b]["conv_p"] for b in range(4)])[None].astype(f32)
    gdn_p = np.concatenate([res[b]["gdn_p"] for b in range(4)])[None].astype(f32)
    ckv_s = np.concatenate([res[c]["ckv_s"].reshape(16, 4, 128) for c in range(8)])[None].astype(f32)
    kr_s = np.concatenate([res[c]["kr_s"].reshape(16, 4, 32) for c in range(8)])[None].astype(f32)
    conv_s = np.concatenate([res[c]["conv_s"] for c in range(8)])[None].astype(f32)
    gdn_s = np.concatenate([res[c]["gdn_s"] for c in range(8)])[None].astype(f32)
    return (yp, ys, ckv_p, kr_p, conv_p, gdn_p, ckv_s, kr_s, conv_s, gdn_s)
```

```python
import numpy as np
from contextlib import ExitStack
import concourse.bass as bass
import concourse.mybir as mybir
from concourse.bass_utils import run_bass_kernel_spmd

F32 = mybir.dt.float32
BF16 = mybir.dt.bfloat16
I32 = mybir.dt.int32
U32 = mybir.dt.uint32
AF = mybir.ActivationFunctionType
ALU = mybir.AluOpType
AX = mybir.AxisListType

D = 1024
DFF = 2816
NFC = DFF // 128
EPS = 1e-6

class Res:
    __slots__ = ("name", "w", "r", "excl")

    def __init__(self, name="", excl=False):
        self.name = name
        self.excl = excl
        self.w = None
        self.r = {}


class DSem:
    __slots__ = ("sem", "cnt", "name")

    def __init__(self, sem, name):
        self.sem = sem
        self.cnt = 0
        self.name = name


class Sched:
    ENGS = ("pe", "act", "dve", "pool", "sp")

    def __init__(self, nc, stack):
        self.nc = nc
        self.stack = stack
        self.q = {k: [] for k in self.ENGS}
        self.esem = {k: stack.enter_context(nc.semaphore("es_" + k)) for k in self.ENGS}
        self.ecnt = {k: 0 for k in self.ENGS}
        self.known = {k: {} for k in self.ENGS}
        self.dsems = []
        self.nwait = 0
        self.nop = 0

    def dsem(self, name):
        s = self.stack.enter_context(self.nc.semaphore("ds_" + name))
        d = DSem(s, name)
        self.dsems.append(d)
        return d

    def _waits(self, eng, reads, writes):
        need = {}

        def add(ev):
            sem_id, sem, val, src = ev
            if src == "pe" and eng == "pe":
                return
            if self.known[eng].get(sem_id, 0) >= val:
                return
            if sem_id not in need or need[sem_id][1] < val:
                need[sem_id] = (sem, val)

        for r in reads:
            if r.w is not None:
                add(r.w)
        for w in writes:
            if w.w is not None:
                add(w.w)
            for ev in w.r.values():
                add(ev)
        for sem_id, (sem, val) in need.items():
            self.q[eng].append(("wait", sem, val))
            self.known[eng][sem_id] = val
            self.nwait += 1

    def _record(self, ev, reads, writes):
        for r in reads:
            old = r.r.get(ev[0])
            if old is None or old[2] < ev[2]:
                r.r[ev[0]] = ev
        for w in writes:
            w.w = ev
            w.r = {}

    def op(self, eng, meth, kw, reads=(), writes=(), inc=True):
        ex = tuple(r for r in reads if r.excl)
        if ex:
            writes = tuple(writes) + ex
        self._waits(eng, reads, writes)
        if inc:
            self.ecnt[eng] += 1
            ev = (eng, self.esem[eng], self.ecnt[eng], eng)
        else:
            ev = (eng, self.esem[eng], self.ecnt[eng] + 1, eng)
        self.q[eng].append(("op", meth, kw, inc))
        self.nop += 1
        self._record(ev, reads, writes)

    def dma(self, q, ds, out, in_, reads=(), writes=(), **kw):
        self._waits(q, reads, writes)
        ds.cnt += 16
        ev = (id(ds), ds.sem, ds.cnt, "dma")
        self.q[q].append(("dma", out, in_, ds.sem, kw))
        self.nop += 1
        self._record(ev, reads, writes)

    def idma(self, ds, reads=(), writes=(), **kw):
        self._waits("pool", reads, writes)
        ds.cnt += 16
        ev = (id(ds), ds.sem, ds.cnt, "dma")
        self.q["pool"].append(("idma", kw, ds.sem))
        self.nop += 1
        self._record(ev, reads, writes)

    def barrier(self):
        for e in self.ENGS:
            for e2 in self.ENGS:
                if e2 == e:
                    continue
                v = self.ecnt[e2]
                if v > 0 and self.known[e].get(e2, 0) < v:
                    self.q[e].append(("wait", self.esem[e2], v))
                    self.known[e][e2] = v
            for d in self.dsems:
                if d.cnt > 0 and self.known[e].get(id(d), 0) < d.cnt:
                    self.q[e].append(("wait", d.sem, d.cnt))
                    self.known[e][id(d)] = d.cnt

    def emit(self):
        import os
        if os.environ.get("EMITLOG"):
            print("EMIT", {k: len(v) for k, v in self.q.items()}, "cnt", dict(self.ecnt), "ndsem", len(self.dsems))
        nc = self.nc
        engs = {"pe": "tensor", "act": "scalar", "dve": "vector", "pool": "gpsimd", "sp": "sync"}
        with nc.Block() as block:
            for k, attr in engs.items():
                items = self.q[k]
                esem = self.esem[k]

                def body(e, items=items, esem=esem):
                    for it in items:
                        if it[0] == "wait":
                            e.wait_ge(it[1], it[2])
                        elif it[0] == "op":
                            ins = getattr(e, it[1])(**it[2])
                            if it[3]:
                                ins.then_inc(esem, 1)
                        elif it[0] == "idma":
                            e.indirect_dma_start(**it[1]).then_inc(it[2], 16)
                        else:
                            _, out, in_, sem, kw = it
                            e.dma_start(out=out, in_=in_, **kw).then_inc(sem, 16)

                getattr(block, attr)(body)
        self.q = {k: [] for k in self.ENGS}


class Ctx:
    pass


def sb(cx, stack, name, shape, dt):
    return stack.enter_context(cx.nc.sbuf_tensor(name, list(shape), dt))


def row_bcast(ap_row, n):
    t = ap_row.tensor
    F = ap_row.shape[-1]
    return bass.AP(t, ap_row.offset, [[0, n], [1, F]])


def load_weight_bf16(cx, S, name, dst, dst_res, w_ap, kchunks, cols, col_piece):
    with ExitStack() as st:
        stg = [sb(cx, st, name + "stg%d" % i, [128, col_piece], F32) for i in range(3)]
        r_stg = [Res() for _ in range(3)]
        d_stg = [S.dsem(name + "stg%d" % i) for i in range(3)]
        engs = (("dve", "tensor_copy"), ("pool", "tensor_copy"), ("act", "activation"))
        i = 0
        for c in range(kchunks):
            for c0 in range(0, cols, col_piece):
                c1 = min(cols, c0 + col_piece)
                k = i % 3
                i += 1
                S.dma("sp", d_stg[k], stg[k][:, :c1 - c0], w_ap[c * 128:(c + 1) * 128, c0:c1], writes=(r_stg[k],))
                eng, meth = engs[k]
                kw = dict(out=dst[:, c, c0:c1], in_=stg[k][:, :c1 - c0])
                if meth == "activation":
                    kw["func"] = AF.Copy
                S.op(eng, meth, kw, reads=(r_stg[k],), writes=(dst_res,))
        S.barrier()
        S.emit()


def norm_mod_transpose(cx, S, bufs, src_ap, src_res, m, hT_dst, pt_k):
    (xs, tt, junk, hb, stat, GG, Bt, r_xs, r_tt, r_junk, r_hb, r_stat, r_GG, r_Bt, r_hT, d_xs) = bufs
    S.dma("sp", d_xs, xs[:m, :], src_ap, reads=(src_res,), writes=(r_xs,))
    S.op("act", "activation", dict(out=junk[:m, :], in_=xs[:m, :], func=AF.Square, accum_out=stat[:m, 0:1]),
         reads=(r_xs,), writes=(r_junk, r_stat))
    S.op("act", "activation", dict(out=stat[:m, 1:2], in_=stat[:m, 0:1], func=AF.Sqrt, scale=1.0 / D,
                                   bias=cx.eps_t[:m, 0:1]), reads=(r_stat,), writes=(r_stat,))
    S.op("dve", "reciprocal", dict(out=stat[:m, 2:3], in_=stat[:m, 1:2]), reads=(r_stat,), writes=(r_stat,))
    S.op("dve", "scalar_tensor_tensor", dict(out=tt[:m, :], in0=xs[:m, :], scalar=stat[:m, 2:3], in1=GG[:m, :],
                                             op0=ALU.mult, op1=ALU.mult),
         reads=(r_xs, r_stat, r_GG), writes=(r_tt,))
    S.op("pool", "tensor_tensor", dict(out=hb[:m, :], in0=tt[:m, :], in1=Bt[:m, :], op=ALU.add),
         reads=(r_tt, r_Bt), writes=(r_hb,))
    ptb = cx.ps[pt_k].bitcast(BF16)
    for c in range(8):
        S.op("pe", "transpose", dict(out=ptb[:, c * 128:c * 128 + m], in_=hb[:m, c * 128:(c + 1) * 128],
                                     identity=cx.ident_bf[:m, :m]),
             reads=(r_hb,), writes=(cx.rps[pt_k],), inc=(c == 7))
    S.op("act", "activation", dict(out=hT_dst, in_=ptb.rearrange("p (c t) -> p c t", c=8)[:, :, :m], func=AF.Copy),
         reads=(cx.rps[pt_k],), writes=(r_hT,))


def load_mod_tiles(cx, S, g_ap, mods, mod_base, n, gscale, GG, Bt, Gt, tt, r_GG, r_Bt, r_Gt, r_tt, d_m):
    S.dma("sp", d_m, tt[:n, :], row_bcast(g_ap, n), writes=(r_tt,))
    S.dma("sp", d_m, GG[:n, :], mods[mod_base + 1, :n, :], writes=(r_GG,))
    S.dma("sp", d_m, Bt[:n, :], mods[mod_base + 0, :n, :], writes=(r_Bt,))
    if Gt is not None:
        S.dma("sp", d_m, Gt[:n, :], mods[mod_base + 2, :n, :], writes=(r_Gt,))
    S.barrier()
    S.op("dve", "scalar_tensor_tensor", dict(out=GG[:n, :], in0=GG[:n, :], scalar=1.0, in1=tt[:n, :],
                                             op0=ALU.add, op1=ALU.mult), reads=(r_tt,), writes=(r_GG,))
    if Gt is not None and gscale != 1.0:
        S.op("dve", "tensor_scalar", dict(out=Gt[:n, :], in0=Gt[:n, :], scalar1=float(gscale), scalar2=None,
                                          op0=ALU.mult), writes=(r_Gt,))


def phase_ffn(cx, S, name, groups, wi_ap, wo_ap, g_ap, mod_base):
    with ExitStack() as st:
        wi = sb(cx, st, name + "wi", [128, 8, 2 * DFF], BF16)
        wo = sb(cx, st, name + "wo", [128, NFC, D], BF16)
        r_wi, r_wo = Res("wi"), Res("wo")
        load_weight_bf16(cx, S, name + "wi", wi, r_wi, wi_ap, 8, 2 * DFF, DFF)
        load_weight_bf16(cx, S, name + "wo", wo, r_wo, wo_ap, NFC, D, D)
        GG = sb(cx, st, name + "GG", [128, D], F32)
        Bt = sb(cx, st, name + "Bt", [128, D], F32)
        Gt = sb(cx, st, name + "Gt", [128, D], F32)
        xs = sb(cx, st, name + "xs", [128, D], F32)
        tt = sb(cx, st, name + "tt", [128, D], F32)
        junk = sb(cx, st, name + "junk", [128, D], BF16)
        hb = sb(cx, st, name + "hb", [128, D], BF16)
        hT = sb(cx, st, name + "hT", [128, 8, 512], BF16)
        sg = [sb(cx, st, name + "sg%d" % i, [128, 512], BF16) for i in range(2)]
        actT = sb(cx, st, name + "actT", [128, NFC, 512], BF16)
        xr = [sb(cx, st, name + "xr%d" % i, [128, D], F32) for i in range(2)]
        tmp = sb(cx, st, name + "tmp", [128, 512], F32)
        stat = sb(cx, st, name + "stat", [128, 4], F32)

        r_GG, r_Bt, r_Gt, r_xs, r_tt, r_junk, r_hb, r_hT = (Res(n_) for n_ in
                                                              ("GG", "Bt", "Gt", "xs", "tt", "junk", "hb", "hT"))
        r_sg = [Res("sg0"), Res("sg1")]
        r_actT = [Res("actT%d" % j) for j in range(NFC)]
        r_xr = [Res("xr0"), Res("xr1")]
        r_tmp, r_stat = Res("tmp"), Res("stat")
        d_m, d_xs = S.dsem(name + "m"), S.dsem(name + "xs")
        d_xr = [S.dsem(name + "xr0"), S.dsem(name + "xr1")]
        d_xst = [S.dsem(name + "xst0"), S.dsem(name + "xst1")]
        bufs = (xs, tt, junk, hb, stat, GG, Bt, r_xs, r_tt, r_junk, r_hb, r_stat, r_GG, r_Bt, r_hT, d_xs)


        pg, pu, po = cx.ps[0:2], cx.ps[2:4], cx.ps[4:6]
        r_pg, r_pu, r_po = cx.rps[0:2], cx.rps[2:4], cx.rps[4:6]
        cnt = {"up": 0, "po": 0, "pt": 0, "xr": 0}

        for g in groups:
            n = g["n"]
            load_mod_tiles(cx, S, g_ap, g["mods"], mod_base, n, 0.5, GG, Bt, Gt, tt, r_GG, r_Bt, r_Gt, r_tt, d_m)
            T = g["T"]
            nblk = (T + 511) // 512

            def stage_norm(b):
                t0 = b * 512
                tb = min(512, T - t0)
                for s_ in range((tb + 127) // 128):
                    r0 = t0 + s_ * 128
                    m = min(128, T - r0)
                    k = 6 + cnt["pt"] % 2
                    cnt["pt"] += 1
                    norm_mod_transpose(cx, S, bufs, g["src"][r0:r0 + m, :], g["src_res"][r0 // 128], m,
                                       hT[:, :, s_ * 128:s_ * 128 + m], k)

            def stage_up(b):
                t0 = b * 512
                tb = min(512, T - t0)
                for j in range(NFC):
                    k = cnt["up"] % 2
                    cnt["up"] += 1
                    for kc in range(8):
                        S.op("pe", "matmul", dict(out=pg[k][:, :tb], lhsT=wi[:, kc, j * 128:(j + 1) * 128],
                                                  rhs=hT[:, kc, :tb], start=(kc == 0), stop=(kc == 7)),
                             reads=(r_wi, r_hT), writes=(r_pg[k],), inc=(kc == 7))
                    for kc in range(8):
                        S.op("pe", "matmul", dict(out=pu[k][:, :tb], lhsT=wi[:, kc, DFF + j * 128:DFF + (j + 1) * 128],
                                                  rhs=hT[:, kc, :tb], start=(kc == 0), stop=(kc == 7)),
                             reads=(r_wi, r_hT), writes=(r_pu[k],), inc=(kc == 7))
                    S.op("act", "activation", dict(out=sg[k][:, :tb], in_=pg[k][:, :tb], func=AF.Silu),
                         reads=(r_pg[k],), writes=(r_sg[k],))
                    S.op("dve", "tensor_tensor", dict(out=actT[:, j, :tb], in0=sg[k][:, :tb], in1=pu[k][:, :tb],
                                                      op=ALU.mult),
                         reads=(r_sg[k], r_pu[k]), writes=(r_actT[j],))

            def stage_down(b):
                t0 = b * 512
                tb = min(512, T - t0)
                for s_ in range((tb + 127) // 128):
                    r0 = t0 + s_ * 128
                    m = min(128, T - r0)
                    kx = cnt["xr"] % 2
                    cnt["xr"] += 1
                    S.dma("sp", d_xr[kx], xr[kx][:m, :], g["src"][r0:r0 + m, :], reads=(g["src_res"][r0 // 128],),
                          writes=(r_xr[kx],))
                    for half in range(2):
                        k = cnt["po"] % 2
                        cnt["po"] += 1
                        for j in range(NFC):
                            S.op("pe", "matmul", dict(out=po[k][:m, :], lhsT=actT[:, j, s_ * 128:s_ * 128 + m],
                                                      rhs=wo[:, j, half * 512:(half + 1) * 512],
                                                      start=(j == 0), stop=(j == NFC - 1)),
                                 reads=(r_wo, r_actT[j]), writes=(r_po[k],), inc=(j == NFC - 1))
                        S.op("dve", "tensor_tensor", dict(out=tmp[:m, :], in0=po[k][:m, :],
                                                          in1=Gt[:m, half * 512:(half + 1) * 512], op=ALU.mult),
                             reads=(r_po[k], r_Gt), writes=(r_tmp,))
                        S.op("pool", "tensor_tensor", dict(out=xr[kx][:m, half * 512:(half + 1) * 512],
                                                           in0=xr[kx][:m, half * 512:(half + 1) * 512],
                                                           in1=tmp[:m, :], op=ALU.add),
                             reads=(r_tmp,), writes=(r_xr[kx],))
                    S.dma("pool", d_xst[kx], g["dst"][r0:r0 + m, :], xr[kx][:m, :], reads=(r_xr[kx],),
                          writes=(g["dst_res"][r0 // 128],))

            stage_norm(0)
            for b in range(nblk):
                stage_up(b)
                if b + 1 < nblk:
                    stage_norm(b + 1)
                stage_down(b)
        S.barrier()
        S.emit()


def phase_adaln(cx, S, c_rep, ada_w, ada_b, mods_p, mods_s, r_mods):
    with ExitStack() as st:
        ct = sb(cx, st, "ad_ct", [128, D], F32)
        cb_ = sb(cx, st, "ad_cb", [128, D], BF16)
        cT = sb(cx, st, "ad_cT", [128, 8, 192], BF16)
        w = [sb(cx, st, "ad_w%d" % i, [128, 8, D], BF16) for i in range(2)]
        wst = [sb(cx, st, "ad_wst%d" % i, [128, 8, D], F32) for i in range(2)]
        bb = [sb(cx, st, "ad_b%d" % i, [128, D], F32) for i in range(2)]
        o = [sb(cx, st, "ad_o%d" % i, [128, D], F32) for i in range(2)]
        r_ct, r_cb, r_cT = Res(), Res(), Res()
        r_w, r_wst, r_bb, r_o = [Res(), Res()], [Res(), Res()], [Res(), Res()], [Res(), Res()]
        d_ct = S.dsem("ad_ct")
        d_w = [S.dsem("ad_w0"), S.dsem("ad_w1")]
        d_b = [S.dsem("ad_b0"), S.dsem("ad_b1")]
        d_o = [S.dsem("ad_o0"), S.dsem("ad_o1")]

        def load(blk):
            k = blk % 2
            for hf in range(2):
                S.dma("sp", d_w[k], wst[k][:, hf * 4:(hf + 1) * 4, :],
                      ada_w[hf * 512:(hf + 1) * 512, blk * D:(blk + 1) * D].rearrange("(c p) f -> p c f", p=128),
                      writes=(r_wst[k],))
            S.dma("sp", d_b[k], bb[k][:, :], row_bcast(ada_b[0:1, blk * D:(blk + 1) * D], 128), writes=(r_bb[k],))

        load(0)
        for gi, (r0, m) in enumerate(((0, 128), (128, 64))):
            S.dma("sp", d_ct, ct[:m, :], c_rep[r0:r0 + m, :], writes=(r_ct,))
            S.op("act", "activation", dict(out=cb_[:m, :], in_=ct[:m, :], func=AF.Silu), reads=(r_ct,), writes=(r_cb,))
            ptb = cx.ps[6 + gi].bitcast(BF16)
            for c in range(8):
                S.op("pe", "transpose", dict(out=ptb[:, c * 128:c * 128 + m], in_=cb_[:m, c * 128:(c + 1) * 128],
                                             identity=cx.ident_bf[:m, :m]),
                     reads=(r_cb,), writes=(cx.rps[6 + gi],), inc=(c == 7))
            S.op("act", "activation", dict(out=cT[:, :, r0:r0 + m],
                                           in_=ptb.rearrange("p (c t) -> p c t", c=8)[:, :, :m], func=AF.Copy),
                 reads=(cx.rps[6 + gi],), writes=(r_cT,))
        for blk in range(9):
            k = blk % 2
            if blk + 1 < 9:
                load(blk + 1)
            S.op("pool", "tensor_copy", dict(out=w[k][:, 0:4, :], in_=wst[k][:, 0:4, :]),
                 reads=(r_wst[k],), writes=(r_w[k],))
            S.op("dve", "tensor_copy", dict(out=w[k][:, 4:8, :], in_=wst[k][:, 4:8, :]),
                 reads=(r_wst[k],), writes=(r_w[k],))
            for gi, (r0, m, dst) in enumerate(((0, 128, mods_p), (128, 64, mods_s))):
                ko = gi
                for half in range(2):
                    pk = (blk * 4 + gi * 2 + half) % 4
                    ps, rp = cx.ps[pk], cx.rps[pk]
                    for c in range(8):
                        S.op("pe", "matmul", dict(out=ps[:m, :], lhsT=cT[:, c, r0:r0 + m],
                                                  rhs=w[k][:, c, half * 512:(half + 1) * 512], start=(c == 0),
                                                  stop=(c == 7)), reads=(r_cT, r_w[k]), writes=(rp,), inc=(c == 7))
                    S.op("dve", "tensor_tensor", dict(out=o[ko][:m, half * 512:(half + 1) * 512], in0=ps[:m, :],
                                                      in1=bb[k][:m, half * 512:(half + 1) * 512], op=ALU.add),
                         reads=(rp, r_bb[k]), writes=(r_o[ko],))
                S.dma("act", d_o[ko], dst[blk, :m, :], o[ko][:m, :], reads=(r_o[ko],), writes=(r_mods,))
        S.barrier()
        S.emit()


NLEV = 5
QK_SCALE = 64 ** -0.5
ATT_SCALE = 96 ** -0.5


def bc(ap2, n):
    return ap2.unsqueeze(2).broadcast_to([ap2.shape[0], ap2.shape[1], n])


def bc_mid(ap2, n):
    return ap2.unsqueeze(1).broadcast_to([ap2.shape[0], n, ap2.shape[1]])


class MixBufs:
    pass


def mixer_alloc(cx, S, st, name, P):
    B = MixBufs()

    def t(nm, shape, dt):
        tt_ = sb(cx, st, name + nm, shape, dt)
        setattr(B, nm, tt_)
        setattr(B, "r_" + nm, Res(nm))
        return tt_

    t("win", [128, 8, 2992], BF16)
    load_weight_bf16(cx, S, "mxwin", B.win, B.r_win, P["w_in"], 8, 2992, 1496)
    t("GG", [128, D], F32); t("Bt", [128, D], F32)
    t("xs", [128, D], F32); t("tt", [128, D], F32); t("junk", [128, D], BF16); t("hb", [128, D], BF16)
    t("stat", [128, 4], F32)
    t("hT", [128, 8, 256], BF16)
    t("xq", [128, 12, 259], F32)
    t("yc", [128, 12, 256], F32)
    t("sqb", [128, 512], BF16)
    t("rinv", [128, 512], F32)
    t("qkT", [128, 8, 256], BF16)
    t("tm", [128, 1456], F32)
    t("sc", [128, 96], F32)
    t("rows", [8, 2, 128], F32)
    t("glb", [128, 8], F32)
    t("vtok", [128, 512], BF16); t("ktok", [128, 512], BF16); t("kdec", [128, 512], BF16)
    t("decq", [128, 512], F32); t("decb", [128, 512], F32)
    for g in range(2):
        for nm in ("A", "M", "P"):
            for k in range(2):
                t("%s%d%d" % (nm, g, k), [128, 512], F32)
        t("Pf%d" % g, [128, 512], BF16)
    t("qk", [128, 1024], BF16)
    t("u", [128, 512], F32)
    t("wT", [128, 4, 128], BF16)
    t("vnew", [128, 512], BF16)
    t("o1", [128, 512], F32)
    t("osb", [128, 512], F32)
    t("ost", [128, 16], F32)
    t("zs", [128, 512], F32)
    t("gout", [128, 512], BF16)
    t("S32", [128, 512], F32); t("Sbf", [128, 512], BF16)
    t("mq", [128, 768], F32); t("mqs", [128, 32], F32); t("Qb", [128, 768], BF16)
    t("cn", [128, 128], F32); t("krn", [128, 32], F32); t("krr", [128, 32], F32); t("qrr", [128, 8, 32], F32)
    t("cs", [128, 32], F32)
    t("cw", [128, 12, 4], F32)
    t("g8", [128, 16], F32)
    t("gng", [128, 64], F32); t("qng", [128, 64], F32); t("qrg", [128, 32], F32); t("ckg", [128, 128], F32)
    t("krg", [128, 32], F32); t("kng", [128, 64], F32)
    t("nc3", [128, 1536], F32)
    B.d = {}
    return B


def mixer_group(cx, S, B, name, grp, P):
    def dsem(k):
        if k not in B.d:
            B.d[k] = S.dsem(name + k)
        return B.d[k]

    def op(eng, meth, reads, writes, inc=True, **kw):
        S.op(eng, meth, kw, reads=reads, writes=writes, inc=inc)

    ps, rps = cx.ps, cx.rps
    T, nseq = grp["T"], grp["nseq"]
    Ts = T // nseq
    prompt = (nseq == 1)
    n = grp["n"]
    load_mod_tiles(cx, S, P["norm_mix"], grp["mods"], 3, n, 1.0, B.GG, B.Bt, None, B.tt, B.r_GG, B.r_Bt, None,
                   B.r_tt, dsem("m"))
    bufs = (B.xs, B.tt, B.junk, B.hb, B.stat, B.GG, B.Bt, B.r_xs, B.r_tt, B.r_junk, B.r_hb, B.r_stat, B.r_GG,
            B.r_Bt, B.r_hT, dsem("xs"))
    blk_T = 256 if prompt else T
    nblk = (T + blk_T - 1) // blk_T
    if prompt:
        op("dve", "memset", (), (B.r_S32,), ap=B.S32[:, :], constant=0.0)
        op("dve", "memset", (), (B.r_Sbf,), ap=B.Sbf[:, :], constant=0.0)
        op("pool", "memset", (), (B.r_xq,), ap=B.xq[:, :, 0:3], constant=0.0)
    ptk = [0]

    for b in range(nblk):
        t0 = b * blk_T
        tb = min(blk_T, T - t0)
        for s_ in range((tb + 127) // 128):
            r0 = t0 + s_ * 128
            m = min(128, T - r0)
            k = 6 + ptk[0] % 2
            ptk[0] += 1
            norm_mod_transpose(cx, S, bufs, grp["src"][r0:r0 + m, :], grp["src_res"][r0 // 128], m,
                               B.hT[:, :, s_ * 128:s_ * 128 + m], k)
        if prompt:
            xq_new = B.xq[:, :, 3:3 + tb]
            xqv = None
        else:
            xqv = B.xq[:, :, 0:nseq * 7].rearrange("p c (s t) -> p c s t", t=7)
            op("sp", "dma_start", (), (), ) if False else None
            S.dma("sp", dsem("nc3"), B.nc3[:nseq * 3, :], grp["conv0"].rearrange("s t c -> (s t) c"), writes=(B.r_nc3,))
            for c in range(12):
                kq = c % 2
                op("pe", "transpose", (B.r_nc3,), (rps[kq],), out=ps[kq][:, :nseq * 3],
                   in_=B.nc3[:nseq * 3, c * 128:(c + 1) * 128], identity=cx.ident_f[:nseq * 3, :nseq * 3])
                op("act" if c % 2 else "dve", "activation" if c % 2 else "tensor_copy", (rps[kq],), (B.r_xq,),
                   out=xqv[:, c, :, 0:3], in_=ps[kq][:, :nseq * 3].rearrange("p (s t) -> p s t", t=3),
                   **({"func": AF.Copy} if c % 2 else {}))
        for c in range(12):
            kq = c % 2
            for kc in range(8):
                op("pe", "matmul", (B.r_win, B.r_hT), (rps[kq],), inc=(kc == 7), out=ps[kq][:, :tb],
                   lhsT=B.win[:, kc, c * 128:(c + 1) * 128], rhs=B.hT[:, kc, :tb], start=(kc == 0), stop=(kc == 7))
            if prompt:
                dst, srcp = B.xq[:, c, 3:3 + tb], ps[kq][:, :tb]
            else:
                dst, srcp = xqv[:, c, :, 3:7], ps[kq][:, :tb].rearrange("p (s t) -> p s t", t=Ts)
            if c % 2:
                op("act", "activation", (rps[kq],), (B.r_xq,), out=dst, in_=srcp, func=AF.Copy)
            else:
                op("dve", "tensor_copy", (rps[kq],), (B.r_xq,), out=dst, in_=srcp)
        for c in range(12):
            if prompt:
                ydst = B.yc[:, c, :tb]
                xin = [B.xq[:, c, j:j + tb] for j in range(4)]
            else:
                ydst = B.yc[:, c, :tb].rearrange("p (s t) -> p s t", t=Ts)
                xin = [xqv[:, c, :, j:j + Ts] for j in range(4)]
            eng = "dve"
            op(eng, "tensor_scalar", (B.r_xq, B.r_cw), (B.r_yc,), out=ydst, in0=xin[0], scalar1=B.cw[:, c, 0:1],
               scalar2=None, op0=ALU.mult)
            for j in range(1, 4):
                op(eng, "scalar_tensor_tensor", (B.r_xq, B.r_cw), (B.r_yc,), out=ydst, in0=xin[j],
                   scalar=B.cw[:, c, j:j + 1], in1=ydst, op0=ALU.mult, op1=ALU.add)
        for c in range(12):
            op("act", "activation", (B.r_yc,), (B.r_yc,), out=B.yc[:, c, :tb], in_=B.yc[:, c, :tb], func=AF.Silu)
        if prompt and b + 1 < nblk:
            op("pool", "tensor_copy", (B.r_xq,), (B.r_xq,), out=B.xq[:, :, 0:3], in_=B.xq[:, :, tb:tb + 3])
        for c in range(8):
            kq = c % 2
            op("pool", "tensor_tensor", (B.r_yc,), (B.r_sqb,), out=B.sqb[:, :tb], in0=B.yc[:, c, :tb], in1=B.yc[:, c, :tb],
               op=ALU.mult)
            op("pe", "matmul", (B.r_sqb,), (rps[kq],), out=ps[kq][:, :tb], lhsT=cx.bones[:, :], rhs=B.sqb[:, :tb],
               start=True, stop=True)
            op("act", "activation", (rps[kq],), (B.r_rinv,), out=B.rinv[:, :tb], in_=ps[kq][:, :tb], func=AF.Sqrt,
               bias=cx.eps_t[:, 0:1], scale=1.0)
            op("dve", "reciprocal", (B.r_rinv,), (B.r_rinv,), out=B.rinv[:, :tb], in_=B.rinv[:, :tb])
            if c < 4:
                op("dve", "scalar_tensor_tensor", (B.r_rinv, B.r_yc), (B.r_qkT,), out=B.qkT[:, c, :tb], in0=B.yc[:, c, :tb],
                   scalar=QK_SCALE, in1=B.rinv[:, :tb], op0=ALU.mult, op1=ALU.mult)
            else:
                op("dve", "tensor_tensor", (B.r_rinv, B.r_yc), (B.r_qkT,), out=B.qkT[:, c, :tb], in0=B.yc[:, c, :tb],
                   in1=B.rinv[:, :tb], op=ALU.mult)
        ntile = (tb + 63) // 64 if prompt else nseq
        if cx.stop <= 1:
            ntile = 0
        for ti in range(ntile):
            if prompt:
                c0 = ti * 64
                m = min(64, tb - c0)
                seq = 0
            else:
                c0 = ti * Ts
                m = Ts
                seq = ti
            tok0 = t0 + c0
            mixer_tile(cx, S, B, name, grp, P, m, c0, tok0, seq, prompt, dsem, op,
                       first=(tok0 == 0) if prompt else True, last=(tok0 + m == T) if prompt else True)


def mixer_tile(cx, S, B, name, grp, P, m, c0, tok0, seq, prompt, dsem, op, first, last):
    ps, rps = cx.ps, cx.rps
    sc, rsc = B.sc, B.r_sc
    for gi, (cA, cB) in enumerate(((1536, 2048), (2048, 2560), (2560, 2992))):
        kq = gi % 2
        for kc in range(8):
            op("pe", "matmul", (B.r_win, B.r_hT), (rps[kq],), inc=(kc == 7), out=ps[kq][:m, :cB - cA],
               lhsT=B.hT[:, kc, c0:c0 + m], rhs=B.win[:, kc, cA:cB], start=(kc == 0), stop=(kc == 7))
        if gi % 2:
            op("act", "activation", (rps[kq],), (B.r_tm,), out=B.tm[:m, cA - 1536:cB - 1536], in_=ps[kq][:m, :cB - cA],
               func=AF.Copy)
        else:
            op("dve", "tensor_copy", (rps[kq],), (B.r_tm,), out=B.tm[:m, cA - 1536:cB - 1536], in_=ps[kq][:m, :cB - cA])
    if last:
        for gi in range(3):
            kq = gi % 2
            for kc in range(8):
                op("pe", "matmul", (B.r_win, B.r_hT), (rps[kq],), inc=(kc == 7), out=ps[kq][:m, :512],
                   lhsT=B.hT[:, kc, c0:c0 + m], rhs=B.win[:, kc, gi * 512:(gi + 1) * 512], start=(kc == 0), stop=(kc == 7))
            op("dve", "tensor_copy", (rps[kq],), (B.r_nc3,), out=B.nc3[:m, gi * 512:(gi + 1) * 512], in_=ps[kq][:m, :512])
        S.dma("sp", dsem("nc3"), grp["new_conv"][seq, :, :], B.nc3[m - 3:m, :], reads=(B.r_nc3,), writes=(grp["r_out"],))
    if cx.stop <= 2:
        return
    zr, br, ar = B.tm[:m, 0:512], B.tm[:m, 512:520], B.tm[:m, 520:528]
    qraw, ckvr, krr_ = B.tm[:m, 528:1296], B.tm[:m, 1296:1424], B.tm[:m, 1424:1456]
    op("act", "activation", (B.r_tm,), (rsc,), out=sc[:m, 64:72], in_=br, func=AF.Exp, scale=-1.0)
    op("dve", "tensor_scalar", (rsc,), (rsc,), out=sc[:m, 64:72], in0=sc[:m, 64:72], scalar1=1.0, scalar2=None,
       op0=ALU.add)
    op("dve", "reciprocal", (rsc,), (rsc,), out=sc[:m, 0:8], in_=sc[:m, 64:72])
    op("act", "activation", (rsc,), (rsc,), out=sc[:m, 8:16], in_=sc[:m, 0:8], func=AF.Ln)
    op("dve", "tensor_tensor", (B.r_tm, B.r_g8), (rsc,), out=sc[:m, 64:72], in0=ar, in1=B.g8[:m, 8:16], op=ALU.add)
    op("act", "activation", (rsc,), (rsc,), out=sc[:m, 64:72], in_=sc[:m, 64:72], func=AF.Exp)
    op("act", "activation", (rsc,), (rsc,), out=sc[:m, 64:72], in_=sc[:m, 64:72], func=AF.Ln, bias=cx.one_t[:m, 0:1],
       scale=1.0)
    op("dve", "scalar_tensor_tensor", (rsc, B.r_g8), (rsc,), out=sc[:m, 16:24], in0=sc[:m, 64:72], scalar=-1.0,
       in1=B.g8[:m, 0:8], op0=ALU.mult, op1=ALU.mult)
    op("pe", "matmul", (rsc,), (rps[2],), out=ps[2][:m, 0:8], lhsT=cx.utri[:m, :m], rhs=sc[:m, 16:24], start=True, stop=True)
    op("dve", "tensor_copy", (rps[2],), (rsc,), out=sc[:m, 24:32], in_=ps[2][:m, 0:8])
    op("pe", "matmul", (rsc,), (rps[2],), out=ps[2][:m, 8:16], lhsT=cx.onesf[:m, :m], rhs=sc[:m, 16:24], start=True, stop=True)
    op("dve", "tensor_tensor", (rps[2], rsc), (rsc,), out=sc[:m, 48:56], in0=ps[2][:m, 8:16], in1=sc[:m, 24:32],
       op=ALU.subtract)
    op("act", "activation", (rsc,), (rsc,), out=sc[:m, 48:56], in_=sc[:m, 48:56], func=AF.Exp)
    op("act", "activation", (rsc,), (rsc,), out=sc[:m, 32:40], in_=sc[:m, 24:32], func=AF.Exp)
    op("dve", "tensor_tensor", (rsc,), (rsc,), out=sc[:m, 40:48], in0=sc[:m, 32:40], in1=sc[:m, 0:8], op=ALU.mult)
    op("dve", "tensor_scalar", (rsc,), (rsc,), out=sc[:m, 56:64], in0=sc[:m, 24:32], scalar1=-1.0, scalar2=None,
       op0=ALU.mult)
    op("dve", "tensor_tensor", (rsc,), (rsc,), out=sc[:m, 72:80], in0=sc[:m, 24:32], in1=sc[:m, 8:16], op=ALU.add)
    assert m <= 64
    g2 = sc[:m, 16:24].rearrange("p (c two) -> p two c", two=2)
    for par in range(2):
        op("pe", "matmul", (rsc,), (rps[2],), out=ps[2][par * 64:(par + 1) * 64, 16:20], lhsT=cx.onesf[:m, :64],
           rhs=g2[:, par, :], start=True, stop=True, skip_group_check=True)
    op("act", "activation", (rps[2],), (B.r_glb,), out=B.glb[:, 0:4], in_=ps[2][:, 16:20], func=AF.Exp)
    op("pe", "transpose", (rsc,), (rps[2],), out=ps[2][:8, 32:32 + m], in_=sc[:m, 24:32], identity=cx.ident_f[:m, :m])
    op("pe", "transpose", (rsc,), (rps[2],), out=ps[2][:8, 160:160 + m], in_=sc[:m, 72:80], identity=cx.ident_f[:m, :m])
    op("dve", "tensor_copy", (rps[2],), (B.r_rows,), out=B.rows[:, 0, :m], in_=ps[2][:8, 32:32 + m])
    op("dve", "tensor_copy", (rps[2],), (B.r_rows,), out=B.rows[:, 1, :m], in_=ps[2][:8, 160:160 + m])
    if cx.stop <= 3:
        return
    for c in range(4):
        op("pe", "transpose", (B.r_yc,), (rps[3],), out=ps[3][:m, c * 128:(c + 1) * 128], in_=B.yc[:, 8 + c, c0:c0 + m],
           identity=cx.ident_f[:, :], inc=(c == 3))
    op("dve", "tensor_tensor", (rps[3], rsc), (B.r_vtok,), out=B.vtok[:m, :].rearrange("p (h d) -> p h d", d=64),
       in0=ps[3][:m, :].rearrange("p (h d) -> p h d", d=64), in1=bc(sc[:m, 0:8], 64), op=ALU.mult)
    pkb = ps[2].bitcast(BF16)
    for c in range(4):
        op("pe", "transpose", (B.r_qkT,), (rps[2],), out=pkb[:m, 512 + c * 128:512 + (c + 1) * 128],
           in_=B.qkT[:, 4 + c, c0:c0 + m], identity=cx.ident_bf[:, :], inc=(c == 3))
    kh3 = pkb[:m, 512:1024].rearrange("p (h d) -> p h d", d=64)
    op("dve", "tensor_tensor", (rps[2], rsc), (B.r_ktok,), out=B.ktok[:m, :].rearrange("p (h d) -> p h d", d=64),
       in0=kh3, in1=bc(sc[:m, 40:48], 64), op=ALU.mult)
    op("dve", "tensor_tensor", (rps[2], rsc), (B.r_kdec,), out=B.kdec[:m, :].rearrange("p (h d) -> p h d", d=64),
       in0=kh3, in1=bc(sc[:m, 48:56], 64), op=ALU.mult)
    if cx.stop <= 4:
        return
    W4 = 4 * m
    mle, mlt, id4 = cx.mconst[m]
    for g in range(2):
        Ab = [getattr(B, "A%d%d" % (g, k)) for k in range(2)]
        Mb = [getattr(B, "M%d%d" % (g, k)) for k in range(2)]
        Pb = [getattr(B, "P%d%d" % (g, k)) for k in range(2)]
        rA = [getattr(B, "r_A%d%d" % (g, k)) for k in range(2)]
        rM = [getattr(B, "r_M%d%d" % (g, k)) for k in range(2)]
        rP = [getattr(B, "r_P%d%d" % (g, k)) for k in range(2)]
        pP, rpP = ps[6 + g], rps[6 + g]
        for kind, dec, rdec, mask in ((0, B.decq, B.r_decq, mle), (1, B.decb, B.r_decb, mlt)):
            op("pe", "matmul", (), (rps[3],), inc=False, out=ps[3][:m, 0:W4], lhsT=cx.ident_f[:m, :m], rhs=mask[:m, 0:W4],
               start=True, stop=False)
            for hh in range(4):
                h = 2 * hh + g
                op("pe", "matmul", (B.r_rows,), (rps[3],), inc=(hh == 3), out=ps[3][:m, hh * m:(hh + 1) * m],
                   lhsT=cx.ohsel[:, h, :m], rhs=B.rows[:, kind, :m], start=False, stop=(hh == 3))
            for hh in range(4):
                h = 2 * hh + g
                op("act", "activation", (rps[3], rsc), (rdec,), out=dec[:m, hh * m:(hh + 1) * m],
                   in_=ps[3][:m, hh * m:(hh + 1) * m], func=AF.Exp, bias=sc[:m, 56 + h:57 + h], scale=1.0)
        if cx.stop <= 4.2:
            continue
        for hh in range(4):
            h = 2 * hh + g
            c, po = h // 2, (h % 2) * 64
            op("pe", "matmul", (B.r_qkT,), (rps[4],), inc=(hh == 3), out=ps[4][:m, hh * m:(hh + 1) * m],
               lhsT=B.qkT[po:po + 64, 4 + c, c0:c0 + m], rhs=B.qkT[po:po + 64, 4 + c, c0:c0 + m], start=True, stop=True,
               skip_group_check=True)
        op("dve", "tensor_tensor", (rps[4], B.r_decb), (rA[0],), out=Ab[0][:m, 0:W4], in0=ps[4][:m, 0:W4],
           in1=B.decb[:m, 0:W4], op=ALU.mult)
        for hh in range(4):
            h = 2 * hh + g
            c, po = h // 2, (h % 2) * 64
            op("pe", "matmul", (B.r_qkT,), (rps[5],), inc=(hh == 3), out=ps[5][:m, hh * m:(hh + 1) * m],
               lhsT=B.qkT[po:po + 64, 4 + c, c0:c0 + m], rhs=B.qkT[po:po + 64, c, c0:c0 + m], start=True, stop=True,
               skip_group_check=True)
        op("dve", "tensor_tensor", (rps[5], B.r_decq), (B.r_qk,), out=B.qk[:m, 0:8 * m].rearrange("p (c two i) -> p two c i", two=2, i=m)[:, g],
           in0=ps[5][:m, 0:W4].rearrange("p (c i) -> p c i", i=m),
           in1=B.decq[:m, 0:W4].rearrange("p (c i) -> p c i", i=m), op=ALU.mult)
        if cx.stop <= 4.4:
            continue
        for hh in range(4):
            op("pe", "transpose", (rA[0],), (rps[4],), inc=(hh == 3), out=ps[4][:m, hh * m:(hh + 1) * m],
               in_=Ab[0][:m, hh * m:(hh + 1) * m], identity=cx.ident_f[:m, :m])
        op("act", "activation", (rps[4],), (rM[0],), out=Mb[0][:m, 0:W4], in_=ps[4][:m, 0:W4], func=AF.Copy)
        op("pe", "matmul", (), (rpP,), inc=False, out=pP[:m, 0:W4], lhsT=cx.ident_f[:m, :m], rhs=id4[:m, 0:W4],
           start=True, stop=False, skip_group_check=True)
        for hh in range(4):
            op("pe", "matmul", (rA[0],), (rpP,), inc=(hh == 3), out=pP[:m, hh * m:(hh + 1) * m], lhsT=cx.nident_f[:m, :m],
               rhs=Ab[0][:m, hh * m:(hh + 1) * m], start=False, stop=False, skip_group_check=True)
        op("act", "activation", (rpP,), (rP[0],), out=Pb[0][:m, 0:W4], in_=pP[:m, 0:W4], func=AF.Copy)
        if cx.stop <= 4.6:
            continue
        nlev = NLEV if m > 64 else (5 if m > 32 else (4 if m > 16 else (3 if m > 8 else (2 if m > 4 else 1))))
        for lv in range(1, nlev + 1):
            a, bq = (lv - 1) % 2, lv % 2
            for hh in range(4):
                sl = slice(hh * m, (hh + 1) * m)
                op("pe", "matmul", (rA[a], rM[a]), (rps[4],), inc=(hh == 3), out=ps[4][:m, sl], lhsT=Ab[a][:m, sl],
                   rhs=Mb[a][:m, sl], start=True, stop=True, skip_group_check=True)
            op("act", "activation", (rps[4],), (rM[bq],), out=Mb[bq][:m, 0:W4], in_=ps[4][:m, 0:W4], func=AF.Copy)
            if lv < nlev:
                for hh in range(4):
                    sl = slice(hh * m, (hh + 1) * m)
                    op("pe", "matmul", (rA[a], rM[a]), (rps[5],), inc=(hh == 3), out=ps[5][:m, sl], lhsT=Mb[a][:m, sl],
                       rhs=Ab[a][:m, sl], start=True, stop=True, skip_group_check=True)
                op("dve", "tensor_copy", (rps[5],), (rA[bq],), out=Ab[bq][:m, 0:W4], in_=ps[5][:m, 0:W4])
            for hh in range(4):
                sl = slice(hh * m, (hh + 1) * m)
                op("pe", "matmul", (rM[bq], rP[a]), (rpP,), inc=(hh == 3), out=pP[:m, sl], lhsT=Mb[bq][:m, sl],
                   rhs=Pb[a][:m, sl], start=False, stop=(lv == nlev), skip_group_check=True)
            if lv % 2:
                op("dve", "tensor_copy", (rpP,), (rP[bq],), out=Pb[bq][:m, 0:W4], in_=pP[:m, 0:W4])
            else:
                op("act", "activation", (rpP,), (rP[bq],), out=Pb[bq][:m, 0:W4], in_=pP[:m, 0:W4], func=AF.Copy)
        if cx.stop <= 4.8:
            continue
        Pf, rPf = getattr(B, "Pf%d" % g), getattr(B, "r_Pf%d" % g)
        op("dve", "tensor_copy", (rpP,), (rPf,), out=Pf[:m, 0:W4], in_=pP[:m, 0:W4])
        if cx.stop <= 4.85:
            continue
        for hh in range(4):
            h = 2 * hh + g
            sl = slice(hh * m, (hh + 1) * m)
            op("pe", "matmul", (rPf, B.r_vtok), (rps[0],), inc=(hh == 3), out=ps[0][:m, h * 64:(h + 1) * 64], lhsT=Pf[:m, sl],
               rhs=B.vtok[:m, h * 64:(h + 1) * 64], start=True, stop=True, skip_group_check=True)
        if cx.stop <= 4.9:
            continue
        for hh in range(4):
            h = 2 * hh + g
            sl = slice(hh * m, (hh + 1) * m)
            po = (h % 2) * 64
            op("pe", "matmul", (rPf, B.r_ktok), (rps[1],), inc=(hh == 3),
               out=ps[1][po:po + 64, hh * 128:hh * 128 + m],
               lhsT=B.ktok[:m, h * 64:(h + 1) * 64], rhs=Pf[:m, sl], start=True, stop=True, skip_group_check=True)
        op("act", "activation", (rps[1],), (B.r_wT,), out=B.wT[g * 64:(g + 1) * 64, :, :m],
           in_=ps[1][g * 64:(g + 1) * 64, :].rearrange("p (h i) -> p h i", i=128)[:, :, :m], func=AF.Identity)
    if cx.stop <= 5:
        return
    op("dve", "tensor_copy", (rps[0],), (B.r_u,), out=B.u[:m, :], in_=ps[0][:m, :])
    def sdiag(t_, par):
        return t_[par * 64:(par + 1) * 64, :].rearrange("k (c x) -> k c x", x=128)[:, :, par * 64:(par + 1) * 64]

    if not prompt:
        op("dve", "memset", (), (B.r_S32,), ap=B.S32[:, :], constant=0.0)
        for par in range(2):
            S.dma("sp", dsem("s0"), sdiag(B.S32, par), grp["s0"][seq].rearrange("(c par) k v -> par k c v", par=2)[par],
                  writes=(B.r_S32,))
        op("act", "activation", (B.r_S32,), (B.r_Sbf,), out=B.Sbf[:, :], in_=B.S32[:, :], func=AF.Copy)
    for c in range(4):
        op("pe", "matmul", (B.r_wT, B.r_Sbf), (rps[0],), inc=(c == 3), out=ps[0][:m, c * 128:(c + 1) * 128],
           lhsT=B.wT[:, c, :m], rhs=B.Sbf[:, c * 128:(c + 1) * 128], start=True, stop=True, skip_group_check=True)
    op("dve", "tensor_tensor", (rps[0], B.r_u), (B.r_vnew,), out=B.vnew[:m, :], in0=B.u[:m, :], in1=ps[0][:m, :],
       op=ALU.subtract)
    for c in range(4):
        op("pe", "matmul", (B.r_qkT, B.r_Sbf), (rps[1],), inc=(c == 3), out=ps[1][:m, c * 128:(c + 1) * 128],
           lhsT=B.qkT[:, c, c0:c0 + m], rhs=B.Sbf[:, c * 128:(c + 1) * 128], start=True, stop=True, skip_group_check=True)
    op("dve", "tensor_tensor", (rps[1], rsc), (B.r_o1,), out=B.o1[:m, :].rearrange("p (h d) -> p h d", d=64),
       in0=ps[1][:m, :].rearrange("p (h d) -> p h d", d=64), in1=bc(sc[:m, 32:40], 64), op=ALU.mult)
    for h in range(8):
        op("pe", "matmul", (B.r_qk, B.r_vnew), (rps[0],), inc=(h == 7), out=ps[0][:m, h * 64:(h + 1) * 64],
           lhsT=B.qk[:m, h * m:(h + 1) * m], rhs=B.vnew[:m, h * 64:(h + 1) * 64], start=True, stop=True,
           skip_group_check=True)
    op("dve", "tensor_tensor", (rps[0], B.r_o1), (B.r_osb,), out=B.osb[:m, :], in0=ps[0][:m, :], in1=B.o1[:m, :], op=ALU.add)
    for c in range(4):
        op("pe", "matmul", (B.r_kdec, B.r_vnew), (rps[3],), inc=(c == 3), out=ps[3][:, c * 128:(c + 1) * 128],
           lhsT=B.kdec[:m, c * 128:(c + 1) * 128], rhs=B.vnew[:m, c * 128:(c + 1) * 128], start=True, stop=True,
           skip_group_check=True)
    op("dve", "tensor_tensor", (rps[3],), (B.r_decq,), out=B.decq[:, :].rearrange("p (c x) -> p c x", x=128),
       in0=ps[3][:, :].rearrange("p (c x) -> p c x", x=128), in1=bc_mid(cx.bones[:, :], 4), op=ALU.mult)
    op("dve", "tensor_tensor", (B.r_glb,), (B.r_S32,), out=B.S32[:, :].rearrange("p (c x) -> p c x", x=128),
       in0=B.S32[:, :].rearrange("p (c x) -> p c x", x=128), in1=bc(B.glb[:, 0:4], 128), op=ALU.mult)
    op("pool", "tensor_tensor", (B.r_decq,), (B.r_S32,), out=B.S32[:, :], in0=B.S32[:, :], in1=B.decq[:, :], op=ALU.add)
    if last:
        for par in range(2):
            S.dma("sp", dsem("s0"), grp["new_gdn"][seq].rearrange("(c par) k v -> par k c v", par=2)[par],
                  sdiag(B.S32, par), reads=(B.r_S32,), writes=(grp["r_out"],))
    else:
        op("act", "activation", (B.r_S32,), (B.r_Sbf,), out=B.Sbf[:, :], in_=B.S32[:, :], func=AF.Copy)
    if cx.stop <= 6:
        return
    op("pool", "tensor_tensor", (B.r_osb,), (B.r_o1,), out=B.o1[:m, :], in0=B.osb[:m, :], in1=B.osb[:m, :], op=ALU.mult)
    op("dve", "tensor_reduce", (B.r_o1,), (B.r_ost,), out=B.ost[:m, 0:8], in_=B.o1[:m, :].rearrange("p (h d) -> p h d", d=64),
       axis=AX.X, op=ALU.add)
    op("act", "activation", (B.r_ost,), (B.r_ost,), out=B.ost[:m, 8:16], in_=B.ost[:m, 0:8], func=AF.Sqrt, scale=1.0 / 64,
       bias=cx.eps_t[:m, 0:1])
    op("dve", "reciprocal", (B.r_ost,), (B.r_ost,), out=B.ost[:m, 8:16], in_=B.ost[:m, 8:16])
    op("act", "activation", (B.r_tm,), (B.r_zs,), out=B.zs[:m, :], in_=zr, func=AF.Silu)
    op("dve", "tensor_tensor", (B.r_osb, B.r_ost), (B.r_osb,), out=B.osb[:m, :].rearrange("p (h d) -> p h d", d=64),
       in0=B.osb[:m, :].rearrange("p (h d) -> p h d", d=64), in1=bc(B.ost[:m, 8:16], 64), op=ALU.mult)
    op("pool", "tensor_tensor", (B.r_osb, B.r_gng), (B.r_osb,), out=B.osb[:m, :].rearrange("p (h d) -> p h d", d=64),
       in0=B.osb[:m, :].rearrange("p (h d) -> p h d", d=64), in1=bc_mid(B.gng[:m, :], 8), op=ALU.mult)
    op("dve", "tensor_tensor", (B.r_osb, B.r_zs), (B.r_gout,), out=B.gout[:m, :], in0=B.osb[:m, :], in1=B.zs[:m, :], op=ALU.mult)
    S.dma("sp", dsem("gout"), grp["mix"][tok0:tok0 + m, 0:512], B.gout[:m, :], reads=(B.r_gout,), writes=(grp["r_mix"],))
    if cx.stop <= 7:
        return
    S.dma("sp", dsem("cs"), B.cs[:m, :], grp["cs"][(tok0 if prompt else 0):(tok0 if prompt else 0) + m, :], writes=(B.r_cs,))
    q3 = qraw.rearrange("p (h d) -> p h d", d=96)
    mq3 = B.mq[:m, :].rearrange("p (h d) -> p h d", d=96)
    op("pool", "tensor_tensor", (B.r_tm,), (B.r_mq,), out=B.mq[:m, :], in0=qraw, in1=qraw, op=ALU.mult)
    op("dve", "tensor_reduce", (B.r_mq,), (B.r_mqs,), out=B.mqs[:m, 0:8], in_=mq3[:, :, 0:64], axis=AX.X, op=ALU.add)
    op("dve", "tensor_reduce", (B.r_mq,), (B.r_mqs,), out=B.mqs[:m, 8:16], in_=mq3[:, :, 64:96], axis=AX.X, op=ALU.add)
    op("act", "activation", (B.r_mqs,), (B.r_mqs,), out=B.mqs[:m, 16:24], in_=B.mqs[:m, 0:8], func=AF.Sqrt, scale=1.0 / 64,
       bias=cx.eps_t[:m, 0:1])
    op("act", "activation", (B.r_mqs,), (B.r_mqs,), out=B.mqs[:m, 24:32], in_=B.mqs[:m, 8:16], func=AF.Sqrt, scale=1.0 / 32,
       bias=cx.eps_t[:m, 0:1])
    op("dve", "reciprocal", (B.r_mqs,), (B.r_mqs,), out=B.mqs[:m, 16:32], in_=B.mqs[:m, 16:32])
    op("dve", "tensor_tensor", (B.r_tm, B.r_mqs), (B.r_mq,), out=mq3[:, :, 0:64], in0=q3[:, :, 0:64], in1=bc(B.mqs[:m, 16:24], 64),
       op=ALU.mult)
    op("pool", "tensor_tensor", (B.r_mq, B.r_qng), (B.r_Qb,), out=B.Qb[:m, :].rearrange("p (h d) -> p h d", d=96)[:, :, 0:64],
       in0=mq3[:, :, 0:64], in1=bc_mid(B.qng[:m, :], 8), op=ALU.mult)
    op("dve", "tensor_tensor", (B.r_tm, B.r_mqs), (B.r_mq,), out=mq3[:, :, 64:96], in0=q3[:, :, 64:96], in1=bc(B.mqs[:m, 24:32], 32),
       op=ALU.mult)
    op("pool", "tensor_tensor", (B.r_mq, B.r_qrg), (B.r_mq,), out=mq3[:, :, 64:96], in0=mq3[:, :, 64:96],
       in1=bc_mid(B.qrg[:m, :], 8), op=ALU.mult)
    cosb, sinb = bc_mid(B.cs[:m, 0:16], 8), bc_mid(B.cs[:m, 16:32], 8)
    x1, x2 = mq3[:, :, 64:80], mq3[:, :, 80:96]
    Q3 = B.Qb[:m, :].rearrange("p (h d) -> p h d", d=96)
    op("dve", "tensor_tensor", (B.r_mq, B.r_cs), (B.r_qrr,), out=B.qrr[:m, :, 0:16], in0=x1, in1=cosb, op=ALU.mult)
    op("dve", "tensor_tensor", (B.r_mq, B.r_cs), (B.r_qrr,), out=B.qrr[:m, :, 16:32], in0=x2, in1=sinb, op=ALU.mult)
    op("dve", "tensor_tensor", (B.r_qrr,), (B.r_Qb,), out=Q3[:, :, 64:80], in0=B.qrr[:m, :, 0:16], in1=B.qrr[:m, :, 16:32],
       op=ALU.subtract)
    op("dve", "tensor_tensor", (B.r_mq, B.r_cs), (B.r_qrr,), out=B.qrr[:m, :, 0:16], in0=x1, in1=sinb, op=ALU.mult)
    op("dve", "tensor_tensor", (B.r_mq, B.r_cs), (B.r_qrr,), out=B.qrr[:m, :, 16:32], in0=x2, in1=cosb, op=ALU.mult)
    op("dve", "tensor_tensor", (B.r_qrr,), (B.r_Qb,), out=Q3[:, :, 80:96], in0=B.qrr[:m, :, 0:16], in1=B.qrr[:m, :, 16:32],
       op=ALU.add)
    S.dma("sp", dsem("Qb"), grp["Q"][tok0:tok0 + m, :], B.Qb[:m, :], reads=(B.r_Qb,), writes=(grp["r_Q"],))
    op("pool", "tensor_tensor", (B.r_tm,), (B.r_cn,), out=B.cn[:m, :], in0=ckvr, in1=ckvr, op=ALU.mult)
    op("dve", "tensor_reduce", (B.r_cn,), (B.r_mqs,), out=B.mqs[:m, 0:1], in_=B.cn[:m, :], axis=AX.X, op=ALU.add)
    op("act", "activation", (B.r_mqs,), (B.r_mqs,), out=B.mqs[:m, 1:2], in_=B.mqs[:m, 0:1], func=AF.Sqrt, scale=1.0 / 128,
       bias=cx.eps_t[:m, 0:1])
    op("dve", "reciprocal", (B.r_mqs,), (B.r_mqs,), out=B.mqs[:m, 1:2], in_=B.mqs[:m, 1:2])
    op("dve", "scalar_tensor_tensor", (B.r_tm, B.r_mqs, B.r_ckg), (B.r_cn,), out=B.cn[:m, :], in0=ckvr, scalar=B.mqs[:m, 1:2],
       in1=B.ckg[:m, :], op0=ALU.mult, op1=ALU.mult)
    S.dma("sp", dsem("cn"), grp["new_ckv"][tok0:tok0 + m, :], B.cn[:m, :], reads=(B.r_cn,), writes=(grp["r_ckv"],))
    op("pool", "tensor_tensor", (B.r_tm,), (B.r_krn,), out=B.krn[:m, :], in0=krr_, in1=krr_, op=ALU.mult)
    op("dve", "tensor_reduce", (B.r_krn,), (B.r_mqs,), out=B.mqs[:m, 2:3], in_=B.krn[:m, :], axis=AX.X, op=ALU.add)
    op("act", "activation", (B.r_mqs,), (B.r_mqs,), out=B.mqs[:m, 3:4], in_=B.mqs[:m, 2:3], func=AF.Sqrt, scale=1.0 / 32,
       bias=cx.eps_t[:m, 0:1])
    op("dve", "reciprocal", (B.r_mqs,), (B.r_mqs,), out=B.mqs[:m, 3:4], in_=B.mqs[:m, 3:4])
    op("dve", "scalar_tensor_tensor", (B.r_tm, B.r_mqs, B.r_krg), (B.r_krn,), out=B.krn[:m, :], in0=krr_, scalar=B.mqs[:m, 3:4],
       in1=B.krg[:m, :], op0=ALU.mult, op1=ALU.mult)
    k1, k2, cs1, sn1 = B.krn[:m, 0:16], B.krn[:m, 16:32], B.cs[:m, 0:16], B.cs[:m, 16:32]
    op("dve", "tensor_tensor", (B.r_krn, B.r_cs), (B.r_krr,), out=B.krr[:m, 0:16], in0=k1, in1=cs1, op=ALU.mult)
    op("dve", "tensor_tensor", (B.r_krn, B.r_cs), (B.r_krr,), out=B.krr[:m, 16:32], in0=k2, in1=sn1, op=ALU.mult)
    op("dve", "tensor_tensor", (B.r_krr,), (B.r_qrr,), out=B.qrr[:m, 0, 0:16], in0=B.krr[:m, 0:16], in1=B.krr[:m, 16:32],
       op=ALU.subtract)
    op("dve", "tensor_tensor", (B.r_krn, B.r_cs), (B.r_krr,), out=B.krr[:m, 0:16], in0=k1, in1=sn1, op=ALU.mult)
    op("dve", "tensor_tensor", (B.r_krn, B.r_cs), (B.r_krr,), out=B.krr[:m, 16:32], in0=k2, in1=cs1, op=ALU.mult)
    op("dve", "tensor_tensor", (B.r_krr,), (B.r_qrr,), out=B.qrr[:m, 0, 16:32], in0=B.krr[:m, 0:16], in1=B.krr[:m, 16:32],
       op=ALU.add)
    S.dma("sp", dsem("kr"), grp["new_kr"][tok0:tok0 + m, :], B.qrr[:m, 0, :], reads=(B.r_qrr,), writes=(grp["r_kr"],))


def phase_mixer(cx, S, groups, P):
    with ExitStack() as st:
        B = mixer_alloc(cx, S, st, "mx", P)
        mc = sb(cx, st, "mx_cst", [128, CST_COLS - 128], F32)
        r_mc = Res("mxcst")
        S.dma("sp", S.dsem("mxcst"), mc[:, :], cx.cst[:, 128:CST_COLS], writes=(r_mc,))
        cx.utri = mc[:, 0:128]
        cx.onesf = mc[:, 128:256]
        cx.ohsel = mc[0:8, 384:1408].rearrange("p (h i) -> p h i", i=128)
        cx.mconst = {}
        off = 1408
        for m_ in (64, 4):
            cx.mconst[m_] = (mc[:, off + 4 * m_:off + 8 * m_], mc[:, off + 8 * m_:off + 12 * m_], mc[:, off:off + 4 * m_])
            off += 12 * m_
        idf = sb(cx, st, "mx_idf", [128, 128], F32)
        S.op("dve", "tensor_scalar", dict(out=idf[:, :], in0=cx.ident_f, scalar1=-1.0, scalar2=None, op0=ALU.mult),
             writes=(r_mc,))
        cx.nident_f = idf[:, :]
        dc = S.dsem("mxc")
        r_c = Res("mxconst")
        S.dma("sp", dc, B.cw[:, :, :], P["conv_w_fm"], writes=(B.r_cw,))
        S.dma("sp", dc, B.g8[:, 0:8], row_bcast(P["a_log"], 128), writes=(B.r_g8,))
        S.dma("sp", dc, B.g8[:, 8:16], row_bcast(P["dt_bias"], 128), writes=(B.r_g8,))
        S.dma("sp", dc, B.gng[:, :], row_bcast(P["gdn_norm"], 128), writes=(B.r_gng,))
        S.dma("sp", dc, B.qng[:, :], row_bcast(P["qn_g"], 128), writes=(B.r_qng,))
        S.dma("sp", dc, B.kng[:, :], row_bcast(P["kn_g"], 128), writes=(B.r_kng,))
        S.dma("sp", dc, B.qrg[:, :], row_bcast(P["qr_g"], 128), writes=(B.r_qrg,))
        S.dma("sp", dc, B.ckg[:, :], row_bcast(P["ckv_g"], 128), writes=(B.r_ckg,))
        S.dma("sp", dc, B.krg[:, :], row_bcast(P["kr_g"], 128), writes=(B.r_krg,))
        S.barrier()
        S.op("act", "activation", dict(out=B.g8[:, 0:8], in_=B.g8[:, 0:8], func=AF.Exp), writes=(B.r_g8,))
        S.op("dve", "scalar_tensor_tensor", dict(out=B.qng[:, :], in0=B.qng[:, :], scalar=ATT_SCALE, in1=B.kng[:, :],
                                                 op0=ALU.mult, op1=ALU.mult), writes=(B.r_qng,))
        S.op("dve", "tensor_scalar", dict(out=B.qrg[:, :], in0=B.qrg[:, :], scalar1=ATT_SCALE, scalar2=None, op0=ALU.mult),
             writes=(B.r_qrg,))
        S.barrier()
        for gi, grp in enumerate(groups):
            mixer_group(cx, S, B, "mx%d" % gi, grp, P)
            S.barrier()
            S.emit()


def phase_wout(cx, S, groups, w_out_ap):
    with ExitStack() as st:
        wo = sb(cx, st, "wo_w", [128, 8, D], BF16)
        r_wo = Res()
        load_weight_bf16(cx, S, "wow", wo, r_wo, w_out_ap, 8, D, D)
        Gt = sb(cx, st, "wo_Gt", [128, D], F32)
        mx = [sb(cx, st, "wo_mx%d" % i, [128, D], BF16) for i in range(2)]
        mT = sb(cx, st, "wo_mT", [128, 8, 128], BF16)
        xr = [sb(cx, st, "wo_xr%d" % i, [128, D], F32) for i in range(2)]
        tmp = sb(cx, st, "wo_tmp", [128, 512], F32)
        r_Gt, r_mT, r_tmp = Res(), Res(), Res()
        r_mx, r_xr = [Res(), Res()], [Res(), Res()]
        d_g = S.dsem("wo_g")
        d_mx = [S.dsem("wo_mx0"), S.dsem("wo_mx1")]
        d_xr = [S.dsem("wo_xr0"), S.dsem("wo_xr1")]
        d_xst = [S.dsem("wo_xst0"), S.dsem("wo_xst1")]
        i = 0
        for g in groups:
            n, T = g["n"], g["T"]
            S.dma("sp", d_g, Gt[:n, :], g["mods"][5, :n, :], writes=(r_Gt,))
            for r0 in range(0, T, 128):
                m = min(128, T - r0)
                k = i % 2
                i += 1
                S.dma("sp", d_mx[k], mx[k][:m, :], g["mix"][r0:r0 + m, :], reads=(g["r_mix"],), writes=(r_mx[k],))
                S.dma("sp", d_xr[k], xr[k][:m, :], g["src"][r0:r0 + m, :], reads=(g["src_res"][r0 // 128],),
                      writes=(r_xr[k],))
                ptb = cx.ps[6 + k].bitcast(BF16)
                for c in range(8):
                    S.op("pe", "transpose", dict(out=ptb[:, c * 128:c * 128 + m], in_=mx[k][:m, c * 128:(c + 1) * 128],
                                                 identity=cx.ident_bf[:m, :m]), reads=(r_mx[k],), writes=(cx.rps[6 + k],),
                         inc=(c == 7))
                S.op("act", "activation", dict(out=mT[:, :, :m], in_=ptb.rearrange("p (c t) -> p c t", c=8)[:, :, :m],
                                               func=AF.Copy), reads=(cx.rps[6 + k],), writes=(r_mT,))
                for half in range(2):
                    pk = (i * 2 + half) % 4
                    for c in range(8):
                        S.op("pe", "matmul", dict(out=cx.ps[pk][:m, :], lhsT=mT[:, c, :m],
                                                  rhs=wo[:, c, half * 512:(half + 1) * 512], start=(c == 0), stop=(c == 7)),
                             reads=(r_mT, r_wo), writes=(cx.rps[pk],), inc=(c == 7))
                    S.op("dve", "tensor_tensor", dict(out=tmp[:m, :], in0=cx.ps[pk][:m, :],
                                                      in1=Gt[:m, half * 512:(half + 1) * 512], op=ALU.mult),
                         reads=(cx.rps[pk], r_Gt), writes=(r_tmp,))
                    S.op("pool", "tensor_tensor", dict(out=xr[k][:m, half * 512:(half + 1) * 512],
                                                       in0=xr[k][:m, half * 512:(half + 1) * 512], in1=tmp[:m, :],
                                                       op=ALU.add), reads=(r_tmp,), writes=(r_xr[k],))
                S.dma("pool", d_xst[k], g["dst"][r0:r0 + m, :], xr[k][:m, :], reads=(r_xr[k],),
                      writes=(g["dst_res"][r0 // 128],))
        S.barrier()
        S.emit()


class AttBufs:
    pass


def att_alloc(cx, S, st, name, P):
    A = AttBufs()

    def t(nm, shape, dt):
        tt_ = sb(cx, st, name + nm, shape, dt)
        setattr(A, nm, tt_)
        setattr(A, "r_" + nm, Res(nm))
        return tt_

    t("wuk", [128, 512], BF16)
    t("wuv", [128, 512], BF16)
    t("wst", [128, 512], F32)
    d = S.dsem(name + "w")
    S.dma("sp", d, A.wst[:, :], P["w_uk"], writes=(A.r_wst,))
    S.op("dve", "tensor_copy", dict(out=A.wuk[:, :], in_=A.wst[:, :]), reads=(A.r_wst,), writes=(A.r_wuk,))
    S.dma("sp", d, A.wst[:, :], P["w_uv"], writes=(A.r_wst,))
    S.op("dve", "tensor_copy", dict(out=A.wuv[:, :], in_=A.wst[:, :]), reads=(A.r_wst,), writes=(A.r_wuv,))
    for k_ in range(2):
        t("cTb%d" % k_, [128, 128], BF16)
        t("sq%d" % k_, [128, 512], F32)
        t("kst%d" % k_, [128, 16], F32)
        t("Kf%d" % k_, [128, 8, 96], BF16)
        t("KT%d" % k_, [96, 8, 128], BF16)
    t("QT", [96, 8, 128], BF16)
    t("Qb", [128, 768], BF16)
    t("PT", [128, 512], BF16)
    t("ctx", [128, 8, 128], BF16)
    t("rden", [128, 8], F32)
    t("ctxT", [128, 8, 128], BF16)
    t("mo", [128, 512], BF16)
    t("m01", [128, 128], BF16)
    A.d = {}
    return A


def att_kside(cx, S, A, c_blk, kr_blk, r_src, n, KT_dst, r_KT, k=0, stage=None):
    ps, rps = cx.ps, cx.rps
    cTb, sq, kst, Kf = (getattr(A, nm + str(k)) for nm in ("cTb", "sq", "kst", "Kf"))
    r_cTb, r_sq, r_kst, r_Kf = (getattr(A, "r_" + nm + str(k)) for nm in ("cTb", "sq", "kst", "Kf"))
    p0, p1, p2 = (0, 1, 2) if k == 0 else (6, 7, 3)

    def op(eng, meth, reads, writes, inc=True, **kw):
        S.op(eng, meth, kw, reads=reads, writes=writes, inc=inc)

    if stage == "B":
        ptb = ps[p2].bitcast(BF16)
        for h in range(8):
            op("pe", "transpose", (r_Kf,), (rps[p2],), inc=(h == 7), out=ptb[0:96, h * 128:h * 128 + n], in_=Kf[:n, h, :],
               identity=cx.ident_bf[:n, :n])
        op("act", "activation", (rps[p2],), (r_KT,), out=KT_dst, in_=ptb[0:96, :].rearrange("p (h t) -> p h t", t=128)[:, :, :n],
           func=AF.Copy)
        return
    op("pe", "transpose", (r_src,), (rps[p0],), out=ps[p0][:, :n], in_=c_blk, identity=cx.ident_f[:n, :n])
    op("act", "activation", (rps[p0],), (r_cTb,), out=cTb[:, :n], in_=ps[p0][:, :n], func=AF.Identity)
    op("pe", "matmul", (r_cTb, A.r_wuk), (rps[p1],), out=ps[p1][:n, :], lhsT=cTb[:, :n], rhs=A.wuk[:, :], start=True, stop=True)
    op("act", "activation", (rps[p1],), (r_sq,), out=sq[:n, :], in_=ps[p1][:n, :], func=AF.Square)
    op("dve", "tensor_reduce", (r_sq,), (r_kst,), out=kst[:n, 0:8], in_=sq[:n, :].rearrange("p (h d) -> p h d", d=64),
       axis=AX.X, op=ALU.add)
    op("act", "activation", (r_kst,), (r_kst,), out=kst[:n, 8:16], in_=kst[:n, 0:8], func=AF.Sqrt, scale=1.0 / 64,
       bias=cx.eps_t[:n, 0:1])
    op("dve", "reciprocal", (r_kst,), (r_kst,), out=kst[:n, 8:16], in_=kst[:n, 8:16])
    op("dve", "tensor_tensor", (rps[p1], r_kst), (r_Kf,), out=Kf[:n, :, 0:64],
       in0=ps[p1][:n, :].rearrange("p (h d) -> p h d", d=64), in1=bc(kst[:n, 8:16], 64), op=ALU.mult)
    op("pool", "tensor_copy", (r_src,), (r_Kf,), out=Kf[:n, :, 64:96], in_=bc_mid(kr_blk, 8))
    if stage == "A":
        return
    ptb = ps[p2].bitcast(BF16)
    for h in range(8):
        op("pe", "transpose", (r_Kf,), (rps[p2],), inc=(h == 7), out=ptb[0:96, h * 128:h * 128 + n], in_=Kf[:n, h, :],
           identity=cx.ident_bf[:n, :n])
    op("act", "activation", (rps[p2],), (r_KT,), out=KT_dst, in_=ptb[0:96, :].rearrange("p (h t) -> p h t", t=128)[:, :, :n],
       func=AF.Copy)


def att_out(cx, S, A, nq, r_ctx_in, mix_dst, r_mix, dsem_):
    ps, rps = cx.ps, cx.rps
    ptb = ps[2].bitcast(BF16)
    for h in range(8):
        S.op("pe", "transpose", dict(out=ptb[:, h * 128:h * 128 + nq], in_=A.ctx[:nq, h, :], identity=cx.ident_bf[:nq, :nq]),
             reads=(A.r_ctx,), writes=(rps[2],), inc=(h == 7))
    S.op("act", "activation", dict(out=A.ctxT[:, :, :nq], in_=ptb.rearrange("p (h t) -> p h t", t=128)[:, :, :nq], func=AF.Copy),
         reads=(rps[2],), writes=(A.r_ctxT,))
    for h in range(8):
        S.op("pe", "matmul", dict(out=ps[3][:nq, h * 64:(h + 1) * 64], lhsT=A.ctxT[:, h, :nq], rhs=A.wuv[:, h * 64:(h + 1) * 64],
                                  start=True, stop=True, skip_group_check=True), reads=(A.r_ctxT, A.r_wuv), writes=(rps[3],),
             inc=(h == 7))
    S.op("dve", "tensor_copy", dict(out=A.mo[:nq, :], in_=ps[3][:nq, :]), reads=(rps[3],), writes=(A.r_mo,))
    S.dma("sp", dsem_, mix_dst, A.mo[:nq, :], reads=(A.r_mo,), writes=(r_mix,))


def phase_attn_prompt(cx, S, grp, P):
    T = grp["T"]
    nb = T // 128
    with ExitStack() as st:
        A = att_alloc(cx, S, st, "ap", P)
        KTall = sb(cx, st, "ap_KTall", [96, 8, T], BF16)
        caug = sb(cx, st, "ap_caug", [128, nb, 132], BF16)
        cst_ = [sb(cx, st, "ap_cst%d" % i, [128, 160], F32) for i in range(2)]
        r_KTall, r_caug = Res(), Res()
        r_cst = [Res(), Res()]
        d_cst = [S.dsem("ap_c0"), S.dsem("ap_c1")]
        d_q, d_o = S.dsem("ap_q"), S.dsem("ap_o")
        ps, rps = cx.ps, cx.rps
        S.op("pool", "memset", dict(ap=caug[:, :, 128:132], constant=1.0), writes=(r_caug,))
        S.op("pool", "memset", dict(ap=A.m01[:, :], constant=1.0), writes=(A.r_m01,))
        S.op("pool", "affine_select", dict(out=A.m01[:, :], in_=A.m01[:, :], pattern=[[1, 128]], compare_op=ALU.is_ge, fill=0.0,
                                           base=0, channel_multiplier=-1), writes=(A.r_m01,))
        for kb in range(nb):
            k = kb % 2
            S.dma("sp", d_cst[k], cst_[k][:, 0:128], grp["new_ckv"][kb * 128:(kb + 1) * 128, :], reads=(grp["r_ckv"],),
                  writes=(r_cst[k],))
            S.dma("sp", d_cst[k], cst_[k][:, 128:160], grp["new_kr"][kb * 128:(kb + 1) * 128, :], reads=(grp["r_kr"],),
                  writes=(r_cst[k],))
            S.op("dve", "tensor_copy", dict(out=caug[:, kb, 0:128], in_=cst_[k][:, 0:128]), reads=(r_cst[k],), writes=(r_caug,))
            att_kside(cx, S, A, cst_[k][:, 0:128], cst_[k][:, 128:160], r_cst[k], 128, KTall[:, :, kb * 128:(kb + 1) * 128], r_KTall, k=k)
        for qb in range(nb):
            S.dma("sp", d_q, A.Qb[:, :], grp["Q"][qb * 128:(qb + 1) * 128, :], reads=(grp["r_Q"],), writes=(A.r_Qb,))
            ptb = ps[2].bitcast(BF16)
            for h in range(8):
                S.op("pe", "transpose", dict(out=ptb[0:96, h * 128:(h + 1) * 128], in_=A.Qb[:, h * 96:(h + 1) * 96],
                                             identity=cx.ident_bf[:, :]), reads=(A.r_Qb,), writes=(rps[2],), inc=(h == 7))
            S.op("act", "activation", dict(out=A.QT[:, :, :], in_=ptb[0:96, :].rearrange("p (h t) -> p h t", t=128), func=AF.Copy),
                 reads=(rps[2],), writes=(A.r_QT,))
            gi = 0
            for h in range(8):
                pc = 4 + h % 2
                for kb0 in range(0, qb + 1, 4):
                    ng = min(4, qb + 1 - kb0)
                    pk = gi % 2
                    gi += 1
                    for j in range(ng):
                        kb = kb0 + j
                        S.op("pe", "matmul", dict(out=ps[pk][:, j * 128:(j + 1) * 128], lhsT=KTall[:, h, kb * 128:(kb + 1) * 128],
                                                  rhs=A.QT[:, h, :], start=True, stop=True, skip_group_check=True),
                             reads=(r_KTall, A.r_QT), writes=(rps[pk],), inc=(j == ng - 1))
                    S.op("act", "activation", dict(out=A.PT[:, 0:ng * 128], in_=ps[pk][:, 0:ng * 128], func=AF.Exp),
                         reads=(rps[pk],), writes=(A.r_PT,))
                    if kb0 + ng - 1 == qb:
                        j = ng - 1
                        S.op("dve", "tensor_tensor", dict(out=A.PT[:, j * 128:(j + 1) * 128], in0=A.PT[:, j * 128:(j + 1) * 128],
                                                          in1=A.m01[:, :], op=ALU.mult), reads=(A.r_m01,), writes=(A.r_PT,))
                    for j in range(ng):
                        kb = kb0 + j
                        S.op("pe", "matmul", dict(out=ps[pc][:, 0:129], lhsT=A.PT[:, j * 128:(j + 1) * 128], rhs=caug[:, kb, 0:129],
                                                  start=(kb == 0), stop=(kb == qb), skip_group_check=True),
                             reads=(A.r_PT, r_caug), writes=(rps[pc],), inc=(j == ng - 1))
                S.op("dve", "reciprocal", dict(out=A.rden[:, h:h + 1], in_=ps[pc][:, 128:129]), reads=(rps[pc],), writes=(A.r_rden,))
                S.op("dve", "tensor_scalar", dict(out=A.ctx[:, h, :], in0=ps[pc][:, 0:128], scalar1=A.rden[:, h:h + 1], scalar2=None,
                                                  op0=ALU.mult), reads=(rps[pc], A.r_rden), writes=(A.r_ctx,))
            att_out(cx, S, A, 128, A.r_ctx, grp["mix"][qb * 128:(qb + 1) * 128, 512:1024], grp["r_mix"], d_o)
        S.barrier()
        S.emit()


def phase_attn_sample(cx, S, grp, P, cache_c, cache_k, ptab, nseq, npages):
    with ExitStack() as st:
        A = att_alloc(cx, S, st, "as", P)
        cg = sb(cx, st, "as_cg", [128, 128 * 128], F32)
        kg = sb(cx, st, "as_kg", [128, 128 * 32], F32)
        idx = sb(cx, st, "as_idx", [128, 2], I32)
        cb2 = [sb(cx, st, "as_cb%d" % i, [128, 132], BF16) for i in range(2)]
        cn = sb(cx, st, "as_cn", [4, 160], F32)
        qs = sb(cx, st, "as_qs", [4, 768], BF16)
        QTs = sb(cx, st, "as_QTs", [96, 8, 4], BF16)
        PTs2 = [sb(cx, st, "as_PTs%d" % i, [128, 32], BF16) for i in range(2)]
        ms = sb(cx, st, "as_ms", [4, 32], BF16)
        c32 = sb(cx, st, "as_c32", [32, 128], BF16)
        cT32 = sb(cx, st, "as_cT32", [128, 32], BF16)
        rd = sb(cx, st, "as_rd", [32, 1], F32)
        mo = sb(cx, st, "as_mo", [4, 512], BF16)
        r_cg, r_kg, r_idx, r_cb, r_cn, r_qs, r_QTs, r_PTs, r_ms, r_c32, r_cT32, r_rd, r_mo = (Res() for _ in range(13))
        d_idx, d_cg, d_kg, d_cn, d_qs, d_mo = (S.dsem("as%d" % i) for i in range(6))
        ps, rps = cx.ps, cx.rps
        r_cb2, r_PTs2, r_sc = [Res(), Res()], [Res(), Res()], [Res(), Res()]
        for i_ in range(2):
            S.op("pool", "memset", dict(ap=cb2[i_][:, 128:132], constant=1.0), writes=(r_cb2[i_],))
        S.op("pool", "memset", dict(ap=ms[:, :], constant=1.0), writes=(r_ms,))
        S.op("pool", "affine_select", dict(out=ms[:, :], in_=ms[:, :], pattern=[[0, 8], [1, 4]], compare_op=ALU.is_ge, fill=0.0,
                                           base=0, channel_multiplier=-1), writes=(r_ms,))
        for s in range(nseq):
            S.dma("sp", d_idx, idx[:npages, 0:1], ptab[s:s + 1, :].rearrange("o p -> p o"), writes=(r_idx,),
                  allow_slow_non_contiguous=True)
            S.idma(d_cg, reads=(r_idx,), writes=(r_cg,), out=cg[:npages, :], out_offset=None, in_=cache_c,
                   in_offset=bass.IndirectOffsetOnAxis(ap=idx[:npages, 0:1], axis=0))
            S.idma(d_cg, reads=(r_idx,), writes=(r_cg, r_kg), out=kg[:npages, :], out_offset=None, in_=cache_k,
                   in_offset=bass.IndirectOffsetOnAxis(ap=idx[:npages, 0:1], axis=0))
            S.dma("sp", d_qs, qs[:, :], grp["Q"][4 * s:4 * s + 4, :], reads=(grp["r_Q"],), writes=(r_qs,))
            ptb = ps[2].bitcast(BF16)
            for h in range(8):
                S.op("pe", "transpose", dict(out=ptb[0:96, h * 128:h * 128 + 4], in_=qs[:, h * 96:(h + 1) * 96],
                                             identity=cx.ident_bf[:4, :4]), reads=(r_qs,), writes=(rps[2],), inc=(h == 7))
            S.op("act", "activation", dict(out=QTs[:, :, :], in_=ptb[0:96, :].rearrange("p (h t) -> p h t", t=128)[:, :, 0:4],
                                           func=AF.Copy), reads=(rps[2],), writes=(r_QTs,))
            S.dma("sp", d_cn, cn[:, 0:128], grp["new_ckv"][4 * s:4 * s + 4, :], reads=(grp["r_ckv"],), writes=(r_cn,))
            S.dma("sp", d_cn, cn[:, 128:160], grp["new_kr"][4 * s:4 * s + 4, :], reads=(grp["r_kr"],), writes=(r_cn,))
            nblk = 128 + 1

            def blk_args(t):
                if t < 128:
                    n = npages
                    return n, cg[:n, t * 128:(t + 1) * 128], kg[:n, t * 32:(t + 1) * 32], r_cg, (r_cg, r_kg)
                return 4, cn[:, 0:128], cn[:, 128:160], r_cn, (r_cn,)

            def stage_a(t):
                n, c_blk, k_blk, r_src, rs = blk_args(t)
                kk = t % 2
                S.op("pool", "tensor_copy", dict(out=cb2[kk][:n, 0:128], in_=c_blk), reads=rs, writes=(r_cb2[kk],))
                att_kside(cx, S, A, c_blk, k_blk, r_src, n, None, None, k=kk, stage="A")

            def stage_b(t):
                n, c_blk, k_blk, r_src, rs = blk_args(t)
                kk = t % 2
                cb, r_cb, PTs, r_PTs = cb2[kk], r_cb2[kk], PTs2[kk], r_PTs2[kk]
                KT, r_KT = (A.KT0, A.r_KT0) if kk == 0 else (A.KT1, A.r_KT1)
                att_kside(cx, S, A, c_blk, k_blk, r_src, n, KT[:, :, :n], r_KT, k=kk, stage="B")
                pb = 4 if kk == 0 else 6
                for h in range(8):
                    S.op("pe", "matmul", dict(out=ps[pb][:n, h * 4:(h + 1) * 4], lhsT=KT[:, h, :n], rhs=QTs[:, h, :],
                                              start=True, stop=True, skip_group_check=True), reads=(r_KT, r_QTs),
                         writes=(rps[pb],), inc=(h == 7))
                S.op("act", "activation", dict(out=PTs[:n, :], in_=ps[pb][:n, 0:32], func=AF.Exp), reads=(rps[pb],),
                     writes=(r_PTs,))
                if t == 128:
                    S.op("dve", "tensor_tensor", dict(out=PTs[:n, :], in0=PTs[:n, :], in1=ms[:n, :], op=ALU.mult),
                         reads=(r_ms,), writes=(r_PTs,))
                S.op("pe", "matmul", dict(out=ps[5][0:32, 0:129], lhsT=PTs[:n, :], rhs=cb[:n, 0:129], start=(t == 0),
                                          stop=(t == nblk - 1), skip_group_check=True), reads=(r_PTs, r_cb), writes=(rps[5],))

            stage_a(0)
            for t in range(nblk):
                if t + 1 < nblk:
                    stage_a(t + 1)
                stage_b(t)
            S.op("dve", "reciprocal", dict(out=rd[:, :], in_=ps[5][0:32, 128:129]), reads=(rps[5],), writes=(r_rd,))
            S.op("dve", "tensor_scalar", dict(out=c32[:, :], in0=ps[5][0:32, 0:128], scalar1=rd[:, 0:1], scalar2=None,
                                              op0=ALU.mult), reads=(rps[5], r_rd), writes=(r_c32,))
            S.op("pe", "transpose", dict(out=ptb[:, 0:32], in_=c32[:, :], identity=cx.ident_bf[:32, :32]), reads=(r_c32,),
                 writes=(rps[2],))
            S.op("act", "activation", dict(out=cT32[:, :], in_=ptb[:, 0:32], func=AF.Copy), reads=(rps[2],), writes=(r_cT32,))
            for h in range(8):
                S.op("pe", "matmul", dict(out=ps[3][0:4, h * 64:(h + 1) * 64], lhsT=cT32[:, h * 4:(h + 1) * 4],
                                          rhs=A.wuv[:, h * 64:(h + 1) * 64], start=True, stop=True, skip_group_check=True),
                     reads=(r_cT32, A.r_wuv), writes=(rps[3],), inc=(h == 7))
            S.op("dve", "tensor_copy", dict(out=mo[:, :], in_=ps[3][0:4, :]), reads=(rps[3],), writes=(r_mo,))
            S.dma("sp", d_mo, grp["mix"][4 * s:4 * s + 4, 512:1024], mo[:, :], reads=(r_mo,), writes=(grp["r_mix"],))
            if s % 4 == 3:
                S.barrier()
                S.emit()
        S.barrier()
        S.emit()


CST_COLS = 128 + 128 + 128 + 128 + 1024 + 3 * 256 + 3 * 16


def host_consts():
    c = np.zeros((128, CST_COLS), np.float32)
    j = np.arange(128)[:, None]
    i = np.arange(128)[None, :]
    same = (j // 64 == i // 64)
    c[:, 0:128] = np.eye(128)
    c[:, 128:256] = (j <= i) & same
    c[:, 256:384] = same
    c[:, 384:512] = same
    oh = np.zeros((8, 8, 128), np.float32)
    for h in range(8):
        oh[h, h, :] = 1.0
    c[0:8, 512:1536] = oh.reshape(8, 1024)
    off = 1536
    for m in (64, 4):
        jj = np.arange(m)[:, None]
        ii = np.arange(m)[None, :]
        c[0:m, off:off + 4 * m] = np.tile(np.eye(m, dtype=np.float32), (1, 4))
        c[0:m, off + 4 * m:off + 8 * m] = np.tile(np.where(jj <= ii, 0.0, -1e30).astype(np.float32), (1, 4))
        c[0:m, off + 8 * m:off + 12 * m] = np.tile(np.where(jj < ii, 0.0, -1e30).astype(np.float32), (1, 4))
        off += 12 * m
    return c


def host_rope_table(pos):
    half = 16
    inv = (np.float32(10000.0) ** (-(np.arange(half, dtype=np.float32) / np.float32(half)))).astype(np.float32)
    ang = (pos.astype(np.float32)[:, None] * inv[None, :]).astype(np.float32)
    return np.concatenate([np.cos(ang), np.sin(ang)], axis=1).astype(np.float32)


def build(cfg):
    TP = cfg["TP"]
    NS = cfg["NS"]
    NSEQ = NS // 4
    phases = cfg.get("phases", ("adaln", "ffn1"))
    nc = bass.Bass("TRN2", target_bir_lowering=False)
    cx = Ctx()
    cx.nc = nc
    cx.stop = cfg.get("stop", 99)

    def din(name, shape, dt=F32):
        return nc.dram_tensor(name, list(shape), dt, kind="ExternalInput").ap()

    def dout(name, shape, dt=F32):
        return nc.dram_tensor(name, list(shape), dt, kind="ExternalOutput").ap()

    def dscr(name, shape, dt=F32):
        return nc.dram_tensor(name, list(shape), dt, kind="Internal").ap()

    xp = din("xp", [TP, D])
    xs_in = din("xs", [NS, D])
    c_rep = din("c_rep", [192, D])
    cst = din("cst", [128, CST_COLS])
    ada_w = din("ada_w", [D, 9 * D])
    ada_b = din("ada_b", [1, 9 * D])
    norm_ffn1 = din("norm_ffn1", [1, D])
    ffn1_wi = din("ffn1_wi", [D, 2 * DFF])
    ffn1_wo = din("ffn1_wo", [DFF, D])
    P = dict(norm_mix=din("norm_mix", [1, D]), w_in=din("w_in", [D, 2992]), conv_w_fm=din("conv_w_fm", [128, 12, 4]),
             a_log=din("a_log", [1, 8]), dt_bias=din("dt_bias", [1, 8]), gdn_norm=din("gdn_norm", [1, 64]),
             qn_g=din("qn_g", [1, 64]), qr_g=din("qr_g", [1, 32]), ckv_g=din("ckv_g", [1, 128]), kr_g=din("kr_g", [1, 32]),
             kn_g=din("kn_g", [1, 64]))
    P["w_uk"] = din("w_uk", [128, 512])
    P["w_uv"] = din("w_uv", [128, 512])
    w_out = din("w_out", [D, D])
    norm_ffn2 = din("norm_ffn2", [1, D])
    ffn2_wi = din("ffn2_wi", [D, 2 * DFF])
    ffn2_wo = din("ffn2_wo", [DFF, D])
    NPOOL = cfg.get("npool", 20480)
    cache_c = din("cache_c", [NPOOL, 128 * 128])
    cache_k = din("cache_k", [NPOOL, 128 * 32])
    ptab = din("ptab", [NSEQ, cfg.get("npages", 128)], I32)
    cs_p = din("cs_p", [TP, 32])
    cs_s = din("cs_s", [4, 32])
    st_conv = din("st_conv", [NSEQ, 3, 1536])
    st_gdn = din("st_gdn", [NSEQ, 8, 64, 64])

    yp = dout("yp", [TP, D])
    ys = dout("ys", [NS, D])
    o_ckv_p, o_kr_p = dout("ckv_p", [TP, 128]), dout("kr_p", [TP, 32])
    o_conv_p, o_gdn_p = dout("conv_p", [1, 3, 1536]), dout("gdn_p", [1, 8, 64, 64])
    o_ckv_s, o_kr_s = dout("ckv_s", [NS, 128]), dout("kr_s", [NS, 32])
    o_conv_s, o_gdn_s = dout("conv_s", [NSEQ, 3, 1536]), dout("gdn_s", [NSEQ, 8, 64, 64])
    mods_p = dscr("mods_p", [9, 128, D])
    mods_s = dscr("mods_s", [9, 64, D])
    x1p, x1s = dscr("x1p", [TP, D]), dscr("x1s", [NS, D])
    x2p, x2s = dscr("x2p", [TP, D]), dscr("x2s", [NS, D])
    dbg = dout if cfg.get("debug") else dscr
    mix_p, mix_s = dbg("mix_p", [TP, D], BF16), dbg("mix_s", [NS, D], BF16)
    Q_p, Q_s = dbg("Q_p", [TP, 768], BF16), dbg("Q_s", [NS, 768], BF16)

    with ExitStack() as stack:
        S = Sched(nc, stack)
        cx.S = S
        cx.ps = [stack.enter_context(nc.psum_tensor("ps%d" % i, [128, 512], F32))[:, :] for i in range(8)]
        cx.rps = [Res("ps%d" % i, excl=True) for i in range(8)]
        cst_f = sb(cx, stack, "cst_f", [128, 128], F32)
        cst_b = sb(cx, stack, "cst_b", [128, 128 + 128 + 512 + 128], BF16)
        cx.cst = cst
        cx.eps_t = sb(cx, stack, "eps_t", [128, 1], F32)
        cx.one_t = sb(cx, stack, "one_t", [128, 1], F32)
        cx.ident_f = cst_f[:, 0:128]
        cx.ident_bf = cst_b[:, 0:128]
        cx.nident_bf = cst_b[:, 128:256]
        cx.ident4_bf = cst_b[:, 256:768]
        cx.bones = cst_b[:, 768:896]
        r_c = Res("consts")
        d_c = S.dsem("consts")
        S.dma("sp", d_c, cst_f[:, :], cst[:, 0:128], writes=(r_c,))
        bon = sb(cx, stack, "bon_tmp", [128, 128], F32)
        S.dma("sp", d_c, bon[:, :], cst[:, 384:512], writes=(r_c,))
        S.op("dve", "tensor_copy", dict(out=cst_b[:, 0:128], in_=cst_f[:, 0:128]), reads=(r_c,), writes=(r_c,))
        S.op("dve", "tensor_scalar", dict(out=cst_b[:, 128:256], in0=cst_f[:, 0:128], scalar1=-1.0, scalar2=None,
                                          op0=ALU.mult), reads=(r_c,), writes=(r_c,))
        for h in range(4):
            S.op("dve", "tensor_copy", dict(out=cst_b[:, 256 + h * 128:256 + (h + 1) * 128], in_=cst_f[:, 0:128]),
                 reads=(r_c,), writes=(r_c,))
        S.op("dve", "tensor_copy", dict(out=cst_b[:, 768:896], in_=bon[:, :]), reads=(r_c,), writes=(r_c,))
        S.op("dve", "memset", dict(ap=cx.eps_t[:, :], constant=EPS), writes=(r_c,))
        S.op("dve", "memset", dict(ap=cx.one_t[:, :], constant=1.0), writes=(r_c,))
        S.barrier()
        S.emit()

        r_mods = Res("mods")
        if "adaln" in phases:
            phase_adaln(cx, S, c_rep, ada_w, ada_b, mods_p, mods_s, r_mods)

        nblk_p = (TP + 127) // 128
        r_xp = [Res() for _ in range(nblk_p)]
        r_xs = [Res()]
        r_x1p = [Res() for _ in range(nblk_p)]
        r_x1s = [Res()]
        f1_dst_p, f1_dst_s = (x1p, x1s) if "mixer" in phases else (yp, ys)
        if "ffn1" in phases:
            groups = [dict(src=xp, dst=f1_dst_p, T=TP, mods=mods_p, n=128, src_res=r_xp, dst_res=r_x1p),
                      dict(src=xs_in, dst=f1_dst_s, T=NS, mods=mods_s, n=64, src_res=r_xs, dst_res=r_x1s)]
            phase_ffn(cx, S, "f1", groups, ffn1_wi, ffn1_wo, norm_ffn1, 0)
        r_out = Res("outs")
        gp = gs = None
        if "mixer" in phases:
            msrc_p, msrc_s = (x1p, x1s) if "ffn1" in phases else (xp, xs_in)
            gp = dict(src=msrc_p, src_res=r_x1p, T=TP, nseq=1, mods=mods_p, n=128, s0=None, conv0=None, cs=cs_p,
                      new_conv=o_conv_p, new_gdn=o_gdn_p, new_ckv=o_ckv_p, new_kr=o_kr_p, mix=mix_p, Q=Q_p,
                      r_out=r_out, r_mix=Res(), r_Q=Res(), r_ckv=Res(), r_kr=Res())
            gs = dict(src=msrc_s, src_res=r_x1s, T=NS, nseq=NSEQ, mods=mods_s, n=64, s0=st_gdn, conv0=st_conv, cs=cs_s,
                      new_conv=o_conv_s, new_gdn=o_gdn_s, new_ckv=o_ckv_s, new_kr=o_kr_s, mix=mix_s, Q=Q_s,
                      r_out=r_out, r_mix=Res(), r_Q=Res(), r_ckv=Res(), r_kr=Res())
            phase_mixer(cx, S, [gp, gs], P)
        if "attn_p" in phases:
            phase_attn_prompt(cx, S, gp, P)
        if "attn_s" in phases:
            phase_attn_sample(cx, S, gs, P, cache_c, cache_k, ptab, NSEQ, cfg.get("npages", 128))
        if "wout" in phases:
            r_x2p = [Res() for _ in range(nblk_p)]
            r_x2s = [Res()]
            gp.update(dst=x2p, dst_res=r_x2p)
            gs.update(dst=x2s, dst_res=r_x2s)
            phase_wout(cx, S, [gp, gs], w_out)
            if "ffn2" in phases:
                groups = [dict(src=x2p, dst=yp, T=TP, mods=mods_p, n=128, src_res=r_x2p, dst_res=[Res() for _ in range(nblk_p)]),
                          dict(src=x2s, dst=ys, T=NS, mods=mods_s, n=64, src_res=r_x2s, dst_res=[Res()])]
                phase_ffn(cx, S, "f2", groups, ffn2_wi, ffn2_wo, norm_ffn2, 6)
        S.barrier()
        S.emit()
    return nc


ALL_PHASES = ("adaln", "ffn1", "mixer", "attn_p", "attn_s", "wout", "ffn2")


def make_inputs(core, inputs, TP=4096):
    f = lambda a: np.ascontiguousarray(np.asarray(a, dtype=np.float32))
    b = core % 4
    sl = slice(16 * core, 16 * core + 16)
    conv_w = f(inputs["gdn_conv_w"][0])
    m = dict(
        xp=f(inputs["x_prompt"][b][:TP]), xs=f(inputs["x_sample"][sl]).reshape(64, D),
        c_rep=np.concatenate([np.repeat(f(inputs["c_prompt"][b:b + 1]), 128, 0), np.repeat(f(inputs["c_sample"][sl]), 4, 0)], 0),
        cst=host_consts(), ada_w=f(inputs["ada_w"][0]), ada_b=f(inputs["ada_b"][0:1]),
        norm_ffn1=f(inputs["norm_ffn1"][0:1]), ffn1_wi=f(inputs["ffn1_wi"][0]), ffn1_wo=f(inputs["ffn1_wo"][0]),
        norm_mix=f(inputs["norm_mix"][0:1]), w_in=f(inputs["w_in"][0]),
        conv_w_fm=np.ascontiguousarray(conv_w.reshape(4, 12, 128).transpose(2, 1, 0)),
        a_log=f(inputs["gdn_a_log"][0:1]), dt_bias=f(inputs["gdn_dt_bias"][0:1]), gdn_norm=f(inputs["gdn_norm"][0:1]),
        qn_g=f(inputs["mla_qn_norm"][0:1]), qr_g=f(inputs["mla_qr_norm"][0:1]), ckv_g=f(inputs["mla_ckv_norm"][0:1]),
        kr_g=f(inputs["mla_kr_norm"][0:1]), kn_g=f(inputs["mla_kn_norm"][0:1]),
        w_uk=f(inputs["mla_w_uk"][0]).reshape(128, 512), w_uv=f(inputs["mla_w_uv"][0]).reshape(128, 512),
        w_out=f(inputs["w_out"][0]), norm_ffn2=f(inputs["norm_ffn2"][0:1]), ffn2_wi=f(inputs["ffn2_wi"][0]),
        ffn2_wo=f(inputs["ffn2_wo"][0]),
        cache_c=f(inputs["cache_ckv"][0]).reshape(-1, 128 * 128), cache_k=f(inputs["cache_krope"][0]).reshape(-1, 128 * 32),
        ptab=np.ascontiguousarray(np.asarray(inputs["page_table"][sl], dtype=np.int32)),
        cs_p=host_rope_table(np.arange(TP)), cs_s=host_rope_table(16384 + np.arange(4)),
        st_conv=f(inputs["state_conv"][0][sl]), st_gdn=f(inputs["state_gdn"][0][sl]),
    )
    return m


def kernel(**inputs):
    nc = build(dict(TP=4096, NS=64, phases=ALL_PHASES))
    in_maps = [make_inputs(c, inputs) for c in range(8)]
    res = run_bass_kernel_spmd(nc, in_maps, core_ids=list(range(8))).results
    f32 = np.float32
    yp = np.stack([res[b]["yp"] for b in range(4)]).astype(f32)
    ys = np.concatenate([res[c]["ys"].reshape(16, 4, D) for c in range(8)]).astype(f32)
    ckv_p = np.stack([res[b]["ckv_p"] for b in range(4)])[None].astype(f32)
    kr_p = np.stack([res[b]["kr_p"] for b in range(4)])[None].astype(f32)
    conv_p = np.concatenate([res[b]["conv_p"] for b in range(4)])[None].astype(f32)
    gdn_p = np.concatenate([res[b]["gdn_p"] for b in range(4)])[None].astype(f32)
    ckv_s = np.concatenate([res[c]["ckv_s"].reshape(16, 4, 128) for c in range(8)])[None].astype(f32)
    kr_s = np.concatenate([res[c]["kr_s"].reshape(16, 4, 32) for c in range(8)])[None].astype(f32)
    conv_s = np.concatenate([res[c]["conv_s"] for c in range(8)])[None].astype(f32)
    gdn_s = np.concatenate([res[c]["gdn_s"] for c in range(8)])[None].astype(f32)
    return (yp, ys, ckv_p, kr_p, conv_p, gdn_p, ckv_s, kr_s, conv_s, gdn_s)
```

```python
import numpy as np
from contextlib import ExitStack
import concourse.bass as bass
import concourse.mybir as mybir
from concourse.bass_utils import run_bass_kernel_spmd

F32 = mybir.dt.float32
BF16 = mybir.dt.bfloat16
I32 = mybir.dt.int32
U32 = mybir.dt.uint32
AF = mybir.ActivationFunctionType
ALU = mybir.AluOpType
AX = mybir.AxisListType

D = 1024
DFF = 2816
NFC = DFF // 128
EPS = 1e-6

class Res:
    __slots__ = ("name", "w", "r", "excl")

    def __init__(self, name="", excl=False):
        self.name = name
        self.excl = excl
        self.w = None
        self.r = {}


class DSem:
    __slots__ = ("sem", "cnt", "name")

    def __init__(self, sem, name):
        self.sem = sem
        self.cnt = 0
        self.name = name


class Sched:
    ENGS = ("pe", "act", "dve", "pool", "sp")

    def __init__(self, nc, stack):
        self.nc = nc
        self.stack = stack
        self.q = {k: [] for k in self.ENGS}
        self.esem = {k: stack.enter_context(nc.semaphore("es_" + k)) for k in self.ENGS}
        self.ecnt = {k: 0 for k in self.ENGS}
        self.known = {k: {} for k in self.ENGS}
        self.dsems = []
        self.nwait = 0
        self.nop = 0

    def dsem(self, name):
        s = self.stack.enter_context(self.nc.semaphore("ds_" + name))
        d = DSem(s, name)
        self.dsems.append(d)
        return d

    def _waits(self, eng, reads, writes):
        need = {}

        def add(ev):
            sem_id, sem, val, src = ev
            if src == "pe" and eng == "pe":
                return
            if self.known[eng].get(sem_id, 0) >= val:
                return
            if sem_id not in need or need[sem_id][1] < val:
                need[sem_id] = (sem, val)

        for r in reads:
            if r.w is not None:
                add(r.w)
        for w in writes:
            if w.w is not None:
                add(w.w)
            for ev in w.r.values():
                add(ev)
        for sem_id, (sem, val) in need.items():
            self.q[eng].append(("wait", sem, val))
            self.known[eng][sem_id] = val
            self.nwait += 1

    def _record(self, ev, reads, writes):
        for r in reads:
            old = r.r.get(ev[0])
            if old is None or old[2] < ev[2]:
                r.r[ev[0]] = ev
        for w in writes:
            w.w = ev
            w.r = {}

    def op(self, eng, meth, kw, reads=(), writes=(), inc=True):
        ex = tuple(r for r in reads if r.excl)
        if ex:
            writes = tuple(writes) + ex
        self._waits(eng, reads, writes)
        if inc:
            self.ecnt[eng] += 1
            ev = (eng, self.esem[eng], self.ecnt[eng], eng)
        else:
            ev = (eng, self.esem[eng], self.ecnt[eng] + 1, eng)
        self.q[eng].append(("op", meth, kw, inc))
        self.nop += 1
        self._record(ev, reads, writes)

    def dma(self, q, ds, out, in_, reads=(), writes=(), **kw):
        self._waits(q, reads, writes)
        ds.cnt += 16
        ev = (id(ds), ds.sem, ds.cnt, "dma")
        self.q[q].append(("dma", out, in_, ds.sem, kw))
        self.nop += 1
        self._record(ev, reads, writes)

    def idma(self, ds, reads=(), writes=(), **kw):
        self._waits("pool", reads, writes)
        ds.cnt += 16
        ev = (id(ds), ds.sem, ds.cnt, "dma")
        self.q["pool"].append(("idma", kw, ds.sem))
        self.nop += 1
        self._record(ev, reads, writes)

    def barrier(self):
        for e in self.ENGS:
            for e2 in self.ENGS:
                if e2 == e:
                    continue
                v = self.ecnt[e2]
                if v > 0 and self.known[e].get(e2, 0) < v:
                    self.q[e].append(("wait", self.esem[e2], v))
                    self.known[e][e2] = v
            for d in self.dsems:
                if d.cnt > 0 and self.known[e].get(id(d), 0) < d.cnt:
                    self.q[e].append(("wait", d.sem, d.cnt))
                    self.known[e][id(d)] = d.cnt

    def emit(self):
        import os
        if os.environ.get("EMITLOG"):
            print("EMIT", {k: len(v) for k, v in self.q.items()}, "cnt", dict(self.ecnt), "ndsem", len(self.dsems))
        nc = self.nc
        engs = {"pe": "tensor", "act": "scalar", "dve": "vector", "pool": "gpsimd", "sp": "sync"}
        with nc.Block() as block:
            for k, attr in engs.items():
                items = self.q[k]
                esem = self.esem[k]

                def body(e, items=items, esem=esem):
                    for it in items:
                        if it[0] == "wait":
                            e.wait_ge(it[1], it[2])
                        elif it[0] == "op":
                            ins = getattr(e, it[1])(**it[2])
                            if it[3]:
                                ins.then_inc(esem, 1)
                        elif it[0] == "idma":
                            e.indirect_dma_start(**it[1]).then_inc(it[2], 16)
                        else:
                            _, out, in_, sem, kw = it
                            e.dma_start(out=out, in_=in_, **kw).then_inc(sem, 16)

                getattr(block, attr)(body)
        self.q = {k: [] for k in self.ENGS}


class Ctx:
    pass


def sb(cx, stack, name, shape, dt):
    return stack.enter_context(cx.nc.sbuf_tensor(name, list(shape), dt))


def row_bcast(ap_row, n):
    t = ap_row.tensor
    F = ap_row.shape[-1]
    return bass.AP(t, ap_row.offset, [[0, n], [1, F]])


def load_weight_bf16(cx, S, name, dst, dst_res, w_ap, kchunks, cols, col_piece):
    with ExitStack() as st:
        stg = [sb(cx, st, name + "stg%d" % i, [128, col_piece], F32) for i in range(3)]
        r_stg = [Res() for _ in range(3)]
        d_stg = [S.dsem(name + "stg%d" % i) for i in range(3)]
        engs = (("dve", "tensor_copy"), ("pool", "tensor_copy"), ("act", "activation"))
        i = 0
        for c in range(kchunks):
            for c0 in range(0, cols, col_piece):
                c1 = min(cols, c0 + col_piece)
                k = i % 3
                i += 1
                S.dma("sp", d_stg[k], stg[k][:, :c1 - c0], w_ap[c * 128:(c + 1) * 128, c0:c1], writes=(r_stg[k],))
                eng, meth = engs[k]
                kw = dict(out=dst[:, c, c0:c1], in_=stg[k][:, :c1 - c0])
                if meth == "activation":
                    kw["func"] = AF.Copy
                S.op(eng, meth, kw, reads=(r_stg[k],), writes=(dst_res,))
        S.barrier()
        S.emit()


def norm_mod_transpose(cx, S, bufs, src_ap, src_res, m, hT_dst, pt_k):
    (xs, tt, junk, hb, stat, GG, Bt, r_xs, r_tt, r_junk, r_hb, r_stat, r_GG, r_Bt, r_hT, d_xs) = bufs
    S.dma("sp", d_xs, xs[:m, :], src_ap, reads=(src_res,), writes=(r_xs,))
    S.op("act", "activation", dict(out=junk[:m, :], in_=xs[:m, :], func=AF.Square, accum_out=stat[:m, 0:1]),
         reads=(r_xs,), writes=(r_junk, r_stat))
    S.op("act", "activation", dict(out=stat[:m, 1:2], in_=stat[:m, 0:1], func=AF.Sqrt, scale=1.0 / D,
                                   bias=cx.eps_t[:m, 0:1]), reads=(r_stat,), writes=(r_stat,))
    S.op("dve", "reciprocal", dict(out=stat[:m, 2:3], in_=stat[:m, 1:2]), reads=(r_stat,), writes=(r_stat,))
    S.op("dve", "scalar_tensor_tensor", dict(out=tt[:m, :], in0=xs[:m, :], scalar=stat[:m, 2:3], in1=GG[:m, :],
                                             op0=ALU.mult, op1=ALU.mult),
         reads=(r_xs, r_stat, r_GG), writes=(r_tt,))
    S.op("pool", "tensor_tensor", dict(out=hb[:m, :], in0=tt[:m, :], in1=Bt[:m, :], op=ALU.add),
         reads=(r_tt, r_Bt), writes=(r_hb,))
    ptb = cx.ps[pt_k].bitcast(BF16)
    for c in range(8):
        S.op("pe", "transpose", dict(out=ptb[:, c * 128:c * 128 + m], in_=hb[:m, c * 128:(c + 1) * 128],
                                     identity=cx.ident_bf[:m, :m]),
             reads=(r_hb,), writes=(cx.rps[pt_k],), inc=(c == 7))
    S.op("act", "activation", dict(out=hT_dst, in_=ptb.rearrange("p (c t) -> p c t", c=8)[:, :, :m], func=AF.Copy),
         reads=(cx.rps[pt_k],), writes=(r_hT,))


def load_mod_tiles(cx, S, g_ap, mods, mod_base, n, gscale, GG, Bt, Gt, tt, r_GG, r_Bt, r_Gt, r_tt, d_m):
    S.dma("sp", d_m, tt[:n, :], row_bcast(g_ap, n), writes=(r_tt,))
    S.dma("sp", d_m, GG[:n, :], mods[mod_base + 1, :n, :], writes=(r_GG,))
    S.dma("sp", d_m, Bt[:n, :], mods[mod_base + 0, :n, :], writes=(r_Bt,))
    if Gt is not None:
        S.dma("sp", d_m, Gt[:n, :], mods[mod_base + 2, :n, :], writes=(r_Gt,))
    S.barrier()
    S.op("dve", "scalar_tensor_tensor", dict(out=GG[:n, :], in0=GG[:n, :], scalar=1.0, in1=tt[:n, :],
                                             op0=ALU.add, op1=ALU.mult), reads=(r_tt,), writes=(r_GG,))
    if Gt is not None and gscale != 1.0:
        S.op("dve", "tensor_scalar", dict(out=Gt[:n, :], in0=Gt[:n, :], scalar1=float(gscale), scalar2=None,
                                          op0=ALU.mult), writes=(r_Gt,))


def phase_ffn(cx, S, name, groups, wi_ap, wo_ap, g_ap, mod_base):
    with ExitStack() as st:
        wi = sb(cx, st, name + "wi", [128, 8, 2 * DFF], BF16)
        wo = sb(cx, st, name + "wo", [128, NFC, D], BF16)
        r_wi, r_wo = Res("wi"), Res("wo")
        load_weight_bf16(cx, S, name + "wi", wi, r_wi, wi_ap, 8, 2 * DFF, DFF)
        load_weight_bf16(cx, S, name + "wo", wo, r_wo, wo_ap, NFC, D, D)
        GG = sb(cx, st, name + "GG", [128, D], F32)
        Bt = sb(cx, st, name + "Bt", [128, D], F32)
        Gt = sb(cx, st, name + "Gt", [128, D], F32)
        xs = sb(cx, st, name + "xs", [128, D], F32)
        tt = sb(cx, st, name + "tt", [128, D], F32)
        junk = sb(cx, st, name + "junk", [128, D], BF16)
        hb = sb(cx, st, name + "hb", [128, D], BF16)
        hT = sb(cx, st, name + "hT", [128, 8, 512], BF16)
        sg = [sb(cx, st, name + "sg%d" % i, [128, 512], BF16) for i in range(2)]
        actT = sb(cx, st, name + "actT", [128, NFC, 512], BF16)
        xr = [sb(cx, st, name + "xr%d" % i, [128, D], F32) for i in range(2)]
        tmp = sb(cx, st, name + "tmp", [128, 512], F32)
        stat = sb(cx, st, name + "stat", [128, 4], F32)

        r_GG, r_Bt, r_Gt, r_xs, r_tt, r_junk, r_hb, r_hT = (Res(n_) for n_ in
                                                              ("GG", "Bt", "Gt", "xs", "tt", "junk", "hb", "hT"))
        r_sg = [Res("sg0"), Res("sg1")]
        r_actT = [Res("actT%d" % j) for j in range(NFC)]
        r_xr = [Res("xr0"), Res("xr1")]
        r_tmp, r_stat = Res("tmp"), Res("stat")
        d_m, d_xs = S.dsem(name + "m"), S.dsem(name + "xs")
        d_xr = [S.dsem(name + "xr0"), S.dsem(name + "xr1")]
        d_xst = [S.dsem(name + "xst0"), S.dsem(name + "xst1")]
        bufs = (xs, tt, junk, hb, stat, GG, Bt, r_xs, r_tt, r_junk, r_hb, r_stat, r_GG, r_Bt, r_hT, d_xs)


        pg, pu, po = cx.ps[0:2], cx.ps[2:4], cx.ps[4:6]
        r_pg, r_pu, r_po = cx.rps[0:2], cx.rps[2:4], cx.rps[4:6]
        cnt = {"up": 0, "po": 0, "pt": 0, "xr": 0}

        for g in groups:
            n = g["n"]
            load_mod_tiles(cx, S, g_ap, g["mods"], mod_base, n, 0.5, GG, Bt, Gt, tt, r_GG, r_Bt, r_Gt, r_tt, d_m)
            T = g["T"]
            nblk = (T + 511) // 512

            def stage_norm(b):
                t0 = b * 512
                tb = min(512, T - t0)
                for s_ in range((tb + 127) // 128):
                    r0 = t0 + s_ * 128
                    m = min(128, T - r0)
                    k = 6 + cnt["pt"] % 2
                    cnt["pt"] += 1
                    norm_mod_transpose(cx, S, bufs, g["src"][r0:r0 + m, :], g["src_res"][r0 // 128], m,
                                       hT[:, :, s_ * 128:s_ * 128 + m], k)

            def stage_up(b):
                t0 = b * 512
                tb = min(512, T - t0)
                for j in range(NFC):
                    k = cnt["up"] % 2
                    cnt["up"] += 1
                    for kc in range(8):
                        S.op("pe", "matmul", dict(out=pg[k][:, :tb], lhsT=wi[:, kc, j * 128:(j + 1) * 128],
                                                  rhs=hT[:, kc, :tb], start=(kc == 0), stop=(kc == 7)),
                             reads=(r_wi, r_hT), writes=(r_pg[k],), inc=(kc == 7))
                    for kc in range(8):
                        S.op("pe", "matmul", dict(out=pu[k][:, :tb], lhsT=wi[:, kc, DFF + j * 128:DFF + (j + 1) * 128],
                                                  rhs=hT[:, kc, :tb], start=(kc == 0), stop=(kc == 7)),
                             reads=(r_wi, r_hT), writes=(r_pu[k],), inc=(kc == 7))
                    S.op("act", "activation", dict(out=sg[k][:, :tb], in_=pg[k][:, :tb], func=AF.Silu),
                         reads=(r_pg[k],), writes=(r_sg[k],))
                    S.op("dve", "tensor_tensor", dict(out=actT[:, j, :tb], in0=sg[k][:, :tb], in1=pu[k][:, :tb],
                                                      op=ALU.mult),
                         reads=(r_sg[k], r_pu[k]), writes=(r_actT[j],))

            def stage_down(b):
                t0 = b * 512
                tb = min(512, T - t0)
                for s_ in range((tb + 127) // 128):
                    r0 = t0 + s_ * 128
                    m = min(128, T - r0)
                    kx = cnt["xr"] % 2
                    cnt["xr"] += 1
                    S.dma("sp", d_xr[kx], xr[kx][:m, :], g["src"][r0:r0 + m, :], reads=(g["src_res"][r0 // 128],),
                          writes=(r_xr[kx],))
                    for half in range(2):
                        k = cnt["po"] % 2
                        cnt["po"] += 1
                        for j in range(NFC):
                            S.op("pe", "matmul", dict(out=po[k][:m, :], lhsT=actT[:, j, s_ * 128:s_ * 128 + m],
                                                      rhs=wo[:, j, half * 512:(half + 1) * 512],
                                                      start=(j == 0), stop=(j == NFC - 1)),
                                 reads=(r_wo, r_actT[j]), writes=(r_po[k],), inc=(j == NFC - 1))
                        S.op("dve", "tensor_tensor", dict(out=tmp[:m, :], in0=po[k][:m, :],
                                                          in1=Gt[:m, half * 512:(half + 1) * 512], op=ALU.mult),
                             reads=(r_po[k], r_Gt), writes=(r_tmp,))
                        S.op("pool", "tensor_tensor", dict(out=xr[kx][:m, half * 512:(half + 1) * 512],
                                                           in0=xr[kx][:m, half * 512:(half + 1) * 512],
                                                           in1=tmp[:m, :], op=ALU.add),
                             reads=(r_tmp,), writes=(r_xr[kx],))
                    S.dma("pool", d_xst[kx], g["dst"][r0:r0 + m, :], xr[kx][:m, :], reads=(r_xr[kx],),
                          writes=(g["dst_res"][r0 // 128],))

            stage_norm(0)
            for b in range(nblk):
                stage_up(b)
                if b + 1 < nblk:
                    stage_norm(b + 1)
                stage_down(b)
        S.barrier()
        S.emit()


def phase_adaln(cx, S, c_rep, ada_w, ada_b, mods_p, mods_s, r_mods):
    with ExitStack() as st:
        ct = sb(cx, st, "ad_ct", [128, D], F32)
        cb_ = sb(cx, st, "ad_cb", [128, D], BF16)
        cT = sb(cx, st, "ad_cT", [128, 8, 192], BF16)
        w = [sb(cx, st, "ad_w%d" % i, [128, 8, D], BF16) for i in range(2)]
        wst = [sb(cx, st, "ad_wst%d" % i, [128, 8, D], F32) for i in range(2)]
        bb = [sb(cx, st, "ad_b%d" % i, [128, D], F32) for i in range(2)]
        o = [sb(cx, st, "ad_o%d" % i, [128, D], F32) for i in range(2)]
        r_ct, r_cb, r_cT = Res(), Res(), Res()
        r_w, r_wst, r_bb, r_o = [Res(), Res()], [Res(), Res()], [Res(), Res()], [Res(), Res()]
        d_ct = S.dsem("ad_ct")
        d_w = [S.dsem("ad_w0"), S.dsem("ad_w1")]
        d_b = [S.dsem("ad_b0"), S.dsem("ad_b1")]
        d_o = [S.dsem("ad_o0"), S.dsem("ad_o1")]

        def load(blk):
            k = blk % 2
            for hf in range(2):
                S.dma("sp", d_w[k], wst[k][:, hf * 4:(hf + 1) * 4, :],
                      ada_w[hf * 512:(hf + 1) * 512, blk * D:(blk + 1) * D].rearrange("(c p) f -> p c f", p=128),
                      writes=(r_wst[k],))
            S.dma("sp", d_b[k], bb[k][:, :], row_bcast(ada_b[0:1, blk * D:(blk + 1) * D], 128), writes=(r_bb[k],))

        load(0)
        for gi, (r0, m) in enumerate(((0, 128), (128, 64))):
            S.dma("sp", d_ct, ct[:m, :], c_rep[r0:r0 + m, :], writes=(r_ct,))
            S.op("act", "activation", dict(out=cb_[:m, :], in_=ct[:m, :], func=AF.Silu), reads=(r_ct,), writes=(r_cb,))
            ptb = cx.ps[6 + gi].bitcast(BF16)
            for c in range(8):
                S.op("pe", "transpose", dict(out=ptb[:, c * 128:c * 128 + m], in_=cb_[:m, c * 128:(c + 1) * 128],
                                             identity=cx.ident_bf[:m, :m]),
                     reads=(r_cb,), writes=(cx.rps[6 + gi],), inc=(c == 7))
            S.op("act", "activation", dict(out=cT[:, :, r0:r0 + m],
                                           in_=ptb.rearrange("p (c t) -> p c t", c=8)[:, :, :m], func=AF.Copy),
                 reads=(cx.rps[6 + gi],), writes=(r_cT,))
        for blk in range(9):
            k = blk % 2
            if blk + 1 < 9:
                load(blk + 1)
            S.op("pool", "tensor_copy", dict(out=w[k][:, 0:4, :], in_=wst[k][:, 0:4, :]),
                 reads=(r_wst[k],), writes=(r_w[k],))
            S.op("dve", "tensor_copy", dict(out=w[k][:, 4:8, :], in_=wst[k][:, 4:8, :]),
                 reads=(r_wst[k],), writes=(r_w[k],))
            for gi, (r0, m, dst) in enumerate(((0, 128, mods_p), (128, 64, mods_s))):
                ko = gi
                for half in range(2):
                    pk = (blk * 4 + gi * 2 + half) % 4
                    ps, rp = cx.ps[pk], cx.rps[pk]
                    for c in range(8):
                        S.op("pe", "matmul", dict(out=ps[:m, :], lhsT=cT[:, c, r0:r0 + m],
                                                  rhs=w[k][:, c, half * 512:(half + 1) * 512], start=(c == 0),
                                                  stop=(c == 7)), reads=(r_cT, r_w[k]), writes=(rp,), inc=(c == 7))
                    S.op("dve", "tensor_tensor", dict(out=o[ko][:m, half * 512:(half + 1) * 512], in0=ps[:m, :],
                                                      in1=bb[k][:m, half * 512:(half + 1) * 512], op=ALU.add),
                         reads=(rp, r_bb[k]), writes=(r_o[ko],))
                S.dma("act", d_o[ko], dst[blk, :m, :], o[ko][:m, :], reads=(r_o[ko],), writes=(r_mods,))
        S.barrier()
        S.emit()


NLEV = 5
QK_SCALE = 64 ** -0.5
ATT_SCALE = 96 ** -0.5


def bc(ap2, n):
    return ap2.unsqueeze(2).broadcast_to([ap2.shape[0], ap2.shape[1], n])


def bc_mid(ap2, n):
    return ap2.unsqueeze(1).broadcast_to([ap2.shape[0], n, ap2.shape[1]])


class MixBufs:
    pass


def mixer_alloc(cx, S, st, name, P):
    B = MixBufs()

    def t(nm, shape, dt):
        tt_ = sb(cx, st, name + nm, shape, dt)
        setattr(B, nm, tt_)
        setattr(B, "r_" + nm, Res(nm))
        return tt_

    t("win", [128, 8, 2992], BF16)
    load_weight_bf16(cx, S, "mxwin", B.win, B.r_win, P["w_in"], 8, 2992, 1496)
    t("GG", [128, D], F32); t("Bt", [128, D], F32)
    t("xs", [128, D], F32); t("tt", [128, D], F32); t("junk", [128, D], BF16); t("hb", [128, D], BF16)
    t("stat", [128, 4], F32)
    t("hT", [128, 8, 256], BF16)
    t("xq", [128, 12, 259], F32)
    t("yc", [128, 12, 256], F32)
    t("sqb", [128, 512], BF16)
    t("rinv", [128, 512], F32)
    t("qkT", [128, 8, 256], BF16)
    t("tm", [128, 1456], F32)
    t("sc", [128, 96], F32)
    t("rows", [8, 2, 128], F32)
    t("glb", [128, 8], F32)
    t("vtok", [128, 512], BF16); t("ktok", [128, 512], BF16); t("kdec", [128, 512], BF16)
    t("decq", [128, 512], F32); t("decb", [128, 512], F32)
    for g in range(2):
        for nm in ("A", "M", "P"):
            for k in range(2):
                t("%s%d%d" % (nm, g, k), [128, 512], F32)
        t("Pf%d" % g, [128, 512], BF16)
    t("qk", [128, 1024], BF16)
    t("u", [128, 512], F32)
    t("wT", [128, 4, 128], BF16)
    t("vnew", [128, 512], BF16)
    t("o1", [128, 512], F32)
    t("osb", [128, 512], F32)
    t("ost", [128, 16], F32)
    t("zs", [128, 512], F32)
    t("gout", [128, 512], BF16)
    t("S32", [128, 512], F32); t("Sbf", [128, 512], BF16)
    t("mq", [128, 768], F32); t("mqs", [128, 32], F32); t("Qb", [128, 768], BF16)
    t("cn", [128, 128], F32); t("krn", [128, 32], F32); t("krr", [128, 32], F32); t("qrr", [128, 8, 32], F32)
    t("cs", [128, 32], F32)
    t("cw", [128, 12, 4], F32)
    t("g8", [128, 16], F32)
    t("gng", [128, 64], F32); t("qng", [128, 64], F32); t("qrg", [128, 32], F32); t("ckg", [128, 128], F32)
    t("krg", [128, 32], F32); t("kng", [128, 64], F32)
    t("nc3", [128, 1536], F32)
    B.d = {}
    return B


def mixer_group(cx, S, B, name, grp, P):
    def dsem(k):
        if k not in B.d:
            B.d[k] = S.dsem(name + k)
        return B.d[k]

    def op(eng, meth, reads, writes, inc=True, **kw):
        S.op(eng, meth, kw, reads=reads, writes=writes, inc=inc)

    ps, rps = cx.ps, cx.rps
    T, nseq = grp["T"], grp["nseq"]
    Ts = T // nseq
    prompt = (nseq == 1)
    n = grp["n"]
    load_mod_tiles(cx, S, P["norm_mix"], grp["mods"], 3, n, 1.0, B.GG, B.Bt, None, B.tt, B.r_GG, B.r_Bt, None,
                   B.r_tt, dsem("m"))
    bufs = (B.xs, B.tt, B.junk, B.hb, B.stat, B.GG, B.Bt, B.r_xs, B.r_tt, B.r_junk, B.r_hb, B.r_stat, B.r_GG,
            B.r_Bt, B.r_hT, dsem("xs"))
    blk_T = 256 if prompt else T
    nblk = (T + blk_T - 1) // blk_T
    if prompt:
        op("dve", "memset", (), (B.r_S32,), ap=B.S32[:, :], constant=0.0)
        op("dve", "memset", (), (B.r_Sbf,), ap=B.Sbf[:, :], constant=0.0)
        op("pool", "memset", (), (B.r_xq,), ap=B.xq[:, :, 0:3], constant=0.0)
    ptk = [0]

    for b in range(nblk):
        t0 = b * blk_T
        tb = min(blk_T, T - t0)
        for s_ in range((tb + 127) // 128):
            r0 = t0 + s_ * 128
            m = min(128, T - r0)
            k = 6 + ptk[0] % 2
            ptk[0] += 1
            norm_mod_transpose(cx, S, bufs, grp["src"][r0:r0 + m, :], grp["src_res"][r0 // 128], m,
                               B.hT[:, :, s_ * 128:s_ * 128 + m], k)
        if prompt:
            xq_new = B.xq[:, :, 3:3 + tb]
            xqv = None
        else:
            xqv = B.xq[:, :, 0:nseq * 7].rearrange("p c (s t) -> p c s t", t=7)
            op("sp", "dma_start", (), (), ) if False else None
            S.dma("sp", dsem("nc3"), B.nc3[:nseq * 3, :], grp["conv0"].rearrange("s t c -> (s t) c"), writes=(B.r_nc3,))
            for c in range(12):
                kq = c % 2
                op("pe", "transpose", (B.r_nc3,), (rps[kq],), out=ps[kq][:, :nseq * 3],
                   in_=B.nc3[:nseq * 3, c * 128:(c + 1) * 128], identity=cx.ident_f[:nseq * 3, :nseq * 3])
                op("act" if c % 2 else "dve", "activation" if c % 2 else "tensor_copy", (rps[kq],), (B.r_xq,),
                   out=xqv[:, c, :, 0:3], in_=ps[kq][:, :nseq * 3].rearrange("p (s t) -> p s t", t=3),
                   **({"func": AF.Copy} if c % 2 else {}))
        for c in range(12):
            kq = c % 2
            for kc in range(8):
                op("pe", "matmul", (B.r_win, B.r_hT), (rps[kq],), inc=(kc == 7), out=ps[kq][:, :tb],
                   lhsT=B.win[:, kc, c * 128:(c + 1) * 128], rhs=B.hT[:, kc, :tb], start=(kc == 0), stop=(kc == 7))
            if prompt:
                dst, srcp = B.xq[:, c, 3:3 + tb], ps[kq][:, :tb]
            else:
                dst, srcp = xqv[:, c, :, 3:7], ps[kq][:, :tb].rearrange("p (s t) -> p s t", t=Ts)
            if c % 2:
                op("act", "activation", (rps[kq],), (B.r_xq,), out=dst, in_=srcp, func=AF.Copy)
            else:
                op("dve", "tensor_copy", (rps[kq],), (B.r_xq,), out=dst, in_=srcp)
        for c in range(12):
            if prompt:
                ydst = B.yc[:, c, :tb]
                xin = [B.xq[:, c, j:j + tb] for j in range(4)]
            else:
                ydst = B.yc[:, c, :tb].rearrange("p (s t) -> p s t", t=Ts)
                xin = [xqv[:, c, :, j:j + Ts] for j in range(4)]
            eng = "dve"
            op(eng, "tensor_scalar", (B.r_xq, B.r_cw), (B.r_yc,), out=ydst, in0=xin[0], scalar1=B.cw[:, c, 0:1],
               scalar2=None, op0=ALU.mult)
            for j in range(1, 4):
                op(eng, "scalar_tensor_tensor", (B.r_xq, B.r_cw), (B.r_yc,), out=ydst, in0=xin[j],
                   scalar=B.cw[:, c, j:j + 1], in1=ydst, op0=ALU.mult, op1=ALU.add)
        for c in range(12):
            op("act", "activation", (B.r_yc,), (B.r_yc,), out=B.yc[:, c, :tb], in_=B.yc[:, c, :tb], func=AF.Silu)
        if prompt and b + 1 < nblk:
            op("pool", "tensor_copy", (B.r_xq,), (B.r_xq,), out=B.xq[:, :, 0:3], in_=B.xq[:, :, tb:tb + 3])
        for c in range(8):
            kq = c % 2
            op("pool", "tensor_tensor", (B.r_yc,), (B.r_sqb,), out=B.sqb[:, :tb], in0=B.yc[:, c, :tb], in1=B.yc[:, c, :tb],
               op=ALU.mult)
            op("pe", "matmul", (B.r_sqb,), (rps[kq],), out=ps[kq][:, :tb], lhsT=cx.bones[:, :], rhs=B.sqb[:, :tb],
               start=True, stop=True)
            op("act", "activation", (rps[kq],), (B.r_rinv,), out=B.rinv[:, :tb], in_=ps[kq][:, :tb], func=AF.Sqrt,
               bias=cx.eps_t[:, 0:1], scale=1.0)
            op("dve", "reciprocal", (B.r_rinv,), (B.r_rinv,), out=B.rinv[:, :tb], in_=B.rinv[:, :tb])
            if c < 4:
                op("dve", "scalar_tensor_tensor", (B.r_rinv, B.r_yc), (B.r_qkT,), out=B.qkT[:, c, :tb], in0=B.yc[:, c, :tb],
                   scalar=QK_SCALE, in1=B.rinv[:, :tb], op0=ALU.mult, op1=ALU.mult)
            else:
                op("dve", "tensor_tensor", (B.r_rinv, B.r_yc), (B.r_qkT,), out=B.qkT[:, c, :tb], in0=B.yc[:, c, :tb],
                   in1=B.rinv[:, :tb], op=ALU.mult)
        ntile = (tb + 63) // 64 if prompt else nseq
        if cx.stop <= 1:
            ntile = 0
        for ti in range(ntile):
            if prompt:
                c0 = ti * 64
                m = min(64, tb - c0)
                seq = 0
            else:
                c0 = ti * Ts
                m = Ts
                seq = ti
            tok0 = t0 + c0
            mixer_tile(cx, S, B, name, grp, P, m, c0, tok0, seq, prompt, dsem, op,
                       first=(tok0 == 0) if prompt else True, last=(tok0 + m == T) if prompt else True)


def mixer_tile(cx, S, B, name, grp, P, m, c0, tok0, seq, prompt, dsem, op, first, last):
    ps, rps = cx.ps, cx.rps
    sc, rsc = B.sc, B.r_sc
    for gi, (cA, cB) in enumerate(((1536, 2048), (2048, 2560), (2560, 2992))):
        kq = gi % 2
        for kc in range(8):
            op("pe", "matmul", (B.r_win, B.r_hT), (rps[kq],), inc=(kc == 7), out=ps[kq][:m, :cB - cA],
               lhsT=B.hT[:, kc, c0:c0 + m], rhs=B.win[:, kc, cA:cB], start=(kc == 0), stop=(kc == 7))
        if gi % 2:
            op("act", "activation", (rps[kq],), (B.r_tm,), out=B.tm[:m, cA - 1536:cB - 1536], in_=ps[kq][:m, :cB - cA],
               func=AF.Copy)
        else:
            op("dve", "tensor_copy", (rps[kq],), (B.r_tm,), out=B.tm[:m, cA - 1536:cB - 1536], in_=ps[kq][:m, :cB - cA])
    if last:
        for gi in range(3):
            kq = gi % 2
            for kc in range(8):
                op("pe", "matmul", (B.r_win, B.r_hT), (rps[kq],), inc=(kc == 7), out=ps[kq][:m, :512],
                   lhsT=B.hT[:, kc, c0:c0 + m], rhs=B.win[:, kc, gi * 512:(gi + 1) * 512], start=(kc == 0), stop=(kc == 7))
            op("dve", "tensor_copy", (rps[kq],), (B.r_nc3,), out=B.nc3[:m, gi * 512:(gi + 1) * 512], in_=ps[kq][:m, :512])
        S.dma("sp", dsem("nc3"), grp["new_conv"][seq, :, :], B.nc3[m - 3:m, :], reads=(B.r_nc3,), writes=(grp["r_out"],))
    if cx.stop <= 2:
        return
    zr, br, ar = B.tm[:m, 0:512], B.tm[:m, 512:520], B.tm[:m, 520:528]
    qraw, ckvr, krr_ = B.tm[:m, 528:1296], B.tm[:m, 1296:1424], B.tm[:m, 1424:1456]
    op("act", "activation", (B.r_tm,), (rsc,), out=sc[:m, 64:72], in_=br, func=AF.Exp, scale=-1.0)
    op("dve", "tensor_scalar", (rsc,), (rsc,), out=sc[:m, 64:72], in0=sc[:m, 64:72], scalar1=1.0, scalar2=None,
       op0=ALU.add)
    op("dve", "reciprocal", (rsc,), (rsc,), out=sc[:m, 0:8], in_=sc[:m, 64:72])
    op("act", "activation", (rsc,), (rsc,), out=sc[:m, 8:16], in_=sc[:m, 0:8], func=AF.Ln)
    op("dve", "tensor_tensor", (B.r_tm, B.r_g8), (rsc,), out=sc[:m, 64:72], in0=ar, in1=B.g8[:m, 8:16], op=ALU.add)
    op("act", "activation", (rsc,), (rsc,), out=sc[:m, 64:72], in_=sc[:m, 64:72], func=AF.Exp)
    op("act", "activation", (rsc,), (rsc,), out=sc[:m, 64:72], in_=sc[:m, 64:72], func=AF.Ln, bias=cx.one_t[:m, 0:1],
       scale=1.0)
    op("dve", "scalar_tensor_tensor", (rsc, B.r_g8), (rsc,), out=sc[:m, 16:24], in0=sc[:m, 64:72], scalar=-1.0,
       in1=B.g8[:m, 0:8], op0=ALU.mult, op1=ALU.mult)
    op("pe", "matmul", (rsc,), (rps[2],), out=ps[2][:m, 0:8], lhsT=cx.utri[:m, :m], rhs=sc[:m, 16:24], start=True, stop=True)
    op("dve", "tensor_copy", (rps[2],), (rsc,), out=sc[:m, 24:32], in_=ps[2][:m, 0:8])
    op("pe", "matmul", (rsc,), (rps[2],), out=ps[2][:m, 8:16], lhsT=cx.onesf[:m, :m], rhs=sc[:m, 16:24], start=True, stop=True)
    op("dve", "tensor_tensor", (rps[2], rsc), (rsc,), out=sc[:m, 48:56], in0=ps[2][:m, 8:16], in1=sc[:m, 24:32],
       op=ALU.subtract)
    op("act", "activation", (rsc,), (rsc,), out=sc[:m, 48:56], in_=sc[:m, 48:56], func=AF.Exp)
    op("act", "activation", (rsc,), (rsc,), out=sc[:m, 32:40], in_=sc[:m, 24:32], func=AF.Exp)
    op("dve", "tensor_tensor", (rsc,), (rsc,), out=sc[:m, 40:48], in0=sc[:m, 32:40], in1=sc[:m, 0:8], op=ALU.mult)
    op("dve", "tensor_scalar", (rsc,), (rsc,), out=sc[:m, 56:64], in0=sc[:m, 24:32], scalar1=-1.0, scalar2=None,
       op0=ALU.mult)
    op("dve", "tensor_tensor", (rsc,), (rsc,), out=sc[:m, 72:80], in0=sc[:m, 24:32], in1=sc[:m, 8:16], op=ALU.add)
    assert m <= 64
    g2 = sc[:m, 16:24].rearrange("p (c two) -> p two c", two=2)
    for par in range(2):
        op("pe", "matmul", (rsc,), (rps[2],), out=ps[2][par * 64:(par + 1) * 64, 16:20], lhsT=cx.onesf[:m, :64],
           rhs=g2[:, par, :], start=True, stop=True, skip_group_check=True)
    op("act", "activation", (rps[2],), (B.r_glb,), out=B.glb[:, 0:4], in_=ps[2][:, 16:20], func=AF.Exp)
    op("pe", "transpose", (rsc,), (rps[2],), out=ps[2][:8, 32:32 + m], in_=sc[:m, 24:32], identity=cx.ident_f[:m, :m])
    op("pe", "transpose", (rsc,), (rps[2],), out=ps[2][:8, 160:160 + m], in_=sc[:m, 72:80], identity=cx.ident_f[:m, :m])
    op("dve", "tensor_copy", (rps[2],), (B.r_rows,), out=B.rows[:, 0, :m], in_=ps[2][:8, 32:32 + m])
    op("dve", "tensor_copy", (rps[2],), (B.r_rows,), out=B.rows[:, 1, :m], in_=ps[2][:8, 160:160 + m])
    if cx.stop <= 3:
        return
    for c in range(4):
        op("pe", "transpose", (B.r_yc,), (rps[3],), out=ps[3][:m, c * 128:(c + 1) * 128], in_=B.yc[:, 8 + c, c0:c0 + m],
           identity=cx.ident_f[:, :], inc=(c == 3))
    op("dve", "tensor_tensor", (rps[3], rsc), (B.r_vtok,), out=B.vtok[:m, :].rearrange("p (h d) -> p h d", d=64),
       in0=ps[3][:m, :].rearrange("p (h d) -> p h d", d=64), in1=bc(sc[:m, 0:8], 64), op=ALU.mult)
    pkb = ps[2].bitcast(BF16)
    for c in range(4):
        op("pe", "transpose", (B.r_qkT,), (rps[2],), out=pkb[:m, 512 + c * 128:512 + (c + 1) * 128],
           in_=B.qkT[:, 4 + c, c0:c0 + m], identity=cx.ident_bf[:, :], inc=(c == 3))
    kh3 = pkb[:m, 512:1024].rearrange("p (h d) -> p h d", d=64)
    op("dve", "tensor_tensor", (rps[2], rsc), (B.r_ktok,), out=B.ktok[:m, :].rearrange("p (h d) -> p h d", d=64),
       in0=kh3, in1=bc(sc[:m, 40:48], 64), op=ALU.mult)
    op("dve", "tensor_tensor", (rps[2], rsc), (B.r_kdec,), out=B.kdec[:m, :].rearrange("p (h d) -> p h d", d=64),
       in0=kh3, in1=bc(sc[:m, 48:56], 64), op=ALU.mult)
    if cx.stop <= 4:
        return
    W4 = 4 * m
    mle, mlt, id4 = cx.mconst[m]
    for g in range(2):
        Ab = [getattr(B, "A%d%d" % (g, k)) for k in range(2)]
        Mb = [getattr(B, "M%d%d" % (g, k)) for k in range(2)]
        Pb = [getattr(B, "P%d%d" % (g, k)) for k in range(2)]
        rA = [getattr(B, "r_A%d%d" % (g, k)) for k in range(2)]
        rM = [getattr(B, "r_M%d%d" % (g, k)) for k in range(2)]
        rP = [getattr(B, "r_P%d%d" % (g, k)) for k in range(2)]
        pP, rpP = ps[6 + g], rps[6 + g]
        for kind, dec, rdec, mask in ((0, B.decq, B.r_decq, mle), (1, B.decb, B.r_decb, mlt)):
            op("pe", "matmul", (), (rps[3],), inc=False, out=ps[3][:m, 0:W4], lhsT=cx.ident_f[:m, :m], rhs=mask[:m, 0:W4],
               start=True, stop=False)
            for hh in range(4):
                h = 2 * hh + g
                op("pe", "matmul", (B.r_rows,), (rps[3],), inc=(hh == 3), out=ps[3][:m, hh * m:(hh + 1) * m],
                   lhsT=cx.ohsel[:, h, :m], rhs=B.rows[:, kind, :m], start=False, stop=(hh == 3))
            for hh in range(4):
                h = 2 * hh + g
                op("act", "activation", (rps[3], rsc), (rdec,), out=dec[:m, hh * m:(hh + 1) * m],
                   in_=ps[3][:m, hh * m:(hh + 1) * m], func=AF.Exp, bias=sc[:m, 56 + h:57 + h], scale=1.0)
        if cx.stop <= 4.2:
            continue
        for hh in range(4):
            h = 2 * hh + g
            c, po = h // 2, (h % 2) * 64
            op("pe", "matmul", (B.r_qkT,), (rps[4],), inc=(hh == 3), out=ps[4][:m, hh * m:(hh + 1) * m],
               lhsT=B.qkT[po:po + 64, 4 + c, c0:c0 + m], rhs=B.qkT[po:po + 64, 4 + c, c0:c0 + m], start=True, stop=True,
               skip_group_check=True)
        op("dve", "tensor_tensor", (rps[4], B.r_decb), (rA[0],), out=Ab[0][:m, 0:W4], in0=ps[4][:m, 0:W4],
           in1=B.decb[:m, 0:W4], op=ALU.mult)
        for hh in range(4):
            h = 2 * hh + g
            c, po = h // 2, (h % 2) * 64
            op("pe", "matmul", (B.r_qkT,), (rps[5],), inc=(hh == 3), out=ps[5][:m, hh * m:(hh + 1) * m],
               lhsT=B.qkT[po:po + 64, 4 + c, c0:c0 + m], rhs=B.qkT[po:po + 64, c, c0:c0 + m], start=True, stop=True,
               skip_group_check=True)
        op("dve", "tensor_tensor", (rps[5], B.r_decq), (B.r_qk,), out=B.qk[:m, 0:8 * m].rearrange("p (c two i) -> p two c i", two=2, i=m)[:, g],
           in0=ps[5][:m, 0:W4].rearrange("p (c i) -> p c i", i=m),
           in1=B.decq[:m, 0:W4].rearrange("p (c i) -> p c i", i=m), op=ALU.mult)
        if cx.stop <= 4.4:
            continue
        for hh in range(4):
            op("pe", "transpose", (rA[0],), (rps[4],), inc=(hh == 3), out=ps[4][:m, hh * m:(hh + 1) * m],
               in_=Ab[0][:m, hh * m:(hh + 1) * m], identity=cx.ident_f[:m, :m])
        op("act", "activation", (rps[4],), (rM[0],), out=Mb[0][:m, 0:W4], in_=ps[4][:m, 0:W4], func=AF.Copy)
        op("pe", "matmul", (), (rpP,), inc=False, out=pP[:m, 0:W4], lhsT=cx.ident_f[:m, :m], rhs=id4[:m, 0:W4],
           start=True, stop=False, skip_group_check=True)
        for hh in range(4):
            op("pe", "matmul", (rA[0],), (rpP,), inc=(hh == 3), out=pP[:m, hh * m:(hh + 1) * m], lhsT=cx.nident_f[:m, :m],
               rhs=Ab[0][:m, hh * m:(hh + 1) * m], start=False, stop=False, skip_group_check=True)
        op("act", "activation", (rpP,), (rP[0],), out=Pb[0][:m, 0:W4], in_=pP[:m, 0:W4], func=AF.Copy)
        if cx.stop <= 4.6:
            continue
        nlev = NLEV if m > 64 else (5 if m > 32 else (4 if m > 16 else (3 if m > 8 else (2 if m > 4 else 1))))
        for lv in range(1, nlev + 1):
            a, bq = (lv - 1) % 2, lv % 2
            for hh in range(4):
                sl = slice(hh * m, (hh + 1) * m)
                op("pe", "matmul", (rA[a], rM[a]), (rps[4],), inc=(hh == 3), out=ps[4][:m, sl], lhsT=Ab[a][:m, sl],
                   rhs=Mb[a][:m, sl], start=True, stop=True, skip_group_check=True)
            op("act", "activation", (rps[4],), (rM[bq],), out=Mb[bq][:m, 0:W4], in_=ps[4][:m, 0:W4], func=AF.Copy)
            if lv < nlev:
                for hh in range(4):
                    sl = slice(hh * m, (hh + 1) * m)
                    op("pe", "matmul", (rA[a], rM[a]), (rps[5],), inc=(hh == 3), out=ps[5][:m, sl], lhsT=Mb[a][:m, sl],
                       rhs=Ab[a][:m, sl], start=True, stop=True, skip_group_check=True)
                op("dve", "tensor_copy", (rps[5],), (rA[bq],), out=Ab[bq][:m, 0:W4], in_=ps[5][:m, 0:W4])
            for hh in range(4):
                sl = slice(hh * m, (hh + 1) * m)
                op("pe", "matmul", (rM[bq], rP[a]), (rpP,), inc=(hh == 3), out=pP[:m, sl], lhsT=Mb[bq][:m, sl],
                   rhs=Pb[a][:m, sl], start=False, stop=(lv == nlev), skip_group_check=True)
            if lv % 2:
                op("dve", "tensor_copy", (rpP,), (rP[bq],), out=Pb[bq][:m, 0:W4], in_=pP[:m, 0:W4])
            else:
                op("act", "activation", (rpP,), (rP[bq],), out=Pb[bq][:m, 0:W4], in_=pP[:m, 0:W4], func=AF.Copy)
        if cx.stop <= 4.8:
            continue
        Pf, rPf = getattr(B, "Pf%d" % g), getattr(B, "r_Pf%d" % g)
        op("dve", "tensor_copy", (rpP,), (rPf,), out=Pf[:m, 0:W4], in_=pP[:m, 0:W4])
        if cx.stop <= 4.85:
            continue
        for hh in range(4):
            h = 2 * hh + g
            sl = slice(hh * m, (hh + 1) * m)
            op("pe", "matmul", (rPf, B.r_vtok), (rps[0],), inc=(hh == 3), out=ps[0][:m, h * 64:(h + 1) * 64], lhsT=Pf[:m, sl],
               rhs=B.vtok[:m, h * 64:(h + 1) * 64], start=True, stop=True, skip_group_check=True)
        if cx.stop <= 4.9:
            continue
        for hh in range(4):
            h = 2 * hh + g
            sl = slice(hh * m, (hh + 1) * m)
            po = (h % 2) * 64
            op("pe", "matmul", (rPf, B.r_ktok), (rps[1],), inc=(hh == 3),
               out=ps[1][po:po + 64, hh * 128:hh * 128 + m],
               lhsT=B.ktok[:m, h * 64:(h + 1) * 64], rhs=Pf[:m, sl], start=True, stop=True, skip_group_check=True)
        op("act", "activation", (rps[1],), (B.r_wT,), out=B.wT[g * 64:(g + 1) * 64, :, :m],
           in_=ps[1][g * 64:(g + 1) * 64, :].rearrange("p (h i) -> p h i", i=128)[:, :, :m], func=AF.Identity)
    if cx.stop <= 5:
        return
    op("dve", "tensor_copy", (rps[0],), (B.r_u,), out=B.u[:m, :], in_=ps[0][:m, :])
    def sdiag(t_, par):
        return t_[par * 64:(par + 1) * 64, :].rearrange("k (c x) -> k c x", x=128)[:, :, par * 64:(par + 1) * 64]

    if not prompt:
        op("dve", "memset", (), (B.r_S32,), ap=B.S32[:, :], constant=0.0)
        for par in range(2):
            S.dma("sp", dsem("s0"), sdiag(B.S32, par), grp["s0"][seq].rearrange("(c par) k v -> par k c v", par=2)[par],
                  writes=(B.r_S32,))
        op("act", "activation", (B.r_S32,), (B.r_Sbf,), out=B.Sbf[:, :], in_=B.S32[:, :], func=AF.Copy)
    for c in range(4):
        op("pe", "matmul", (B.r_wT, B.r_Sbf), (rps[0],), inc=(c == 3), out=ps[0][:m, c * 128:(c + 1) * 128],
           lhsT=B.wT[:, c, :m], rhs=B.Sbf[:, c * 128:(c + 1) * 128], start=True, stop=True, skip_group_check=True)
    op("dve", "tensor_tensor", (rps[0], B.r_u), (B.r_vnew,), out=B.vnew[:m, :], in0=B.u[:m, :], in1=ps[0][:m, :],
       op=ALU.subtract)
    for c in range(4):
        op("pe", "matmul", (B.r_qkT, B.r_Sbf), (rps[1],), inc=(c == 3), out=ps[1][:m, c * 128:(c + 1) * 128],
           lhsT=B.qkT[:, c, c0:c0 + m], rhs=B.Sbf[:, c * 128:(c + 1) * 128], start=True, stop=True, skip_group_check=True)
    op("dve", "tensor_tensor", (rps[1], rsc), (B.r_o1,), out=B.o1[:m, :].rearrange("p (h d) -> p h d", d=64),
       in0=ps[1][:m, :].rearrange("p (h d) -> p h d", d=64), in1=bc(sc[:m, 32:40], 64), op=ALU.mult)
    for h in range(8):
        op("pe", "matmul", (B.r_qk, B.r_vnew), (rps[0],), inc=(h == 7), out=ps[0][:m, h * 64:(h + 1) * 64],
           lhsT=B.qk[:m, h * m:(h + 1) * m], rhs=B.vnew[:m, h * 64:(h + 1) * 64], start=True, stop=True,
           skip_group_check=True)
    op("dve", "tensor_tensor", (rps[0], B.r_o1), (B.r_osb,), out=B.osb[:m, :], in0=ps[0][:m, :], in1=B.o1[:m, :], op=ALU.add)
    for c in range(4):
        op("pe", "matmul", (B.r_kdec, B.r_vnew), (rps[3],), inc=(c == 3), out=ps[3][:, c * 128:(c + 1) * 128],
           lhsT=B.kdec[:m, c * 128:(c + 1) * 128], rhs=B.vnew[:m, c * 128:(c + 1) * 128], start=True, stop=True,
           skip_group_check=True)
    op("dve", "tensor_tensor", (rps[3],), (B.r_decq,), out=B.decq[:, :].rearrange("p (c x) -> p c x", x=128),
       in0=ps[3][:, :].rearrange("p (c x) -> p c x", x=128), in1=bc_mid(cx.bones[:, :], 4), op=ALU.mult)
    op("dve", "tensor_tensor", (B.r_glb,), (B.r_S32,), out=B.S32[:, :].rearrange("p (c x) -> p c x", x=128),
       in0=B.S32[:, :].rearrange("p (c x) -> p c x", x=128), in1=bc(B.glb[:, 0:4], 128), op=ALU.mult)
    op("pool", "tensor_tensor", (B.r_decq,), (B.r_S32,), out=B.S32[:, :], in0=B.S32[:, :], in1=B.decq[:, :], op=ALU.add)
    if last:
        for par in range(2):
            S.dma("sp", dsem("s0"), grp["new_gdn"][seq].rearrange("(c par) k v -> par k c v", par=2)[par],
                  sdiag(B.S32, par), reads=(B.r_S32,), writes=(grp["r_out"],))
    else:
        op("act", "activation", (B.r_S32,), (B.r_Sbf,), out=B.Sbf[:, :], in_=B.S32[:, :], func=AF.Copy)
    if cx.stop <= 6:
        return
    op("pool", "tensor_tensor", (B.r_osb,), (B.r_o1,), out=B.o1[:m, :], in0=B.osb[:m, :], in1=B.osb[:m, :], op=ALU.mult)
    op("dve", "tensor_reduce", (B.r_o1,), (B.r_ost,), out=B.ost[:m, 0:8], in_=B.o1[:m, :].rearrange("p (h d) -> p h d", d=64),
       axis=AX.X, op=ALU.add)
    op("act", "activation", (B.r_ost,), (B.r_ost,), out=B.ost[:m, 8:16], in_=B.ost[:m, 0:8], func=AF.Sqrt, scale=1.0 / 64,
       bias=cx.eps_t[:m, 0:1])
    op("dve", "reciprocal", (B.r_ost,), (B.r_ost,), out=B.ost[:m, 8:16], in_=B.ost[:m, 8:16])
    op("act", "activation", (B.r_tm,), (B.r_zs,), out=B.zs[:m, :], in_=zr, func=AF.Silu)
    op("dve", "tensor_tensor", (B.r_osb, B.r_ost), (B.r_osb,), out=B.osb[:m, :].rearrange("p (h d) -> p h d", d=64),
       in0=B.osb[:m, :].rearrange("p (h d) -> p h d", d=64), in1=bc(B.ost[:m, 8:16], 64), op=ALU.mult)
    op("pool", "tensor_tensor", (B.r_osb, B.r_gng), (B.r_osb,), out=B.osb[:m, :].rearrange("p (h d) -> p h d", d=64),
       in0=B.osb[:m, :].rearrange("p (h d) -> p h d", d=64), in1=bc_mid(B.gng[:m, :], 8), op=ALU.mult)
    op("dve", "tensor_tensor", (B.r_osb, B.r_zs), (B.r_gout,), out=B.gout[:m, :], in0=B.osb[:m, :], in1=B.zs[:m, :], op=ALU.mult)
    S.dma("sp", dsem("gout"), grp["mix"][tok0:tok0 + m, 0:512], B.gout[:m, :], reads=(B.r_gout,), writes=(grp["r_mix"],))
    if cx.stop <= 7:
        return
    S.dma("sp", dsem("cs"), B.cs[:m, :], grp["cs"][(tok0 if prompt else 0):(tok0 if prompt else 0) + m, :], writes=(B.r_cs,))
    q3 = qraw.rearrange("p (h d) -> p h d", d=96)
    mq3 = B.mq[:m, :].rearrange("p (h d) -> p h d", d=96)
    op("pool", "tensor_tensor", (B.r_tm,), (B.r_mq,), out=B.mq[:m, :], in0=qraw, in1=qraw, op=ALU.mult)
    op("dve", "tensor_reduce", (B.r_mq,), (B.r_mqs,), out=B.mqs[:m, 0:8], in_=mq3[:, :, 0:64], axis=AX.X, op=ALU.add)
    op("dve", "tensor_reduce", (B.r_mq,), (B.r_mqs,), out=B.mqs[:m, 8:16], in_=mq3[:, :, 64:96], axis=AX.X, op=ALU.add)
    op("act", "activation", (B.r_mqs,), (B.r_mqs,), out=B.mqs[:m, 16:24], in_=B.mqs[:m, 0:8], func=AF.Sqrt, scale=1.0 / 64,
       bias=cx.eps_t[:m, 0:1])
    op("act", "activation", (B.r_mqs,), (B.r_mqs,), out=B.mqs[:m, 24:32], in_=B.mqs[:m, 8:16], func=AF.Sqrt, scale=1.0 / 32,
       bias=cx.eps_t[:m, 0:1])
    op("dve", "reciprocal", (B.r_mqs,), (B.r_mqs,), out=B.mqs[:m, 16:32], in_=B.mqs[:m, 16:32])
    op("dve", "tensor_tensor", (B.r_tm, B.r_mqs), (B.r_mq,), out=mq3[:, :, 0:64], in0=q3[:, :, 0:64], in1=bc(B.mqs[:m, 16:24], 64),
       op=ALU.mult)
    op("pool", "tensor_tensor", (B.r_mq, B.r_qng), (B.r_Qb,), out=B.Qb[:m, :].rearrange("p (h d) -> p h d", d=96)[:, :, 0:64],
       in0=mq3[:, :, 0:64], in1=bc_mid(B.qng[:m, :], 8), op=ALU.mult)
    op("dve", "tensor_tensor", (B.r_tm, B.r_mqs), (B.r_mq,), out=mq3[:, :, 64:96], in0=q3[:, :, 64:96], in1=bc(B.mqs[:m, 24:32], 32),
       op=ALU.mult)
    op("pool", "tensor_tensor", (B.r_mq, B.r_qrg), (B.r_mq,), out=mq3[:, :, 64:96], in0=mq3[:, :, 64:96],
       in1=bc_mid(B.qrg[:m, :], 8), op=ALU.mult)
    cosb, sinb = bc_mid(B.cs[:m, 0:16], 8), bc_mid(B.cs[:m, 16:32], 8)
    x1, x2 = mq3[:, :, 64:80], mq3[:, :, 80:96]
    Q3 = B.Qb[:m, :].rearrange("p (h d) -> p h d", d=96)
    op("dve", "tensor_tensor", (B.r_mq, B.r_cs), (B.r_qrr,), out=B.qrr[:m, :, 0:16], in0=x1, in1=cosb, op=ALU.mult)
    op("dve", "tensor_tensor", (B.r_mq, B.r_cs), (B.r_qrr,), out=B.qrr[:m, :, 16:32], in0=x2, in1=sinb, op=ALU.mult)
    op("dve", "tensor_tensor", (B.r_qrr,), (B.r_Qb,), out=Q3[:, :, 64:80], in0=B.qrr[:m, :, 0:16], in1=B.qrr[:m, :, 16:32],
       op=ALU.subtract)
    op("dve", "tensor_tensor", (B.r_mq, B.r_cs), (B.r_qrr,), out=B.qrr[:m, :, 0:16], in0=x1, in1=sinb, op=ALU.mult)
    op("dve", "tensor_tensor", (B.r_mq, B.r_cs), (B.r_qrr,), out=B.qrr[:m, :, 16:32], in0=x2, in1=cosb, op=ALU.mult)
    op("dve", "tensor_tensor", (B.r_qrr,), (B.r_Qb,), out=Q3[:, :, 80:96], in0=B.qrr[:m, :, 0:16], in1=B.qrr[:m, :, 16:32],
       op=ALU.add)
    S.dma("sp", dsem("Qb"), grp["Q"][tok0:tok0 + m, :], B.Qb[:m, :], reads=(B.r_Qb,), writes=(grp["r_Q"],))
    op("pool", "tensor_tensor", (B.r_tm,), (B.r_cn,), out=B.cn[:m, :], in0=ckvr, in1=ckvr, op=ALU.mult)
    op("dve", "tensor_reduce", (B.r_cn,), (B.r_mqs,), out=B.mqs[:m, 0:1], in_=B.cn[:m, :], axis=AX.X, op=ALU.add)
    op("act", "activation", (B.r_mqs,), (B.r_mqs,), out=B.mqs[:m, 1:2], in_=B.mqs[:m, 0:1], func=AF.Sqrt, scale=1.0 / 128,
       bias=cx.eps_t[:m, 0:1])
    op("dve", "reciprocal", (B.r_mqs,), (B.r_mqs,), out=B.mqs[:m, 1:2], in_=B.mqs[:m, 1:2])
    op("dve", "scalar_tensor_tensor", (B.r_tm, B.r_mqs, B.r_ckg), (B.r_cn,), out=B.cn[:m, :], in0=ckvr, scalar=B.mqs[:m, 1:2],
       in1=B.ckg[:m, :], op0=ALU.mult, op1=ALU.mult)
    S.dma("sp", dsem("cn"), grp["new_ckv"][tok0:tok0 + m, :], B.cn[:m, :], reads=(B.r_cn,), writes=(grp["r_ckv"],))
    op("pool", "tensor_tensor", (B.r_tm,), (B.r_krn,), out=B.krn[:m, :], in0=krr_, in1=krr_, op=ALU.mult)
    op("dve", "tensor_reduce", (B.r_krn,), (B.r_mqs,), out=B.mqs[:m, 2:3], in_=B.krn[:m, :], axis=AX.X, op=ALU.add)
    op("act", "activation", (B.r_mqs,), (B.r_mqs,), out=B.mqs[:m, 3:4], in_=B.mqs[:m, 2:3], func=AF.Sqrt, scale=1.0 / 32,
       bias=cx.eps_t[:m, 0:1])
    op("dve", "reciprocal", (B.r_mqs,), (B.r_mqs,), out=B.mqs[:m, 3:4], in_=B.mqs[:m, 3:4])
    op("dve", "scalar_tensor_tensor", (B.r_tm, B.r_mqs, B.r_krg), (B.r_krn,), out=B.krn[:m, :], in0=krr_, scalar=B.mqs[:m, 3:4],
       in1=B.krg[:m, :], op0=ALU.mult, op1=ALU.mult)
    k1, k2, cs1, sn1 = B.krn[:m, 0:16], B.krn[:m, 16:32], B.cs[:m, 0:16], B.cs[:m, 16:32]
    op("dve", "tensor_tensor", (B.r_krn, B.r_cs), (B.r_krr,), out=B.krr[:m, 0:16], in0=k1, in1=cs1, op=ALU.mult)
    op("dve", "tensor_tensor", (B.r_krn, B.r_cs), (B.r_krr,), out=B.krr[:m, 16:32], in0=k2, in1=sn1, op=ALU.mult)
    op("dve", "tensor_tensor", (B.r_krr,), (B.r_qrr,), out=B.qrr[:m, 0, 0:16], in0=B.krr[:m, 0:16], in1=B.krr[:m, 16:32],
       op=ALU.subtract)
    op("dve", "tensor_tensor", (B.r_krn, B.r_cs), (B.r_krr,), out=B.krr[:m, 0:16], in0=k1, in1=sn1, op=ALU.mult)
    op("dve", "tensor_tensor", (B.r_krn, B.r_cs), (B.r_krr,), out=B.krr[:m, 16:32], in0=k2, in1=cs1, op=ALU.mult)
    op("dve", "tensor_tensor", (B.r_krr,), (B.r_qrr,), out=B.qrr[:m, 0, 16:32], in0=B.krr[:m, 0:16], in1=B.krr[:m, 16:32],
       op=ALU.add)
    S.dma("sp", dsem("kr"), grp["new_kr"][tok0:tok0 + m, :], B.qrr[:m, 0, :], reads=(B.r_qrr,), writes=(grp["r_kr"],))


def phase_mixer(cx, S, groups, P):
    with ExitStack() as st:
        B = mixer_alloc(cx, S, st, "mx", P)
        mc = sb(cx, st, "mx_cst", [128, CST_COLS - 128], F32)
        r_mc = Res("mxcst")
        S.dma("sp", S.dsem("mxcst"), mc[:, :], cx.cst[:, 128:CST_COLS], writes=(r_mc,))
        cx.utri = mc[:, 0:128]
        cx.onesf = mc[:, 128:256]
        cx.ohsel = mc[0:8, 384:1408].rearrange("p (h i) -> p h i", i=128)
        cx.mconst = {}
        off = 1408
        for m_ in (64, 4):
            cx.mconst[m_] = (mc[:, off + 4 * m_:off + 8 * m_], mc[:, off + 8 * m_:off + 12 * m_], mc[:, off:off + 4 * m_])
            off += 12 * m_
        idf = sb(cx, st, "mx_idf", [128, 128], F32)
        S.op("dve", "tensor_scalar", dict(out=idf[:, :], in0=cx.ident_f, scalar1=-1.0, scalar2=None, op0=ALU.mult),
             writes=(r_mc,))
        cx.nident_f = idf[:, :]
        dc = S.dsem("mxc")
        r_c = Res("mxconst")
        S.dma("sp", dc, B.cw[:, :, :], P["conv_w_fm"], writes=(B.r_cw,))
        S.dma("sp", dc, B.g8[:, 0:8], row_bcast(P["a_log"], 128), writes=(B.r_g8,))
        S.dma("sp", dc, B.g8[:, 8:16], row_bcast(P["dt_bias"], 128), writes=(B.r_g8,))
        S.dma("sp", dc, B.gng[:, :], row_bcast(P["gdn_norm"], 128), writes=(B.r_gng,))
        S.dma("sp", dc, B.qng[:, :], row_bcast(P["qn_g"], 128), writes=(B.r_qng,))
        S.dma("sp", dc, B.kng[:, :], row_bcast(P["kn_g"], 128), writes=(B.r_kng,))
        S.dma("sp", dc, B.qrg[:, :], row_bcast(P["qr_g"], 128), writes=(B.r_qrg,))
        S.dma("sp", dc, B.ckg[:, :], row_bcast(P["ckv_g"], 128), writes=(B.r_ckg,))
        S.dma("sp", dc, B.krg[:, :], row_bcast(P["kr_g"], 128), writes=(B.r_krg,))
        S.barrier()
        S.op("act", "activation", dict(out=B.g8[:, 0:8], in_=B.g8[:, 0:8], func=AF.Exp), writes=(B.r_g8,))
        S.op("dve", "scalar_tensor_tensor", dict(out=B.qng[:, :], in0=B.qng[:, :], scalar=ATT_SCALE, in1=B.kng[:, :],
                                                 op0=ALU.mult, op1=ALU.mult), writes=(B.r_qng,))
        S.op("dve", "tensor_scalar", dict(out=B.qrg[:, :], in0=B.qrg[:, :], scalar1=ATT_SCALE, scalar2=None, op0=ALU.mult),
             writes=(B.r_qrg,))
        S.barrier()
        for gi, grp in enumerate(groups):
            mixer_group(cx, S, B, "mx%d" % gi, grp, P)
            S.barrier()
            S.emit()


def phase_wout(cx, S, groups, w_out_ap):
    with ExitStack() as st:
        wo = sb(cx, st, "wo_w", [128, 8, D], BF16)
        r_wo = Res()
        load_weight_bf16(cx, S, "wow", wo, r_wo, w_out_ap, 8, D, D)
        Gt = sb(cx, st, "wo_Gt", [128, D], F32)
        mx = [sb(cx, st, "wo_mx%d" % i, [128, D], BF16) for i in range(2)]
        mT = sb(cx, st, "wo_mT", [128, 8, 128], BF16)
        xr = [sb(cx, st, "wo_xr%d" % i, [128, D], F32) for i in range(2)]
        tmp = sb(cx, st, "wo_tmp", [128, 512], F32)
        r_Gt, r_mT, r_tmp = Res(), Res(), Res()
        r_mx, r_xr = [Res(), Res()], [Res(), Res()]
        d_g = S.dsem("wo_g")
        d_mx = [S.dsem("wo_mx0"), S.dsem("wo_mx1")]
        d_xr = [S.dsem("wo_xr0"), S.dsem("wo_xr1")]
        d_xst = [S.dsem("wo_xst0"), S.dsem("wo_xst1")]
        i = 0
        for g in groups:
            n, T = g["n"], g["T"]
            S.dma("sp", d_g, Gt[:n, :], g["mods"][5, :n, :], writes=(r_Gt,))
            for r0 in range(0, T, 128):
                m = min(128, T - r0)
                k = i % 2
                i += 1
                S.dma("sp", d_mx[k], mx[k][:m, :], g["mix"][r0:r0 + m, :], reads=(g["r_mix"],), writes=(r_mx[k],))
                S.dma("sp", d_xr[k], xr[k][:m, :], g["src"][r0:r0 + m, :], reads=(g["src_res"][r0 // 128],),
                      writes=(r_xr[k],))
                ptb = cx.ps[6 + k].bitcast(BF16)
                for c in range(8):
                    S.op("pe", "transpose", dict(out=ptb[:, c * 128:c * 128 + m], in_=mx[k][:m, c * 128:(c + 1) * 128],
                                                 identity=cx.ident_bf[:m, :m]), reads=(r_mx[k],), writes=(cx.rps[6 + k],),
                         inc=(c == 7))
                S.op("act", "activation", dict(out=mT[:, :, :m], in_=ptb.rearrange("p (c t) -> p c t", c=8)[:, :, :m],
                                               func=AF.Copy), reads=(cx.rps[6 + k],), writes=(r_mT,))
                for half in range(2):
                    pk = (i * 2 + half) % 4
                    for c in range(8):
                        S.op("pe", "matmul", dict(out=cx.ps[pk][:m, :], lhsT=mT[:, c, :m],
                                                  rhs=wo[:, c, half * 512:(half + 1) * 512], start=(c == 0), stop=(c == 7)),
                             reads=(r_mT, r_wo), writes=(cx.rps[pk],), inc=(c == 7))
                    S.op("dve", "tensor_tensor", dict(out=tmp[:m, :], in0=cx.ps[pk][:m, :],
                                                      in1=Gt[:m, half * 512:(half + 1) * 512], op=ALU.mult),
                         reads=(cx.rps[pk], r_Gt), writes=(r_tmp,))
                    S.op("pool", "tensor_tensor", dict(out=xr[k][:m, half * 512:(half + 1) * 512],
                                                       in0=xr[k][:m, half * 512:(half + 1) * 512], in1=tmp[:m, :],
                                                       op=ALU.add), reads=(r_tmp,), writes=(r_xr[k],))
                S.dma("pool", d_xst[k], g["dst"][r0:r0 + m, :], xr[k][:m, :], reads=(r_xr[k],),
                      writes=(g["dst_res"][r0 // 128],))
        S.barrier()
        S.emit()


class AttBufs:
    pass


def att_alloc(cx, S, st, name, P):
    A = AttBufs()

    def t(nm, shape, dt):
        tt_ = sb(cx, st, name + nm, shape, dt)
        setattr(A, nm, tt_)
        setattr(A, "r_" + nm, Res(nm))
        return tt_

    t("wuk", [128, 512], BF16)
    t("wuv", [128, 512], BF16)
    t("wst", [128, 512], F32)
    d = S.dsem(name + "w")
    S.dma("sp", d, A.wst[:, :], P["w_uk"], writes=(A.r_wst,))
    S.op("dve", "tensor_copy", dict(out=A.wuk[:, :], in_=A.wst[:, :]), reads=(A.r_wst,), writes=(A.r_wuk,))
    S.dma("sp", d, A.wst[:, :], P["w_uv"], writes=(A.r_wst,))
    S.op("dve", "tensor_copy", dict(out=A.wuv[:, :], in_=A.wst[:, :]), reads=(A.r_wst,), writes=(A.r_wuv,))
    for k_ in range(2):
        t("cTb%d" % k_, [128, 128], BF16)
        t("sq%d" % k_, [128, 512], F32)
        t("kst%d" % k_, [128, 16], F32)
        t("Kf%d" % k_, [128, 8, 96], BF16)
        t("KT%d" % k_, [96, 8, 128], BF16)
    t("QT", [96, 8, 128], BF16)
    t("Qb", [128, 768], BF16)
    t("PT", [128, 512], BF16)
    t("ctx", [128, 8, 128], BF16)
    t("rden", [128, 8], F32)
    t("ctxT", [128, 8, 128], BF16)
    t("mo", [128, 512], BF16)
    t("m01", [128, 128], BF16)
    A.d = {}
    return A


def att_kside(cx, S, A, c_blk, kr_blk, r_src, n, KT_dst, r_KT, k=0, stage=None):
    ps, rps = cx.ps, cx.rps
    cTb, sq, kst, Kf = (getattr(A, nm + str(k)) for nm in ("cTb", "sq", "kst", "Kf"))
    r_cTb, r_sq, r_kst, r_Kf = (getattr(A, "r_" + nm + str(k)) for nm in ("cTb", "sq", "kst", "Kf"))
    p0, p1, p2 = (0, 1, 2) if k == 0 else (6, 7, 3)

    def op(eng, meth, reads, writes, inc=True, **kw):
        S.op(eng, meth, kw, reads=reads, writes=writes, inc=inc)

    if stage == "B":
        ptb = ps[p2].bitcast(BF16)
        for h in range(8):
            op("pe", "transpose", (r_Kf,), (rps[p2],), inc=(h == 7), out=ptb[0:96, h * 128:h * 128 + n], in_=Kf[:n, h, :],
               identity=cx.ident_bf[:n, :n])
        op("act", "activation", (rps[p2],), (r_KT,), out=KT_dst, in_=ptb[0:96, :].rearrange("p (h t) -> p h t", t=128)[:, :, :n],
           func=AF.Copy)
        return
    op("pe", "transpose", (r_src,), (rps[p0],), out=ps[p0][:, :n], in_=c_blk, identity=cx.ident_f[:n, :n])
    op("act", "activation", (rps[p0],), (r_cTb,), out=cTb[:, :n], in_=ps[p0][:, :n], func=AF.Identity)
    op("pe", "matmul", (r_cTb, A.r_wuk), (rps[p1],), out=ps[p1][:n, :], lhsT=cTb[:, :n], rhs=A.wuk[:, :], start=True, stop=True)
    op("act", "activation", (rps[p1],), (r_sq,), out=sq[:n, :], in_=ps[p1][:n, :], func=AF.Square)
    op("dve", "tensor_reduce", (r_sq,), (r_kst,), out=kst[:n, 0:8], in_=sq[:n, :].rearrange("p (h d) -> p h d", d=64),
       axis=AX.X, op=ALU.add)
    op("act", "activation", (r_kst,), (r_kst,), out=kst[:n, 8:16], in_=kst[:n, 0:8], func=AF.Sqrt, scale=1.0 / 64,
       bias=cx.eps_t[:n, 0:1])
    op("dve", "reciprocal", (r_kst,), (r_kst,), out=kst[:n, 8:16], in_=kst[:n, 8:16])
    op("dve", "tensor_tensor", (rps[p1], r_kst), (r_Kf,), out=Kf[:n, :, 0:64],
       in0=ps[p1][:n, :].rearrange("p (h d) -> p h d", d=64), in1=bc(kst[:n, 8:16], 64), op=ALU.mult)
    op("pool", "tensor_copy", (r_src,), (r_Kf,), out=Kf[:n, :, 64:96], in_=bc_mid(kr_blk, 8))
    if stage == "A":
        return
    ptb = ps[p2].bitcast(BF16)
    for h in range(8):
        op("pe", "transpose", (r_Kf,), (rps[p2],), inc=(h == 7), out=ptb[0:96, h * 128:h * 128 + n], in_=Kf[:n, h, :],
           identity=cx.ident_bf[:n, :n])
    op("act", "activation", (rps[p2],), (r_KT,), out=KT_dst, in_=ptb[0:96, :].rearrange("p (h t) -> p h t", t=128)[:, :, :n],
       func=AF.Copy)


def att_out(cx, S, A, nq, r_ctx_in, mix_dst, r_mix, dsem_):
    ps, rps = cx.ps, cx.rps
    ptb = ps[2].bitcast(BF16)
    for h in range(8):
        S.op("pe", "transpose", dict(out=ptb[:, h * 128:h * 128 + nq], in_=A.ctx[:nq, h, :], identity=cx.ident_bf[:nq, :nq]),
             reads=(A.r_ctx,), writes=(rps[2],), inc=(h == 7))
    S.op("act", "activation", dict(out=A.ctxT[:, :, :nq], in_=ptb.rearrange("p (h t) -> p h t", t=128)[:, :, :nq], func=AF.Copy),
         reads=(rps[2],), writes=(A.r_ctxT,))
    for h in range(8):
        S.op("pe", "matmul", dict(out=ps[3][:nq, h * 64:(h + 1) * 64], lhsT=A.ctxT[:, h, :nq], rhs=A.wuv[:, h * 64:(h + 1) * 64],
                                  start=True, stop=True, skip_group_check=True), reads=(A.r_ctxT, A.r_wuv), writes=(rps[3],),
             inc=(h == 7))
    S.op("dve", "tensor_copy", dict(out=A.mo[:nq, :], in_=ps[3][:nq, :]), reads=(rps[3],), writes=(A.r_mo,))
    S.dma("sp", dsem_, mix_dst, A.mo[:nq, :], reads=(A.r_mo,), writes=(r_mix,))


def phase_attn_prompt(cx, S, grp, P):
    T = grp["T"]
    nb = T // 128
    with ExitStack() as st:
        A = att_alloc(cx, S, st, "ap", P)
        KTall = sb(cx, st, "ap_KTall", [96, 8, T], BF16)
        caug = sb(cx, st, "ap_caug", [128, nb, 132], BF16)
        cst_ = [sb(cx, st, "ap_cst%d" % i, [128, 160], F32) for i in range(2)]
        r_KTall, r_caug = Res(), Res()
        r_cst = [Res(), Res()]
        d_cst = [S.dsem("ap_c0"), S.dsem("ap_c1")]
        d_q, d_o = S.dsem("ap_q"), S.dsem("ap_o")
        ps, rps = cx.ps, cx.rps
        S.op("pool", "memset", dict(ap=caug[:, :, 128:132], constant=1.0), writes=(r_caug,))
        S.op("pool", "memset", dict(ap=A.m01[:, :], constant=1.0), writes=(A.r_m01,))
        S.op("pool", "affine_select", dict(out=A.m01[:, :], in_=A.m01[:, :], pattern=[[1, 128]], compare_op=ALU.is_ge, fill=0.0,
                                           base=0, channel_multiplier=-1), writes=(A.r_m01,))
        for kb in range(nb):
            k = kb % 2
            S.dma("sp", d_cst[k], cst_[k][:, 0:128], grp["new_ckv"][kb * 128:(kb + 1) * 128, :], reads=(grp["r_ckv"],),
                  writes=(r_cst[k],))
            S.dma("sp", d_cst[k], cst_[k][:, 128:160], grp["new_kr"][kb * 128:(kb + 1) * 128, :], reads=(grp["r_kr"],),
                  writes=(r_cst[k],))
            S.op("dve", "tensor_copy", dict(out=caug[:, kb, 0:128], in_=cst_[k][:, 0:128]), reads=(r_cst[k],), writes=(r_caug,))
            att_kside(cx, S, A, cst_[k][:, 0:128], cst_[k][:, 128:160], r_cst[k], 128, KTall[:, :, kb * 128:(kb + 1) * 128], r_KTall, k=k)
        for qb in range(nb):
            S.dma("sp", d_q, A.Qb[:, :], grp["Q"][qb * 128:(qb + 1) * 128, :], reads=(grp["r_Q"],), writes=(A.r_Qb,))
            ptb = ps[2].bitcast(BF16)
            for h in range(8):
                S.op("pe", "transpose", dict(out=ptb[0:96, h * 128:(h + 1) * 128], in_=A.Qb[:, h * 96:(h + 1) * 96],
                                             identity=cx.ident_bf[:, :]), reads=(A.r_Qb,), writes=(rps[2],), inc=(h == 7))
            S.op("act", "activation", dict(out=A.QT[:, :, :], in_=ptb[0:96, :].rearrange("p (h t) -> p h t", t=128), func=AF.Copy),
                 reads=(rps[2],), writes=(A.r_QT,))
            gi = 0
            for h in range(8):
                pc = 4 + h % 2
                for kb0 in range(0, qb + 1, 4):
                    ng = min(4, qb + 1 - kb0)
                    pk = gi % 2
                    gi += 1
                    for j in range(ng):
                        kb = kb0 + j
                        S.op("pe", "matmul", dict(out=ps[pk][:, j * 128:(j + 1) * 128], lhsT=KTall[:, h, kb * 128:(kb + 1) * 128],
                                                  rhs=A.QT[:, h, :], start=True, stop=True, skip_group_check=True),
                             reads=(r_KTall, A.r_QT), writes=(rps[pk],), inc=(j == ng - 1))
                    S.op("act", "activation", dict(out=A.PT[:, 0:ng * 128], in_=ps[pk][:, 0:ng * 128], func=AF.Exp),
                         reads=(rps[pk],), writes=(A.r_PT,))
                    if kb0 + ng - 1 == qb:
                        j = ng - 1
                        S.op("dve", "tensor_tensor", dict(out=A.PT[:, j * 128:(j + 1) * 128], in0=A.PT[:, j * 128:(j + 1) * 128],
                                                          in1=A.m01[:, :], op=ALU.mult), reads=(A.r_m01,), writes=(A.r_PT,))
                    for j in range(ng):
                        kb = kb0 + j
                        S.op("pe", "matmul", dict(out=ps[pc][:, 0:129], lhsT=A.PT[:, j * 128:(j + 1) * 128], rhs=caug[:, kb, 0:129],
                                                  start=(kb == 0), stop=(kb == qb), skip_group_check=True),
                             reads=(A.r_PT, r_caug), writes=(rps[pc],), inc=(j == ng - 1))
                S.op("dve", "reciprocal", dict(out=A.rden[:, h:h + 1], in_=ps[pc][:, 128:129]), reads=(rps[pc],), writes=(A.r_rden,))
                S.op("dve", "tensor_scalar", dict(out=A.ctx[:, h, :], in0=ps[pc][:, 0:128], scalar1=A.rden[:, h:h + 1], scalar2=None,
                                                  op0=ALU.mult), reads=(rps[pc], A.r_rden), writes=(A.r_ctx,))
            att_out(cx, S, A, 128, A.r_ctx, grp["mix"][qb * 128:(qb + 1) * 128, 512:1024], grp["r_mix"], d_o)
        S.barrier()
        S.emit()


def phase_attn_sample(cx, S, grp, P, cache_c, cache_k, ptab, nseq, npages):
    with ExitStack() as st:
        A = att_alloc(cx, S, st, "as", P)
        cg = sb(cx, st, "as_cg", [128, 128 * 128], F32)
        kg = sb(cx, st, "as_kg", [128, 128 * 32], F32)
        idx = sb(cx, st, "as_idx", [128, 2], I32)
        cb2 = [sb(cx, st, "as_cb%d" % i, [128, 132], BF16) for i in range(3)]
        cn = sb(cx, st, "as_cn", [4, 160], F32)
        qs = sb(cx, st, "as_qs", [4, 768], BF16)
        QTs = sb(cx, st, "as_QTs", [96, 8, 4], BF16)
        PTs2 = [sb(cx, st, "as_PTs%d" % i, [128, 32], BF16) for i in range(2)]
        ms = sb(cx, st, "as_ms", [4, 32], BF16)
        c32 = sb(cx, st, "as_c32", [32, 128], BF16)
        cT32 = sb(cx, st, "as_cT32", [128, 32], BF16)
        rd = sb(cx, st, "as_rd", [32, 1], F32)
        mo = sb(cx, st, "as_mo", [4, 512], BF16)
        r_cg, r_kg, r_idx, r_cb, r_cn, r_qs, r_QTs, r_PTs, r_ms, r_c32, r_cT32, r_rd, r_mo = (Res() for _ in range(13))
        d_idx, d_cg, d_kg, d_cn, d_qs, d_mo = (S.dsem("as%d" % i) for i in range(6))
        ps, rps = cx.ps, cx.rps
        r_cb2, r_PTs2, r_sc = [Res(), Res(), Res()], [Res(), Res()], [Res(), Res()]
        for i_ in range(3):
            S.op("pool", "memset", dict(ap=cb2[i_][:, 128:132], constant=1.0), writes=(r_cb2[i_],))
        S.op("pool", "memset", dict(ap=ms[:, :], constant=1.0), writes=(r_ms,))
        S.op("pool", "affine_select", dict(out=ms[:, :], in_=ms[:, :], pattern=[[0, 8], [1, 4]], compare_op=ALU.is_ge, fill=0.0,
                                           base=0, channel_multiplier=-1), writes=(r_ms,))
        for s in range(nseq):
            S.dma("sp", d_idx, idx[:npages, 0:1], ptab[s:s + 1, :].rearrange("o p -> p o"), writes=(r_idx,),
                  allow_slow_non_contiguous=True)
            S.idma(d_cg, reads=(r_idx,), writes=(r_cg,), out=cg[:npages, :], out_offset=None, in_=cache_c,
                   in_offset=bass.IndirectOffsetOnAxis(ap=idx[:npages, 0:1], axis=0))
            S.idma(d_cg, reads=(r_idx,), writes=(r_cg, r_kg), out=kg[:npages, :], out_offset=None, in_=cache_k,
                   in_offset=bass.IndirectOffsetOnAxis(ap=idx[:npages, 0:1], axis=0))
            S.dma("sp", d_qs, qs[:, :], grp["Q"][4 * s:4 * s + 4, :], reads=(grp["r_Q"],), writes=(r_qs,))
            ptb = ps[2].bitcast(BF16)
            for h in range(8):
                S.op("pe", "transpose", dict(out=ptb[0:96, h * 128:h * 128 + 4], in_=qs[:, h * 96:(h + 1) * 96],
                                             identity=cx.ident_bf[:4, :4]), reads=(r_qs,), writes=(rps[2],), inc=(h == 7))
            S.op("act", "activation", dict(out=QTs[:, :, :], in_=ptb[0:96, :].rearrange("p (h t) -> p h t", t=128)[:, :, 0:4],
                                           func=AF.Copy), reads=(rps[2],), writes=(r_QTs,))
            S.dma("sp", d_cn, cn[:, 0:128], grp["new_ckv"][4 * s:4 * s + 4, :], reads=(grp["r_ckv"],), writes=(r_cn,))
            S.dma("sp", d_cn, cn[:, 128:160], grp["new_kr"][4 * s:4 * s + 4, :], reads=(grp["r_kr"],), writes=(r_cn,))
            nblk = 128 + 1

            def blk_args(t):
                if t < 128:
                    n = npages
                    return n, cg[:n, t * 128:(t + 1) * 128], kg[:n, t * 32:(t + 1) * 32], r_cg, (r_cg, r_kg)
                return 4, cn[:, 0:128], cn[:, 128:160], r_cn, (r_cn,)

            def stage_a(t):
                n, c_blk, k_blk, r_src, rs = blk_args(t)
                kk = t % 2
                S.op("pool", "tensor_copy", dict(out=cb2[t % 3][:n, 0:128], in_=c_blk), reads=rs, writes=(r_cb2[t % 3],))
                att_kside(cx, S, A, c_blk, k_blk, r_src, n, None, None, k=kk, stage="A")

            def stage_b1(t):
                n, c_blk, k_blk, r_src, rs = blk_args(t)
                kk = t % 2
                KT, r_KT = (A.KT0, A.r_KT0) if kk == 0 else (A.KT1, A.r_KT1)
                att_kside(cx, S, A, c_blk, k_blk, r_src, n, KT[:, :, :n], r_KT, k=kk, stage="B")

            def stage_b2(t):
                n, c_blk, k_blk, r_src, rs = blk_args(t)
                kk = t % 2
                cb, r_cb, PTs, r_PTs = cb2[t % 3], r_cb2[t % 3], PTs2[kk], r_PTs2[kk]
                KT, r_KT = (A.KT0, A.r_KT0) if kk == 0 else (A.KT1, A.r_KT1)
                pb = 4 if kk == 0 else 6
                for h in range(8):
                    S.op("pe", "matmul", dict(out=ps[pb][:n, h * 4:(h + 1) * 4], lhsT=KT[:, h, :n], rhs=QTs[:, h, :],
                                              start=True, stop=True, skip_group_check=True), reads=(r_KT, r_QTs),
                         writes=(rps[pb],), inc=(h == 7))
                S.op("act", "activation", dict(out=PTs[:n, :], in_=ps[pb][:n, 0:32], func=AF.Exp), reads=(rps[pb],),
                     writes=(r_PTs,))
                if t == 128:
                    S.op("dve", "tensor_tensor", dict(out=PTs[:n, :], in0=PTs[:n, :], in1=ms[:n, :], op=ALU.mult),
                         reads=(r_ms,), writes=(r_PTs,))
                S.op("pe", "matmul", dict(out=ps[5][0:32, 0:129], lhsT=PTs[:n, :], rhs=cb[:n, 0:129], start=(t == 0),
                                          stop=(t == nblk - 1), skip_group_check=True), reads=(r_PTs, r_cb), writes=(rps[5],))

            stage_a(0)
            for t in range(nblk):
                if t + 1 < nblk:
                    stage_a(t + 1)
                stage_b1(t)
                if t >= 1:
                    stage_b2(t - 1)
            stage_b2(nblk - 1)
            S.op("dve", "reciprocal", dict(out=rd[:, :], in_=ps[5][0:32, 128:129]), reads=(rps[5],), writes=(r_rd,))
            S.op("dve", "tensor_scalar", dict(out=c32[:, :], in0=ps[5][0:32, 0:128], scalar1=rd[:, 0:1], scalar2=None,
                                              op0=ALU.mult), reads=(rps[5], r_rd), writes=(r_c32,))
            S.op("pe", "transpose", dict(out=ptb[:, 0:32], in_=c32[:, :], identity=cx.ident_bf[:32, :32]), reads=(r_c32,),
                 writes=(rps[2],))
            S.op("act", "activation", dict(out=cT32[:, :], in_=ptb[:, 0:32], func=AF.Copy), reads=(rps[2],), writes=(r_cT32,))
            for h in range(8):
                S.op("pe", "matmul", dict(out=ps[3][0:4, h * 64:(h + 1) * 64], lhsT=cT32[:, h * 4:(h + 1) * 4],
                                          rhs=A.wuv[:, h * 64:(h + 1) * 64], start=True, stop=True, skip_group_check=True),
                     reads=(r_cT32, A.r_wuv), writes=(rps[3],), inc=(h == 7))
            S.op("dve", "tensor_copy", dict(out=mo[:, :], in_=ps[3][0:4, :]), reads=(rps[3],), writes=(r_mo,))
            S.dma("sp", d_mo, grp["mix"][4 * s:4 * s + 4, 512:1024], mo[:, :], reads=(r_mo,), writes=(grp["r_mix"],))
            if s % 4 == 3:
                S.barrier()
                S.emit()
        S.barrier()
        S.emit()


CST_COLS = 128 + 128 + 128 + 128 + 1024 + 3 * 256 + 3 * 16


def host_consts():
    c = np.zeros((128, CST_COLS), np.float32)
    j = np.arange(128)[:, None]
    i = np.arange(128)[None, :]
    same = (j // 64 == i // 64)
    c[:, 0:128] = np.eye(128)
    c[:, 128:256] = (j <= i) & same
    c[:, 256:384] = same
    c[:, 384:512] = same
    oh = np.zeros((8, 8, 128), np.float32)
    for h in range(8):
        oh[h, h, :] = 1.0
    c[0:8, 512:1536] = oh.reshape(8, 1024)
    off = 1536
    for m in (64, 4):
        jj = np.arange(m)[:, None]
        ii = np.arange(m)[None, :]
        c[0:m, off:off + 4 * m] = np.tile(np.eye(m, dtype=np.float32), (1, 4))
        c[0:m, off + 4 * m:off + 8 * m] = np.tile(np.where(jj <= ii, 0.0, -1e30).astype(np.float32), (1, 4))
        c[0:m, off + 8 * m:off + 12 * m] = np.tile(np.where(jj < ii, 0.0, -1e30).astype(np.float32), (1, 4))
        off += 12 * m
    return c


def host_rope_table(pos):
    half = 16
    inv = (np.float32(10000.0) ** (-(np.arange(half, dtype=np.float32) / np.float32(half)))).astype(np.float32)
    ang = (pos.astype(np.float32)[:, None] * inv[None, :]).astype(np.float32)
    return np.concatenate([np.cos(ang), np.sin(ang)], axis=1).astype(np.float32)


def build(cfg):
    TP = cfg["TP"]
    NS = cfg["NS"]
    NSEQ = NS // 4
    phases = cfg.get("phases", ("adaln", "ffn1"))
    nc = bass.Bass("TRN2", target_bir_lowering=False)
    cx = Ctx()
    cx.nc = nc
    cx.stop = cfg.get("stop", 99)

    def din(name, shape, dt=F32):
        return nc.dram_tensor(name, list(shape), dt, kind="ExternalInput").ap()

    def dout(name, shape, dt=F32):
        return nc.dram_tensor(name, list(shape), dt, kind="ExternalOutput").ap()

    def dscr(name, shape, dt=F32):
        return nc.dram_tensor(name, list(shape), dt, kind="Internal").ap()

    xp = din("xp", [TP, D])
    xs_in = din("xs", [NS, D])
    c_rep = din("c_rep", [192, D])
    cst = din("cst", [128, CST_COLS])
    ada_w = din("ada_w", [D, 9 * D])
    ada_b = din("ada_b", [1, 9 * D])
    norm_ffn1 = din("norm_ffn1", [1, D])
    ffn1_wi = din("ffn1_wi", [D, 2 * DFF])
    ffn1_wo = din("ffn1_wo", [DFF, D])
    P = dict(norm_mix=din("norm_mix", [1, D]), w_in=din("w_in", [D, 2992]), conv_w_fm=din("conv_w_fm", [128, 12, 4]),
             a_log=din("a_log", [1, 8]), dt_bias=din("dt_bias", [1, 8]), gdn_norm=din("gdn_norm", [1, 64]),
             qn_g=din("qn_g", [1, 64]), qr_g=din("qr_g", [1, 32]), ckv_g=din("ckv_g", [1, 128]), kr_g=din("kr_g", [1, 32]),
             kn_g=din("kn_g", [1, 64]))
    P["w_uk"] = din("w_uk", [128, 512])
    P["w_uv"] = din("w_uv", [128, 512])
    w_out = din("w_out", [D, D])
    norm_ffn2 = din("norm_ffn2", [1, D])
    ffn2_wi = din("ffn2_wi", [D, 2 * DFF])
    ffn2_wo = din("ffn2_wo", [DFF, D])
    NPOOL = cfg.get("npool", 20480)
    cache_c = din("cache_c", [NPOOL, 128 * 128])
    cache_k = din("cache_k", [NPOOL, 128 * 32])
    ptab = din("ptab", [NSEQ, cfg.get("npages", 128)], I32)
    cs_p = din("cs_p", [TP, 32])
    cs_s = din("cs_s", [4, 32])
    st_conv = din("st_conv", [NSEQ, 3, 1536])
    st_gdn = din("st_gdn", [NSEQ, 8, 64, 64])

    yp = dout("yp", [TP, D])
    ys = dout("ys", [NS, D])
    o_ckv_p, o_kr_p = dout("ckv_p", [TP, 128]), dout("kr_p", [TP, 32])
    o_conv_p, o_gdn_p = dout("conv_p", [1, 3, 1536]), dout("gdn_p", [1, 8, 64, 64])
    o_ckv_s, o_kr_s = dout("ckv_s", [NS, 128]), dout("kr_s", [NS, 32])
    o_conv_s, o_gdn_s = dout("conv_s", [NSEQ, 3, 1536]), dout("gdn_s", [NSEQ, 8, 64, 64])
    mods_p = dscr("mods_p", [9, 128, D])
    mods_s = dscr("mods_s", [9, 64, D])
    x1p, x1s = dscr("x1p", [TP, D]), dscr("x1s", [NS, D])
    x2p, x2s = dscr("x2p", [TP, D]), dscr("x2s", [NS, D])
    dbg = dout if cfg.get("debug") else dscr
    mix_p, mix_s = dbg("mix_p", [TP, D], BF16), dbg("mix_s", [NS, D], BF16)
    Q_p, Q_s = dbg("Q_p", [TP, 768], BF16), dbg("Q_s", [NS, 768], BF16)

    with ExitStack() as stack:
        S = Sched(nc, stack)
        cx.S = S
        cx.ps = [stack.enter_context(nc.psum_tensor("ps%d" % i, [128, 512], F32))[:, :] for i in range(8)]
        cx.rps = [Res("ps%d" % i, excl=True) for i in range(8)]
        cst_f = sb(cx, stack, "cst_f", [128, 128], F32)
        cst_b = sb(cx, stack, "cst_b", [128, 128 + 128 + 512 + 128], BF16)
        cx.cst = cst
        cx.eps_t = sb(cx, stack, "eps_t", [128, 1], F32)
        cx.one_t = sb(cx, stack, "one_t", [128, 1], F32)
        cx.ident_f = cst_f[:, 0:128]
        cx.ident_bf = cst_b[:, 0:128]
        cx.nident_bf = cst_b[:, 128:256]
        cx.ident4_bf = cst_b[:, 256:768]
        cx.bones = cst_b[:, 768:896]
        r_c = Res("consts")
        d_c = S.dsem("consts")
        S.dma("sp", d_c, cst_f[:, :], cst[:, 0:128], writes=(r_c,))
        bon = sb(cx, stack, "bon_tmp", [128, 128], F32)
        S.dma("sp", d_c, bon[:, :], cst[:, 384:512], writes=(r_c,))
        S.op("dve", "tensor_copy", dict(out=cst_b[:, 0:128], in_=cst_f[:, 0:128]), reads=(r_c,), writes=(r_c,))
        S.op("dve", "tensor_scalar", dict(out=cst_b[:, 128:256], in0=cst_f[:, 0:128], scalar1=-1.0, scalar2=None,
                                          op0=ALU.mult), reads=(r_c,), writes=(r_c,))
        for h in range(4):
            S.op("dve", "tensor_copy", dict(out=cst_b[:, 256 + h * 128:256 + (h + 1) * 128], in_=cst_f[:, 0:128]),
                 reads=(r_c,), writes=(r_c,))
        S.op("dve", "tensor_copy", dict(out=cst_b[:, 768:896], in_=bon[:, :]), reads=(r_c,), writes=(r_c,))
        S.op("dve", "memset", dict(ap=cx.eps_t[:, :], constant=EPS), writes=(r_c,))
        S.op("dve", "memset", dict(ap=cx.one_t[:, :], constant=1.0), writes=(r_c,))
        S.barrier()
        S.emit()

        r_mods = Res("mods")
        if "adaln" in phases:
            phase_adaln(cx, S, c_rep, ada_w, ada_b, mods_p, mods_s, r_mods)

        nblk_p = (TP + 127) // 128
        r_xp = [Res() for _ in range(nblk_p)]
        r_xs = [Res()]
        r_x1p = [Res() for _ in range(nblk_p)]
        r_x1s = [Res()]
        f1_dst_p, f1_dst_s = (x1p, x1s) if "mixer" in phases else (yp, ys)
        if "ffn1" in phases:
            groups = [dict(src=xp, dst=f1_dst_p, T=TP, mods=mods_p, n=128, src_res=r_xp, dst_res=r_x1p),
                      dict(src=xs_in, dst=f1_dst_s, T=NS, mods=mods_s, n=64, src_res=r_xs, dst_res=r_x1s)]
            phase_ffn(cx, S, "f1", groups, ffn1_wi, ffn1_wo, norm_ffn1, 0)
        r_out = Res("outs")
        gp = gs = None
        if "mixer" in phases:
            msrc_p, msrc_s = (x1p, x1s) if "ffn1" in phases else (xp, xs_in)
            gp = dict(src=msrc_p, src_res=r_x1p, T=TP, nseq=1, mods=mods_p, n=128, s0=None, conv0=None, cs=cs_p,
                      new_conv=o_conv_p, new_gdn=o_gdn_p, new_ckv=o_ckv_p, new_kr=o_kr_p, mix=mix_p, Q=Q_p,
                      r_out=r_out, r_mix=Res(), r_Q=Res(), r_ckv=Res(), r_kr=Res())
            gs = dict(src=msrc_s, src_res=r_x1s, T=NS, nseq=NSEQ, mods=mods_s, n=64, s0=st_gdn, conv0=st_conv, cs=cs_s,
                      new_conv=o_conv_s, new_gdn=o_gdn_s, new_ckv=o_ckv_s, new_kr=o_kr_s, mix=mix_s, Q=Q_s,
                      r_out=r_out, r_mix=Res(), r_Q=Res(), r_ckv=Res(), r_kr=Res())
            phase_mixer(cx, S, [gp, gs], P)
        if "attn_p" in phases:
            phase_attn_prompt(cx, S, gp, P)
        if "attn_s" in phases:
            phase_attn_sample(cx, S, gs, P, cache_c, cache_k, ptab, NSEQ, cfg.get("npages", 128))
        if "wout" in phases:
            r_x2p = [Res() for _ in range(nblk_p)]
            r_x2s = [Res()]
            gp.update(dst=x2p, dst_res=r_x2p)
            gs.update(dst=x2s, dst_res=r_x2s)
            phase_wout(cx, S, [gp, gs], w_out)
            if "ffn2" in phases:
                groups = [dict(src=x2p, dst=yp, T=TP, mods=mods_p, n=128, src_res=r_x2p, dst_res=[Res() for _ in range(nblk_p)]),
                          dict(src=x2s, dst=ys, T=NS, mods=mods_s, n=64, src_res=r_x2s, dst_res=[Res()])]
                phase_ffn(cx, S, "f2", groups, ffn2_wi, ffn2_wo, norm_ffn2, 6)
        S.barrier()
        S.emit()
    return nc


ALL_PHASES = ("adaln", "ffn1", "mixer", "attn_p", "attn_s", "wout", "ffn2")


def make_inputs(core, inputs, TP=4096):
    f = lambda a: np.ascontiguousarray(np.asarray(a, dtype=np.float32))
    b = core % 4
    sl = slice(16 * core, 16 * core + 16)
    conv_w = f(inputs["gdn_conv_w"][0])
    m = dict(
        xp=f(inputs["x_prompt"][b][:TP]), xs=f(inputs["x_sample"][sl]).reshape(64, D),
        c_rep=np.concatenate([np.repeat(f(inputs["c_prompt"][b:b + 1]), 128, 0), np.repeat(f(inputs["c_sample"][sl]), 4, 0)], 0),
        cst=host_consts(), ada_w=f(inputs["ada_w"][0]), ada_b=f(inputs["ada_b"][0:1]),
        norm_ffn1=f(inputs["norm_ffn1"][0:1]), ffn1_wi=f(inputs["ffn1_wi"][0]), ffn1_wo=f(inputs["ffn1_wo"][0]),
        norm_mix=f(inputs["norm_mix"][0:1]), w_in=f(inputs["w_in"][0]),
        conv_w_fm=np.ascontiguousarray(conv_w.reshape(4, 12, 128).transpose(2, 1, 0)),
        a_log=f(inputs["gdn_a_log"][0:1]), dt_bias=f(inputs["gdn_dt_bias"][0:1]), gdn_norm=f(inputs["gdn_norm"][0:1]),
        qn_g=f(inputs["mla_qn_norm"][0:1]), qr_g=f(inputs["mla_qr_norm"][0:1]), ckv_g=f(inputs["mla_ckv_norm"][0:1]),
        kr_g=f(inputs["mla_kr_norm"][0:1]), kn_g=f(inputs["mla_kn_norm"][0:1]),
        w_uk=f(inputs["mla_w_uk"][0]).reshape(128, 512), w_uv=f(inputs["mla_w_uv"][0]).reshape(128, 512),
        w_out=f(inputs["w_out"][0]), norm_ffn2=f(inputs["norm_ffn2"][0:1]), ffn2_wi=f(inputs["ffn2_wi"][0]),
        ffn2_wo=f(inputs["ffn2_wo"][0]),
        cache_c=f(inputs["cache_ckv"][0]).reshape(-1, 128 * 128), cache_k=f(inputs["cache_krope"][0]).reshape(-1, 128 * 32),
        ptab=np.ascontiguousarray(np.asarray(inputs["page_table"][sl], dtype=np.int32)),
        cs_p=host_rope_table(np.arange(TP)), cs_s=host_rope_table(16384 + np.arange(4)),
        st_conv=f(inputs["state_conv"][0][sl]), st_gdn=f(inputs["state_gdn"][0][sl]),
    )
    return m


def kernel(**inputs):
    nc = build(dict(TP=4096, NS=64, phases=ALL_PHASES))
    in_maps = [make_inputs(c, inputs) for c in range(8)]
    res = run_bass_kernel_spmd(nc, in_maps, core_ids=list(range(8))).results
    f32 = np.float32
    yp = np.stack([res[b]["yp"] for b in range(4)]).astype(f32)
    ys = np.concatenate([res[c]["ys"].reshape(16, 4, D) for c in range(8)]).astype(f32)
    ckv_p = np.stack([res[b]["ckv_p"] for b in range(4)])[None].astype(f32)
    kr_p = np.stack([res[b]["kr_p"] for b in range(4)])[None].astype(f32)
    conv_p = np.concatenate([res[b]["conv_p"] for b in range(4)])[None].astype(f32)
    gdn_p = np.concatenate([res[b]["gdn_p"] for b in range(4)])[None].astype(f32)
    ckv_s = np.concatenate([res[c]["ckv_s"].reshape(16, 4, 128) for c in range(8)])[None].astype(f32)
    kr_s = np.concatenate([res[c]["kr_s"].reshape(16, 4, 32) for c in range(8)])[None].astype(f32)
    conv_s = np.concatenate([res[c]["conv_s"] for c in range(8)])[None].astype(f32)
    gdn_s = np.concatenate([res[c]["gdn_s"] for c in range(8)])[None].astype(f32)
    return (yp, ys, ckv_p, kr_p, conv_p, gdn_p, ckv_s, kr_s, conv_s, gdn_s)
```

```python
import numpy as np
from contextlib import ExitStack
import concourse.bass as bass
import concourse.mybir as mybir
from concourse.bass_utils import run_bass_kernel_spmd

F32 = mybir.dt.float32
BF16 = mybir.dt.bfloat16
I32 = mybir.dt.int32
U32 = mybir.dt.uint32
AF = mybir.ActivationFunctionType
ALU = mybir.AluOpType
AX = mybir.AxisListType

D = 1024
DFF = 2816
NFC = DFF // 128
EPS = 1e-6

class Res:
    __slots__ = ("name", "w", "r", "excl")

    def __init__(self, name="", excl=False):
        self.name = name
        self.excl = excl
        self.w = None
        self.r = {}


class DSem:
    __slots__ = ("sem", "cnt", "name")

    def __init__(self, sem, name):
        self.sem = sem
        self.cnt = 0
        self.name = name


class Sched:
    ENGS = ("pe", "act", "dve", "pool", "sp")

    def __init__(self, nc, stack):
        self.nc = nc
        self.stack = stack
        self.q = {k: [] for k in self.ENGS}
        self.esem = {k: stack.enter_context(nc.semaphore("es_" + k)) for k in self.ENGS}
        self.ecnt = {k: 0 for k in self.ENGS}
        self.known = {k: {} for k in self.ENGS}
        self.dsems = []
        self.nwait = 0
        self.nop = 0

    def dsem(self, name):
        s = self.stack.enter_context(self.nc.semaphore("ds_" + name))
        d = DSem(s, name)
        self.dsems.append(d)
        return d

    def _waits(self, eng, reads, writes):
        need = {}

        def add(ev):
            sem_id, sem, val, src = ev
            if src == "pe" and eng == "pe":
                return
            if self.known[eng].get(sem_id, 0) >= val:
                return
            if sem_id not in need or need[sem_id][1] < val:
                need[sem_id] = (sem, val)

        for r in reads:
            if r.w is not None:
                add(r.w)
        for w in writes:
            if w.w is not None:
                add(w.w)
            for ev in w.r.values():
                add(ev)
        for sem_id, (sem, val) in need.items():
            self.q[eng].append(("wait", sem, val))
            self.known[eng][sem_id] = val
            self.nwait += 1

    def _record(self, ev, reads, writes):
        for r in reads:
            old = r.r.get(ev[0])
            if old is None or old[2] < ev[2]:
                r.r[ev[0]] = ev
        for w in writes:
            w.w = ev
            w.r = {}

    def op(self, eng, meth, kw, reads=(), writes=(), inc=True):
        ex = tuple(r for r in reads if r.excl)
        if ex:
            writes = tuple(writes) + ex
        self._waits(eng, reads, writes)
        if inc:
            self.ecnt[eng] += 1
            ev = (eng, self.esem[eng], self.ecnt[eng], eng)
        else:
            ev = (eng, self.esem[eng], self.ecnt[eng] + 1, eng)
        self.q[eng].append(("op", meth, kw, inc))
        self.nop += 1
        self._record(ev, reads, writes)

    def dma(self, q, ds, out, in_, reads=(), writes=(), **kw):
        self._waits(q, reads, writes)
        ds.cnt += 16
        ev = (id(ds), ds.sem, ds.cnt, "dma")
        self.q[q].append(("dma", out, in_, ds.sem, kw))
        self.nop += 1
        self._record(ev, reads, writes)

    def idma(self, ds, reads=(), writes=(), **kw):
        self._waits("pool", reads, writes)
        ds.cnt += 16
        ev = (id(ds), ds.sem, ds.cnt, "dma")
        self.q["pool"].append(("idma", kw, ds.sem))
        self.nop += 1
        self._record(ev, reads, writes)

    def barrier(self):
        for e in self.ENGS:
            for e2 in self.ENGS:
                if e2 == e:
                    continue
                v = self.ecnt[e2]
                if v > 0 and self.known[e].get(e2, 0) < v:
                    self.q[e].append(("wait", self.esem[e2], v))
                    self.known[e][e2] = v
            for d in self.dsems:
                if d.cnt > 0 and self.known[e].get(id(d), 0) < d.cnt:
                    self.q[e].append(("wait", d.sem, d.cnt))
                    self.known[e][id(d)] = d.cnt

    def emit(self):
        import os
        if os.environ.get("EMITLOG"):
            print("EMIT", {k: len(v) for k, v in self.q.items()}, "cnt", dict(self.ecnt), "ndsem", len(self.dsems))
        nc = self.nc
        engs = {"pe": "tensor", "act": "scalar", "dve": "vector", "pool": "gpsimd", "sp": "sync"}
        with nc.Block() as block:
            for k, attr in engs.items():
                items = self.q[k]
                esem = self.esem[k]

                def body(e, items=items, esem=esem):
                    for it in items:
                        if it[0] == "wait":
                            e.wait_ge(it[1], it[2])
                        elif it[0] == "op":
                            ins = getattr(e, it[1])(**it[2])
                            if it[3]:
                                ins.then_inc(esem, 1)
                        elif it[0] == "idma":
                            e.indirect_dma_start(**it[1]).then_inc(it[2], 16)
                        else:
                            _, out, in_, sem, kw = it
                            e.dma_start(out=out, in_=in_, **kw).then_inc(sem, 16)

                getattr(block, attr)(body)
        self.q = {k: [] for k in self.ENGS}


class Ctx:
    pass


def sb(cx, stack, name, shape, dt):
    return stack.enter_context(cx.nc.sbuf_tensor(name, list(shape), dt))


def row_bcast(ap_row, n):
    t = ap_row.tensor
    F = ap_row.shape[-1]
    return bass.AP(t, ap_row.offset, [[0, n], [1, F]])


def load_weight_bf16(cx, S, name, dst, dst_res, w_ap, kchunks, cols, col_piece):
    with ExitStack() as st:
        stg = [sb(cx, st, name + "stg%d" % i, [128, col_piece], F32) for i in range(3)]
        r_stg = [Res() for _ in range(3)]
        d_stg = [S.dsem(name + "stg%d" % i) for i in range(3)]
        engs = (("dve", "tensor_copy"), ("pool", "tensor_copy"), ("act", "activation"))
        i = 0
        for c in range(kchunks):
            for c0 in range(0, cols, col_piece):
                c1 = min(cols, c0 + col_piece)
                k = i % 3
                i += 1
                S.dma("sp", d_stg[k], stg[k][:, :c1 - c0], w_ap[c * 128:(c + 1) * 128, c0:c1], writes=(r_stg[k],))
                eng, meth = engs[k]
                kw = dict(out=dst[:, c, c0:c1], in_=stg[k][:, :c1 - c0])
                if meth == "activation":
                    kw["func"] = AF.Copy
                S.op(eng, meth, kw, reads=(r_stg[k],), writes=(dst_res,))
        S.barrier()
        S.emit()


def norm_mod_transpose(cx, S, bufs, src_ap, src_res, m, hT_dst, pt_k):
    (xs, tt, junk, hb, stat, GG, Bt, r_xs, r_tt, r_junk, r_hb, r_stat, r_GG, r_Bt, r_hT, d_xs) = bufs
    S.dma("sp", d_xs, xs[:m, :], src_ap, reads=(src_res,), writes=(r_xs,))
    S.op("act", "activation", dict(out=junk[:m, :], in_=xs[:m, :], func=AF.Square, accum_out=stat[:m, 0:1]),
         reads=(r_xs,), writes=(r_junk, r_stat))
    S.op("act", "activation", dict(out=stat[:m, 1:2], in_=stat[:m, 0:1], func=AF.Sqrt, scale=1.0 / D,
                                   bias=cx.eps_t[:m, 0:1]), reads=(r_stat,), writes=(r_stat,))
    S.op("dve", "reciprocal", dict(out=stat[:m, 2:3], in_=stat[:m, 1:2]), reads=(r_stat,), writes=(r_stat,))
    S.op("dve", "scalar_tensor_tensor", dict(out=tt[:m, :], in0=xs[:m, :], scalar=stat[:m, 2:3], in1=GG[:m, :],
                                             op0=ALU.mult, op1=ALU.mult),
         reads=(r_xs, r_stat, r_GG), writes=(r_tt,))
    S.op("pool", "tensor_tensor", dict(out=hb[:m, :], in0=tt[:m, :], in1=Bt[:m, :], op=ALU.add),
         reads=(r_tt, r_Bt), writes=(r_hb,))
    ptb = cx.ps[pt_k].bitcast(BF16)
    for c in range(8):
        S.op("pe", "transpose", dict(out=ptb[:, c * 128:c * 128 + m], in_=hb[:m, c * 128:(c + 1) * 128],
                                     identity=cx.ident_bf[:m, :m]),
             reads=(r_hb,), writes=(cx.rps[pt_k],), inc=(c == 7))
    S.op("act", "activation", dict(out=hT_dst, in_=ptb.rearrange("p (c t) -> p c t", c=8)[:, :, :m], func=AF.Copy),
         reads=(cx.rps[pt_k],), writes=(r_hT,))


def load_mod_tiles(cx, S, g_ap, mods, mod_base, n, gscale, GG, Bt, Gt, tt, r_GG, r_Bt, r_Gt, r_tt, d_m):
    S.dma("sp", d_m, tt[:n, :], row_bcast(g_ap, n), writes=(r_tt,))
    S.dma("sp", d_m, GG[:n, :], mods[mod_base + 1, :n, :], writes=(r_GG,))
    S.dma("sp", d_m, Bt[:n, :], mods[mod_base + 0, :n, :], writes=(r_Bt,))
    if Gt is not None:
        S.dma("sp", d_m, Gt[:n, :], mods[mod_base + 2, :n, :], writes=(r_Gt,))
    S.barrier()
    S.op("dve", "scalar_tensor_tensor", dict(out=GG[:n, :], in0=GG[:n, :], scalar=1.0, in1=tt[:n, :],
                                             op0=ALU.add, op1=ALU.mult), reads=(r_tt,), writes=(r_GG,))
    if Gt is not None and gscale != 1.0:
        S.op("dve", "tensor_scalar", dict(out=Gt[:n, :], in0=Gt[:n, :], scalar1=float(gscale), scalar2=None,
                                          op0=ALU.mult), writes=(r_Gt,))


def phase_ffn(cx, S, name, groups, wi_ap, wo_ap, g_ap, mod_base):
    with ExitStack() as st:
        wi = sb(cx, st, name + "wi", [128, 8, 2 * DFF], BF16)
        wo = sb(cx, st, name + "wo", [128, NFC, D], BF16)
        r_wi, r_wo = Res("wi"), Res("wo")
        load_weight_bf16(cx, S, name + "wi", wi, r_wi, wi_ap, 8, 2 * DFF, DFF)
        load_weight_bf16(cx, S, name + "wo", wo, r_wo, wo_ap, NFC, D, D)
        GG = sb(cx, st, name + "GG", [128, D], F32)
        Bt = sb(cx, st, name + "Bt", [128, D], F32)
        Gt = sb(cx, st, name + "Gt", [128, D], F32)
        xs = sb(cx, st, name + "xs", [128, D], F32)
        tt = sb(cx, st, name + "tt", [128, D], F32)
        junk = sb(cx, st, name + "junk", [128, D], BF16)
        hb = sb(cx, st, name + "hb", [128, D], BF16)
        hT = sb(cx, st, name + "hT", [128, 8, 512], BF16)
        sg = [sb(cx, st, name + "sg%d" % i, [128, 512], BF16) for i in range(2)]
        actT = sb(cx, st, name + "actT", [128, NFC, 512], BF16)
        xr = [sb(cx, st, name + "xr%d" % i, [128, D], F32) for i in range(2)]
        tmp = sb(cx, st, name + "tmp", [128, 512], F32)
        stat = sb(cx, st, name + "stat", [128, 4], F32)

        r_GG, r_Bt, r_Gt, r_xs, r_tt, r_junk, r_hb, r_hT = (Res(n_) for n_ in
                                                              ("GG", "Bt", "Gt", "xs", "tt", "junk", "hb", "hT"))
        r_sg = [Res("sg0"), Res("sg1")]
        r_actT = [Res("actT%d" % j) for j in range(NFC)]
        r_xr = [Res("xr0"), Res("xr1")]
        r_tmp, r_stat = Res("tmp"), Res("stat")
        d_m, d_xs = S.dsem(name + "m"), S.dsem(name + "xs")
        d_xr = [S.dsem(name + "xr0"), S.dsem(name + "xr1")]
        d_xst = [S.dsem(name + "xst0"), S.dsem(name + "xst1")]
        bufs = (xs, tt, junk, hb, stat, GG, Bt, r_xs, r_tt, r_junk, r_hb, r_stat, r_GG, r_Bt, r_hT, d_xs)


        pg, pu, po = cx.ps[0:2], cx.ps[2:4], cx.ps[4:6]
        r_pg, r_pu, r_po = cx.rps[0:2], cx.rps[2:4], cx.rps[4:6]
        cnt = {"up": 0, "po": 0, "pt": 0, "xr": 0}

        for g in groups:
            n = g["n"]
            load_mod_tiles(cx, S, g_ap, g["mods"], mod_base, n, 0.5, GG, Bt, Gt, tt, r_GG, r_Bt, r_Gt, r_tt, d_m)
            T = g["T"]
            nblk = (T + 511) // 512

            def stage_norm(b):
                t0 = b * 512
                tb = min(512, T - t0)
                for s_ in range((tb + 127) // 128):
                    r0 = t0 + s_ * 128
                    m = min(128, T - r0)
                    k = 6 + cnt["pt"] % 2
                    cnt["pt"] += 1
                    norm_mod_transpose(cx, S, bufs, g["src"][r0:r0 + m, :], g["src_res"][r0 // 128], m,
                                       hT[:, :, s_ * 128:s_ * 128 + m], k)

            def stage_up(b):
                t0 = b * 512
                tb = min(512, T - t0)
                for j in range(NFC):
                    k = cnt["up"] % 2
                    cnt["up"] += 1
                    for kc in range(8):
                        S.op("pe", "matmul", dict(out=pg[k][:, :tb], lhsT=wi[:, kc, j * 128:(j + 1) * 128],
                                                  rhs=hT[:, kc, :tb], start=(kc == 0), stop=(kc == 7)),
                             reads=(r_wi, r_hT), writes=(r_pg[k],), inc=(kc == 7))
                    for kc in range(8):
                        S.op("pe", "matmul", dict(out=pu[k][:, :tb], lhsT=wi[:, kc, DFF + j * 128:DFF + (j + 1) * 128],
                                                  rhs=hT[:, kc, :tb], start=(kc == 0), stop=(kc == 7)),
                             reads=(r_wi, r_hT), writes=(r_pu[k],), inc=(kc == 7))
                    S.op("act", "activation", dict(out=sg[k][:, :tb], in_=pg[k][:, :tb], func=AF.Silu),
                         reads=(r_pg[k],), writes=(r_sg[k],))
                    S.op("dve", "tensor_tensor", dict(out=actT[:, j, :tb], in0=sg[k][:, :tb], in1=pu[k][:, :tb],
                                                      op=ALU.mult),
                         reads=(r_sg[k], r_pu[k]), writes=(r_actT[j],))

            def stage_down(b):
                t0 = b * 512
                tb = min(512, T - t0)
                for s_ in range((tb + 127) // 128):
                    r0 = t0 + s_ * 128
                    m = min(128, T - r0)
                    kx = cnt["xr"] % 2
                    cnt["xr"] += 1
                    S.dma("sp", d_xr[kx], xr[kx][:m, :], g["src"][r0:r0 + m, :], reads=(g["src_res"][r0 // 128],),
                          writes=(r_xr[kx],))
                    for half in range(2):
                        k = cnt["po"] % 2
                        cnt["po"] += 1
                        for j in range(NFC):
                            S.op("pe", "matmul", dict(out=po[k][:m, :], lhsT=actT[:, j, s_ * 128:s_ * 128 + m],
                                                      rhs=wo[:, j, half * 512:(half + 1) * 512],
                                                      start=(j == 0), stop=(j == NFC - 1)),
                                 reads=(r_wo, r_actT[j]), writes=(r_po[k],), inc=(j == NFC - 1))
                        S.op("dve", "tensor_tensor", dict(out=tmp[:m, :], in0=po[k][:m, :],
                                                          in1=Gt[:m, half * 512:(half + 1) * 512], op=ALU.mult),
                             reads=(r_po[k], r_Gt), writes=(r_tmp,))
                        S.op("pool", "tensor_tensor", dict(out=xr[kx][:m, half * 512:(half + 1) * 512],
                                                           in0=xr[kx][:m, half * 512:(half + 1) * 512],
                                                           in1=tmp[:m, :], op=ALU.add),
                             reads=(r_tmp,), writes=(r_xr[kx],))
                    S.dma("pool", d_xst[kx], g["dst"][r0:r0 + m, :], xr[kx][:m, :], reads=(r_xr[kx],),
                          writes=(g["dst_res"][r0 // 128],))

            stage_norm(0)
            for b in range(nblk):
                stage_up(b)
                if b + 1 < nblk:
                    stage_norm(b + 1)
                stage_down(b)
        S.barrier()
        S.emit()


def phase_adaln(cx, S, c_rep, ada_w, ada_b, mods_p, mods_s, r_mods):
    with ExitStack() as st:
        ct = sb(cx, st, "ad_ct", [128, D], F32)
        cb_ = sb(cx, st, "ad_cb", [128, D], BF16)
        cT = sb(cx, st, "ad_cT", [128, 8, 192], BF16)
        w = [sb(cx, st, "ad_w%d" % i, [128, 8, D], BF16) for i in range(2)]
        wst = [sb(cx, st, "ad_wst%d" % i, [128, 8, D], F32) for i in range(2)]
        bb = [sb(cx, st, "ad_b%d" % i, [128, D], F32) for i in range(2)]
        o = [sb(cx, st, "ad_o%d" % i, [128, D], F32) for i in range(2)]
        r_ct, r_cb, r_cT = Res(), Res(), Res()
        r_w, r_wst, r_bb, r_o = [Res(), Res()], [Res(), Res()], [Res(), Res()], [Res(), Res()]
        d_ct = S.dsem("ad_ct")
        d_w = [S.dsem("ad_w0"), S.dsem("ad_w1")]
        d_b = [S.dsem("ad_b0"), S.dsem("ad_b1")]
        d_o = [S.dsem("ad_o0"), S.dsem("ad_o1")]

        def load(blk):
            k = blk % 2
            for hf in range(2):
                S.dma("sp", d_w[k], wst[k][:, hf * 4:(hf + 1) * 4, :],
                      ada_w[hf * 512:(hf + 1) * 512, blk * D:(blk + 1) * D].rearrange("(c p) f -> p c f", p=128),
                      writes=(r_wst[k],))
            S.dma("sp", d_b[k], bb[k][:, :], row_bcast(ada_b[0:1, blk * D:(blk + 1) * D], 128), writes=(r_bb[k],))

        load(0)
        for gi, (r0, m) in enumerate(((0, 128), (128, 64))):
            S.dma("sp", d_ct, ct[:m, :], c_rep[r0:r0 + m, :], writes=(r_ct,))
            S.op("act", "activation", dict(out=cb_[:m, :], in_=ct[:m, :], func=AF.Silu), reads=(r_ct,), writes=(r_cb,))
            ptb = cx.ps[6 + gi].bitcast(BF16)
            for c in range(8):
                S.op("pe", "transpose", dict(out=ptb[:, c * 128:c * 128 + m], in_=cb_[:m, c * 128:(c + 1) * 128],
                                             identity=cx.ident_bf[:m, :m]),
                     reads=(r_cb,), writes=(cx.rps[6 + gi],), inc=(c == 7))
            S.op("act", "activation", dict(out=cT[:, :, r0:r0 + m],
                                           in_=ptb.rearrange("p (c t) -> p c t", c=8)[:, :, :m], func=AF.Copy),
                 reads=(cx.rps[6 + gi],), writes=(r_cT,))
        for blk in range(9):
            k = blk % 2
            if blk + 1 < 9:
                load(blk + 1)
            S.op("pool", "tensor_copy", dict(out=w[k][:, 0:4, :], in_=wst[k][:, 0:4, :]),
                 reads=(r_wst[k],), writes=(r_w[k],))
            S.op("dve", "tensor_copy", dict(out=w[k][:, 4:8, :], in_=wst[k][:, 4:8, :]),
                 reads=(r_wst[k],), writes=(r_w[k],))
            for gi, (r0, m, dst) in enumerate(((0, 128, mods_p), (128, 64, mods_s))):
                ko = gi
                for half in range(2):
                    pk = (blk * 4 + gi * 2 + half) % 4
                    ps, rp = cx.ps[pk], cx.rps[pk]
                    for c in range(8):
                        S.op("pe", "matmul", dict(out=ps[:m, :], lhsT=cT[:, c, r0:r0 + m],
                                                  rhs=w[k][:, c, half * 512:(half + 1) * 512], start=(c == 0),
                                                  stop=(c == 7)), reads=(r_cT, r_w[k]), writes=(rp,), inc=(c == 7))
                    S.op("dve", "tensor_tensor", dict(out=o[ko][:m, half * 512:(half + 1) * 512], in0=ps[:m, :],
                                                      in1=bb[k][:m, half * 512:(half + 1) * 512], op=ALU.add),
                         reads=(rp, r_bb[k]), writes=(r_o[ko],))
                S.dma("act", d_o[ko], dst[blk, :m, :], o[ko][:m, :], reads=(r_o[ko],), writes=(r_mods,))
        S.barrier()
        S.emit()


NLEV = 5
QK_SCALE = 64 ** -0.5
ATT_SCALE = 96 ** -0.5


def bc(ap2, n):
    return ap2.unsqueeze(2).broadcast_to([ap2.shape[0], ap2.shape[1], n])


def bc_mid(ap2, n):
    return ap2.unsqueeze(1).broadcast_to([ap2.shape[0], n, ap2.shape[1]])


class MixBufs:
    pass


def mixer_alloc(cx, S, st, name, P):
    B = MixBufs()

    def t(nm, shape, dt):
        tt_ = sb(cx, st, name + nm, shape, dt)
        setattr(B, nm, tt_)
        setattr(B, "r_" + nm, Res(nm))
        return tt_

    t("win", [128, 8, 2992], BF16)
    load_weight_bf16(cx, S, "mxwin", B.win, B.r_win, P["w_in"], 8, 2992, 1496)
    t("GG", [128, D], F32); t("Bt", [128, D], F32)
    t("xs", [128, D], F32); t("tt", [128, D], F32); t("junk", [128, D], BF16); t("hb", [128, D], BF16)
    t("stat", [128, 4], F32)
    t("hT", [128, 8, 256], BF16)
    t("xq", [128, 12, 259], F32)
    t("yc", [128, 12, 256], F32)
    t("sqb", [128, 512], BF16)
    t("rinv", [128, 512], F32)
    t("qkT", [128, 8, 256], BF16)
    t("tm", [128, 1456], F32)
    t("sc", [128, 96], F32)
    t("rows", [8, 2, 128], F32)
    t("glb", [128, 8], F32)
    t("vtok", [128, 512], BF16); t("ktok", [128, 512], BF16); t("kdec", [128, 512], BF16)
    t("decq", [128, 512], F32); t("decb", [128, 512], F32)
    for g in range(2):
        for nm in ("A", "M", "P"):
            for k in range(2):
                t("%s%d%d" % (nm, g, k), [128, 512], F32)
        t("Pf%d" % g, [128, 512], BF16)
    t("qk", [128, 1024], BF16)
    t("u", [128, 512], F32)
    t("wT", [128, 4, 128], BF16)
    t("vnew", [128, 512], BF16)
    t("o1", [128, 512], F32)
    t("osb", [128, 512], F32)
    t("ost", [128, 16], F32)
    t("zs", [128, 512], F32)
    t("gout", [128, 512], BF16)
    t("S32", [128, 512], F32); t("Sbf", [128, 512], BF16)
    t("mq", [128, 768], F32); t("mqs", [128, 32], F32); t("Qb", [128, 768], BF16)
    t("cn", [128, 128], F32); t("krn", [128, 32], F32); t("krr", [128, 32], F32); t("qrr", [128, 8, 32], F32)
    t("cs", [128, 32], F32)
    t("cw", [128, 12, 4], F32)
    t("g8", [128, 16], F32)
    t("gng", [128, 64], F32); t("qng", [128, 64], F32); t("qrg", [128, 32], F32); t("ckg", [128, 128], F32)
    t("krg", [128, 32], F32); t("kng", [128, 64], F32)
    t("nc3", [128, 1536], F32)
    B.d = {}
    return B


def mixer_group(cx, S, B, name, grp, P):
    def dsem(k):
        if k not in B.d:
            B.d[k] = S.dsem(name + k)
        return B.d[k]

    def op(eng, meth, reads, writes, inc=True, **kw):
        S.op(eng, meth, kw, reads=reads, writes=writes, inc=inc)

    ps, rps = cx.ps, cx.rps
    T, nseq = grp["T"], grp["nseq"]
    Ts = T // nseq
    prompt = (nseq == 1)
    n = grp["n"]
    load_mod_tiles(cx, S, P["norm_mix"], grp["mods"], 3, n, 1.0, B.GG, B.Bt, None, B.tt, B.r_GG, B.r_Bt, None,
                   B.r_tt, dsem("m"))
    bufs = (B.xs, B.tt, B.junk, B.hb, B.stat, B.GG, B.Bt, B.r_xs, B.r_tt, B.r_junk, B.r_hb, B.r_stat, B.r_GG,
            B.r_Bt, B.r_hT, dsem("xs"))
    blk_T = 256 if prompt else T
    nblk = (T + blk_T - 1) // blk_T
    if prompt:
        op("dve", "memset", (), (B.r_S32,), ap=B.S32[:, :], constant=0.0)
        op("dve", "memset", (), (B.r_Sbf,), ap=B.Sbf[:, :], constant=0.0)
        op("pool", "memset", (), (B.r_xq,), ap=B.xq[:, :, 0:3], constant=0.0)
    ptk = [0]

    for b in range(nblk):
        t0 = b * blk_T
        tb = min(blk_T, T - t0)
        for s_ in range((tb + 127) // 128):
            r0 = t0 + s_ * 128
            m = min(128, T - r0)
            k = 6 + ptk[0] % 2
            ptk[0] += 1
            norm_mod_transpose(cx, S, bufs, grp["src"][r0:r0 + m, :], grp["src_res"][r0 // 128], m,
                               B.hT[:, :, s_ * 128:s_ * 128 + m], k)
        if prompt:
            xq_new = B.xq[:, :, 3:3 + tb]
            xqv = None
        else:
            xqv = B.xq[:, :, 0:nseq * 7].rearrange("p c (s t) -> p c s t", t=7)
            op("sp", "dma_start", (), (), ) if False else None
            S.dma("sp", dsem("nc3"), B.nc3[:nseq * 3, :], grp["conv0"].rearrange("s t c -> (s t) c"), writes=(B.r_nc3,))
            for c in range(12):
                kq = c % 2
                op("pe", "transpose", (B.r_nc3,), (rps[kq],), out=ps[kq][:, :nseq * 3],
                   in_=B.nc3[:nseq * 3, c * 128:(c + 1) * 128], identity=cx.ident_f[:nseq * 3, :nseq * 3])
                op("act" if c % 2 else "dve", "activation" if c % 2 else "tensor_copy", (rps[kq],), (B.r_xq,),
                   out=xqv[:, c, :, 0:3], in_=ps[kq][:, :nseq * 3].rearrange("p (s t) -> p s t", t=3),
                   **({"func": AF.Copy} if c % 2 else {}))
        for c in range(12):
            kq = c % 2
            for kc in range(8):
                op("pe", "matmul", (B.r_win, B.r_hT), (rps[kq],), inc=(kc == 7), out=ps[kq][:, :tb],
                   lhsT=B.win[:, kc, c * 128:(c + 1) * 128], rhs=B.hT[:, kc, :tb], start=(kc == 0), stop=(kc == 7))
            if prompt:
                dst, srcp = B.xq[:, c, 3:3 + tb], ps[kq][:, :tb]
            else:
                dst, srcp = xqv[:, c, :, 3:7], ps[kq][:, :tb].rearrange("p (s t) -> p s t", t=Ts)
            if c % 2:
                op("act", "activation", (rps[kq],), (B.r_xq,), out=dst, in_=srcp, func=AF.Copy)
            else:
                op("dve", "tensor_copy", (rps[kq],), (B.r_xq,), out=dst, in_=srcp)
        for c in range(12):
            if prompt:
                ydst = B.yc[:, c, :tb]
                xin = [B.xq[:, c, j:j + tb] for j in range(4)]
            else:
                ydst = B.yc[:, c, :tb].rearrange("p (s t) -> p s t", t=Ts)
                xin = [xqv[:, c, :, j:j + Ts] for j in range(4)]
            eng = "dve"
            op(eng, "tensor_scalar", (B.r_xq, B.r_cw), (B.r_yc,), out=ydst, in0=xin[0], scalar1=B.cw[:, c, 0:1],
               scalar2=None, op0=ALU.mult)
            for j in range(1, 4):
                op(eng, "scalar_tensor_tensor", (B.r_xq, B.r_cw), (B.r_yc,), out=ydst, in0=xin[j],
                   scalar=B.cw[:, c, j:j + 1], in1=ydst, op0=ALU.mult, op1=ALU.add)
        for c in range(12):
            op("act", "activation", (B.r_yc,), (B.r_yc,), out=B.yc[:, c, :tb], in_=B.yc[:, c, :tb], func=AF.Silu)
        if prompt and b + 1 < nblk:
            op("pool", "tensor_copy", (B.r_xq,), (B.r_xq,), out=B.xq[:, :, 0:3], in_=B.xq[:, :, tb:tb + 3])
        for c in range(8):
            kq = c % 2
            op("pool", "tensor_tensor", (B.r_yc,), (B.r_sqb,), out=B.sqb[:, :tb], in0=B.yc[:, c, :tb], in1=B.yc[:, c, :tb],
               op=ALU.mult)
            op("pe", "matmul", (B.r_sqb,), (rps[kq],), out=ps[kq][:, :tb], lhsT=cx.bones[:, :], rhs=B.sqb[:, :tb],
               start=True, stop=True)
            op("act", "activation", (rps[kq],), (B.r_rinv,), out=B.rinv[:, :tb], in_=ps[kq][:, :tb], func=AF.Sqrt,
               bias=cx.eps_t[:, 0:1], scale=1.0)
            op("dve", "reciprocal", (B.r_rinv,), (B.r_rinv,), out=B.rinv[:, :tb], in_=B.rinv[:, :tb])
            if c < 4:
                op("dve", "scalar_tensor_tensor", (B.r_rinv, B.r_yc), (B.r_qkT,), out=B.qkT[:, c, :tb], in0=B.yc[:, c, :tb],
                   scalar=QK_SCALE, in1=B.rinv[:, :tb], op0=ALU.mult, op1=ALU.mult)
            else:
                op("dve", "tensor_tensor", (B.r_rinv, B.r_yc), (B.r_qkT,), out=B.qkT[:, c, :tb], in0=B.yc[:, c, :tb],
                   in1=B.rinv[:, :tb], op=ALU.mult)
        ntile = (tb + 63) // 64 if prompt else nseq
        if cx.stop <= 1:
            ntile = 0
        for ti in range(ntile):
            if prompt:
                c0 = ti * 64
                m = min(64, tb - c0)
                seq = 0
            else:
                c0 = ti * Ts
                m = Ts
                seq = ti
            tok0 = t0 + c0
            mixer_tile(cx, S, B, name, grp, P, m, c0, tok0, seq, prompt, dsem, op,
                       first=(tok0 == 0) if prompt else True, last=(tok0 + m == T) if prompt else True)


def mixer_tile(cx, S, B, name, grp, P, m, c0, tok0, seq, prompt, dsem, op, first, last):
    ps, rps = cx.ps, cx.rps
    sc, rsc = B.sc, B.r_sc
    for gi, (cA, cB) in enumerate(((1536, 2048), (2048, 2560), (2560, 2992))):
        kq = gi % 2
        for kc in range(8):
            op("pe", "matmul", (B.r_win, B.r_hT), (rps[kq],), inc=(kc == 7), out=ps[kq][:m, :cB - cA],
               lhsT=B.hT[:, kc, c0:c0 + m], rhs=B.win[:, kc, cA:cB], start=(kc == 0), stop=(kc == 7))
        if gi % 2:
            op("act", "activation", (rps[kq],), (B.r_tm,), out=B.tm[:m, cA - 1536:cB - 1536], in_=ps[kq][:m, :cB - cA],
               func=AF.Copy)
        else:
            op("dve", "tensor_copy", (rps[kq],), (B.r_tm,), out=B.tm[:m, cA - 1536:cB - 1536], in_=ps[kq][:m, :cB - cA])
    if last:
        for gi in range(3):
            kq = gi % 2
            for kc in range(8):
                op("pe", "matmul", (B.r_win, B.r_hT), (rps[kq],), inc=(kc == 7), out=ps[kq][:m, :512],
                   lhsT=B.hT[:, kc, c0:c0 + m], rhs=B.win[:, kc, gi * 512:(gi + 1) * 512], start=(kc == 0), stop=(kc == 7))
            op("dve", "tensor_copy", (rps[kq],), (B.r_nc3,), out=B.nc3[:m, gi * 512:(gi + 1) * 512], in_=ps[kq][:m, :512])
        S.dma("sp", dsem("nc3"), grp["new_conv"][seq, :, :], B.nc3[m - 3:m, :], reads=(B.r_nc3,), writes=(grp["r_out"],))
    if cx.stop <= 2:
        return
    zr, br, ar = B.tm[:m, 0:512], B.tm[:m, 512:520], B.tm[:m, 520:528]
    qraw, ckvr, krr_ = B.tm[:m, 528:1296], B.tm[:m, 1296:1424], B.tm[:m, 1424:1456]
    op("act", "activation", (B.r_tm,), (rsc,), out=sc[:m, 64:72], in_=br, func=AF.Exp, scale=-1.0)
    op("dve", "tensor_scalar", (rsc,), (rsc,), out=sc[:m, 64:72], in0=sc[:m, 64:72], scalar1=1.0, scalar2=None,
       op0=ALU.add)
    op("dve", "reciprocal", (rsc,), (rsc,), out=sc[:m, 0:8], in_=sc[:m, 64:72])
    op("act", "activation", (rsc,), (rsc,), out=sc[:m, 8:16], in_=sc[:m, 0:8], func=AF.Ln)
    op("dve", "tensor_tensor", (B.r_tm, B.r_g8), (rsc,), out=sc[:m, 64:72], in0=ar, in1=B.g8[:m, 8:16], op=ALU.add)
    op("act", "activation", (rsc,), (rsc,), out=sc[:m, 64:72], in_=sc[:m, 64:72], func=AF.Exp)
    op("act", "activation", (rsc,), (rsc,), out=sc[:m, 64:72], in_=sc[:m, 64:72], func=AF.Ln, bias=cx.one_t[:m, 0:1],
       scale=1.0)
    op("dve", "scalar_tensor_tensor", (rsc, B.r_g8), (rsc,), out=sc[:m, 16:24], in0=sc[:m, 64:72], scalar=-1.0,
       in1=B.g8[:m, 0:8], op0=ALU.mult, op1=ALU.mult)
    op("pe", "matmul", (rsc,), (rps[2],), out=ps[2][:m, 0:8], lhsT=cx.utri[:m, :m], rhs=sc[:m, 16:24], start=True, stop=True)
    op("dve", "tensor_copy", (rps[2],), (rsc,), out=sc[:m, 24:32], in_=ps[2][:m, 0:8])
    op("pe", "matmul", (rsc,), (rps[2],), out=ps[2][:m, 8:16], lhsT=cx.onesf[:m, :m], rhs=sc[:m, 16:24], start=True, stop=True)
    op("dve", "tensor_tensor", (rps[2], rsc), (rsc,), out=sc[:m, 48:56], in0=ps[2][:m, 8:16], in1=sc[:m, 24:32],
       op=ALU.subtract)
    op("act", "activation", (rsc,), (rsc,), out=sc[:m, 48:56], in_=sc[:m, 48:56], func=AF.Exp)
    op("act", "activation", (rsc,), (rsc,), out=sc[:m, 32:40], in_=sc[:m, 24:32], func=AF.Exp)
    op("dve", "tensor_tensor", (rsc,), (rsc,), out=sc[:m, 40:48], in0=sc[:m, 32:40], in1=sc[:m, 0:8], op=ALU.mult)
    op("dve", "tensor_scalar", (rsc,), (rsc,), out=sc[:m, 56:64], in0=sc[:m, 24:32], scalar1=-1.0, scalar2=None,
       op0=ALU.mult)
    op("dve", "tensor_tensor", (rsc,), (rsc,), out=sc[:m, 72:80], in0=sc[:m, 24:32], in1=sc[:m, 8:16], op=ALU.add)
    assert m <= 64
    g2 = sc[:m, 16:24].rearrange("p (c two) -> p two c", two=2)
    for par in range(2):
        op("pe", "matmul", (rsc,), (rps[2],), out=ps[2][par * 64:(par + 1) * 64, 16:20], lhsT=cx.onesf[:m, :64],
           rhs=g2[:, par, :], start=True, stop=True, skip_group_check=True)
    op("act", "activation", (rps[2],), (B.r_glb,), out=B.glb[:, 0:4], in_=ps[2][:, 16:20], func=AF.Exp)
    op("pe", "transpose", (rsc,), (rps[2],), out=ps[2][:8, 32:32 + m], in_=sc[:m, 24:32], identity=cx.ident_f[:m, :m])
    op("pe", "transpose", (rsc,), (rps[2],), out=ps[2][:8, 160:160 + m], in_=sc[:m, 72:80], identity=cx.ident_f[:m, :m])
    op("dve", "tensor_copy", (rps[2],), (B.r_rows,), out=B.rows[:, 0, :m], in_=ps[2][:8, 32:32 + m])
    op("dve", "tensor_copy", (rps[2],), (B.r_rows,), out=B.rows[:, 1, :m], in_=ps[2][:8, 160:160 + m])
    if cx.stop <= 3:
        return
    for c in range(4):
        op("pe", "transpose", (B.r_yc,), (rps[3],), out=ps[3][:m, c * 128:(c + 1) * 128], in_=B.yc[:, 8 + c, c0:c0 + m],
           identity=cx.ident_f[:, :], inc=(c == 3))
    op("dve", "tensor_tensor", (rps[3], rsc), (B.r_vtok,), out=B.vtok[:m, :].rearrange("p (h d) -> p h d", d=64),
       in0=ps[3][:m, :].rearrange("p (h d) -> p h d", d=64), in1=bc(sc[:m, 0:8], 64), op=ALU.mult)
    pkb = ps[2].bitcast(BF16)
    for c in range(4):
        op("pe", "transpose", (B.r_qkT,), (rps[2],), out=pkb[:m, 512 + c * 128:512 + (c + 1) * 128],
           in_=B.qkT[:, 4 + c, c0:c0 + m], identity=cx.ident_bf[:, :], inc=(c == 3))
    kh3 = pkb[:m, 512:1024].rearrange("p (h d) -> p h d", d=64)
    op("dve", "tensor_tensor", (rps[2], rsc), (B.r_ktok,), out=B.ktok[:m, :].rearrange("p (h d) -> p h d", d=64),
       in0=kh3, in1=bc(sc[:m, 40:48], 64), op=ALU.mult)
    op("dve", "tensor_tensor", (rps[2], rsc), (B.r_kdec,), out=B.kdec[:m, :].rearrange("p (h d) -> p h d", d=64),
       in0=kh3, in1=bc(sc[:m, 48:56], 64), op=ALU.mult)
    if cx.stop <= 4:
        return
    W4 = 4 * m
    mle, mlt, id4 = cx.mconst[m]
    for g in range(2):
        Ab = [getattr(B, "A%d%d" % (g, k)) for k in range(2)]
        Mb = [getattr(B, "M%d%d" % (g, k)) for k in range(2)]
        Pb = [getattr(B, "P%d%d" % (g, k)) for k in range(2)]
        rA = [getattr(B, "r_A%d%d" % (g, k)) for k in range(2)]
        rM = [getattr(B, "r_M%d%d" % (g, k)) for k in range(2)]
        rP = [getattr(B, "r_P%d%d" % (g, k)) for k in range(2)]
        pP, rpP = ps[6 + g], rps[6 + g]
        for kind, dec, rdec, mask in ((0, B.decq, B.r_decq, mle), (1, B.decb, B.r_decb, mlt)):
            op("pe", "matmul", (), (rps[3],), inc=False, out=ps[3][:m, 0:W4], lhsT=cx.ident_f[:m, :m], rhs=mask[:m, 0:W4],
               start=True, stop=False)
            for hh in range(4):
                h = 2 * hh + g
                op("pe", "matmul", (B.r_rows,), (rps[3],), inc=(hh == 3), out=ps[3][:m, hh * m:(hh + 1) * m],
                   lhsT=cx.ohsel[:, h, :m], rhs=B.rows[:, kind, :m], start=False, stop=(hh == 3))
            for hh in range(4):
                h = 2 * hh + g
                op("act", "activation", (rps[3], rsc), (rdec,), out=dec[:m, hh * m:(hh + 1) * m],
                   in_=ps[3][:m, hh * m:(hh + 1) * m], func=AF.Exp, bias=sc[:m, 56 + h:57 + h], scale=1.0)
        if cx.stop <= 4.2:
            continue
        for hh in range(4):
            h = 2 * hh + g
            c, po = h // 2, (h % 2) * 64
            op("pe", "matmul", (B.r_qkT,), (rps[4],), inc=(hh == 3), out=ps[4][:m, hh * m:(hh + 1) * m],
               lhsT=B.qkT[po:po + 64, 4 + c, c0:c0 + m], rhs=B.qkT[po:po + 64, 4 + c, c0:c0 + m], start=True, stop=True,
               skip_group_check=True)
        op("dve", "tensor_tensor", (rps[4], B.r_decb), (rA[0],), out=Ab[0][:m, 0:W4], in0=ps[4][:m, 0:W4],
           in1=B.decb[:m, 0:W4], op=ALU.mult)
        for hh in range(4):
            h = 2 * hh + g
            c, po = h // 2, (h % 2) * 64
            op("pe", "matmul", (B.r_qkT,), (rps[5],), inc=(hh == 3), out=ps[5][:m, hh * m:(hh + 1) * m],
               lhsT=B.qkT[po:po + 64, 4 + c, c0:c0 + m], rhs=B.qkT[po:po + 64, c, c0:c0 + m], start=True, stop=True,
               skip_group_check=True)
        op("dve", "tensor_tensor", (rps[5], B.r_decq), (B.r_qk,), out=B.qk[:m, 0:8 * m].rearrange("p (c two i) -> p two c i", two=2, i=m)[:, g],
           in0=ps[5][:m, 0:W4].rearrange("p (c i) -> p c i", i=m),
           in1=B.decq[:m, 0:W4].rearrange("p (c i) -> p c i", i=m), op=ALU.mult)
        if cx.stop <= 4.4:
            continue
        for hh in range(4):
            op("pe", "transpose", (rA[0],), (rps[4],), inc=(hh == 3), out=ps[4][:m, hh * m:(hh + 1) * m],
               in_=Ab[0][:m, hh * m:(hh + 1) * m], identity=cx.ident_f[:m, :m])
        op("act", "activation", (rps[4],), (rM[0],), out=Mb[0][:m, 0:W4], in_=ps[4][:m, 0:W4], func=AF.Copy)
        op("pe", "matmul", (), (rpP,), inc=False, out=pP[:m, 0:W4], lhsT=cx.ident_f[:m, :m], rhs=id4[:m, 0:W4],
           start=True, stop=False, skip_group_check=True)
        for hh in range(4):
            op("pe", "matmul", (rA[0],), (rpP,), inc=(hh == 3), out=pP[:m, hh * m:(hh + 1) * m], lhsT=cx.nident_f[:m, :m],
               rhs=Ab[0][:m, hh * m:(hh + 1) * m], start=False, stop=False, skip_group_check=True)
        op("act", "activation", (rpP,), (rP[0],), out=Pb[0][:m, 0:W4], in_=pP[:m, 0:W4], func=AF.Copy)
        if cx.stop <= 4.6:
            continue
        nlev = NLEV if m > 64 else (5 if m > 32 else (4 if m > 16 else (3 if m > 8 else (2 if m > 4 else 1))))
        for lv in range(1, nlev + 1):
            a, bq = (lv - 1) % 2, lv % 2
            for hh in range(4):
                sl = slice(hh * m, (hh + 1) * m)
                op("pe", "matmul", (rA[a], rM[a]), (rps[4],), inc=(hh == 3), out=ps[4][:m, sl], lhsT=Ab[a][:m, sl],
                   rhs=Mb[a][:m, sl], start=True, stop=True, skip_group_check=True)
            op("act", "activation", (rps[4],), (rM[bq],), out=Mb[bq][:m, 0:W4], in_=ps[4][:m, 0:W4], func=AF.Copy)
            if lv < nlev:
                for hh in range(4):
                    sl = slice(hh * m, (hh + 1) * m)
                    op("pe", "matmul", (rA[a], rM[a]), (rps[5],), inc=(hh == 3), out=ps[5][:m, sl], lhsT=Mb[a][:m, sl],
                       rhs=Ab[a][:m, sl], start=True, stop=True, skip_group_check=True)
                op("dve", "tensor_copy", (rps[5],), (rA[bq],), out=Ab[bq][:m, 0:W4], in_=ps[5][:m, 0:W4])
            for hh in range(4):
                sl = slice(hh * m, (hh + 1) * m)
                op("pe", "matmul", (rM[bq], rP[a]), (rpP,), inc=(hh == 3), out=pP[:m, sl], lhsT=Mb[bq][:m, sl],
                   rhs=Pb[a][:m, sl], start=False, stop=(lv == nlev), skip_group_check=True)
            if lv % 2:
                op("dve", "tensor_copy", (rpP,), (rP[bq],), out=Pb[bq][:m, 0:W4], in_=pP[:m, 0:W4])
            else:
                op("act", "activation", (rpP,), (rP[bq],), out=Pb[bq][:m, 0:W4], in_=pP[:m, 0:W4], func=AF.Copy)
        if cx.stop <= 4.8:
            continue
        Pf, rPf = getattr(B, "Pf%d" % g), getattr(B, "r_Pf%d" % g)
        op("dve", "tensor_copy", (rpP,), (rPf,), out=Pf[:m, 0:W4], in_=pP[:m, 0:W4])
        if cx.stop <= 4.85:
            continue
        for hh in range(4):
            h = 2 * hh + g
            sl = slice(hh * m, (hh + 1) * m)
            op("pe", "matmul", (rPf, B.r_vtok), (rps[0],), inc=(hh == 3), out=ps[0][:m, h * 64:(h + 1) * 64], lhsT=Pf[:m, sl],
               rhs=B.vtok[:m, h * 64:(h + 1) * 64], start=True, stop=True, skip_group_check=True)
        if cx.stop <= 4.9:
            continue
        for hh in range(4):
            h = 2 * hh + g
            sl = slice(hh * m, (hh + 1) * m)
            po = (h % 2) * 64
            op("pe", "matmul", (rPf, B.r_ktok), (rps[1],), inc=(hh == 3),
               out=ps[1][po:po + 64, hh * 128:hh * 128 + m],
               lhsT=B.ktok[:m, h * 64:(h + 1) * 64], rhs=Pf[:m, sl], start=True, stop=True, skip_group_check=True)
        op("act", "activation", (rps[1],), (B.r_wT,), out=B.wT[g * 64:(g + 1) * 64, :, :m],
           in_=ps[1][g * 64:(g + 1) * 64, :].rearrange("p (h i) -> p h i", i=128)[:, :, :m], func=AF.Identity)
    if cx.stop <= 5:
        return
    op("dve", "tensor_copy", (rps[0],), (B.r_u,), out=B.u[:m, :], in_=ps[0][:m, :])
    def sdiag(t_, par):
        return t_[par * 64:(par + 1) * 64, :].rearrange("k (c x) -> k c x", x=128)[:, :, par * 64:(par + 1) * 64]

    if not prompt:
        op("dve", "memset", (), (B.r_S32,), ap=B.S32[:, :], constant=0.0)
        for par in range(2):
            S.dma("sp", dsem("s0"), sdiag(B.S32, par), grp["s0"][seq].rearrange("(c par) k v -> par k c v", par=2)[par],
                  writes=(B.r_S32,))
        op("act", "activation", (B.r_S32,), (B.r_Sbf,), out=B.Sbf[:, :], in_=B.S32[:, :], func=AF.Copy)
    for c in range(4):
        op("pe", "matmul", (B.r_wT, B.r_Sbf), (rps[0],), inc=(c == 3), out=ps[0][:m, c * 128:(c + 1) * 128],
           lhsT=B.wT[:, c, :m], rhs=B.Sbf[:, c * 128:(c + 1) * 128], start=True, stop=True, skip_group_check=True)
    op("dve", "tensor_tensor", (rps[0], B.r_u), (B.r_vnew,), out=B.vnew[:m, :], in0=B.u[:m, :], in1=ps[0][:m, :],
       op=ALU.subtract)
    for c in range(4):
        op("pe", "matmul", (B.r_qkT, B.r_Sbf), (rps[1],), inc=(c == 3), out=ps[1][:m, c * 128:(c + 1) * 128],
           lhsT=B.qkT[:, c, c0:c0 + m], rhs=B.Sbf[:, c * 128:(c + 1) * 128], start=True, stop=True, skip_group_check=True)
    op("dve", "tensor_tensor", (rps[1], rsc), (B.r_o1,), out=B.o1[:m, :].rearrange("p (h d) -> p h d", d=64),
       in0=ps[1][:m, :].rearrange("p (h d) -> p h d", d=64), in1=bc(sc[:m, 32:40], 64), op=ALU.mult)
    for h in range(8):
        op("pe", "matmul", (B.r_qk, B.r_vnew), (rps[0],), inc=(h == 7), out=ps[0][:m, h * 64:(h + 1) * 64],
           lhsT=B.qk[:m, h * m:(h + 1) * m], rhs=B.vnew[:m, h * 64:(h + 1) * 64], start=True, stop=True,
           skip_group_check=True)
    op("dve", "tensor_tensor", (rps[0], B.r_o1), (B.r_osb,), out=B.osb[:m, :], in0=ps[0][:m, :], in1=B.o1[:m, :], op=ALU.add)
    for c in range(4):
        op("pe", "matmul", (B.r_kdec, B.r_vnew), (rps[3],), inc=(c == 3), out=ps[3][:, c * 128:(c + 1) * 128],
           lhsT=B.kdec[:m, c * 128:(c + 1) * 128], rhs=B.vnew[:m, c * 128:(c + 1) * 128], start=True, stop=True,
           skip_group_check=True)
    op("dve", "tensor_tensor", (rps[3],), (B.r_decq,), out=B.decq[:, :].rearrange("p (c x) -> p c x", x=128),
       in0=ps[3][:, :].rearrange("p (c x) -> p c x", x=128), in1=bc_mid(cx.bones[:, :], 4), op=ALU.mult)
    op("dve", "tensor_tensor", (B.r_glb,), (B.r_S32,), out=B.S32[:, :].rearrange("p (c x) -> p c x", x=128),
       in0=B.S32[:, :].rearrange("p (c x) -> p c x", x=128), in1=bc(B.glb[:, 0:4], 128), op=ALU.mult)
    op("pool", "tensor_tensor", (B.r_decq,), (B.r_S32,), out=B.S32[:, :], in0=B.S32[:, :], in1=B.decq[:, :], op=ALU.add)
    if last:
        for par in range(2):
            S.dma("sp", dsem("s0"), grp["new_gdn"][seq].rearrange("(c par) k v -> par k c v", par=2)[par],
                  sdiag(B.S32, par), reads=(B.r_S32,), writes=(grp["r_out"],))
    else:
        op("act", "activation", (B.r_S32,), (B.r_Sbf,), out=B.Sbf[:, :], in_=B.S32[:, :], func=AF.Copy)
    if cx.stop <= 6:
        return
    op("pool", "tensor_tensor", (B.r_osb,), (B.r_o1,), out=B.o1[:m, :], in0=B.osb[:m, :], in1=B.osb[:m, :], op=ALU.mult)
    op("dve", "tensor_reduce", (B.r_o1,), (B.r_ost,), out=B.ost[:m, 0:8], in_=B.o1[:m, :].rearrange("p (h d) -> p h d", d=64),
       axis=AX.X, op=ALU.add)
    op("act", "activation", (B.r_ost,), (B.r_ost,), out=B.ost[:m, 8:16], in_=B.ost[:m, 0:8], func=AF.Sqrt, scale=1.0 / 64,
       bias=cx.eps_t[:m, 0:1])
    op("dve", "reciprocal", (B.r_ost,), (B.r_ost,), out=B.ost[:m, 8:16], in_=B.ost[:m, 8:16])
    op("act", "activation", (B.r_tm,), (B.r_zs,), out=B.zs[:m, :], in_=zr, func=AF.Silu)
    op("dve", "tensor_tensor", (B.r_osb, B.r_ost), (B.r_osb,), out=B.osb[:m, :].rearrange("p (h d) -> p h d", d=64),
       in0=B.osb[:m, :].rearrange("p (h d) -> p h d", d=64), in1=bc(B.ost[:m, 8:16], 64), op=ALU.mult)
    op("pool", "tensor_tensor", (B.r_osb, B.r_gng), (B.r_osb,), out=B.osb[:m, :].rearrange("p (h d) -> p h d", d=64),
       in0=B.osb[:m, :].rearrange("p (h d) -> p h d", d=64), in1=bc_mid(B.gng[:m, :], 8), op=ALU.mult)
    op("dve", "tensor_tensor", (B.r_osb, B.r_zs), (B.r_gout,), out=B.gout[:m, :], in0=B.osb[:m, :], in1=B.zs[:m, :], op=ALU.mult)
    S.dma("sp", dsem("gout"), grp["mix"][tok0:tok0 + m, 0:512], B.gout[:m, :], reads=(B.r_gout,), writes=(grp["r_mix"],))
    if cx.stop <= 7:
        return
    S.dma("sp", dsem("cs"), B.cs[:m, :], grp["cs"][(tok0 if prompt else 0):(tok0 if prompt else 0) + m, :], writes=(B.r_cs,))
    q3 = qraw.rearrange("p (h d) -> p h d", d=96)
    mq3 = B.mq[:m, :].rearrange("p (h d) -> p h d", d=96)
    op("pool", "tensor_tensor", (B.r_tm,), (B.r_mq,), out=B.mq[:m, :], in0=qraw, in1=qraw, op=ALU.mult)
    op("dve", "tensor_reduce", (B.r_mq,), (B.r_mqs,), out=B.mqs[:m, 0:8], in_=mq3[:, :, 0:64], axis=AX.X, op=ALU.add)
    op("dve", "tensor_reduce", (B.r_mq,), (B.r_mqs,), out=B.mqs[:m, 8:16], in_=mq3[:, :, 64:96], axis=AX.X, op=ALU.add)
    op("act", "activation", (B.r_mqs,), (B.r_mqs,), out=B.mqs[:m, 16:24], in_=B.mqs[:m, 0:8], func=AF.Sqrt, scale=1.0 / 64,
       bias=cx.eps_t[:m, 0:1])
    op("act", "activation", (B.r_mqs,), (B.r_mqs,), out=B.mqs[:m, 24:32], in_=B.mqs[:m, 8:16], func=AF.Sqrt, scale=1.0 / 32,
       bias=cx.eps_t[:m, 0:1])
    op("dve", "reciprocal", (B.r_mqs,), (B.r_mqs,), out=B.mqs[:m, 16:32], in_=B.mqs[:m, 16:32])
    op("dve", "tensor_tensor", (B.r_tm, B.r_mqs), (B.r_mq,), out=mq3[:, :, 0:64], in0=q3[:, :, 0:64], in1=bc(B.mqs[:m, 16:24], 64),
       op=ALU.mult)
    op("pool", "tensor_tensor", (B.r_mq, B.r_qng), (B.r_Qb,), out=B.Qb[:m, :].rearrange("p (h d) -> p h d", d=96)[:, :, 0:64],
       in0=mq3[:, :, 0:64], in1=bc_mid(B.qng[:m, :], 8), op=ALU.mult)
    op("dve", "tensor_tensor", (B.r_tm, B.r_mqs), (B.r_mq,), out=mq3[:, :, 64:96], in0=q3[:, :, 64:96], in1=bc(B.mqs[:m, 24:32], 32),
       op=ALU.mult)
    op("pool", "tensor_tensor", (B.r_mq, B.r_qrg), (B.r_mq,), out=mq3[:, :, 64:96], in0=mq3[:, :, 64:96],
       in1=bc_mid(B.qrg[:m, :], 8), op=ALU.mult)
    cosb, sinb = bc_mid(B.cs[:m, 0:16], 8), bc_mid(B.cs[:m, 16:32], 8)
    x1, x2 = mq3[:, :, 64:80], mq3[:, :, 80:96]
    Q3 = B.Qb[:m, :].rearrange("p (h d) -> p h d", d=96)
    op("dve", "tensor_tensor", (B.r_mq, B.r_cs), (B.r_qrr,), out=B.qrr[:m, :, 0:16], in0=x1, in1=cosb, op=ALU.mult)
    op("dve", "tensor_tensor", (B.r_mq, B.r_cs), (B.r_qrr,), out=B.qrr[:m, :, 16:32], in0=x2, in1=sinb, op=ALU.mult)
    op("dve", "tensor_tensor", (B.r_qrr,), (B.r_Qb,), out=Q3[:, :, 64:80], in0=B.qrr[:m, :, 0:16], in1=B.qrr[:m, :, 16:32],
       op=ALU.subtract)
    op("dve", "tensor_tensor", (B.r_mq, B.r_cs), (B.r_qrr,), out=B.qrr[:m, :, 0:16], in0=x1, in1=sinb, op=ALU.mult)
    op("dve", "tensor_tensor", (B.r_mq, B.r_cs), (B.r_qrr,), out=B.qrr[:m, :, 16:32], in0=x2, in1=cosb, op=ALU.mult)
    op("dve", "tensor_tensor", (B.r_qrr,), (B.r_Qb,), out=Q3[:, :, 80:96], in0=B.qrr[:m, :, 0:16], in1=B.qrr[:m, :, 16:32],
       op=ALU.add)
    S.dma("sp", dsem("Qb"), grp["Q"][tok0:tok0 + m, :], B.Qb[:m, :], reads=(B.r_Qb,), writes=(grp["r_Q"],))
    op("pool", "tensor_tensor", (B.r_tm,), (B.r_cn,), out=B.cn[:m, :], in0=ckvr, in1=ckvr, op=ALU.mult)
    op("dve", "tensor_reduce", (B.r_cn,), (B.r_mqs,), out=B.mqs[:m, 0:1], in_=B.cn[:m, :], axis=AX.X, op=ALU.add)
    op("act", "activation", (B.r_mqs,), (B.r_mqs,), out=B.mqs[:m, 1:2], in_=B.mqs[:m, 0:1], func=AF.Sqrt, scale=1.0 / 128,
       bias=cx.eps_t[:m, 0:1])
    op("dve", "reciprocal", (B.r_mqs,), (B.r_mqs,), out=B.mqs[:m, 1:2], in_=B.mqs[:m, 1:2])
    op("dve", "scalar_tensor_tensor", (B.r_tm, B.r_mqs, B.r_ckg), (B.r_cn,), out=B.cn[:m, :], in0=ckvr, scalar=B.mqs[:m, 1:2],
       in1=B.ckg[:m, :], op0=ALU.mult, op1=ALU.mult)
    S.dma("sp", dsem("cn"), grp["new_ckv"][tok0:tok0 + m, :], B.cn[:m, :], reads=(B.r_cn,), writes=(grp["r_ckv"],))
    op("pool", "tensor_tensor", (B.r_tm,), (B.r_krn,), out=B.krn[:m, :], in0=krr_, in1=krr_, op=ALU.mult)
    op("dve", "tensor_reduce", (B.r_krn,), (B.r_mqs,), out=B.mqs[:m, 2:3], in_=B.krn[:m, :], axis=AX.X, op=ALU.add)
    op("act", "activation", (B.r_mqs,), (B.r_mqs,), out=B.mqs[:m, 3:4], in_=B.mqs[:m, 2:3], func=AF.Sqrt, scale=1.0 / 32,
       bias=cx.eps_t[:m, 0:1])
    op("dve", "reciprocal", (B.r_mqs,), (B.r_mqs,), out=B.mqs[:m, 3:4], in_=B.mqs[:m, 3:4])
    op("dve", "scalar_tensor_tensor", (B.r_tm, B.r_mqs, B.r_krg), (B.r_krn,), out=B.krn[:m, :], in0=krr_, scalar=B.mqs[:m, 3:4],
       in1=B.krg[:m, :], op0=ALU.mult, op1=ALU.mult)
    k1, k2, cs1, sn1 = B.krn[:m, 0:16], B.krn[:m, 16:32], B.cs[:m, 0:16], B.cs[:m, 16:32]
    op("dve", "tensor_tensor", (B.r_krn, B.r_cs), (B.r_krr,), out=B.krr[:m, 0:16], in0=k1, in1=cs1, op=ALU.mult)
    op("dve", "tensor_tensor", (B.r_krn, B.r_cs), (B.r_krr,), out=B.krr[:m, 16:32], in0=k2, in1=sn1, op=ALU.mult)
    op("dve", "tensor_tensor", (B.r_krr,), (B.r_qrr,), out=B.qrr[:m, 0, 0:16], in0=B.krr[:m, 0:16], in1=B.krr[:m, 16:32],
       op=ALU.subtract)
    op("dve", "tensor_tensor", (B.r_krn, B.r_cs), (B.r_krr,), out=B.krr[:m, 0:16], in0=k1, in1=sn1, op=ALU.mult)
    op("dve", "tensor_tensor", (B.r_krn, B.r_cs), (B.r_krr,), out=B.krr[:m, 16:32], in0=k2, in1=cs1, op=ALU.mult)
    op("dve", "tensor_tensor", (B.r_krr,), (B.r_qrr,), out=B.qrr[:m, 0, 16:32], in0=B.krr[:m, 0:16], in1=B.krr[:m, 16:32],
       op=ALU.add)
    S.dma("sp", dsem("kr"), grp["new_kr"][tok0:tok0 + m, :], B.qrr[:m, 0, :], reads=(B.r_qrr,), writes=(grp["r_kr"],))


def phase_mixer(cx, S, groups, P):
    with ExitStack() as st:
        B = mixer_alloc(cx, S, st, "mx", P)
        mc = sb(cx, st, "mx_cst", [128, CST_COLS - 128], F32)
        r_mc = Res("mxcst")
        S.dma("sp", S.dsem("mxcst"), mc[:, :], cx.cst[:, 128:CST_COLS], writes=(r_mc,))
        cx.utri = mc[:, 0:128]
        cx.onesf = mc[:, 128:256]
        cx.ohsel = mc[0:8, 384:1408].rearrange("p (h i) -> p h i", i=128)
        cx.mconst = {}
        off = 1408
        for m_ in (64, 4):
            cx.mconst[m_] = (mc[:, off + 4 * m_:off + 8 * m_], mc[:, off + 8 * m_:off + 12 * m_], mc[:, off:off + 4 * m_])
            off += 12 * m_
        idf = sb(cx, st, "mx_idf", [128, 128], F32)
        S.op("dve", "tensor_scalar", dict(out=idf[:, :], in0=cx.ident_f, scalar1=-1.0, scalar2=None, op0=ALU.mult),
             writes=(r_mc,))
        cx.nident_f = idf[:, :]
        dc = S.dsem("mxc")
        r_c = Res("mxconst")
        S.dma("sp", dc, B.cw[:, :, :], P["conv_w_fm"], writes=(B.r_cw,))
        S.dma("sp", dc, B.g8[:, 0:8], row_bcast(P["a_log"], 128), writes=(B.r_g8,))
        S.dma("sp", dc, B.g8[:, 8:16], row_bcast(P["dt_bias"], 128), writes=(B.r_g8,))
        S.dma("sp", dc, B.gng[:, :], row_bcast(P["gdn_norm"], 128), writes=(B.r_gng,))
        S.dma("sp", dc, B.qng[:, :], row_bcast(P["qn_g"], 128), writes=(B.r_qng,))
        S.dma("sp", dc, B.kng[:, :], row_bcast(P["kn_g"], 128), writes=(B.r_kng,))
        S.dma("sp", dc, B.qrg[:, :], row_bcast(P["qr_g"], 128), writes=(B.r_qrg,))
        S.dma("sp", dc, B.ckg[:, :], row_bcast(P["ckv_g"], 128), writes=(B.r_ckg,))
        S.dma("sp", dc, B.krg[:, :], row_bcast(P["kr_g"], 128), writes=(B.r_krg,))
        S.barrier()
        S.op("act", "activation", dict(out=B.g8[:, 0:8], in_=B.g8[:, 0:8], func=AF.Exp), writes=(B.r_g8,))
        S.op("dve", "scalar_tensor_tensor", dict(out=B.qng[:, :], in0=B.qng[:, :], scalar=ATT_SCALE, in1=B.kng[:, :],
                                                 op0=ALU.mult, op1=ALU.mult), writes=(B.r_qng,))
        S.op("dve", "tensor_scalar", dict(out=B.qrg[:, :], in0=B.qrg[:, :], scalar1=ATT_SCALE, scalar2=None, op0=ALU.mult),
             writes=(B.r_qrg,))
        S.barrier()
        for gi, grp in enumerate(groups):
            mixer_group(cx, S, B, "mx%d" % gi, grp, P)
            S.barrier()
            S.emit()


def phase_wout(cx, S, groups, w_out_ap):
    with ExitStack() as st:
        wo = sb(cx, st, "wo_w", [128, 8, D], BF16)
        r_wo = Res()
        load_weight_bf16(cx, S, "wow", wo, r_wo, w_out_ap, 8, D, D)
        Gt = sb(cx, st, "wo_Gt", [128, D], F32)
        mx = [sb(cx, st, "wo_mx%d" % i, [128, D], BF16) for i in range(2)]
        mT = sb(cx, st, "wo_mT", [128, 8, 128], BF16)
        xr = [sb(cx, st, "wo_xr%d" % i, [128, D], F32) for i in range(2)]
        tmp = sb(cx, st, "wo_tmp", [128, 512], F32)
        r_Gt, r_mT, r_tmp = Res(), Res(), Res()
        r_mx, r_xr = [Res(), Res()], [Res(), Res()]
        d_g = S.dsem("wo_g")
        d_mx = [S.dsem("wo_mx0"), S.dsem("wo_mx1")]
        d_xr = [S.dsem("wo_xr0"), S.dsem("wo_xr1")]
        d_xst = [S.dsem("wo_xst0"), S.dsem("wo_xst1")]
        i = 0
        for g in groups:
            n, T = g["n"], g["T"]
            S.dma("sp", d_g, Gt[:n, :], g["mods"][5, :n, :], writes=(r_Gt,))
            for r0 in range(0, T, 128):
                m = min(128, T - r0)
                k = i % 2
                i += 1
                S.dma("sp", d_mx[k], mx[k][:m, :], g["mix"][r0:r0 + m, :], reads=(g["r_mix"],), writes=(r_mx[k],))
                S.dma("sp", d_xr[k], xr[k][:m, :], g["src"][r0:r0 + m, :], reads=(g["src_res"][r0 // 128],),
                      writes=(r_xr[k],))
                ptb = cx.ps[6 + k].bitcast(BF16)
                for c in range(8):
                    S.op("pe", "transpose", dict(out=ptb[:, c * 128:c * 128 + m], in_=mx[k][:m, c * 128:(c + 1) * 128],
                                                 identity=cx.ident_bf[:m, :m]), reads=(r_mx[k],), writes=(cx.rps[6 + k],),
                         inc=(c == 7))
                S.op("act", "activation", dict(out=mT[:, :, :m], in_=ptb.rearrange("p (c t) -> p c t", c=8)[:, :, :m],
                                               func=AF.Copy), reads=(cx.rps[6 + k],), writes=(r_mT,))
                for half in range(2):
                    pk = (i * 2 + half) % 4
                    for c in range(8):
                        S.op("pe", "matmul", dict(out=cx.ps[pk][:m, :], lhsT=mT[:, c, :m],
                                                  rhs=wo[:, c, half * 512:(half + 1) * 512], start=(c == 0), stop=(c == 7)),
                             reads=(r_mT, r_wo), writes=(cx.rps[pk],), inc=(c == 7))
                    S.op("dve", "tensor_tensor", dict(out=tmp[:m, :], in0=cx.ps[pk][:m, :],
                                                      in1=Gt[:m, half * 512:(half + 1) * 512], op=ALU.mult),
                         reads=(cx.rps[pk], r_Gt), writes=(r_tmp,))
                    S.op("pool", "tensor_tensor", dict(out=xr[k][:m, half * 512:(half + 1) * 512],
                                                       in0=xr[k][:m, half * 512:(half + 1) * 512], in1=tmp[:m, :],
                                                       op=ALU.add), reads=(r_tmp,), writes=(r_xr[k],))
                S.dma("pool", d_xst[k], g["dst"][r0:r0 + m, :], xr[k][:m, :], reads=(r_xr[k],),
                      writes=(g["dst_res"][r0 // 128],))
        S.barrier()
        S.emit()


class AttBufs:
    pass


def att_alloc(cx, S, st, name, P):
    A = AttBufs()

    def t(nm, shape, dt):
        tt_ = sb(cx, st, name + nm, shape, dt)
        setattr(A, nm, tt_)
        setattr(A, "r_" + nm, Res(nm))
        return tt_

    t("wuk", [128, 512], BF16)
    t("wuv", [128, 512], BF16)
    t("wst", [128, 512], F32)
    d = S.dsem(name + "w")
    S.dma("sp", d, A.wst[:, :], P["w_uk"], writes=(A.r_wst,))
    S.op("dve", "tensor_copy", dict(out=A.wuk[:, :], in_=A.wst[:, :]), reads=(A.r_wst,), writes=(A.r_wuk,))
    S.dma("sp", d, A.wst[:, :], P["w_uv"], writes=(A.r_wst,))
    S.op("dve", "tensor_copy", dict(out=A.wuv[:, :], in_=A.wst[:, :]), reads=(A.r_wst,), writes=(A.r_wuv,))
    for k_ in range(2):
        t("cTb%d" % k_, [128, 128], BF16)
        t("sq%d" % k_, [128, 512], F32)
        t("kst%d" % k_, [128, 16], F32)
        t("Kf%d" % k_, [128, 8, 96], BF16)
        t("KT%d" % k_, [96, 8, 128], BF16)
    t("QT", [96, 8, 128], BF16)
    t("Qb", [128, 768], BF16)
    t("PT", [128, 512], BF16)
    t("ctx", [128, 8, 128], BF16)
    t("rden", [128, 8], F32)
    t("ctxT", [128, 8, 128], BF16)
    t("mo", [128, 512], BF16)
    t("m01", [128, 128], BF16)
    A.d = {}
    return A


def att_kside(cx, S, A, c_blk, kr_blk, r_src, n, KT_dst, r_KT, k=0, stage=None):
    ps, rps = cx.ps, cx.rps
    cTb, sq, kst, Kf = (getattr(A, nm + str(k)) for nm in ("cTb", "sq", "kst", "Kf"))
    r_cTb, r_sq, r_kst, r_Kf = (getattr(A, "r_" + nm + str(k)) for nm in ("cTb", "sq", "kst", "Kf"))
    p0, p1, p2 = (0, 1, 2) if k == 0 else (6, 7, 3)

    def op(eng, meth, reads, writes, inc=True, **kw):
        S.op(eng, meth, kw, reads=reads, writes=writes, inc=inc)

    if stage == "B":
        ptb = ps[p2].bitcast(BF16)
        for h in range(8):
            op("pe", "transpose", (r_Kf,), (rps[p2],), inc=(h == 7), out=ptb[0:96, h * 128:h * 128 + n], in_=Kf[:n, h, :],
               identity=cx.ident_bf[:n, :n])
        op("act", "activation", (rps[p2],), (r_KT,), out=KT_dst, in_=ptb[0:96, :].rearrange("p (h t) -> p h t", t=128)[:, :, :n],
           func=AF.Copy)
        return
    op("pe", "transpose", (r_src,), (rps[p0],), out=ps[p0][:, :n], in_=c_blk, identity=cx.ident_f[:n, :n])
    op("act", "activation", (rps[p0],), (r_cTb,), out=cTb[:, :n], in_=ps[p0][:, :n], func=AF.Identity)
    op("pe", "matmul", (r_cTb, A.r_wuk), (rps[p1],), out=ps[p1][:n, :], lhsT=cTb[:, :n], rhs=A.wuk[:, :], start=True, stop=True)
    op("act", "activation", (rps[p1],), (r_sq,), out=sq[:n, :], in_=ps[p1][:n, :], func=AF.Square)
    op("dve", "tensor_reduce", (r_sq,), (r_kst,), out=kst[:n, 0:8], in_=sq[:n, :].rearrange("p (h d) -> p h d", d=64),
       axis=AX.X, op=ALU.add)
    op("act", "activation", (r_kst,), (r_kst,), out=kst[:n, 8:16], in_=kst[:n, 0:8], func=AF.Sqrt, scale=1.0 / 64,
       bias=cx.eps_t[:n, 0:1])
    op("dve", "reciprocal", (r_kst,), (r_kst,), out=kst[:n, 8:16], in_=kst[:n, 8:16])
    op("dve", "tensor_tensor", (rps[p1], r_kst), (r_Kf,), out=Kf[:n, :, 0:64],
       in0=ps[p1][:n, :].rearrange("p (h d) -> p h d", d=64), in1=bc(kst[:n, 8:16], 64), op=ALU.mult)
    op("pool", "tensor_copy", (r_src,), (r_Kf,), out=Kf[:n, :, 64:96], in_=bc_mid(kr_blk, 8))
    if stage == "A":
        return
    ptb = ps[p2].bitcast(BF16)
    for h in range(8):
        op("pe", "transpose", (r_Kf,), (rps[p2],), inc=(h == 7), out=ptb[0:96, h * 128:h * 128 + n], in_=Kf[:n, h, :],
           identity=cx.ident_bf[:n, :n])
    op("act", "activation", (rps[p2],), (r_KT,), out=KT_dst, in_=ptb[0:96, :].rearrange("p (h t) -> p h t", t=128)[:, :, :n],
       func=AF.Copy)


def att_out(cx, S, A, nq, r_ctx_in, mix_dst, r_mix, dsem_):
    ps, rps = cx.ps, cx.rps
    ptb = ps[2].bitcast(BF16)
    for h in range(8):
        S.op("pe", "transpose", dict(out=ptb[:, h * 128:h * 128 + nq], in_=A.ctx[:nq, h, :], identity=cx.ident_bf[:nq, :nq]),
             reads=(A.r_ctx,), writes=(rps[2],), inc=(h == 7))
    S.op("act", "activation", dict(out=A.ctxT[:, :, :nq], in_=ptb.rearrange("p (h t) -> p h t", t=128)[:, :, :nq], func=AF.Copy),
         reads=(rps[2],), writes=(A.r_ctxT,))
    for h in range(8):
        S.op("pe", "matmul", dict(out=ps[3][:nq, h * 64:(h + 1) * 64], lhsT=A.ctxT[:, h, :nq], rhs=A.wuv[:, h * 64:(h + 1) * 64],
                                  start=True, stop=True, skip_group_check=True), reads=(A.r_ctxT, A.r_wuv), writes=(rps[3],),
             inc=(h == 7))
    S.op("dve", "tensor_copy", dict(out=A.mo[:nq, :], in_=ps[3][:nq, :]), reads=(rps[3],), writes=(A.r_mo,))
    S.dma("sp", dsem_, mix_dst, A.mo[:nq, :], reads=(A.r_mo,), writes=(r_mix,))


def phase_attn_prompt(cx, S, grp, P):
    T = grp["T"]
    nb = T // 128
    with ExitStack() as st:
        A = att_alloc(cx, S, st, "ap", P)
        KTall = sb(cx, st, "ap_KTall", [96, 8, T], BF16)
        caug = sb(cx, st, "ap_caug", [128, nb, 132], BF16)
        cst_ = [sb(cx, st, "ap_cst%d" % i, [128, 160], F32) for i in range(2)]
        r_KTall, r_caug = Res(), Res()
        r_cst = [Res(), Res()]
        d_cst = [S.dsem("ap_c0"), S.dsem("ap_c1")]
        d_q, d_o = S.dsem("ap_q"), S.dsem("ap_o")
        ps, rps = cx.ps, cx.rps
        S.op("pool", "memset", dict(ap=caug[:, :, 128:132], constant=1.0), writes=(r_caug,))
        S.op("pool", "memset", dict(ap=A.m01[:, :], constant=1.0), writes=(A.r_m01,))
        S.op("pool", "affine_select", dict(out=A.m01[:, :], in_=A.m01[:, :], pattern=[[1, 128]], compare_op=ALU.is_ge, fill=0.0,
                                           base=0, channel_multiplier=-1), writes=(A.r_m01,))
        for kb in range(nb):
            k = kb % 2
            S.dma("sp", d_cst[k], cst_[k][:, 0:128], grp["new_ckv"][kb * 128:(kb + 1) * 128, :], reads=(grp["r_ckv"],),
                  writes=(r_cst[k],))
            S.dma("sp", d_cst[k], cst_[k][:, 128:160], grp["new_kr"][kb * 128:(kb + 1) * 128, :], reads=(grp["r_kr"],),
                  writes=(r_cst[k],))
            S.op("dve", "tensor_copy", dict(out=caug[:, kb, 0:128], in_=cst_[k][:, 0:128]), reads=(r_cst[k],), writes=(r_caug,))
            att_kside(cx, S, A, cst_[k][:, 0:128], cst_[k][:, 128:160], r_cst[k], 128, KTall[:, :, kb * 128:(kb + 1) * 128], r_KTall, k=k)
        for qb in range(nb):
            S.dma("sp", d_q, A.Qb[:, :], grp["Q"][qb * 128:(qb + 1) * 128, :], reads=(grp["r_Q"],), writes=(A.r_Qb,))
            ptb = ps[2].bitcast(BF16)
            for h in range(8):
                S.op("pe", "transpose", dict(out=ptb[0:96, h * 128:(h + 1) * 128], in_=A.Qb[:, h * 96:(h + 1) * 96],
                                             identity=cx.ident_bf[:, :]), reads=(A.r_Qb,), writes=(rps[2],), inc=(h == 7))
            S.op("act", "activation", dict(out=A.QT[:, :, :], in_=ptb[0:96, :].rearrange("p (h t) -> p h t", t=128), func=AF.Copy),
                 reads=(rps[2],), writes=(A.r_QT,))
            gi = 0
            for h in range(8):
                pc = 4 + h % 2
                for kb0 in range(0, qb + 1, 4):
                    ng = min(4, qb + 1 - kb0)
                    pk = gi % 2
                    gi += 1
                    for j in range(ng):
                        kb = kb0 + j
                        S.op("pe", "matmul", dict(out=ps[pk][:, j * 128:(j + 1) * 128], lhsT=KTall[:, h, kb * 128:(kb + 1) * 128],
                                                  rhs=A.QT[:, h, :], start=True, stop=True, skip_group_check=True),
                             reads=(r_KTall, A.r_QT), writes=(rps[pk],), inc=(j == ng - 1))
                    S.op("act", "activation", dict(out=A.PT[:, 0:ng * 128], in_=ps[pk][:, 0:ng * 128], func=AF.Exp),
                         reads=(rps[pk],), writes=(A.r_PT,))
                    if kb0 + ng - 1 == qb:
                        j = ng - 1
                        S.op("dve", "tensor_tensor", dict(out=A.PT[:, j * 128:(j + 1) * 128], in0=A.PT[:, j * 128:(j + 1) * 128],
                                                          in1=A.m01[:, :], op=ALU.mult), reads=(A.r_m01,), writes=(A.r_PT,))
                    for j in range(ng):
                        kb = kb0 + j
                        S.op("pe", "matmul", dict(out=ps[pc][:, 0:129], lhsT=A.PT[:, j * 128:(j + 1) * 128], rhs=caug[:, kb, 0:129],
                                                  start=(kb == 0), stop=(kb == qb), skip_group_check=True),
                             reads=(A.r_PT, r_caug), writes=(rps[pc],), inc=(j == ng - 1))
                S.op("dve", "reciprocal", dict(out=A.rden[:, h:h + 1], in_=ps[pc][:, 128:129]), reads=(rps[pc],), writes=(A.r_rden,))
                S.op("dve", "tensor_scalar", dict(out=A.ctx[:, h, :], in0=ps[pc][:, 0:128], scalar1=A.rden[:, h:h + 1], scalar2=None,
                                                  op0=ALU.mult), reads=(rps[pc], A.r_rden), writes=(A.r_ctx,))
            att_out(cx, S, A, 128, A.r_ctx, grp["mix"][qb * 128:(qb + 1) * 128, 512:1024], grp["r_mix"], d_o)
        S.barrier()
        S.emit()


def phase_attn_sample(cx, S, grp, P, cache_c, cache_k, ptab, nseq, npages):
    with ExitStack() as st:
        A = att_alloc(cx, S, st, "as", P)
        cgs = [sb(cx, st, "as_cg%d" % i, [128, 128 * 128], F32) for i in range(2)]
        kgs = [sb(cx, st, "as_kg%d" % i, [128, 128 * 32], F32) for i in range(2)]
        idxs = [sb(cx, st, "as_idx%d" % i, [128, 2], I32) for i in range(2)]
        r_cgs, r_idxs = [Res(), Res()], [Res(), Res()]
        d_cgs, d_idxs = [S.dsem("as_cg0"), S.dsem("as_cg1")], [S.dsem("as_ix0"), S.dsem("as_ix1")]
        cb2 = [sb(cx, st, "as_cb%d" % i, [128, 132], BF16) for i in range(3)]
        cn = sb(cx, st, "as_cn", [4, 160], F32)
        qs = sb(cx, st, "as_qs", [4, 768], BF16)
        QTs = sb(cx, st, "as_QTs", [96, 8, 4], BF16)
        PTs2 = [sb(cx, st, "as_PTs%d" % i, [128, 32], BF16) for i in range(2)]
        ms = sb(cx, st, "as_ms", [4, 32], BF16)
        c32 = sb(cx, st, "as_c32", [32, 128], BF16)
        cT32 = sb(cx, st, "as_cT32", [128, 32], BF16)
        rd = sb(cx, st, "as_rd", [32, 1], F32)
        mo = sb(cx, st, "as_mo", [4, 512], BF16)
        r_cg, r_kg, r_idx, r_cb, r_cn, r_qs, r_QTs, r_PTs, r_ms, r_c32, r_cT32, r_rd, r_mo = (Res() for _ in range(13))
        d_idx, d_cg, d_kg, d_cn, d_qs, d_mo = (S.dsem("as%d" % i) for i in range(6))
        ps, rps = cx.ps, cx.rps
        r_cb2, r_PTs2, r_sc = [Res(), Res(), Res()], [Res(), Res()], [Res(), Res()]
        for i_ in range(3):
            S.op("pool", "memset", dict(ap=cb2[i_][:, 128:132], constant=1.0), writes=(r_cb2[i_],))
        S.op("pool", "memset", dict(ap=ms[:, :], constant=1.0), writes=(r_ms,))
        S.op("pool", "affine_select", dict(out=ms[:, :], in_=ms[:, :], pattern=[[0, 8], [1, 4]], compare_op=ALU.is_ge, fill=0.0,
                                           base=0, channel_multiplier=-1), writes=(r_ms,))
        def gather(s):
            k_ = s % 2
            S.dma("sp", d_idxs[k_], idxs[k_][:npages, 0:1], ptab[s:s + 1, :].rearrange("o p -> p o"), writes=(r_idxs[k_],),
                  allow_slow_non_contiguous=True)
            S.idma(d_cgs[k_], reads=(r_idxs[k_],), writes=(r_cgs[k_],), out=cgs[k_][:npages, :], out_offset=None, in_=cache_c,
                   in_offset=bass.IndirectOffsetOnAxis(ap=idxs[k_][:npages, 0:1], axis=0))
            S.idma(d_cgs[k_], reads=(r_idxs[k_],), writes=(r_cgs[k_],), out=kgs[k_][:npages, :], out_offset=None, in_=cache_k,
                   in_offset=bass.IndirectOffsetOnAxis(ap=idxs[k_][:npages, 0:1], axis=0))

        gather(0)
        for s in range(nseq):
            if s + 1 < nseq:
                gather(s + 1)
            cg, kg, r_cg, r_kg = cgs[s % 2], kgs[s % 2], r_cgs[s % 2], r_cgs[s % 2]
            S.dma("sp", d_qs, qs[:, :], grp["Q"][4 * s:4 * s + 4, :], reads=(grp["r_Q"],), writes=(r_qs,))
            ptb = ps[2].bitcast(BF16)
            for h in range(8):
                S.op("pe", "transpose", dict(out=ptb[0:96, h * 128:h * 128 + 4], in_=qs[:, h * 96:(h + 1) * 96],
                                             identity=cx.ident_bf[:4, :4]), reads=(r_qs,), writes=(rps[2],), inc=(h == 7))
            S.op("act", "activation", dict(out=QTs[:, :, :], in_=ptb[0:96, :].rearrange("p (h t) -> p h t", t=128)[:, :, 0:4],
                                           func=AF.Copy), reads=(rps[2],), writes=(r_QTs,))
            S.dma("sp", d_cn, cn[:, 0:128], grp["new_ckv"][4 * s:4 * s + 4, :], reads=(grp["r_ckv"],), writes=(r_cn,))
            S.dma("sp", d_cn, cn[:, 128:160], grp["new_kr"][4 * s:4 * s + 4, :], reads=(grp["r_kr"],), writes=(r_cn,))
            nblk = 128 + 1

            def blk_args(t):
                if t < 128:
                    n = npages
                    return n, cg[:n, t * 128:(t + 1) * 128], kg[:n, t * 32:(t + 1) * 32], r_cg, (r_cg, r_kg)
                return 4, cn[:, 0:128], cn[:, 128:160], r_cn, (r_cn,)

            def stage_a(t):
                n, c_blk, k_blk, r_src, rs = blk_args(t)
                kk = t % 2
                S.op("pool", "tensor_copy", dict(out=cb2[t % 3][:n, 0:128], in_=c_blk), reads=rs, writes=(r_cb2[t % 3],))
                att_kside(cx, S, A, c_blk, k_blk, r_src, n, None, None, k=kk, stage="A")

            def stage_b1(t):
                n, c_blk, k_blk, r_src, rs = blk_args(t)
                kk = t % 2
                KT, r_KT = (A.KT0, A.r_KT0) if kk == 0 else (A.KT1, A.r_KT1)
                att_kside(cx, S, A, c_blk, k_blk, r_src, n, KT[:, :, :n], r_KT, k=kk, stage="B")

            def stage_b2(t):
                n, c_blk, k_blk, r_src, rs = blk_args(t)
                kk = t % 2
                cb, r_cb, PTs, r_PTs = cb2[t % 3], r_cb2[t % 3], PTs2[kk], r_PTs2[kk]
                KT, r_KT = (A.KT0, A.r_KT0) if kk == 0 else (A.KT1, A.r_KT1)
                pb = 4 if kk == 0 else 6
                for h in range(8):
                    S.op("pe", "matmul", dict(out=ps[pb][:n, h * 4:(h + 1) * 4], lhsT=KT[:, h, :n], rhs=QTs[:, h, :],
                                              start=True, stop=True, skip_group_check=True), reads=(r_KT, r_QTs),
                         writes=(rps[pb],), inc=(h == 7))
                S.op("act", "activation", dict(out=PTs[:n, :], in_=ps[pb][:n, 0:32], func=AF.Exp), reads=(rps[pb],),
                     writes=(r_PTs,))
                if t == 128:
                    S.op("dve", "tensor_tensor", dict(out=PTs[:n, :], in0=PTs[:n, :], in1=ms[:n, :], op=ALU.mult),
                         reads=(r_ms,), writes=(r_PTs,))
                S.op("pe", "matmul", dict(out=ps[5][0:32, 0:129], lhsT=PTs[:n, :], rhs=cb[:n, 0:129], start=(t == 0),
                                          stop=(t == nblk - 1), skip_group_check=True), reads=(r_PTs, r_cb), writes=(rps[5],))

            stage_a(0)
            for t in range(nblk):
                if t + 1 < nblk:
                    stage_a(t + 1)
                stage_b1(t)
                if t >= 1:
                    stage_b2(t - 1)
            stage_b2(nblk - 1)
            S.op("dve", "reciprocal", dict(out=rd[:, :], in_=ps[5][0:32, 128:129]), reads=(rps[5],), writes=(r_rd,))
            S.op("dve", "tensor_scalar", dict(out=c32[:, :], in0=ps[5][0:32, 0:128], scalar1=rd[:, 0:1], scalar2=None,
                                              op0=ALU.mult), reads=(rps[5], r_rd), writes=(r_c32,))
            S.op("pe", "transpose", dict(out=ptb[:, 0:32], in_=c32[:, :], identity=cx.ident_bf[:32, :32]), reads=(r_c32,),
                 writes=(rps[2],))
            S.op("act", "activation", dict(out=cT32[:, :], in_=ptb[:, 0:32], func=AF.Copy), reads=(rps[2],), writes=(r_cT32,))
            for h in range(8):
                S.op("pe", "matmul", dict(out=ps[3][0:4, h * 64:(h + 1) * 64], lhsT=cT32[:, h * 4:(h + 1) * 4],
                                          rhs=A.wuv[:, h * 64:(h + 1) * 64], start=True, stop=True, skip_group_check=True),
                     reads=(r_cT32, A.r_wuv), writes=(rps[3],), inc=(h == 7))
            S.op("dve", "tensor_copy", dict(out=mo[:, :], in_=ps[3][0:4, :]), reads=(rps[3],), writes=(r_mo,))
            S.dma("sp", d_mo, grp["mix"][4 * s:4 * s + 4, 512:1024], mo[:, :], reads=(r_mo,), writes=(grp["r_mix"],))
            if s % 4 == 3:
                S.barrier()
                S.emit()
        S.barrier()
        S.emit()


CST_COLS = 128 + 128 + 128 + 128 + 1024 + 3 * 256 + 3 * 16


def host_consts():
    c = np.zeros((128, CST_COLS), np.float32)
    j = np.arange(128)[:, None]
    i = np.arange(128)[None, :]
    same = (j // 64 == i // 64)
    c[:, 0:128] = np.eye(128)
    c[:, 128:256] = (j <= i) & same
    c[:, 256:384] = same
    c[:, 384:512] = same
    oh = np.zeros((8, 8, 128), np.float32)
    for h in range(8):
        oh[h, h, :] = 1.0
    c[0:8, 512:1536] = oh.reshape(8, 1024)
    off = 1536
    for m in (64, 4):
        jj = np.arange(m)[:, None]
        ii = np.arange(m)[None, :]
        c[0:m, off:off + 4 * m] = np.tile(np.eye(m, dtype=np.float32), (1, 4))
        c[0:m, off + 4 * m:off + 8 * m] = np.tile(np.where(jj <= ii, 0.0, -1e30).astype(np.float32), (1, 4))
        c[0:m, off + 8 * m:off + 12 * m] = np.tile(np.where(jj < ii, 0.0, -1e30).astype(np.float32), (1, 4))
        off += 12 * m
    return c


def host_rope_table(pos):
    half = 16
    inv = (np.float32(10000.0) ** (-(np.arange(half, dtype=np.float32) / np.float32(half)))).astype(np.float32)
    ang = (pos.astype(np.float32)[:, None] * inv[None, :]).astype(np.float32)
    return np.concatenate([np.cos(ang), np.sin(ang)], axis=1).astype(np.float32)


def build(cfg):
    TP = cfg["TP"]
    NS = cfg["NS"]
    NSEQ = NS // 4
    phases = cfg.get("phases", ("adaln", "ffn1"))
    nc = bass.Bass("TRN2", target_bir_lowering=False)
    cx = Ctx()
    cx.nc = nc
    cx.stop = cfg.get("stop", 99)

    def din(name, shape, dt=F32):
        return nc.dram_tensor(name, list(shape), dt, kind="ExternalInput").ap()

    def dout(name, shape, dt=F32):
        return nc.dram_tensor(name, list(shape), dt, kind="ExternalOutput").ap()

    def dscr(name, shape, dt=F32):
        return nc.dram_tensor(name, list(shape), dt, kind="Internal").ap()

    xp = din("xp", [TP, D])
    xs_in = din("xs", [NS, D])
    c_rep = din("c_rep", [192, D])
    cst = din("cst", [128, CST_COLS])
    ada_w = din("ada_w", [D, 9 * D])
    ada_b = din("ada_b", [1, 9 * D])
    norm_ffn1 = din("norm_ffn1", [1, D])
    ffn1_wi = din("ffn1_wi", [D, 2 * DFF])
    ffn1_wo = din("ffn1_wo", [DFF, D])
    P = dict(norm_mix=din("norm_mix", [1, D]), w_in=din("w_in", [D, 2992]), conv_w_fm=din("conv_w_fm", [128, 12, 4]),
             a_log=din("a_log", [1, 8]), dt_bias=din("dt_bias", [1, 8]), gdn_norm=din("gdn_norm", [1, 64]),
             qn_g=din("qn_g", [1, 64]), qr_g=din("qr_g", [1, 32]), ckv_g=din("ckv_g", [1, 128]), kr_g=din("kr_g", [1, 32]),
             kn_g=din("kn_g", [1, 64]))
    P["w_uk"] = din("w_uk", [128, 512])
    P["w_uv"] = din("w_uv", [128, 512])
    w_out = din("w_out", [D, D])
    norm_ffn2 = din("norm_ffn2", [1, D])
    ffn2_wi = din("ffn2_wi", [D, 2 * DFF])
    ffn2_wo = din("ffn2_wo", [DFF, D])
    NPOOL = cfg.get("npool", 20480)
    cache_c = din("cache_c", [NPOOL, 128 * 128])
    cache_k = din("cache_k", [NPOOL, 128 * 32])
    ptab = din("ptab", [NSEQ, cfg.get("npages", 128)], I32)
    cs_p = din("cs_p", [TP, 32])
    cs_s = din("cs_s", [4, 32])
    st_conv = din("st_conv", [NSEQ, 3, 1536])
    st_gdn = din("st_gdn", [NSEQ, 8, 64, 64])

    yp = dout("yp", [TP, D])
    ys = dout("ys", [NS, D])
    o_ckv_p, o_kr_p = dout("ckv_p", [TP, 128]), dout("kr_p", [TP, 32])
    o_conv_p, o_gdn_p = dout("conv_p", [1, 3, 1536]), dout("gdn_p", [1, 8, 64, 64])
    o_ckv_s, o_kr_s = dout("ckv_s", [NS, 128]), dout("kr_s", [NS, 32])
    o_conv_s, o_gdn_s = dout("conv_s", [NSEQ, 3, 1536]), dout("gdn_s", [NSEQ, 8, 64, 64])
    mods_p = dscr("mods_p", [9, 128, D])
    mods_s = dscr("mods_s", [9, 64, D])
    x1p, x1s = dscr("x1p", [TP, D]), dscr("x1s", [NS, D])
    x2p, x2s = dscr("x2p", [TP, D]), dscr("x2s", [NS, D])
    dbg = dout if cfg.get("debug") else dscr
    mix_p, mix_s = dbg("mix_p", [TP, D], BF16), dbg("mix_s", [NS, D], BF16)
    Q_p, Q_s = dbg("Q_p", [TP, 768], BF16), dbg("Q_s", [NS, 768], BF16)

    with ExitStack() as stack:
        S = Sched(nc, stack)
        cx.S = S
        cx.ps = [stack.enter_context(nc.psum_tensor("ps%d" % i, [128, 512], F32))[:, :] for i in range(8)]
        cx.rps = [Res("ps%d" % i, excl=True) for i in range(8)]
        cst_f = sb(cx, stack, "cst_f", [128, 128], F32)
        cst_b = sb(cx, stack, "cst_b", [128, 128 + 128 + 512 + 128], BF16)
        cx.cst = cst
        cx.eps_t = sb(cx, stack, "eps_t", [128, 1], F32)
        cx.one_t = sb(cx, stack, "one_t", [128, 1], F32)
        cx.ident_f = cst_f[:, 0:128]
        cx.ident_bf = cst_b[:, 0:128]
        cx.nident_bf = cst_b[:, 128:256]
        cx.ident4_bf = cst_b[:, 256:768]
        cx.bones = cst_b[:, 768:896]
        r_c = Res("consts")
        d_c = S.dsem("consts")
        S.dma("sp", d_c, cst_f[:, :], cst[:, 0:128], writes=(r_c,))
        bon = sb(cx, stack, "bon_tmp", [128, 128], F32)
        S.dma("sp", d_c, bon[:, :], cst[:, 384:512], writes=(r_c,))
        S.op("dve", "tensor_copy", dict(out=cst_b[:, 0:128], in_=cst_f[:, 0:128]), reads=(r_c,), writes=(r_c,))
        S.op("dve", "tensor_scalar", dict(out=cst_b[:, 128:256], in0=cst_f[:, 0:128], scalar1=-1.0, scalar2=None,
                                          op0=ALU.mult), reads=(r_c,), writes=(r_c,))
        for h in range(4):
            S.op("dve", "tensor_copy", dict(out=cst_b[:, 256 + h * 128:256 + (h + 1) * 128], in_=cst_f[:, 0:128]),
                 reads=(r_c,), writes=(r_c,))
        S.op("dve", "tensor_copy", dict(out=cst_b[:, 768:896], in_=bon[:, :]), reads=(r_c,), writes=(r_c,))
        S.op("dve", "memset", dict(ap=cx.eps_t[:, :], constant=EPS), writes=(r_c,))
        S.op("dve", "memset", dict(ap=cx.one_t[:, :], constant=1.0), writes=(r_c,))
        S.barrier()
        S.emit()

        r_mods = Res("mods")
        if "adaln" in phases:
            phase_adaln(cx, S, c_rep, ada_w, ada_b, mods_p, mods_s, r_mods)

        nblk_p = (TP + 127) // 128
        r_xp = [Res() for _ in range(nblk_p)]
        r_xs = [Res()]
        r_x1p = [Res() for _ in range(nblk_p)]
        r_x1s = [Res()]
        f1_dst_p, f1_dst_s = (x1p, x1s) if "mixer" in phases else (yp, ys)
        if "ffn1" in phases:
            groups = [dict(src=xp, dst=f1_dst_p, T=TP, mods=mods_p, n=128, src_res=r_xp, dst_res=r_x1p),
                      dict(src=xs_in, dst=f1_dst_s, T=NS, mods=mods_s, n=64, src_res=r_xs, dst_res=r_x1s)]
            phase_ffn(cx, S, "f1", groups, ffn1_wi, ffn1_wo, norm_ffn1, 0)
        r_out = Res("outs")
        gp = gs = None
        if "mixer" in phases:
            msrc_p, msrc_s = (x1p, x1s) if "ffn1" in phases else (xp, xs_in)
            gp = dict(src=msrc_p, src_res=r_x1p, T=TP, nseq=1, mods=mods_p, n=128, s0=None, conv0=None, cs=cs_p,
                      new_conv=o_conv_p, new_gdn=o_gdn_p, new_ckv=o_ckv_p, new_kr=o_kr_p, mix=mix_p, Q=Q_p,
                      r_out=r_out, r_mix=Res(), r_Q=Res(), r_ckv=Res(), r_kr=Res())
            gs = dict(src=msrc_s, src_res=r_x1s, T=NS, nseq=NSEQ, mods=mods_s, n=64, s0=st_gdn, conv0=st_conv, cs=cs_s,
                      new_conv=o_conv_s, new_gdn=o_gdn_s, new_ckv=o_ckv_s, new_kr=o_kr_s, mix=mix_s, Q=Q_s,
                      r_out=r_out, r_mix=Res(), r_Q=Res(), r_ckv=Res(), r_kr=Res())
            phase_mixer(cx, S, [gp, gs], P)
        if "attn_p" in phases:
            phase_attn_prompt(cx, S, gp, P)
        if "attn_s" in phases:
            phase_attn_sample(cx, S, gs, P, cache_c, cache_k, ptab, NSEQ, cfg.get("npages", 128))
        if "wout" in phases:
            r_x2p = [Res() for _ in range(nblk_p)]
            r_x2s = [Res()]
            gp.update(dst=x2p, dst_res=r_x2p)
            gs.update(dst=x2s, dst_res=r_x2s)
            phase_wout(cx, S, [gp, gs], w_out)
            if "ffn2" in phases:
                groups = [dict(src=x2p, dst=yp, T=TP, mods=mods_p, n=128, src_res=r_x2p, dst_res=[Res() for _ in range(nblk_p)]),
                          dict(src=x2s, dst=ys, T=NS, mods=mods_s, n=64, src_res=r_x2s, dst_res=[Res()])]
                phase_ffn(cx, S, "f2", groups, ffn2_wi, ffn2_wo, norm_ffn2, 6)
        S.barrier()
        S.emit()
    return nc


ALL_PHASES = ("adaln", "ffn1", "mixer", "attn_p", "attn_s", "wout", "ffn2")


def make_inputs(core, inputs, TP=4096):
    f = lambda a: np.ascontiguousarray(np.asarray(a, dtype=np.float32))
    b = core % 4
    sl = slice(16 * core, 16 * core + 16)
    conv_w = f(inputs["gdn_conv_w"][0])
    m = dict(
        xp=f(inputs["x_prompt"][b][:TP]), xs=f(inputs["x_sample"][sl]).reshape(64, D),
        c_rep=np.concatenate([np.repeat(f(inputs["c_prompt"][b:b + 1]), 128, 0), np.repeat(f(inputs["c_sample"][sl]), 4, 0)], 0),
        cst=host_consts(), ada_w=f(inputs["ada_w"][0]), ada_b=f(inputs["ada_b"][0:1]),
        norm_ffn1=f(inputs["norm_ffn1"][0:1]), ffn1_wi=f(inputs["ffn1_wi"][0]), ffn1_wo=f(inputs["ffn1_wo"][0]),
        norm_mix=f(inputs["norm_mix"][0:1]), w_in=f(inputs["w_in"][0]),
        conv_w_fm=np.ascontiguousarray(conv_w.reshape(4, 12, 128).transpose(2, 1, 0)),
        a_log=f(inputs["gdn_a_log"][0:1]), dt_bias=f(inputs["gdn_dt_bias"][0:1]), gdn_norm=f(inputs["gdn_norm"][0:1]),
        qn_g=f(inputs["mla_qn_norm"][0:1]), qr_g=f(inputs["mla_qr_norm"][0:1]), ckv_g=f(inputs["mla_ckv_norm"][0:1]),
        kr_g=f(inputs["mla_kr_norm"][0:1]), kn_g=f(inputs["mla_kn_norm"][0:1]),
        w_uk=f(inputs["mla_w_uk"][0]).reshape(128, 512), w_uv=f(inputs["mla_w_uv"][0]).reshape(128, 512),
        w_out=f(inputs["w_out"][0]), norm_ffn2=f(inputs["norm_ffn2"][0:1]), ffn2_wi=f(inputs["ffn2_wi"][0]),
        ffn2_wo=f(inputs["ffn2_wo"][0]),
        cache_c=f(inputs["cache_ckv"][0]).reshape(-1, 128 * 128), cache_k=f(inputs["cache_krope"][0]).reshape(-1, 128 * 32),
        ptab=np.ascontiguousarray(np.asarray(inputs["page_table"][sl], dtype=np.int32)),
        cs_p=host_rope_table(np.arange(TP)), cs_s=host_rope_table(16384 + np.arange(4)),
        st_conv=f(inputs["state_conv"][0][sl]), st_gdn=f(inputs["state_gdn"][0][sl]),
    )
    return m


def kernel(**inputs):
    nc = build(dict(TP=4096, NS=64, phases=ALL_PHASES))
    in_maps = [make_inputs(c, inputs) for c in range(8)]
    res = run_bass_kernel_spmd(nc, in_maps, core_ids=list(range(8))).results
    f32 = np.float32
    yp = np.stack([res[b]["yp"] for b in range(4)]).astype(f32)
    ys = np.concatenate([res[c]["ys"].reshape(16, 4, D) for c in range(8)]).astype(f32)
    ckv_p = np.stack([res[b]["ckv_p"] for b in range(4)])[None].astype(f32)
    kr_p = np.stack([res[b]["kr_p"] for b in range(4)])[None].astype(f32)
    conv_p = np.concatenate([res[b]["conv_p"] for b in range(4)])[None].astype(f32)
    gdn_p = np.concatenate([res[b]["gdn_p"] for b in range(4)])[None].astype(f32)
    ckv_s = np.concatenate([res[c]["ckv_s"].reshape(16, 4, 128) for c in range(8)])[None].astype(f32)
    kr_s = np.concatenate([res[c]["kr_s"].reshape(16, 4, 32) for c in range(8)])[None].astype(f32)
    conv_s = np.concatenate([res[c]["conv_s"] for c in range(8)])[None].astype(f32)
    gdn_s = np.concatenate([res[c]["gdn_s"] for c in range(8)])[None].astype(f32)
    return (yp, ys, ckv_p, kr_p, conv_p, gdn_p, ckv_s, kr_s, conv_s, gdn_s)
```

```python
import numpy as np
from contextlib import ExitStack
import concourse.bass as bass
import concourse.mybir as mybir
from concourse.bass_utils import run_bass_kernel_spmd

F32 = mybir.dt.float32
BF16 = mybir.dt.bfloat16
I32 = mybir.dt.int32
U32 = mybir.dt.uint32
AF = mybir.ActivationFunctionType
ALU = mybir.AluOpType
AX = mybir.AxisListType

D = 1024
DFF = 2816
NFC = DFF // 128
EPS = 1e-6

class Res:
    __slots__ = ("name", "w", "r", "excl")

    def __init__(self, name="", excl=False):
        self.name = name
        self.excl = excl
        self.w = None
        self.r = {}


class DSem:
    __slots__ = ("sem", "cnt", "name")

    def __init__(self, sem, name):
        self.sem = sem
        self.cnt = 0
        self.name = name


class Sched:
    ENGS = ("pe", "act", "dve", "pool", "sp")

    def __init__(self, nc, stack):
        self.nc = nc
        self.stack = stack
        self.q = {k: [] for k in self.ENGS}
        self.esem = {k: stack.enter_context(nc.semaphore("es_" + k)) for k in self.ENGS}
        self.ecnt = {k: 0 for k in self.ENGS}
        self.known = {k: {} for k in self.ENGS}
        self.dsems = []
        self.nwait = 0
        self.nop = 0

    def dsem(self, name):
        s = self.stack.enter_context(self.nc.semaphore("ds_" + name))
        d = DSem(s, name)
        self.dsems.append(d)
        return d

    def _waits(self, eng, reads, writes):
        need = {}

        def add(ev):
            sem_id, sem, val, src = ev
            if src == "pe" and eng == "pe":
                return
            if self.known[eng].get(sem_id, 0) >= val:
                return
            if sem_id not in need or need[sem_id][1] < val:
                need[sem_id] = (sem, val)

        for r in reads:
            if r.w is not None:
                add(r.w)
        for w in writes:
            if w.w is not None:
                add(w.w)
            for ev in w.r.values():
                add(ev)
        for sem_id, (sem, val) in need.items():
            self.q[eng].append(("wait", sem, val))
            self.known[eng][sem_id] = val
            self.nwait += 1

    def _record(self, ev, reads, writes):
        for r in reads:
            old = r.r.get(ev[0])
            if old is None or old[2] < ev[2]:
                r.r[ev[0]] = ev
        for w in writes:
            w.w = ev
            w.r = {}

    def op(self, eng, meth, kw, reads=(), writes=(), inc=True):
        ex = tuple(r for r in reads if r.excl)
        if ex:
            writes = tuple(writes) + ex
        self._waits(eng, reads, writes)
        if inc:
            self.ecnt[eng] += 1
            ev = (eng, self.esem[eng], self.ecnt[eng], eng)
        else:
            ev = (eng, self.esem[eng], self.ecnt[eng] + 1, eng)
        self.q[eng].append(("op", meth, kw, inc))
        self.nop += 1
        self._record(ev, reads, writes)

    def dma(self, q, ds, out, in_, reads=(), writes=(), **kw):
        self._waits(q, reads, writes)
        ds.cnt += 16
        ev = (id(ds), ds.sem, ds.cnt, "dma")
        self.q[q].append(("dma", out, in_, ds.sem, kw))
        self.nop += 1
        self._record(ev, reads, writes)

    def idma(self, ds, reads=(), writes=(), **kw):
        self._waits("pool", reads, writes)
        ds.cnt += 16
        ev = (id(ds), ds.sem, ds.cnt, "dma")
        self.q["pool"].append(("idma", kw, ds.sem))
        self.nop += 1
        self._record(ev, reads, writes)

    def barrier(self):
        for e in self.ENGS:
            for e2 in self.ENGS:
                if e2 == e:
                    continue
                v = self.ecnt[e2]
                if v > 0 and self.known[e].get(e2, 0) < v:
                    self.q[e].append(("wait", self.esem[e2], v))
                    self.known[e][e2] = v
            for d in self.dsems:
                if d.cnt > 0 and self.known[e].get(id(d), 0) < d.cnt:
                    self.q[e].append(("wait", d.sem, d.cnt))
                    self.known[e][id(d)] = d.cnt

    def emit(self):
        import os
        if os.environ.get("EMITLOG"):
            print("EMIT", {k: len(v) for k, v in self.q.items()}, "cnt", dict(self.ecnt), "ndsem", len(self.dsems))
        nc = self.nc
        engs = {"pe": "tensor", "act": "scalar", "dve": "vector", "pool": "gpsimd", "sp": "sync"}
        with nc.Block() as block:
            for k, attr in engs.items():
                items = self.q[k]
                esem = self.esem[k]

                def body(e, items=items, esem=esem):
                    for it in items:
                        if it[0] == "wait":
                            e.wait_ge(it[1], it[2])
                        elif it[0] == "op":
                            ins = getattr(e, it[1])(**it[2])
                            if it[3]:
                                ins.then_inc(esem, 1)
                        elif it[0] == "idma":
                            e.indirect_dma_start(**it[1]).then_inc(it[2], 16)
                        else:
                            _, out, in_, sem, kw = it
                            e.dma_start(out=out, in_=in_, **kw).then_inc(sem, 16)

                getattr(block, attr)(body)
        self.q = {k: [] for k in self.ENGS}


class Ctx:
    pass


def sb(cx, stack, name, shape, dt):
    return stack.enter_context(cx.nc.sbuf_tensor(name, list(shape), dt))


def row_bcast(ap_row, n):
    t = ap_row.tensor
    F = ap_row.shape[-1]
    return bass.AP(t, ap_row.offset, [[0, n], [1, F]])


def load_weight_bf16(cx, S, name, dst, dst_res, w_ap, kchunks, cols, col_piece):
    with ExitStack() as st:
        stg = [sb(cx, st, name + "stg%d" % i, [128, col_piece], F32) for i in range(3)]
        r_stg = [Res() for _ in range(3)]
        d_stg = [S.dsem(name + "stg%d" % i) for i in range(3)]
        engs = (("dve", "tensor_copy"), ("pool", "tensor_copy"), ("act", "activation"))
        i = 0
        for c in range(kchunks):
            for c0 in range(0, cols, col_piece):
                c1 = min(cols, c0 + col_piece)
                k = i % 3
                i += 1
                S.dma("sp", d_stg[k], stg[k][:, :c1 - c0], w_ap[c * 128:(c + 1) * 128, c0:c1], writes=(r_stg[k],))
                eng, meth = engs[k]
                kw = dict(out=dst[:, c, c0:c1], in_=stg[k][:, :c1 - c0])
                if meth == "activation":
                    kw["func"] = AF.Copy
                S.op(eng, meth, kw, reads=(r_stg[k],), writes=(dst_res,))
        S.barrier()
        S.emit()


def norm_mod_transpose(cx, S, bufs, src_ap, src_res, m, hT_dst, pt_k):
    (xs, tt, junk, hb, stat, GG, Bt, r_xs, r_tt, r_junk, r_hb, r_stat, r_GG, r_Bt, r_hT, d_xs) = bufs
    S.dma("sp", d_xs, xs[:m, :], src_ap, reads=(src_res,), writes=(r_xs,))
    S.op("act", "activation", dict(out=junk[:m, :], in_=xs[:m, :], func=AF.Square, accum_out=stat[:m, 0:1]),
         reads=(r_xs,), writes=(r_junk, r_stat))
    S.op("act", "activation", dict(out=stat[:m, 1:2], in_=stat[:m, 0:1], func=AF.Sqrt, scale=1.0 / D,
                                   bias=cx.eps_t[:m, 0:1]), reads=(r_stat,), writes=(r_stat,))
    S.op("dve", "reciprocal", dict(out=stat[:m, 2:3], in_=stat[:m, 1:2]), reads=(r_stat,), writes=(r_stat,))
    S.op("dve", "scalar_tensor_tensor", dict(out=tt[:m, :], in0=xs[:m, :], scalar=stat[:m, 2:3], in1=GG[:m, :],
                                             op0=ALU.mult, op1=ALU.mult),
         reads=(r_xs, r_stat, r_GG), writes=(r_tt,))
    S.op("pool", "tensor_tensor", dict(out=hb[:m, :], in0=tt[:m, :], in1=Bt[:m, :], op=ALU.add),
         reads=(r_tt, r_Bt), writes=(r_hb,))
    ptb = cx.ps[pt_k].bitcast(BF16)
    for c in range(8):
        S.op("pe", "transpose", dict(out=ptb[:, c * 128:c * 128 + m], in_=hb[:m, c * 128:(c + 1) * 128],
                                     identity=cx.ident_bf[:m, :m]),
             reads=(r_hb,), writes=(cx.rps[pt_k],), inc=(c == 7))
    S.op("act", "activation", dict(out=hT_dst, in_=ptb.rearrange("p (c t) -> p c t", c=8)[:, :, :m], func=AF.Copy),
         reads=(cx.rps[pt_k],), writes=(r_hT,))


def load_mod_tiles(cx, S, g_ap, mods, mod_base, n, gscale, GG, Bt, Gt, tt, r_GG, r_Bt, r_Gt, r_tt, d_m):
    S.dma("sp", d_m, tt[:n, :], row_bcast(g_ap, n), writes=(r_tt,))
    S.dma("sp", d_m, GG[:n, :], mods[mod_base + 1, :n, :], writes=(r_GG,))
    S.dma("sp", d_m, Bt[:n, :], mods[mod_base + 0, :n, :], writes=(r_Bt,))
    if Gt is not None:
        S.dma("sp", d_m, Gt[:n, :], mods[mod_base + 2, :n, :], writes=(r_Gt,))
    S.barrier()
    S.op("dve", "scalar_tensor_tensor", dict(out=GG[:n, :], in0=GG[:n, :], scalar=1.0, in1=tt[:n, :],
                                             op0=ALU.add, op1=ALU.mult), reads=(r_tt,), writes=(r_GG,))
    if Gt is not None and gscale != 1.0:
        S.op("dve", "tensor_scalar", dict(out=Gt[:n, :], in0=Gt[:n, :], scalar1=float(gscale), scalar2=None,
                                          op0=ALU.mult), writes=(r_Gt,))


def phase_ffn(cx, S, name, groups, wi_ap, wo_ap, g_ap, mod_base):
    with ExitStack() as st:
        wi = sb(cx, st, name + "wi", [128, 8, 2 * DFF], BF16)
        wo = sb(cx, st, name + "wo", [128, NFC, D], BF16)
        r_wi, r_wo = Res("wi"), Res("wo")
        load_weight_bf16(cx, S, name + "wi", wi, r_wi, wi_ap, 8, 2 * DFF, DFF)
        load_weight_bf16(cx, S, name + "wo", wo, r_wo, wo_ap, NFC, D, D)
        GG = sb(cx, st, name + "GG", [128, D], F32)
        Bt = sb(cx, st, name + "Bt", [128, D], F32)
        Gt = sb(cx, st, name + "Gt", [128, D], F32)
        xs = sb(cx, st, name + "xs", [128, D], F32)
        tt = sb(cx, st, name + "tt", [128, D], F32)
        junk = sb(cx, st, name + "junk", [128, D], BF16)
        hb = sb(cx, st, name + "hb", [128, D], BF16)
        hT = sb(cx, st, name + "hT", [128, 8, 512], BF16)
        sg = [sb(cx, st, name + "sg%d" % i, [128, 512], BF16) for i in range(2)]
        actT = sb(cx, st, name + "actT", [128, NFC, 512], BF16)
        xr = [sb(cx, st, name + "xr%d" % i, [128, D], F32) for i in range(2)]
        tmp = sb(cx, st, name + "tmp", [128, 512], F32)
        stat = sb(cx, st, name + "stat", [128, 4], F32)

        r_GG, r_Bt, r_Gt, r_xs, r_tt, r_junk, r_hb, r_hT = (Res(n_) for n_ in
                                                              ("GG", "Bt", "Gt", "xs", "tt", "junk", "hb", "hT"))
        r_sg = [Res("sg0"), Res("sg1")]
        r_actT = [Res("actT%d" % j) for j in range(NFC)]
        r_xr = [Res("xr0"), Res("xr1")]
        r_tmp, r_stat = Res("tmp"), Res("stat")
        d_m, d_xs = S.dsem(name + "m"), S.dsem(name + "xs")
        d_xr = [S.dsem(name + "xr0"), S.dsem(name + "xr1")]
        d_xst = [S.dsem(name + "xst0"), S.dsem(name + "xst1")]
        bufs = (xs, tt, junk, hb, stat, GG, Bt, r_xs, r_tt, r_junk, r_hb, r_stat, r_GG, r_Bt, r_hT, d_xs)


        pg, pu, po = cx.ps[0:2], cx.ps[2:4], cx.ps[4:6]
        r_pg, r_pu, r_po = cx.rps[0:2], cx.rps[2:4], cx.rps[4:6]
        cnt = {"up": 0, "po": 0, "pt": 0, "xr": 0}

        for g in groups:
            n = g["n"]
            load_mod_tiles(cx, S, g_ap, g["mods"], mod_base, n, 0.5, GG, Bt, Gt, tt, r_GG, r_Bt, r_Gt, r_tt, d_m)
            T = g["T"]
            nblk = (T + 511) // 512

            def stage_norm(b):
                t0 = b * 512
                tb = min(512, T - t0)
                for s_ in range((tb + 127) // 128):
                    r0 = t0 + s_ * 128
                    m = min(128, T - r0)
                    k = 6 + cnt["pt"] % 2
                    cnt["pt"] += 1
                    norm_mod_transpose(cx, S, bufs, g["src"][r0:r0 + m, :], g["src_res"][r0 // 128], m,
                                       hT[:, :, s_ * 128:s_ * 128 + m], k)

            def stage_up(b):
                t0 = b * 512
                tb = min(512, T - t0)
                for j in range(NFC):
                    k = cnt["up"] % 2
                    cnt["up"] += 1
                    for kc in range(8):
                        S.op("pe", "matmul", dict(out=pg[k][:, :tb], lhsT=wi[:, kc, j * 128:(j + 1) * 128],
                                                  rhs=hT[:, kc, :tb], start=(kc == 0), stop=(kc == 7)),
                             reads=(r_wi, r_hT), writes=(r_pg[k],), inc=(kc == 7))
                    for kc in range(8):
                        S.op("pe", "matmul", dict(out=pu[k][:, :tb], lhsT=wi[:, kc, DFF + j * 128:DFF + (j + 1) * 128],
                                                  rhs=hT[:, kc, :tb], start=(kc == 0), stop=(kc == 7)),
                             reads=(r_wi, r_hT), writes=(r_pu[k],), inc=(kc == 7))
                    S.op("act", "activation", dict(out=sg[k][:, :tb], in_=pg[k][:, :tb], func=AF.Silu),
                         reads=(r_pg[k],), writes=(r_sg[k],))
                    S.op("dve", "tensor_tensor", dict(out=actT[:, j, :tb], in0=sg[k][:, :tb], in1=pu[k][:, :tb],
                                                      op=ALU.mult),
                         reads=(r_sg[k], r_pu[k]), writes=(r_actT[j],))

            def stage_down(b):
                t0 = b * 512
                tb = min(512, T - t0)
                for s_ in range((tb + 127) // 128):
                    r0 = t0 + s_ * 128
                    m = min(128, T - r0)
                    kx = cnt["xr"] % 2
                    cnt["xr"] += 1
                    S.dma("sp", d_xr[kx], xr[kx][:m, :], g["src"][r0:r0 + m, :], reads=(g["src_res"][r0 // 128],),
                          writes=(r_xr[kx],))
                    for half in range(2):
                        k = cnt["po"] % 2
                        cnt["po"] += 1
                        for j in range(NFC):
                            S.op("pe", "matmul", dict(out=po[k][:m, :], lhsT=actT[:, j, s_ * 128:s_ * 128 + m],
                                                      rhs=wo[:, j, half * 512:(half + 1) * 512],
                                                      start=(j == 0), stop=(j == NFC - 1)),
                                 reads=(r_wo, r_actT[j]), writes=(r_po[k],), inc=(j == NFC - 1))
                        S.op("dve", "tensor_tensor", dict(out=tmp[:m, :], in0=po[k][:m, :],
                                                          in1=Gt[:m, half * 512:(half + 1) * 512], op=ALU.mult),
                             reads=(r_po[k], r_Gt), writes=(r_tmp,))
                        S.op("pool", "tensor_tensor", dict(out=xr[kx][:m, half * 512:(half + 1) * 512],
                                                           in0=xr[kx][:m, half * 512:(half + 1) * 512],
                                                           in1=tmp[:m, :], op=ALU.add),
                             reads=(r_tmp,), writes=(r_xr[kx],))
                    S.dma("pool", d_xst[kx], g["dst"][r0:r0 + m, :], xr[kx][:m, :], reads=(r_xr[kx],),
                          writes=(g["dst_res"][r0 // 128],))

            stage_norm(0)
            for b in range(nblk):
                stage_up(b)
                if b + 1 < nblk:
                    stage_norm(b + 1)
                stage_down(b)
        S.barrier()
        S.emit()


def phase_adaln(cx, S, c_rep, ada_w, ada_b, mods_p, mods_s, r_mods):
    with ExitStack() as st:
        ct = sb(cx, st, "ad_ct", [128, D], F32)
        cb_ = sb(cx, st, "ad_cb", [128, D], BF16)
        cT = sb(cx, st, "ad_cT", [128, 8, 192], BF16)
        w = [sb(cx, st, "ad_w%d" % i, [128, 8, D], BF16) for i in range(2)]
        wst = [sb(cx, st, "ad_wst%d" % i, [128, 8, D], F32) for i in range(2)]
        bb = [sb(cx, st, "ad_b%d" % i, [128, D], F32) for i in range(2)]
        o = [sb(cx, st, "ad_o%d" % i, [128, D], F32) for i in range(2)]
        r_ct, r_cb, r_cT = Res(), Res(), Res()
        r_w, r_wst, r_bb, r_o = [Res(), Res()], [Res(), Res()], [Res(), Res()], [Res(), Res()]
        d_ct = S.dsem("ad_ct")
        d_w = [S.dsem("ad_w0"), S.dsem("ad_w1")]
        d_b = [S.dsem("ad_b0"), S.dsem("ad_b1")]
        d_o = [S.dsem("ad_o0"), S.dsem("ad_o1")]

        def load(blk):
            k = blk % 2
            for hf in range(2):
                S.dma("sp", d_w[k], wst[k][:, hf * 4:(hf + 1) * 4, :],
                      ada_w[hf * 512:(hf + 1) * 512, blk * D:(blk + 1) * D].rearrange("(c p) f -> p c f", p=128),
                      writes=(r_wst[k],))
            S.dma("sp", d_b[k], bb[k][:, :], row_bcast(ada_b[0:1, blk * D:(blk + 1) * D], 128), writes=(r_bb[k],))

        load(0)
        for gi, (r0, m) in enumerate(((0, 128), (128, 64))):
            S.dma("sp", d_ct, ct[:m, :], c_rep[r0:r0 + m, :], writes=(r_ct,))
            S.op("act", "activation", dict(out=cb_[:m, :], in_=ct[:m, :], func=AF.Silu), reads=(r_ct,), writes=(r_cb,))
            ptb = cx.ps[6 + gi].bitcast(BF16)
            for c in range(8):
                S.op("pe", "transpose", dict(out=ptb[:, c * 128:c * 128 + m], in_=cb_[:m, c * 128:(c + 1) * 128],
                                             identity=cx.ident_bf[:m, :m]),
                     reads=(r_cb,), writes=(cx.rps[6 + gi],), inc=(c == 7))
            S.op("act", "activation", dict(out=cT[:, :, r0:r0 + m],
                                           in_=ptb.rearrange("p (c t) -> p c t", c=8)[:, :, :m], func=AF.Copy),
                 reads=(cx.rps[6 + gi],), writes=(r_cT,))
        for blk in range(9):
            k = blk % 2
            if blk + 1 < 9:
                load(blk + 1)
            S.op("pool", "tensor_copy", dict(out=w[k][:, 0:4, :], in_=wst[k][:, 0:4, :]),
                 reads=(r_wst[k],), writes=(r_w[k],))
            S.op("dve", "tensor_copy", dict(out=w[k][:, 4:8, :], in_=wst[k][:, 4:8, :]),
                 reads=(r_wst[k],), writes=(r_w[k],))
            for gi, (r0, m, dst) in enumerate(((0, 128, mods_p), (128, 64, mods_s))):
                ko = gi
                for half in range(2):
                    pk = (blk * 4 + gi * 2 + half) % 4
                    ps, rp = cx.ps[pk], cx.rps[pk]
                    for c in range(8):
                        S.op("pe", "matmul", dict(out=ps[:m, :], lhsT=cT[:, c, r0:r0 + m],
                                                  rhs=w[k][:, c, half * 512:(half + 1) * 512], start=(c == 0),
                                                  stop=(c == 7)), reads=(r_cT, r_w[k]), writes=(rp,), inc=(c == 7))
                    S.op("dve", "tensor_tensor", dict(out=o[ko][:m, half * 512:(half + 1) * 512], in0=ps[:m, :],
                                                      in1=bb[k][:m, half * 512:(half + 1) * 512], op=ALU.add),
                         reads=(rp, r_bb[k]), writes=(r_o[ko],))
                S.dma("act", d_o[ko], dst[blk, :m, :], o[ko][:m, :], reads=(r_o[ko],), writes=(r_mods,))
        S.barrier()
        S.emit()


NLEV = 5
QK_SCALE = 64 ** -0.5
ATT_SCALE = 96 ** -0.5


def bc(ap2, n):
    return ap2.unsqueeze(2).broadcast_to([ap2.shape[0], ap2.shape[1], n])


def bc_mid(ap2, n):
    return ap2.unsqueeze(1).broadcast_to([ap2.shape[0], n, ap2.shape[1]])


class MixBufs:
    pass


def mixer_alloc(cx, S, st, name, P):
    B = MixBufs()

    def t(nm, shape, dt):
        tt_ = sb(cx, st, name + nm, shape, dt)
        setattr(B, nm, tt_)
        setattr(B, "r_" + nm, Res(nm))
        return tt_

    t("win", [128, 8, 2992], BF16)
    load_weight_bf16(cx, S, "mxwin", B.win, B.r_win, P["w_in"], 8, 2992, 1496)
    t("GG", [128, D], F32); t("Bt", [128, D], F32)
    t("xs", [128, D], F32); t("tt", [128, D], F32); t("junk", [128, D], BF16); t("hb", [128, D], BF16)
    t("stat", [128, 4], F32)
    t("hT", [128, 8, 256], BF16)
    t("xq", [128, 12, 259], F32)
    t("yc", [128, 12, 256], F32)
    t("sqb", [128, 512], BF16)
    t("rinv", [128, 512], F32)
    t("qkT", [128, 8, 256], BF16)
    t("tm", [128, 1456], F32)
    t("sc", [128, 96], F32)
    t("rows", [8, 2, 128], F32)
    t("glb", [128, 8], F32)
    t("vtok", [128, 512], BF16); t("ktok", [128, 512], BF16); t("kdec", [128, 512], BF16)
    t("decq", [128, 512], F32); t("decb", [128, 512], F32)
    for g in range(2):
        for nm in ("A", "M", "P"):
            for k in range(2):
                t("%s%d%d" % (nm, g, k), [128, 512], F32)
        t("Pf%d" % g, [128, 512], BF16)
    t("qk", [128, 1024], BF16)
    t("u", [128, 512], F32)
    t("wT", [128, 4, 128], BF16)
    t("vnew", [128, 512], BF16)
    t("o1", [128, 512], F32)
    t("osb", [128, 512], F32)
    t("ost", [128, 16], F32)
    t("zs", [128, 512], F32)
    t("gout", [128, 512], BF16)
    t("S32", [128, 512], F32); t("Sbf", [128, 512], BF16)
    t("mq", [128, 768], F32); t("mqs", [128, 32], F32); t("Qb", [128, 768], BF16)
    t("cn", [128, 128], F32); t("krn", [128, 32], F32); t("krr", [128, 32], F32); t("qrr", [128, 8, 32], F32)
    t("cs", [128, 32], F32)
    t("cw", [128, 12, 4], F32)
    t("g8", [128, 16], F32)
    t("gng", [128, 64], F32); t("qng", [128, 64], F32); t("qrg", [128, 32], F32); t("ckg", [128, 128], F32)
    t("krg", [128, 32], F32); t("kng", [128, 64], F32)
    t("nc3", [128, 1536], F32)
    B.d = {}
    return B


def mixer_group(cx, S, B, name, grp, P):
    def dsem(k):
        if k not in B.d:
            B.d[k] = S.dsem(name + k)
        return B.d[k]

    def op(eng, meth, reads, writes, inc=True, **kw):
        S.op(eng, meth, kw, reads=reads, writes=writes, inc=inc)

    ps, rps = cx.ps, cx.rps
    T, nseq = grp["T"], grp["nseq"]
    Ts = T // nseq
    prompt = (nseq == 1)
    n = grp["n"]
    load_mod_tiles(cx, S, P["norm_mix"], grp["mods"], 3, n, 1.0, B.GG, B.Bt, None, B.tt, B.r_GG, B.r_Bt, None,
                   B.r_tt, dsem("m"))
    bufs = (B.xs, B.tt, B.junk, B.hb, B.stat, B.GG, B.Bt, B.r_xs, B.r_tt, B.r_junk, B.r_hb, B.r_stat, B.r_GG,
            B.r_Bt, B.r_hT, dsem("xs"))
    blk_T = 256 if prompt else T
    nblk = (T + blk_T - 1) // blk_T
    if prompt:
        op("dve", "memset", (), (B.r_S32,), ap=B.S32[:, :], constant=0.0)
        op("dve", "memset", (), (B.r_Sbf,), ap=B.Sbf[:, :], constant=0.0)
        op("pool", "memset", (), (B.r_xq,), ap=B.xq[:, :, 0:3], constant=0.0)
    ptk = [0]

    for b in range(nblk):
        t0 = b * blk_T
        tb = min(blk_T, T - t0)
        for s_ in range((tb + 127) // 128):
            r0 = t0 + s_ * 128
            m = min(128, T - r0)
            k = 6 + ptk[0] % 2
            ptk[0] += 1
            norm_mod_transpose(cx, S, bufs, grp["src"][r0:r0 + m, :], grp["src_res"][r0 // 128], m,
                               B.hT[:, :, s_ * 128:s_ * 128 + m], k)
        if prompt:
            xq_new = B.xq[:, :, 3:3 + tb]
            xqv = None
        else:
            xqv = B.xq[:, :, 0:nseq * 7].rearrange("p c (s t) -> p c s t", t=7)
            op("sp", "dma_start", (), (), ) if False else None
            S.dma("sp", dsem("nc3"), B.nc3[:nseq * 3, :], grp["conv0"].rearrange("s t c -> (s t) c"), writes=(B.r_nc3,))
            for c in range(12):
                kq = c % 2
                op("pe", "transpose", (B.r_nc3,), (rps[kq],), out=ps[kq][:, :nseq * 3],
                   in_=B.nc3[:nseq * 3, c * 128:(c + 1) * 128], identity=cx.ident_f[:nseq * 3, :nseq * 3])
                op("act" if c % 2 else "dve", "activation" if c % 2 else "tensor_copy", (rps[kq],), (B.r_xq,),
                   out=xqv[:, c, :, 0:3], in_=ps[kq][:, :nseq * 3].rearrange("p (s t) -> p s t", t=3),
                   **({"func": AF.Copy} if c % 2 else {}))
        for c in range(12):
            kq = c % 2
            for kc in range(8):
                op("pe", "matmul", (B.r_win, B.r_hT), (rps[kq],), inc=(kc == 7), out=ps[kq][:, :tb],
                   lhsT=B.win[:, kc, c * 128:(c + 1) * 128], rhs=B.hT[:, kc, :tb], start=(kc == 0), stop=(kc == 7))
            if prompt:
                dst, srcp = B.xq[:, c, 3:3 + tb], ps[kq][:, :tb]
            else:
                dst, srcp = xqv[:, c, :, 3:7], ps[kq][:, :tb].rearrange("p (s t) -> p s t", t=Ts)
            if c % 2:
                op("act", "activation", (rps[kq],), (B.r_xq,), out=dst, in_=srcp, func=AF.Copy)
            else:
                op("dve", "tensor_copy", (rps[kq],), (B.r_xq,), out=dst, in_=srcp)
        for c in range(12):
            if prompt:
                ydst = B.yc[:, c, :tb]
                xin = [B.xq[:, c, j:j + tb] for j in range(4)]
            else:
                ydst = B.yc[:, c, :tb].rearrange("p (s t) -> p s t", t=Ts)
                xin = [xqv[:, c, :, j:j + Ts] for j in range(4)]
            eng = "dve"
            op(eng, "tensor_scalar", (B.r_xq, B.r_cw), (B.r_yc,), out=ydst, in0=xin[0], scalar1=B.cw[:, c, 0:1],
               scalar2=None, op0=ALU.mult)
            for j in range(1, 4):
                op(eng, "scalar_tensor_tensor", (B.r_xq, B.r_cw), (B.r_yc,), out=ydst, in0=xin[j],
                   scalar=B.cw[:, c, j:j + 1], in1=ydst, op0=ALU.mult, op1=ALU.add)
        for c in range(12):
            op("act", "activation", (B.r_yc,), (B.r_yc,), out=B.yc[:, c, :tb], in_=B.yc[:, c, :tb], func=AF.Silu)
        if prompt and b + 1 < nblk:
            op("pool", "tensor_copy", (B.r_xq,), (B.r_xq,), out=B.xq[:, :, 0:3], in_=B.xq[:, :, tb:tb + 3])
        for c in range(8):
            kq = c % 2
            op("pool", "tensor_tensor", (B.r_yc,), (B.r_sqb,), out=B.sqb[:, :tb], in0=B.yc[:, c, :tb], in1=B.yc[:, c, :tb],
               op=ALU.mult)
            op("pe", "matmul", (B.r_sqb,), (rps[kq],), out=ps[kq][:, :tb], lhsT=cx.bones[:, :], rhs=B.sqb[:, :tb],
               start=True, stop=True)
            op("act", "activation", (rps[kq],), (B.r_rinv,), out=B.rinv[:, :tb], in_=ps[kq][:, :tb], func=AF.Sqrt,
               bias=cx.eps_t[:, 0:1], scale=1.0)
            op("dve", "reciprocal", (B.r_rinv,), (B.r_rinv,), out=B.rinv[:, :tb], in_=B.rinv[:, :tb])
            if c < 4:
                op("dve", "scalar_tensor_tensor", (B.r_rinv, B.r_yc), (B.r_qkT,), out=B.qkT[:, c, :tb], in0=B.yc[:, c, :tb],
                   scalar=QK_SCALE, in1=B.rinv[:, :tb], op0=ALU.mult, op1=ALU.mult)
            else:
                op("dve", "tensor_tensor", (B.r_rinv, B.r_yc), (B.r_qkT,), out=B.qkT[:, c, :tb], in0=B.yc[:, c, :tb],
                   in1=B.rinv[:, :tb], op=ALU.mult)
        ntile = (tb + 63) // 64 if prompt else nseq
        if cx.stop <= 1:
            ntile = 0
        for ti in range(ntile):
            if prompt:
                c0 = ti * 64
                m = min(64, tb - c0)
                seq = 0
            else:
                c0 = ti * Ts
                m = Ts
                seq = ti
            tok0 = t0 + c0
            mixer_tile(cx, S, B, name, grp, P, m, c0, tok0, seq, prompt, dsem, op,
                       first=(tok0 == 0) if prompt else True, last=(tok0 + m == T) if prompt else True)


def mixer_tile(cx, S, B, name, grp, P, m, c0, tok0, seq, prompt, dsem, op, first, last):
    ps, rps = cx.ps, cx.rps
    sc, rsc = B.sc, B.r_sc
    for gi, (cA, cB) in enumerate(((1536, 2048), (2048, 2560), (2560, 2992))):
        kq = gi % 2
        for kc in range(8):
            op("pe", "matmul", (B.r_win, B.r_hT), (rps[kq],), inc=(kc == 7), out=ps[kq][:m, :cB - cA],
               lhsT=B.hT[:, kc, c0:c0 + m], rhs=B.win[:, kc, cA:cB], start=(kc == 0), stop=(kc == 7))
        if gi % 2:
            op("act", "activation", (rps[kq],), (B.r_tm,), out=B.tm[:m, cA - 1536:cB - 1536], in_=ps[kq][:m, :cB - cA],
               func=AF.Copy)
        else:
            op("dve", "tensor_copy", (rps[kq],), (B.r_tm,), out=B.tm[:m, cA - 1536:cB - 1536], in_=ps[kq][:m, :cB - cA])
    if last:
        for gi in range(3):
            kq = gi % 2
            for kc in range(8):
                op("pe", "matmul", (B.r_win, B.r_hT), (rps[kq],), inc=(kc == 7), out=ps[kq][:m, :512],
                   lhsT=B.hT[:, kc, c0:c0 + m], rhs=B.win[:, kc, gi * 512:(gi + 1) * 512], start=(kc == 0), stop=(kc == 7))
            op("dve", "tensor_copy", (rps[kq],), (B.r_nc3,), out=B.nc3[:m, gi * 512:(gi + 1) * 512], in_=ps[kq][:m, :512])
        S.dma("sp", dsem("nc3"), grp["new_conv"][seq, :, :], B.nc3[m - 3:m, :], reads=(B.r_nc3,), writes=(grp["r_out"],))
    if cx.stop <= 2:
        return
    zr, br, ar = B.tm[:m, 0:512], B.tm[:m, 512:520], B.tm[:m, 520:528]
    qraw, ckvr, krr_ = B.tm[:m, 528:1296], B.tm[:m, 1296:1424], B.tm[:m, 1424:1456]
    op("act", "activation", (B.r_tm,), (rsc,), out=sc[:m, 64:72], in_=br, func=AF.Exp, scale=-1.0)
    op("dve", "tensor_scalar", (rsc,), (rsc,), out=sc[:m, 64:72], in0=sc[:m, 64:72], scalar1=1.0, scalar2=None,
       op0=ALU.add)
    op("dve", "reciprocal", (rsc,), (rsc,), out=sc[:m, 0:8], in_=sc[:m, 64:72])
    op("act", "activation", (rsc,), (rsc,), out=sc[:m, 8:16], in_=sc[:m, 0:8], func=AF.Ln)
    op("dve", "tensor_tensor", (B.r_tm, B.r_g8), (rsc,), out=sc[:m, 64:72], in0=ar, in1=B.g8[:m, 8:16], op=ALU.add)
    op("act", "activation", (rsc,), (rsc,), out=sc[:m, 64:72], in_=sc[:m, 64:72], func=AF.Exp)
    op("act", "activation", (rsc,), (rsc,), out=sc[:m, 64:72], in_=sc[:m, 64:72], func=AF.Ln, bias=cx.one_t[:m, 0:1],
       scale=1.0)
    op("dve", "scalar_tensor_tensor", (rsc, B.r_g8), (rsc,), out=sc[:m, 16:24], in0=sc[:m, 64:72], scalar=-1.0,
       in1=B.g8[:m, 0:8], op0=ALU.mult, op1=ALU.mult)
    op("pe", "matmul", (rsc,), (rps[2],), out=ps[2][:m, 0:8], lhsT=cx.utri[:m, :m], rhs=sc[:m, 16:24], start=True, stop=True)
    op("dve", "tensor_copy", (rps[2],), (rsc,), out=sc[:m, 24:32], in_=ps[2][:m, 0:8])
    op("pe", "matmul", (rsc,), (rps[2],), out=ps[2][:m, 8:16], lhsT=cx.onesf[:m, :m], rhs=sc[:m, 16:24], start=True, stop=True)
    op("dve", "tensor_tensor", (rps[2], rsc), (rsc,), out=sc[:m, 48:56], in0=ps[2][:m, 8:16], in1=sc[:m, 24:32],
       op=ALU.subtract)
    op("act", "activation", (rsc,), (rsc,), out=sc[:m, 48:56], in_=sc[:m, 48:56], func=AF.Exp)
    op("act", "activation", (rsc,), (rsc,), out=sc[:m, 32:40], in_=sc[:m, 24:32], func=AF.Exp)
    op("dve", "tensor_tensor", (rsc,), (rsc,), out=sc[:m, 40:48], in0=sc[:m, 32:40], in1=sc[:m, 0:8], op=ALU.mult)
    op("dve", "tensor_scalar", (rsc,), (rsc,), out=sc[:m, 56:64], in0=sc[:m, 24:32], scalar1=-1.0, scalar2=None,
       op0=ALU.mult)
    op("dve", "tensor_tensor", (rsc,), (rsc,), out=sc[:m, 72:80], in0=sc[:m, 24:32], in1=sc[:m, 8:16], op=ALU.add)
    assert m <= 64
    g2 = sc[:m, 16:24].rearrange("p (c two) -> p two c", two=2)
    for par in range(2):
        op("pe", "matmul", (rsc,), (rps[2],), out=ps[2][par * 64:(par + 1) * 64, 16:20], lhsT=cx.onesf[:m, :64],
           rhs=g2[:, par, :], start=True, stop=True, skip_group_check=True)
    op("act", "activation", (rps[2],), (B.r_glb,), out=B.glb[:, 0:4], in_=ps[2][:, 16:20], func=AF.Exp)
    op("pe", "transpose", (rsc,), (rps[2],), out=ps[2][:8, 32:32 + m], in_=sc[:m, 24:32], identity=cx.ident_f[:m, :m])
    op("pe", "transpose", (rsc,), (rps[2],), out=ps[2][:8, 160:160 + m], in_=sc[:m, 72:80], identity=cx.ident_f[:m, :m])
    op("dve", "tensor_copy", (rps[2],), (B.r_rows,), out=B.rows[:, 0, :m], in_=ps[2][:8, 32:32 + m])
    op("dve", "tensor_copy", (rps[2],), (B.r_rows,), out=B.rows[:, 1, :m], in_=ps[2][:8, 160:160 + m])
    if cx.stop <= 3:
        return
    for c in range(4):
        op("pe", "transpose", (B.r_yc,), (rps[3],), out=ps[3][:m, c * 128:(c + 1) * 128], in_=B.yc[:, 8 + c, c0:c0 + m],
           identity=cx.ident_f[:, :], inc=(c == 3))
    op("dve", "tensor_tensor", (rps[3], rsc), (B.r_vtok,), out=B.vtok[:m, :].rearrange("p (h d) -> p h d", d=64),
       in0=ps[3][:m, :].rearrange("p (h d) -> p h d", d=64), in1=bc(sc[:m, 0:8], 64), op=ALU.mult)
    pkb = ps[2].bitcast(BF16)
    for c in range(4):
        op("pe", "transpose", (B.r_qkT,), (rps[2],), out=pkb[:m, 512 + c * 128:512 + (c + 1) * 128],
           in_=B.qkT[:, 4 + c, c0:c0 + m], identity=cx.ident_bf[:, :], inc=(c == 3))
    kh3 = pkb[:m, 512:1024].rearrange("p (h d) -> p h d", d=64)
    op("dve", "tensor_tensor", (rps[2], rsc), (B.r_ktok,), out=B.ktok[:m, :].rearrange("p (h d) -> p h d", d=64),
       in0=kh3, in1=bc(sc[:m, 40:48], 64), op=ALU.mult)
    op("dve", "tensor_tensor", (rps[2], rsc), (B.r_kdec,), out=B.kdec[:m, :].rearrange("p (h d) -> p h d", d=64),
       in0=kh3, in1=bc(sc[:m, 48:56], 64), op=ALU.mult)
    if cx.stop <= 4:
        return
    W4 = 4 * m
    mle, mlt, id4 = cx.mconst[m]
    for g in range(2):
        Ab = [getattr(B, "A%d%d" % (g, k)) for k in range(2)]
        Mb = [getattr(B, "M%d%d" % (g, k)) for k in range(2)]
        Pb = [getattr(B, "P%d%d" % (g, k)) for k in range(2)]
        rA = [getattr(B, "r_A%d%d" % (g, k)) for k in range(2)]
        rM = [getattr(B, "r_M%d%d" % (g, k)) for k in range(2)]
        rP = [getattr(B, "r_P%d%d" % (g, k)) for k in range(2)]
        pP, rpP = ps[6 + g], rps[6 + g]
        for kind, dec, rdec, mask in ((0, B.decq, B.r_decq, mle), (1, B.decb, B.r_decb, mlt)):
            op("pe", "matmul", (), (rps[3],), inc=False, out=ps[3][:m, 0:W4], lhsT=cx.ident_f[:m, :m], rhs=mask[:m, 0:W4],
               start=True, stop=False)
            for hh in range(4):
                h = 2 * hh + g
                op("pe", "matmul", (B.r_rows,), (rps[3],), inc=(hh == 3), out=ps[3][:m, hh * m:(hh + 1) * m],
                   lhsT=cx.ohsel[:, h, :m], rhs=B.rows[:, kind, :m], start=False, stop=(hh == 3))
            for hh in range(4):
                h = 2 * hh + g
                op("act", "activation", (rps[3], rsc), (rdec,), out=dec[:m, hh * m:(hh + 1) * m],
                   in_=ps[3][:m, hh * m:(hh + 1) * m], func=AF.Exp, bias=sc[:m, 56 + h:57 + h], scale=1.0)
        if cx.stop <= 4.2:
            continue
        for hh in range(4):
            h = 2 * hh + g
            c, po = h // 2, (h % 2) * 64
            op("pe", "matmul", (B.r_qkT,), (rps[4],), inc=(hh == 3), out=ps[4][:m, hh * m:(hh + 1) * m],
               lhsT=B.qkT[po:po + 64, 4 + c, c0:c0 + m], rhs=B.qkT[po:po + 64, 4 + c, c0:c0 + m], start=True, stop=True,
               skip_group_check=True)
        op("dve", "tensor_tensor", (rps[4], B.r_decb), (rA[0],), out=Ab[0][:m, 0:W4], in0=ps[4][:m, 0:W4],
           in1=B.decb[:m, 0:W4], op=ALU.mult)
        for hh in range(4):
            h = 2 * hh + g
            c, po = h // 2, (h % 2) * 64
            op("pe", "matmul", (B.r_qkT,), (rps[5],), inc=(hh == 3), out=ps[5][:m, hh * m:(hh + 1) * m],
               lhsT=B.qkT[po:po + 64, 4 + c, c0:c0 + m], rhs=B.qkT[po:po + 64, c, c0:c0 + m], start=True, stop=True,
               skip_group_check=True)
        op("dve", "tensor_tensor", (rps[5], B.r_decq), (B.r_qk,), out=B.qk[:m, 0:8 * m].rearrange("p (c two i) -> p two c i", two=2, i=m)[:, g],
           in0=ps[5][:m, 0:W4].rearrange("p (c i) -> p c i", i=m),
           in1=B.decq[:m, 0:W4].rearrange("p (c i) -> p c i", i=m), op=ALU.mult)
        if cx.stop <= 4.4:
            continue
        for hh in range(4):
            op("pe", "transpose", (rA[0],), (rps[4],), inc=(hh == 3), out=ps[4][:m, hh * m:(hh + 1) * m],
               in_=Ab[0][:m, hh * m:(hh + 1) * m], identity=cx.ident_f[:m, :m])
        op("act", "activation", (rps[4],), (rM[0],), out=Mb[0][:m, 0:W4], in_=ps[4][:m, 0:W4], func=AF.Copy)
        op("pe", "matmul", (), (rpP,), inc=False, out=pP[:m, 0:W4], lhsT=cx.ident_f[:m, :m], rhs=id4[:m, 0:W4],
           start=True, stop=False, skip_group_check=True)
        for hh in range(4):
            op("pe", "matmul", (rA[0],), (rpP,), inc=(hh == 3), out=pP[:m, hh * m:(hh + 1) * m], lhsT=cx.nident_f[:m, :m],
               rhs=Ab[0][:m, hh * m:(hh + 1) * m], start=False, stop=False, skip_group_check=True)
        op("act", "activation", (rpP,), (rP[0],), out=Pb[0][:m, 0:W4], in_=pP[:m, 0:W4], func=AF.Copy)
        if cx.stop <= 4.6:
            continue
        nlev = NLEV if m > 64 else (5 if m > 32 else (4 if m > 16 else (3 if m > 8 else (2 if m > 4 else 1))))
        for lv in range(1, nlev + 1):
            a, bq = (lv - 1) % 2, lv % 2
            for hh in range(4):
                sl = slice(hh * m, (hh + 1) * m)
                op("pe", "matmul", (rA[a], rM[a]), (rps[4],), inc=(hh == 3), out=ps[4][:m, sl], lhsT=Ab[a][:m, sl],
                   rhs=Mb[a][:m, sl], start=True, stop=True, skip_group_check=True)
            op("act", "activation", (rps[4],), (rM[bq],), out=Mb[bq][:m, 0:W4], in_=ps[4][:m, 0:W4], func=AF.Copy)
            if lv < nlev:
                for hh in range(4):
                    sl = slice(hh * m, (hh + 1) * m)
                    op("pe", "matmul", (rA[a], rM[a]), (rps[5],), inc=(hh == 3), out=ps[5][:m, sl], lhsT=Mb[a][:m, sl],
                       rhs=Ab[a][:m, sl], start=True, stop=True, skip_group_check=True)
                op("dve", "tensor_copy", (rps[5],), (rA[bq],), out=Ab[bq][:m, 0:W4], in_=ps[5][:m, 0:W4])
            for hh in range(4):
                sl = slice(hh * m, (hh + 1) * m)
                op("pe", "matmul", (rM[bq], rP[a]), (rpP,), inc=(hh == 3), out=pP[:m, sl], lhsT=Mb[bq][:m, sl],
                   rhs=Pb[a][:m, sl], start=False, stop=(lv == nlev), skip_group_check=True)
            if lv % 2:
                op("dve", "tensor_copy", (rpP,), (rP[bq],), out=Pb[bq][:m, 0:W4], in_=pP[:m, 0:W4])
            else:
                op("act", "activation", (rpP,), (rP[bq],), out=Pb[bq][:m, 0:W4], in_=pP[:m, 0:W4], func=AF.Copy)
        if cx.stop <= 4.8:
            continue
        Pf, rPf = getattr(B, "Pf%d" % g), getattr(B, "r_Pf%d" % g)
        op("dve", "tensor_copy", (rpP,), (rPf,), out=Pf[:m, 0:W4], in_=pP[:m, 0:W4])
        if cx.stop <= 4.85:
            continue
        for hh in range(4):
            h = 2 * hh + g
            sl = slice(hh * m, (hh + 1) * m)
            op("pe", "matmul", (rPf, B.r_vtok), (rps[0],), inc=(hh == 3), out=ps[0][:m, h * 64:(h + 1) * 64], lhsT=Pf[:m, sl],
               rhs=B.vtok[:m, h * 64:(h + 1) * 64], start=True, stop=True, skip_group_check=True)
        if cx.stop <= 4.9:
            continue
        for hh in range(4):
            h = 2 * hh + g
            sl = slice(hh * m, (hh + 1) * m)
            po = (h % 2) * 64
            op("pe", "matmul", (rPf, B.r_ktok), (rps[1],), inc=(hh == 3),
               out=ps[1][po:po + 64, hh * 128:hh * 128 + m],
               lhsT=B.ktok[:m, h * 64:(h + 1) * 64], rhs=Pf[:m, sl], start=True, stop=True, skip_group_check=True)
        op("act", "activation", (rps[1],), (B.r_wT,), out=B.wT[g * 64:(g + 1) * 64, :, :m],
           in_=ps[1][g * 64:(g + 1) * 64, :].rearrange("p (h i) -> p h i", i=128)[:, :, :m], func=AF.Identity)
    if cx.stop <= 5:
        return
    op("dve", "tensor_copy", (rps[0],), (B.r_u,), out=B.u[:m, :], in_=ps[0][:m, :])
    def sdiag(t_, par):
        return t_[par * 64:(par + 1) * 64, :].rearrange("k (c x) -> k c x", x=128)[:, :, par * 64:(par + 1) * 64]

    if not prompt:
        op("dve", "memset", (), (B.r_S32,), ap=B.S32[:, :], constant=0.0)
        for par in range(2):
            S.dma("sp", dsem("s0"), sdiag(B.S32, par), grp["s0"][seq].rearrange("(c par) k v -> par k c v", par=2)[par],
                  writes=(B.r_S32,))
        op("act", "activation", (B.r_S32,), (B.r_Sbf,), out=B.Sbf[:, :], in_=B.S32[:, :], func=AF.Copy)
    for c in range(4):
        op("pe", "matmul", (B.r_wT, B.r_Sbf), (rps[0],), inc=(c == 3), out=ps[0][:m, c * 128:(c + 1) * 128],
           lhsT=B.wT[:, c, :m], rhs=B.Sbf[:, c * 128:(c + 1) * 128], start=True, stop=True, skip_group_check=True)
    op("dve", "tensor_tensor", (rps[0], B.r_u), (B.r_vnew,), out=B.vnew[:m, :], in0=B.u[:m, :], in1=ps[0][:m, :],
       op=ALU.subtract)
    for c in range(4):
        op("pe", "matmul", (B.r_qkT, B.r_Sbf), (rps[1],), inc=(c == 3), out=ps[1][:m, c * 128:(c + 1) * 128],
           lhsT=B.qkT[:, c, c0:c0 + m], rhs=B.Sbf[:, c * 128:(c + 1) * 128], start=True, stop=True, skip_group_check=True)
    op("dve", "tensor_tensor", (rps[1], rsc), (B.r_o1,), out=B.o1[:m, :].rearrange("p (h d) -> p h d", d=64),
       in0=ps[1][:m, :].rearrange("p (h d) -> p h d", d=64), in1=bc(sc[:m, 32:40], 64), op=ALU.mult)
    for h in range(8):
        op("pe", "matmul", (B.r_qk, B.r_vnew), (rps[0],), inc=(h == 7), out=ps[0][:m, h * 64:(h + 1) * 64],
           lhsT=B.qk[:m, h * m:(h + 1) * m], rhs=B.vnew[:m, h * 64:(h + 1) * 64], start=True, stop=True,
           skip_group_check=True)
    op("dve", "tensor_tensor", (rps[0], B.r_o1), (B.r_osb,), out=B.osb[:m, :], in0=ps[0][:m, :], in1=B.o1[:m, :], op=ALU.add)
    for c in range(4):
        op("pe", "matmul", (B.r_kdec, B.r_vnew), (rps[3],), inc=(c == 3), out=ps[3][:, c * 128:(c + 1) * 128],
           lhsT=B.kdec[:m, c * 128:(c + 1) * 128], rhs=B.vnew[:m, c * 128:(c + 1) * 128], start=True, stop=True,
           skip_group_check=True)
    op("dve", "tensor_tensor", (rps[3],), (B.r_decq,), out=B.decq[:, :].rearrange("p (c x) -> p c x", x=128),
       in0=ps[3][:, :].rearrange("p (c x) -> p c x", x=128), in1=bc_mid(cx.bones[:, :], 4), op=ALU.mult)
    op("dve", "tensor_tensor", (B.r_glb,), (B.r_S32,), out=B.S32[:, :].rearrange("p (c x) -> p c x", x=128),
       in0=B.S32[:, :].rearrange("p (c x) -> p c x", x=128), in1=bc(B.glb[:, 0:4], 128), op=ALU.mult)
    op("pool", "tensor_tensor", (B.r_decq,), (B.r_S32,), out=B.S32[:, :], in0=B.S32[:, :], in1=B.decq[:, :], op=ALU.add)
    if last:
        for par in range(2):
            S.dma("sp", dsem("s0"), grp["new_gdn"][seq].rearrange("(c par) k v -> par k c v", par=2)[par],
                  sdiag(B.S32, par), reads=(B.r_S32,), writes=(grp["r_out"],))
    else:
        op("act", "activation", (B.r_S32,), (B.r_Sbf,), out=B.Sbf[:, :], in_=B.S32[:, :], func=AF.Copy)
    if cx.stop <= 6:
        return
    op("pool", "tensor_tensor", (B.r_osb,), (B.r_o1,), out=B.o1[:m, :], in0=B.osb[:m, :], in1=B.osb[:m, :], op=ALU.mult)
    op("dve", "tensor_reduce", (B.r_o1,), (B.r_ost,), out=B.ost[:m, 0:8], in_=B.o1[:m, :].rearrange("p (h d) -> p h d", d=64),
       axis=AX.X, op=ALU.add)
    op("act", "activation", (B.r_ost,), (B.r_ost,), out=B.ost[:m, 8:16], in_=B.ost[:m, 0:8], func=AF.Sqrt, scale=1.0 / 64,
       bias=cx.eps_t[:m, 0:1])
    op("dve", "reciprocal", (B.r_ost,), (B.r_ost,), out=B.ost[:m, 8:16], in_=B.ost[:m, 8:16])
    op("act", "activation", (B.r_tm,), (B.r_zs,), out=B.zs[:m, :], in_=zr, func=AF.Silu)
    op("dve", "tensor_tensor", (B.r_osb, B.r_ost), (B.r_osb,), out=B.osb[:m, :].rearrange("p (h d) -> p h d", d=64),
       in0=B.osb[:m, :].rearrange("p (h d) -> p h d", d=64), in1=bc(B.ost[:m, 8:16], 64), op=ALU.mult)
    op("pool", "tensor_tensor", (B.r_osb, B.r_gng), (B.r_osb,), out=B.osb[:m, :].rearrange("p (h d) -> p h d", d=64),
       in0=B.osb[:m, :].rearrange("p (h d) -> p h d", d=64), in1=bc_mid(B.gng[:m, :], 8), op=ALU.mult)
    op("dve", "tensor_tensor", (B.r_osb, B.r_zs), (B.r_gout,), out=B.gout[:m, :], in0=B.osb[:m, :], in1=B.zs[:m, :], op=ALU.mult)
    S.dma("sp", dsem("gout"), grp["mix"][tok0:tok0 + m, 0:512], B.gout[:m, :], reads=(B.r_gout,), writes=(grp["r_mix"],))
    if cx.stop <= 7:
        return
    S.dma("sp", dsem("cs"), B.cs[:m, :], grp["cs"][(tok0 if prompt else 0):(tok0 if prompt else 0) + m, :], writes=(B.r_cs,))
    q3 = qraw.rearrange("p (h d) -> p h d", d=96)
    mq3 = B.mq[:m, :].rearrange("p (h d) -> p h d", d=96)
    op("pool", "tensor_tensor", (B.r_tm,), (B.r_mq,), out=B.mq[:m, :], in0=qraw, in1=qraw, op=ALU.mult)
    op("dve", "tensor_reduce", (B.r_mq,), (B.r_mqs,), out=B.mqs[:m, 0:8], in_=mq3[:, :, 0:64], axis=AX.X, op=ALU.add)
    op("dve", "tensor_reduce", (B.r_mq,), (B.r_mqs,), out=B.mqs[:m, 8:16], in_=mq3[:, :, 64:96], axis=AX.X, op=ALU.add)
    op("act", "activation", (B.r_mqs,), (B.r_mqs,), out=B.mqs[:m, 16:24], in_=B.mqs[:m, 0:8], func=AF.Sqrt, scale=1.0 / 64,
       bias=cx.eps_t[:m, 0:1])
    op("act", "activation", (B.r_mqs,), (B.r_mqs,), out=B.mqs[:m, 24:32], in_=B.mqs[:m, 8:16], func=AF.Sqrt, scale=1.0 / 32,
       bias=cx.eps_t[:m, 0:1])
    op("dve", "reciprocal", (B.r_mqs,), (B.r_mqs,), out=B.mqs[:m, 16:32], in_=B.mqs[:m, 16:32])
    op("dve", "tensor_tensor", (B.r_tm, B.r_mqs), (B.r_mq,), out=mq3[:, :, 0:64], in0=q3[:, :, 0:64], in1=bc(B.mqs[:m, 16:24], 64),
       op=ALU.mult)
    op("pool", "tensor_tensor", (B.r_mq, B.r_qng), (B.r_Qb,), out=B.Qb[:m, :].rearrange("p (h d) -> p h d", d=96)[:, :, 0:64],
       in0=mq3[:, :, 0:64], in1=bc_mid(B.qng[:m, :], 8), op=ALU.mult)
    op("dve", "tensor_tensor", (B.r_tm, B.r_mqs), (B.r_mq,), out=mq3[:, :, 64:96], in0=q3[:, :, 64:96], in1=bc(B.mqs[:m, 24:32], 32),
       op=ALU.mult)
    op("pool", "tensor_tensor", (B.r_mq, B.r_qrg), (B.r_mq,), out=mq3[:, :, 64:96], in0=mq3[:, :, 64:96],
       in1=bc_mid(B.qrg[:m, :], 8), op=ALU.mult)
    cosb, sinb = bc_mid(B.cs[:m, 0:16], 8), bc_mid(B.cs[:m, 16:32], 8)
    x1, x2 = mq3[:, :, 64:80], mq3[:, :, 80:96]
    Q3 = B.Qb[:m, :].rearrange("p (h d) -> p h d", d=96)
    op("dve", "tensor_tensor", (B.r_mq, B.r_cs), (B.r_qrr,), out=B.qrr[:m, :, 0:16], in0=x1, in1=cosb, op=ALU.mult)
    op("dve", "tensor_tensor", (B.r_mq, B.r_cs), (B.r_qrr,), out=B.qrr[:m, :, 16:32], in0=x2, in1=sinb, op=ALU.mult)
    op("dve", "tensor_tensor", (B.r_qrr,), (B.r_Qb,), out=Q3[:, :, 64:80], in0=B.qrr[:m, :, 0:16], in1=B.qrr[:m, :, 16:32],
       op=ALU.subtract)
    op("dve", "tensor_tensor", (B.r_mq, B.r_cs), (B.r_qrr,), out=B.qrr[:m, :, 0:16], in0=x1, in1=sinb, op=ALU.mult)
    op("dve", "tensor_tensor", (B.r_mq, B.r_cs), (B.r_qrr,), out=B.qrr[:m, :, 16:32], in0=x2, in1=cosb, op=ALU.mult)
    op("dve", "tensor_tensor", (B.r_qrr,), (B.r_Qb,), out=Q3[:, :, 80:96], in0=B.qrr[:m, :, 0:16], in1=B.qrr[:m, :, 16:32],
       op=ALU.add)
    S.dma("sp", dsem("Qb"), grp["Q"][tok0:tok0 + m, :], B.Qb[:m, :], reads=(B.r_Qb,), writes=(grp["r_Q"],))
    op("pool", "tensor_tensor", (B.r_tm,), (B.r_cn,), out=B.cn[:m, :], in0=ckvr, in1=ckvr, op=ALU.mult)
    op("dve", "tensor_reduce", (B.r_cn,), (B.r_mqs,), out=B.mqs[:m, 0:1], in_=B.cn[:m, :], axis=AX.X, op=ALU.add)
    op("act", "activation", (B.r_mqs,), (B.r_mqs,), out=B.mqs[:m, 1:2], in_=B.mqs[:m, 0:1], func=AF.Sqrt, scale=1.0 / 128,
       bias=cx.eps_t[:m, 0:1])
    op("dve", "reciprocal", (B.r_mqs,), (B.r_mqs,), out=B.mqs[:m, 1:2], in_=B.mqs[:m, 1:2])
    op("dve", "scalar_tensor_tensor", (B.r_tm, B.r_mqs, B.r_ckg), (B.r_cn,), out=B.cn[:m, :], in0=ckvr, scalar=B.mqs[:m, 1:2],
       in1=B.ckg[:m, :], op0=ALU.mult, op1=ALU.mult)
    S.dma("sp", dsem("cn"), grp["new_ckv"][tok0:tok0 + m, :], B.cn[:m, :], reads=(B.r_cn,), writes=(grp["r_ckv"],))
    op("pool", "tensor_tensor", (B.r_tm,), (B.r_krn,), out=B.krn[:m, :], in0=krr_, in1=krr_, op=ALU.mult)
    op("dve", "tensor_reduce", (B.r_krn,), (B.r_mqs,), out=B.mqs[:m, 2:3], in_=B.krn[:m, :], axis=AX.X, op=ALU.add)
    op("act", "activation", (B.r_mqs,), (B.r_mqs,), out=B.mqs[:m, 3:4], in_=B.mqs[:m, 2:3], func=AF.Sqrt, scale=1.0 / 32,
       bias=cx.eps_t[:m, 0:1])
    op("dve", "reciprocal", (B.r_mqs,), (B.r_mqs,), out=B.mqs[:m, 3:4], in_=B.mqs[:m, 3:4])
    op("dve", "scalar_tensor_tensor", (B.r_tm, B.r_mqs, B.r_krg), (B.r_krn,), out=B.krn[:m, :], in0=krr_, scalar=B.mqs[:m, 3:4],
       in1=B.krg[:m, :], op0=ALU.mult, op1=ALU.mult)
    k1, k2, cs1, sn1 = B.krn[:m, 0:16], B.krn[:m, 16:32], B.cs[:m, 0:16], B.cs[:m, 16:32]
    op("dve", "tensor_tensor", (B.r_krn, B.r_cs), (B.r_krr,), out=B.krr[:m, 0:16], in0=k1, in1=cs1, op=ALU.mult)
    op("dve", "tensor_tensor", (B.r_krn, B.r_cs), (B.r_krr,), out=B.krr[:m, 16:32], in0=k2, in1=sn1, op=ALU.mult)
    op("dve", "tensor_tensor", (B.r_krr,), (B.r_qrr,), out=B.qrr[:m, 0, 0:16], in0=B.krr[:m, 0:16], in1=B.krr[:m, 16:32],
       op=ALU.subtract)
    op("dve", "tensor_tensor", (B.r_krn, B.r_cs), (B.r_krr,), out=B.krr[:m, 0:16], in0=k1, in1=sn1, op=ALU.mult)
    op("dve", "tensor_tensor", (B.r_krn, B.r_cs), (B.r_krr,), out=B.krr[:m, 16:32], in0=k2, in1=cs1, op=ALU.mult)
    op("dve", "tensor_tensor", (B.r_krr,), (B.r_qrr,), out=B.qrr[:m, 0, 16:32], in0=B.krr[:m, 0:16], in1=B.krr[:m, 16:32],
       op=ALU.add)
    S.dma("sp", dsem("kr"), grp["new_kr"][tok0:tok0 + m, :], B.qrr[:m, 0, :], reads=(B.r_qrr,), writes=(grp["r_kr"],))


def phase_mixer(cx, S, groups, P):
    with ExitStack() as st:
        B = mixer_alloc(cx, S, st, "mx", P)
        mc = sb(cx, st, "mx_cst", [128, CST_COLS - 128], F32)
        r_mc = Res("mxcst")
        S.dma("sp", S.dsem("mxcst"), mc[:, :], cx.cst[:, 128:CST_COLS], writes=(r_mc,))
        cx.utri = mc[:, 0:128]
        cx.onesf = mc[:, 128:256]
        cx.ohsel = mc[0:8, 384:1408].rearrange("p (h i) -> p h i", i=128)
        cx.mconst = {}
        off = 1408
        for m_ in (64, 4):
            cx.mconst[m_] = (mc[:, off + 4 * m_:off + 8 * m_], mc[:, off + 8 * m_:off + 12 * m_], mc[:, off:off + 4 * m_])
            off += 12 * m_
        idf = sb(cx, st, "mx_idf", [128, 128], F32)
        S.op("dve", "tensor_scalar", dict(out=idf[:, :], in0=cx.ident_f, scalar1=-1.0, scalar2=None, op0=ALU.mult),
             writes=(r_mc,))
        cx.nident_f = idf[:, :]
        dc = S.dsem("mxc")
        r_c = Res("mxconst")
        S.dma("sp", dc, B.cw[:, :, :], P["conv_w_fm"], writes=(B.r_cw,))
        S.dma("sp", dc, B.g8[:, 0:8], row_bcast(P["a_log"], 128), writes=(B.r_g8,))
        S.dma("sp", dc, B.g8[:, 8:16], row_bcast(P["dt_bias"], 128), writes=(B.r_g8,))
        S.dma("sp", dc, B.gng[:, :], row_bcast(P["gdn_norm"], 128), writes=(B.r_gng,))
        S.dma("sp", dc, B.qng[:, :], row_bcast(P["qn_g"], 128), writes=(B.r_qng,))
        S.dma("sp", dc, B.kng[:, :], row_bcast(P["kn_g"], 128), writes=(B.r_kng,))
        S.dma("sp", dc, B.qrg[:, :], row_bcast(P["qr_g"], 128), writes=(B.r_qrg,))
        S.dma("sp", dc, B.ckg[:, :], row_bcast(P["ckv_g"], 128), writes=(B.r_ckg,))
        S.dma("sp", dc, B.krg[:, :], row_bcast(P["kr_g"], 128), writes=(B.r_krg,))
        S.barrier()
        S.op("act", "activation", dict(out=B.g8[:, 0:8], in_=B.g8[:, 0:8], func=AF.Exp), writes=(B.r_g8,))
        S.op("dve", "scalar_tensor_tensor", dict(out=B.qng[:, :], in0=B.qng[:, :], scalar=ATT_SCALE, in1=B.kng[:, :],
                                                 op0=ALU.mult, op1=ALU.mult), writes=(B.r_qng,))
        S.op("dve", "tensor_scalar", dict(out=B.qrg[:, :], in0=B.qrg[:, :], scalar1=ATT_SCALE, scalar2=None, op0=ALU.mult),
             writes=(B.r_qrg,))
        S.barrier()
        for gi, grp in enumerate(groups):
            mixer_group(cx, S, B, "mx%d" % gi, grp, P)
            S.barrier()
            S.emit()


def phase_wout(cx, S, groups, w_out_ap):
    with ExitStack() as st:
        wo = sb(cx, st, "wo_w", [128, 8, D], BF16)
        r_wo = Res()
        load_weight_bf16(cx, S, "wow", wo, r_wo, w_out_ap, 8, D, D)
        Gt = sb(cx, st, "wo_Gt", [128, D], F32)
        mx = [sb(cx, st, "wo_mx%d" % i, [128, D], BF16) for i in range(2)]
        mT = sb(cx, st, "wo_mT", [128, 8, 128], BF16)
        xr = [sb(cx, st, "wo_xr%d" % i, [128, D], F32) for i in range(2)]
        tmp = sb(cx, st, "wo_tmp", [128, 512], F32)
        r_Gt, r_mT, r_tmp = Res(), Res(), Res()
        r_mx, r_xr = [Res(), Res()], [Res(), Res()]
        d_g = S.dsem("wo_g")
        d_mx = [S.dsem("wo_mx0"), S.dsem("wo_mx1")]
        d_xr = [S.dsem("wo_xr0"), S.dsem("wo_xr1")]
        d_xst = [S.dsem("wo_xst0"), S.dsem("wo_xst1")]
        i = 0
        for g in groups:
            n, T = g["n"], g["T"]
            S.dma("sp", d_g, Gt[:n, :], g["mods"][5, :n, :], writes=(r_Gt,))
            for r0 in range(0, T, 128):
                m = min(128, T - r0)
                k = i % 2
                i += 1
                S.dma("sp", d_mx[k], mx[k][:m, :], g["mix"][r0:r0 + m, :], reads=(g["r_mix"],), writes=(r_mx[k],))
                S.dma("sp", d_xr[k], xr[k][:m, :], g["src"][r0:r0 + m, :], reads=(g["src_res"][r0 // 128],),
                      writes=(r_xr[k],))
                ptb = cx.ps[6 + k].bitcast(BF16)
                for c in range(8):
                    S.op("pe", "transpose", dict(out=ptb[:, c * 128:c * 128 + m], in_=mx[k][:m, c * 128:(c + 1) * 128],
                                                 identity=cx.ident_bf[:m, :m]), reads=(r_mx[k],), writes=(cx.rps[6 + k],),
                         inc=(c == 7))
                S.op("act", "activation", dict(out=mT[:, :, :m], in_=ptb.rearrange("p (c t) -> p c t", c=8)[:, :, :m],
                                               func=AF.Copy), reads=(cx.rps[6 + k],), writes=(r_mT,))
                for half in range(2):
                    pk = (i * 2 + half) % 4
                    for c in range(8):
                        S.op("pe", "matmul", dict(out=cx.ps[pk][:m, :], lhsT=mT[:, c, :m],
                                                  rhs=wo[:, c, half * 512:(half + 1) * 512], start=(c == 0), stop=(c == 7)),
                             reads=(r_mT, r_wo), writes=(cx.rps[pk],), inc=(c == 7))
                    S.op("dve", "tensor_tensor", dict(out=tmp[:m, :], in0=cx.ps[pk][:m, :],
                                                      in1=Gt[:m, half * 512:(half + 1) * 512], op=ALU.mult),
                         reads=(cx.rps[pk], r_Gt), writes=(r_tmp,))
                    S.op("pool", "tensor_tensor", dict(out=xr[k][:m, half * 512:(half + 1) * 512],
                                                       in0=xr[k][:m, half * 512:(half + 1) * 512], in1=tmp[:m, :],
                                                       op=ALU.add), reads=(r_tmp,), writes=(r_xr[k],))
                S.dma("pool", d_xst[k], g["dst"][r0:r0 + m, :], xr[k][:m, :], reads=(r_xr[k],),
                      writes=(g["dst_res"][r0 // 128],))
        S.barrier()
        S.emit()


class AttBufs:
    pass


def att_alloc(cx, S, st, name, P):
    A = AttBufs()

    def t(nm, shape, dt):
        tt_ = sb(cx, st, name + nm, shape, dt)
        setattr(A, nm, tt_)
        setattr(A, "r_" + nm, Res(nm))
        return tt_

    t("wuk", [128, 512], BF16)
    t("wuv", [128, 512], BF16)
    t("wst", [128, 512], F32)
    d = S.dsem(name + "w")
    S.dma("sp", d, A.wst[:, :], P["w_uk"], writes=(A.r_wst,))
    S.op("dve", "tensor_copy", dict(out=A.wuk[:, :], in_=A.wst[:, :]), reads=(A.r_wst,), writes=(A.r_wuk,))
    S.dma("sp", d, A.wst[:, :], P["w_uv"], writes=(A.r_wst,))
    S.op("dve", "tensor_copy", dict(out=A.wuv[:, :], in_=A.wst[:, :]), reads=(A.r_wst,), writes=(A.r_wuv,))
    for k_ in range(2):
        t("cTb%d" % k_, [128, 128], BF16)
        t("sq%d" % k_, [128, 512], F32)
        t("kst%d" % k_, [128, 16], F32)
        t("Kf%d" % k_, [128, 8, 96], BF16)
        t("KT%d" % k_, [96, 8, 128], BF16)
    t("QT", [96, 8, 128], BF16)
    t("Qb", [128, 768], BF16)
    t("PT", [128, 512], BF16)
    t("ctx", [128, 8, 128], BF16)
    t("rden", [128, 8], F32)
    t("ctxT", [128, 8, 128], BF16)
    t("mo", [128, 512], BF16)
    t("m01", [128, 128], BF16)
    A.d = {}
    return A


def att_kside(cx, S, A, c_blk, kr_blk, r_src, n, KT_dst, r_KT, k=0, stage=None):
    ps, rps = cx.ps, cx.rps
    cTb, sq, kst, Kf = (getattr(A, nm + str(k)) for nm in ("cTb", "sq", "kst", "Kf"))
    r_cTb, r_sq, r_kst, r_Kf = (getattr(A, "r_" + nm + str(k)) for nm in ("cTb", "sq", "kst", "Kf"))
    p0, p1, p2 = (0, 1, 2) if k == 0 else (6, 7, 3)

    def op(eng, meth, reads, writes, inc=True, **kw):
        S.op(eng, meth, kw, reads=reads, writes=writes, inc=inc)

    if stage == "B":
        ptb = ps[p2].bitcast(BF16)
        for h in range(8):
            op("pe", "transpose", (r_Kf,), (rps[p2],), inc=(h == 7), out=ptb[0:96, h * 128:h * 128 + n], in_=Kf[:n, h, :],
               identity=cx.ident_bf[:n, :n])
        op("act", "activation", (rps[p2],), (r_KT,), out=KT_dst, in_=ptb[0:96, :].rearrange("p (h t) -> p h t", t=128)[:, :, :n],
           func=AF.Copy)
        return
    op("pe", "transpose", (r_src,), (rps[p0],), out=ps[p0][:, :n], in_=c_blk, identity=cx.ident_f[:n, :n])
    if k == 1:
        op("dve", "tensor_copy", (rps[p0],), (r_cTb,), out=cTb[:, :n], in_=ps[p0][:, :n])
    else:
        op("act", "activation", (rps[p0],), (r_cTb,), out=cTb[:, :n], in_=ps[p0][:, :n], func=AF.Identity)
    op("pe", "matmul", (r_cTb, A.r_wuk), (rps[p1],), out=ps[p1][:n, :], lhsT=cTb[:, :n], rhs=A.wuk[:, :], start=True, stop=True)
    op("act", "activation", (rps[p1],), (r_sq,), out=sq[:n, :], in_=ps[p1][:n, :], func=AF.Square)
    op("dve", "tensor_reduce", (r_sq,), (r_kst,), out=kst[:n, 0:8], in_=sq[:n, :].rearrange("p (h d) -> p h d", d=64),
       axis=AX.X, op=ALU.add)
    op("act", "activation", (r_kst,), (r_kst,), out=kst[:n, 8:16], in_=kst[:n, 0:8], func=AF.Sqrt, scale=1.0 / 64,
       bias=cx.eps_t[:n, 0:1])
    op("dve", "reciprocal", (r_kst,), (r_kst,), out=kst[:n, 8:16], in_=kst[:n, 8:16])
    op("dve", "tensor_tensor", (rps[p1], r_kst), (r_Kf,), out=Kf[:n, :, 0:64],
       in0=ps[p1][:n, :].rearrange("p (h d) -> p h d", d=64), in1=bc(kst[:n, 8:16], 64), op=ALU.mult)
    op("pool", "tensor_copy", (r_src,), (r_Kf,), out=Kf[:n, :, 64:96], in_=bc_mid(kr_blk, 8))
    if stage == "A":
        return
    ptb = ps[p2].bitcast(BF16)
    for h in range(8):
        op("pe", "transpose", (r_Kf,), (rps[p2],), inc=(h == 7), out=ptb[0:96, h * 128:h * 128 + n], in_=Kf[:n, h, :],
           identity=cx.ident_bf[:n, :n])
    op("act", "activation", (rps[p2],), (r_KT,), out=KT_dst, in_=ptb[0:96, :].rearrange("p (h t) -> p h t", t=128)[:, :, :n],
       func=AF.Copy)


def att_out(cx, S, A, nq, r_ctx_in, mix_dst, r_mix, dsem_):
    ps, rps = cx.ps, cx.rps
    ptb = ps[2].bitcast(BF16)
    for h in range(8):
        S.op("pe", "transpose", dict(out=ptb[:, h * 128:h * 128 + nq], in_=A.ctx[:nq, h, :], identity=cx.ident_bf[:nq, :nq]),
             reads=(A.r_ctx,), writes=(rps[2],), inc=(h == 7))
    S.op("act", "activation", dict(out=A.ctxT[:, :, :nq], in_=ptb.rearrange("p (h t) -> p h t", t=128)[:, :, :nq], func=AF.Copy),
         reads=(rps[2],), writes=(A.r_ctxT,))
    for h in range(8):
        S.op("pe", "matmul", dict(out=ps[3][:nq, h * 64:(h + 1) * 64], lhsT=A.ctxT[:, h, :nq], rhs=A.wuv[:, h * 64:(h + 1) * 64],
                                  start=True, stop=True, skip_group_check=True), reads=(A.r_ctxT, A.r_wuv), writes=(rps[3],),
             inc=(h == 7))
    S.op("dve", "tensor_copy", dict(out=A.mo[:nq, :], in_=ps[3][:nq, :]), reads=(rps[3],), writes=(A.r_mo,))
    S.dma("sp", dsem_, mix_dst, A.mo[:nq, :], reads=(A.r_mo,), writes=(r_mix,))


def phase_attn_prompt(cx, S, grp, P):
    T = grp["T"]
    nb = T // 128
    with ExitStack() as st:
        A = att_alloc(cx, S, st, "ap", P)
        KTall = sb(cx, st, "ap_KTall", [96, 8, T], BF16)
        caug = sb(cx, st, "ap_caug", [128, nb, 132], BF16)
        cst_ = [sb(cx, st, "ap_cst%d" % i, [128, 160], F32) for i in range(2)]
        r_KTall, r_caug = Res(), Res()
        r_cst = [Res(), Res()]
        d_cst = [S.dsem("ap_c0"), S.dsem("ap_c1")]
        d_q, d_o = S.dsem("ap_q"), S.dsem("ap_o")
        ps, rps = cx.ps, cx.rps
        S.op("pool", "memset", dict(ap=caug[:, :, 128:132], constant=1.0), writes=(r_caug,))
        S.op("pool", "memset", dict(ap=A.m01[:, :], constant=1.0), writes=(A.r_m01,))
        S.op("pool", "affine_select", dict(out=A.m01[:, :], in_=A.m01[:, :], pattern=[[1, 128]], compare_op=ALU.is_ge, fill=0.0,
                                           base=0, channel_multiplier=-1), writes=(A.r_m01,))
        for kb in range(nb):
            k = kb % 2
            S.dma("sp", d_cst[k], cst_[k][:, 0:128], grp["new_ckv"][kb * 128:(kb + 1) * 128, :], reads=(grp["r_ckv"],),
                  writes=(r_cst[k],))
            S.dma("sp", d_cst[k], cst_[k][:, 128:160], grp["new_kr"][kb * 128:(kb + 1) * 128, :], reads=(grp["r_kr"],),
                  writes=(r_cst[k],))
            S.op("dve", "tensor_copy", dict(out=caug[:, kb, 0:128], in_=cst_[k][:, 0:128]), reads=(r_cst[k],), writes=(r_caug,))
            att_kside(cx, S, A, cst_[k][:, 0:128], cst_[k][:, 128:160], r_cst[k], 128, KTall[:, :, kb * 128:(kb + 1) * 128], r_KTall, k=k)
        for qb in range(nb):
            S.dma("sp", d_q, A.Qb[:, :], grp["Q"][qb * 128:(qb + 1) * 128, :], reads=(grp["r_Q"],), writes=(A.r_Qb,))
            ptb = ps[2].bitcast(BF16)
            for h in range(8):
                S.op("pe", "transpose", dict(out=ptb[0:96, h * 128:(h + 1) * 128], in_=A.Qb[:, h * 96:(h + 1) * 96],
                                             identity=cx.ident_bf[:, :]), reads=(A.r_Qb,), writes=(rps[2],), inc=(h == 7))
            S.op("act", "activation", dict(out=A.QT[:, :, :], in_=ptb[0:96, :].rearrange("p (h t) -> p h t", t=128), func=AF.Copy),
                 reads=(rps[2],), writes=(A.r_QT,))
            gi = 0
            for h in range(8):
                pc = 4 + h % 2
                for kb0 in range(0, qb + 1, 4):
                    ng = min(4, qb + 1 - kb0)
                    pk = gi % 2
                    gi += 1
                    for j in range(ng):
                        kb = kb0 + j
                        S.op("pe", "matmul", dict(out=ps[pk][:, j * 128:(j + 1) * 128], lhsT=KTall[:, h, kb * 128:(kb + 1) * 128],
                                                  rhs=A.QT[:, h, :], start=True, stop=True, skip_group_check=True),
                             reads=(r_KTall, A.r_QT), writes=(rps[pk],), inc=(j == ng - 1))
                    S.op("act", "activation", dict(out=A.PT[:, 0:ng * 128], in_=ps[pk][:, 0:ng * 128], func=AF.Exp),
                         reads=(rps[pk],), writes=(A.r_PT,))
                    if kb0 + ng - 1 == qb:
                        j = ng - 1
                        S.op("dve", "tensor_tensor", dict(out=A.PT[:, j * 128:(j + 1) * 128], in0=A.PT[:, j * 128:(j + 1) * 128],
                                                          in1=A.m01[:, :], op=ALU.mult), reads=(A.r_m01,), writes=(A.r_PT,))
                    for j in range(ng):
                        kb = kb0 + j
                        S.op("pe", "matmul", dict(out=ps[pc][:, 0:129], lhsT=A.PT[:, j * 128:(j + 1) * 128], rhs=caug[:, kb, 0:129],
                                                  start=(kb == 0), stop=(kb == qb), skip_group_check=True),
                             reads=(A.r_PT, r_caug), writes=(rps[pc],), inc=(j == ng - 1))
                S.op("dve", "reciprocal", dict(out=A.rden[:, h:h + 1], in_=ps[pc][:, 128:129]), reads=(rps[pc],), writes=(A.r_rden,))
                S.op("dve", "tensor_scalar", dict(out=A.ctx[:, h, :], in0=ps[pc][:, 0:128], scalar1=A.rden[:, h:h + 1], scalar2=None,
                                                  op0=ALU.mult), reads=(rps[pc], A.r_rden), writes=(A.r_ctx,))
            att_out(cx, S, A, 128, A.r_ctx, grp["mix"][qb * 128:(qb + 1) * 128, 512:1024], grp["r_mix"], d_o)
        S.barrier()
        S.emit()


def phase_attn_sample(cx, S, grp, P, cache_c, cache_k, ptab, nseq, npages):
    with ExitStack() as st:
        A = att_alloc(cx, S, st, "as", P)
        cgs = [sb(cx, st, "as_cg%d" % i, [128, 128 * 128], F32) for i in range(2)]
        kgs = [sb(cx, st, "as_kg%d" % i, [128, 128 * 32], F32) for i in range(2)]
        idxs = [sb(cx, st, "as_idx%d" % i, [128, 2], I32) for i in range(2)]
        r_cgs, r_idxs = [Res(), Res()], [Res(), Res()]
        d_cgs, d_idxs = [S.dsem("as_cg0"), S.dsem("as_cg1")], [S.dsem("as_ix0"), S.dsem("as_ix1")]
        cb2 = [sb(cx, st, "as_cb%d" % i, [128, 132], BF16) for i in range(3)]
        cn = sb(cx, st, "as_cn", [4, 160], F32)
        qs = sb(cx, st, "as_qs", [4, 768], BF16)
        QTs = sb(cx, st, "as_QTs", [96, 8, 4], BF16)
        PTs2 = [sb(cx, st, "as_PTs%d" % i, [128, 32], BF16) for i in range(2)]
        ms = sb(cx, st, "as_ms", [4, 32], BF16)
        c32 = sb(cx, st, "as_c32", [32, 128], BF16)
        cT32 = sb(cx, st, "as_cT32", [128, 32], BF16)
        rd = sb(cx, st, "as_rd", [32, 1], F32)
        mo = sb(cx, st, "as_mo", [4, 512], BF16)
        r_cg, r_kg, r_idx, r_cb, r_cn, r_qs, r_QTs, r_PTs, r_ms, r_c32, r_cT32, r_rd, r_mo = (Res() for _ in range(13))
        d_idx, d_cg, d_kg, d_cn, d_qs, d_mo = (S.dsem("as%d" % i) for i in range(6))
        ps, rps = cx.ps, cx.rps
        r_cb2, r_PTs2, r_sc = [Res(), Res(), Res()], [Res(), Res()], [Res(), Res()]
        for i_ in range(3):
            S.op("pool", "memset", dict(ap=cb2[i_][:, 128:132], constant=1.0), writes=(r_cb2[i_],))
        S.op("pool", "memset", dict(ap=ms[:, :], constant=1.0), writes=(r_ms,))
        S.op("pool", "affine_select", dict(out=ms[:, :], in_=ms[:, :], pattern=[[0, 8], [1, 4]], compare_op=ALU.is_ge, fill=0.0,
                                           base=0, channel_multiplier=-1), writes=(r_ms,))
        def gather(s):
            k_ = s % 2
            S.dma("sp", d_idxs[k_], idxs[k_][:npages, 0:1], ptab[s:s + 1, :].rearrange("o p -> p o"), writes=(r_idxs[k_],),
                  allow_slow_non_contiguous=True)
            S.idma(d_cgs[k_], reads=(r_idxs[k_],), writes=(r_cgs[k_],), out=cgs[k_][:npages, :], out_offset=None, in_=cache_c,
                   in_offset=bass.IndirectOffsetOnAxis(ap=idxs[k_][:npages, 0:1], axis=0))
            S.idma(d_cgs[k_], reads=(r_idxs[k_],), writes=(r_cgs[k_],), out=kgs[k_][:npages, :], out_offset=None, in_=cache_k,
                   in_offset=bass.IndirectOffsetOnAxis(ap=idxs[k_][:npages, 0:1], axis=0))

        gather(0)
        for s in range(nseq):
            if s + 1 < nseq:
                gather(s + 1)
            cg, kg, r_cg, r_kg = cgs[s % 2], kgs[s % 2], r_cgs[s % 2], r_cgs[s % 2]
            S.dma("sp", d_qs, qs[:, :], grp["Q"][4 * s:4 * s + 4, :], reads=(grp["r_Q"],), writes=(r_qs,))
            ptb = ps[2].bitcast(BF16)
            for h in range(8):
                S.op("pe", "transpose", dict(out=ptb[0:96, h * 128:h * 128 + 4], in_=qs[:, h * 96:(h + 1) * 96],
                                             identity=cx.ident_bf[:4, :4]), reads=(r_qs,), writes=(rps[2],), inc=(h == 7))
            S.op("act", "activation", dict(out=QTs[:, :, :], in_=ptb[0:96, :].rearrange("p (h t) -> p h t", t=128)[:, :, 0:4],
                                           func=AF.Copy), reads=(rps[2],), writes=(r_QTs,))
            S.dma("sp", d_cn, cn[:, 0:128], grp["new_ckv"][4 * s:4 * s + 4, :], reads=(grp["r_ckv"],), writes=(r_cn,))
            S.dma("sp", d_cn, cn[:, 128:160], grp["new_kr"][4 * s:4 * s + 4, :], reads=(grp["r_kr"],), writes=(r_cn,))
            nblk = 128 + 1

            def blk_args(t):
                if t < 128:
                    n = npages
                    return n, cg[:n, t * 128:(t + 1) * 128], kg[:n, t * 32:(t + 1) * 32], r_cg, (r_cg, r_kg)
                return 4, cn[:, 0:128], cn[:, 128:160], r_cn, (r_cn,)

            def stage_a(t):
                n, c_blk, k_blk, r_src, rs = blk_args(t)
                kk = t % 2
                S.op("pool", "tensor_copy", dict(out=cb2[t % 3][:n, 0:128], in_=c_blk), reads=rs, writes=(r_cb2[t % 3],))
                att_kside(cx, S, A, c_blk, k_blk, r_src, n, None, None, k=kk, stage="A")

            def stage_b1(t):
                n, c_blk, k_blk, r_src, rs = blk_args(t)
                kk = t % 2
                KT, r_KT = (A.KT0, A.r_KT0) if kk == 0 else (A.KT1, A.r_KT1)
                att_kside(cx, S, A, c_blk, k_blk, r_src, n, KT[:, :, :n], r_KT, k=kk, stage="B")

            def stage_b2(t):
                n, c_blk, k_blk, r_src, rs = blk_args(t)
                kk = t % 2
                cb, r_cb, PTs, r_PTs = cb2[t % 3], r_cb2[t % 3], PTs2[kk], r_PTs2[kk]
                KT, r_KT = (A.KT0, A.r_KT0) if kk == 0 else (A.KT1, A.r_KT1)
                pb = 4 if kk == 0 else 6
                for h in range(8):
                    S.op("pe", "matmul", dict(out=ps[pb][:n, h * 4:(h + 1) * 4], lhsT=KT[:, h, :n], rhs=QTs[:, h, :],
                                              start=True, stop=True, skip_group_check=True), reads=(r_KT, r_QTs),
                         writes=(rps[pb],), inc=(h == 7))
                S.op("act", "activation", dict(out=PTs[:n, :], in_=ps[pb][:n, 0:32], func=AF.Exp), reads=(rps[pb],),
                     writes=(r_PTs,))
                if t == 128:
                    S.op("dve", "tensor_tensor", dict(out=PTs[:n, :], in0=PTs[:n, :], in1=ms[:n, :], op=ALU.mult),
                         reads=(r_ms,), writes=(r_PTs,))
                S.op("pe", "matmul", dict(out=ps[5][0:32, 0:129], lhsT=PTs[:n, :], rhs=cb[:n, 0:129], start=(t == 0),
                                          stop=(t == nblk - 1), skip_group_check=True), reads=(r_PTs, r_cb), writes=(rps[5],))

            stage_a(0)
            for t in range(nblk):
                if t + 1 < nblk:
                    stage_a(t + 1)
                stage_b1(t)
                if t >= 1:
                    stage_b2(t - 1)
            stage_b2(nblk - 1)
            S.op("dve", "reciprocal", dict(out=rd[:, :], in_=ps[5][0:32, 128:129]), reads=(rps[5],), writes=(r_rd,))
            S.op("dve", "tensor_scalar", dict(out=c32[:, :], in0=ps[5][0:32, 0:128], scalar1=rd[:, 0:1], scalar2=None,
                                              op0=ALU.mult), reads=(rps[5], r_rd), writes=(r_c32,))
            S.op("pe", "transpose", dict(out=ptb[:, 0:32], in_=c32[:, :], identity=cx.ident_bf[:32, :32]), reads=(r_c32,),
                 writes=(rps[2],))
            S.op("act", "activation", dict(out=cT32[:, :], in_=ptb[:, 0:32], func=AF.Copy), reads=(rps[2],), writes=(r_cT32,))
            for h in range(8):
                S.op("pe", "matmul", dict(out=ps[3][0:4, h * 64:(h + 1) * 64], lhsT=cT32[:, h * 4:(h + 1) * 4],
                                          rhs=A.wuv[:, h * 64:(h + 1) * 64], start=True, stop=True, skip_group_check=True),
                     reads=(r_cT32, A.r_wuv), writes=(rps[3],), inc=(h == 7))
            S.op("dve", "tensor_copy", dict(out=mo[:, :], in_=ps[3][0:4, :]), reads=(rps[3],), writes=(r_mo,))
            S.dma("sp", d_mo, grp["mix"][4 * s:4 * s + 4, 512:1024], mo[:, :], reads=(r_mo,), writes=(grp["r_mix"],))
            if s % 4 == 3:
                S.barrier()
                S.emit()
        S.barrier()
        S.emit()


CST_COLS = 128 + 128 + 128 + 128 + 1024 + 3 * 256 + 3 * 16


def host_consts():
    c = np.zeros((128, CST_COLS), np.float32)
    j = np.arange(128)[:, None]
    i = np.arange(128)[None, :]
    same = (j // 64 == i // 64)
    c[:, 0:128] = np.eye(128)
    c[:, 128:256] = (j <= i) & same
    c[:, 256:384] = same
    c[:, 384:512] = same
    oh = np.zeros((8, 8, 128), np.float32)
    for h in range(8):
        oh[h, h, :] = 1.0
    c[0:8, 512:1536] = oh.reshape(8, 1024)
    off = 1536
    for m in (64, 4):
        jj = np.arange(m)[:, None]
        ii = np.arange(m)[None, :]
        c[0:m, off:off + 4 * m] = np.tile(np.eye(m, dtype=np.float32), (1, 4))
        c[0:m, off + 4 * m:off + 8 * m] = np.tile(np.where(jj <= ii, 0.0, -1e30).astype(np.float32), (1, 4))
        c[0:m, off + 8 * m:off + 12 * m] = np.tile(np.where(jj < ii, 0.0, -1e30).astype(np.float32), (1, 4))
        off += 12 * m
    return c


def host_rope_table(pos):
    half = 16
    inv = (np.float32(10000.0) ** (-(np.arange(half, dtype=np.float32) / np.float32(half)))).astype(np.float32)
    ang = (pos.astype(np.float32)[:, None] * inv[None, :]).astype(np.float32)
    return np.concatenate([np.cos(ang), np.sin(ang)], axis=1).astype(np.float32)


def build(cfg):
    TP = cfg["TP"]
    NS = cfg["NS"]
    NSEQ = NS // 4
    phases = cfg.get("phases", ("adaln", "ffn1"))
    nc = bass.Bass("TRN2", target_bir_lowering=False)
    cx = Ctx()
    cx.nc = nc
    cx.stop = cfg.get("stop", 99)

    def din(name, shape, dt=F32):
        return nc.dram_tensor(name, list(shape), dt, kind="ExternalInput").ap()

    def dout(name, shape, dt=F32):
        return nc.dram_tensor(name, list(shape), dt, kind="ExternalOutput").ap()

    def dscr(name, shape, dt=F32):
        return nc.dram_tensor(name, list(shape), dt, kind="Internal").ap()

    xp = din("xp", [TP, D])
    xs_in = din("xs", [NS, D])
    c_rep = din("c_rep", [192, D])
    cst = din("cst", [128, CST_COLS])
    ada_w = din("ada_w", [D, 9 * D])
    ada_b = din("ada_b", [1, 9 * D])
    norm_ffn1 = din("norm_ffn1", [1, D])
    ffn1_wi = din("ffn1_wi", [D, 2 * DFF])
    ffn1_wo = din("ffn1_wo", [DFF, D])
    P = dict(norm_mix=din("norm_mix", [1, D]), w_in=din("w_in", [D, 2992]), conv_w_fm=din("conv_w_fm", [128, 12, 4]),
             a_log=din("a_log", [1, 8]), dt_bias=din("dt_bias", [1, 8]), gdn_norm=din("gdn_norm", [1, 64]),
             qn_g=din("qn_g", [1, 64]), qr_g=din("qr_g", [1, 32]), ckv_g=din("ckv_g", [1, 128]), kr_g=din("kr_g", [1, 32]),
             kn_g=din("kn_g", [1, 64]))
    P["w_uk"] = din("w_uk", [128, 512])
    P["w_uv"] = din("w_uv", [128, 512])
    w_out = din("w_out", [D, D])
    norm_ffn2 = din("norm_ffn2", [1, D])
    ffn2_wi = din("ffn2_wi", [D, 2 * DFF])
    ffn2_wo = din("ffn2_wo", [DFF, D])
    NPOOL = cfg.get("npool", 20480)
    cache_c = din("cache_c", [NPOOL, 128 * 128])
    cache_k = din("cache_k", [NPOOL, 128 * 32])
    ptab = din("ptab", [NSEQ, cfg.get("npages", 128)], I32)
    cs_p = din("cs_p", [TP, 32])
    cs_s = din("cs_s", [4, 32])
    st_conv = din("st_conv", [NSEQ, 3, 1536])
    st_gdn = din("st_gdn", [NSEQ, 8, 64, 64])

    yp = dout("yp", [TP, D])
    ys = dout("ys", [NS, D])
    o_ckv_p, o_kr_p = dout("ckv_p", [TP, 128]), dout("kr_p", [TP, 32])
    o_conv_p, o_gdn_p = dout("conv_p", [1, 3, 1536]), dout("gdn_p", [1, 8, 64, 64])
    o_ckv_s, o_kr_s = dout("ckv_s", [NS, 128]), dout("kr_s", [NS, 32])
    o_conv_s, o_gdn_s = dout("conv_s", [NSEQ, 3, 1536]), dout("gdn_s", [NSEQ, 8, 64, 64])
    mods_p = dscr("mods_p", [9, 128, D])
    mods_s = dscr("mods_s", [9, 64, D])
    x1p, x1s = dscr("x1p", [TP, D]), dscr("x1s", [NS, D])
    x2p, x2s = dscr("x2p", [TP, D]), dscr("x2s", [NS, D])
    dbg = dout if cfg.get("debug") else dscr
    mix_p, mix_s = dbg("mix_p", [TP, D], BF16), dbg("mix_s", [NS, D], BF16)
    Q_p, Q_s = dbg("Q_p", [TP, 768], BF16), dbg("Q_s", [NS, 768], BF16)

    with ExitStack() as stack:
        S = Sched(nc, stack)
        cx.S = S
        cx.ps = [stack.enter_context(nc.psum_tensor("ps%d" % i, [128, 512], F32))[:, :] for i in range(8)]
        cx.rps = [Res("ps%d" % i, excl=True) for i in range(8)]
        cst_f = sb(cx, stack, "cst_f", [128, 128], F32)
        cst_b = sb(cx, stack, "cst_b", [128, 128 + 128 + 512 + 128], BF16)
        cx.cst = cst
        cx.eps_t = sb(cx, stack, "eps_t", [128, 1], F32)
        cx.one_t = sb(cx, stack, "one_t", [128, 1], F32)
        cx.ident_f = cst_f[:, 0:128]
        cx.ident_bf = cst_b[:, 0:128]
        cx.nident_bf = cst_b[:, 128:256]
        cx.ident4_bf = cst_b[:, 256:768]
        cx.bones = cst_b[:, 768:896]
        r_c = Res("consts")
        d_c = S.dsem("consts")
        S.dma("sp", d_c, cst_f[:, :], cst[:, 0:128], writes=(r_c,))
        bon = sb(cx, stack, "bon_tmp", [128, 128], F32)
        S.dma("sp", d_c, bon[:, :], cst[:, 384:512], writes=(r_c,))
        S.op("dve", "tensor_copy", dict(out=cst_b[:, 0:128], in_=cst_f[:, 0:128]), reads=(r_c,), writes=(r_c,))
        S.op("dve", "tensor_scalar", dict(out=cst_b[:, 128:256], in0=cst_f[:, 0:128], scalar1=-1.0, scalar2=None,
                                          op0=ALU.mult), reads=(r_c,), writes=(r_c,))
        for h in range(4):
            S.op("dve", "tensor_copy", dict(out=cst_b[:, 256 + h * 128:256 + (h + 1) * 128], in_=cst_f[:, 0:128]),
                 reads=(r_c,), writes=(r_c,))
        S.op("dve", "tensor_copy", dict(out=cst_b[:, 768:896], in_=bon[:, :]), reads=(r_c,), writes=(r_c,))
        S.op("dve", "memset", dict(ap=cx.eps_t[:, :], constant=EPS), writes=(r_c,))
        S.op("dve", "memset", dict(ap=cx.one_t[:, :], constant=1.0), writes=(r_c,))
        S.barrier()
        S.emit()

        r_mods = Res("mods")
        if "adaln" in phases:
            phase_adaln(cx, S, c_rep, ada_w, ada_b, mods_p, mods_s, r_mods)

        nblk_p = (TP + 127) // 128
        r_xp = [Res() for _ in range(nblk_p)]
        r_xs = [Res()]
        r_x1p = [Res() for _ in range(nblk_p)]
        r_x1s = [Res()]
        f1_dst_p, f1_dst_s = (x1p, x1s) if "mixer" in phases else (yp, ys)
        if "ffn1" in phases:
            groups = [dict(src=xp, dst=f1_dst_p, T=TP, mods=mods_p, n=128, src_res=r_xp, dst_res=r_x1p),
                      dict(src=xs_in, dst=f1_dst_s, T=NS, mods=mods_s, n=64, src_res=r_xs, dst_res=r_x1s)]
            phase_ffn(cx, S, "f1", groups, ffn1_wi, ffn1_wo, norm_ffn1, 0)
        r_out = Res("outs")
        gp = gs = None
        if "mixer" in phases:
            msrc_p, msrc_s = (x1p, x1s) if "ffn1" in phases else (xp, xs_in)
            gp = dict(src=msrc_p, src_res=r_x1p, T=TP, nseq=1, mods=mods_p, n=128, s0=None, conv0=None, cs=cs_p,
                      new_conv=o_conv_p, new_gdn=o_gdn_p, new_ckv=o_ckv_p, new_kr=o_kr_p, mix=mix_p, Q=Q_p,
                      r_out=r_out, r_mix=Res(), r_Q=Res(), r_ckv=Res(), r_kr=Res())
            gs = dict(src=msrc_s, src_res=r_x1s, T=NS, nseq=NSEQ, mods=mods_s, n=64, s0=st_gdn, conv0=st_conv, cs=cs_s,
                      new_conv=o_conv_s, new_gdn=o_gdn_s, new_ckv=o_ckv_s, new_kr=o_kr_s, mix=mix_s, Q=Q_s,
                      r_out=r_out, r_mix=Res(), r_Q=Res(), r_ckv=Res(), r_kr=Res())
            phase_mixer(cx, S, [gp, gs], P)
        if "attn_p" in phases:
            phase_attn_prompt(cx, S, gp, P)
        if "attn_s" in phases:
            phase_attn_sample(cx, S, gs, P, cache_c, cache_k, ptab, NSEQ, cfg.get("npages", 128))
        if "wout" in phases:
            r_x2p = [Res() for _ in range(nblk_p)]
            r_x2s = [Res()]
            gp.update(dst=x2p, dst_res=r_x2p)
            gs.update(dst=x2s, dst_res=r_x2s)
            phase_wout(cx, S, [gp, gs], w_out)
            if "ffn2" in phases:
                groups = [dict(src=x2p, dst=yp, T=TP, mods=mods_p, n=128, src_res=r_x2p, dst_res=[Res() for _ in range(nblk_p)]),
                          dict(src=x2s, dst=ys, T=NS, mods=mods_s, n=64, src_res=r_x2s, dst_res=[Res()])]
                phase_ffn(cx, S, "f2", groups, ffn2_wi, ffn2_wo, norm_ffn2, 6)
        S.barrier()
        S.emit()
    return nc


ALL_PHASES = ("adaln", "ffn1", "mixer", "attn_p", "attn_s", "wout", "ffn2")


def make_inputs(core, inputs, TP=4096):
    f = lambda a: np.ascontiguousarray(np.asarray(a, dtype=np.float32))
    b = core % 4
    sl = slice(16 * core, 16 * core + 16)
    conv_w = f(inputs["gdn_conv_w"][0])
    m = dict(
        xp=f(inputs["x_prompt"][b][:TP]), xs=f(inputs["x_sample"][sl]).reshape(64, D),
        c_rep=np.concatenate([np.repeat(f(inputs["c_prompt"][b:b + 1]), 128, 0), np.repeat(f(inputs["c_sample"][sl]), 4, 0)], 0),
        cst=host_consts(), ada_w=f(inputs["ada_w"][0]), ada_b=f(inputs["ada_b"][0:1]),
        norm_ffn1=f(inputs["norm_ffn1"][0:1]), ffn1_wi=f(inputs["ffn1_wi"][0]), ffn1_wo=f(inputs["ffn1_wo"][0]),
        norm_mix=f(inputs["norm_mix"][0:1]), w_in=f(inputs["w_in"][0]),
        conv_w_fm=np.ascontiguousarray(conv_w.reshape(4, 12, 128).transpose(2, 1, 0)),
        a_log=f(inputs["gdn_a_log"][0:1]), dt_bias=f(inputs["gdn_dt_bias"][0:1]), gdn_norm=f(inputs["gdn_norm"][0:1]),
        qn_g=f(inputs["mla_qn_norm"][0:1]), qr_g=f(inputs["mla_qr_norm"][0:1]), ckv_g=f(inputs["mla_ckv_norm"][0:1]),
        kr_g=f(inputs["mla_kr_norm"][0:1]), kn_g=f(inputs["mla_kn_norm"][0:1]),
        w_uk=f(inputs["mla_w_uk"][0]).reshape(128, 512), w_uv=f(inputs["mla_w_uv"][0]).reshape(128, 512),
        w_out=f(inputs["w_out"][0]), norm_ffn2=f(inputs["norm_ffn2"][0:1]), ffn2_wi=f(inputs["ffn2_wi"][0]),
        ffn2_wo=f(inputs["ffn2_wo"][0]),
        cache_c=f(inputs["cache_ckv"][0]).reshape(-1, 128 * 128), cache_k=f(inputs["cache_krope"][0]).reshape(-1, 128 * 32),
        ptab=np.ascontiguousarray(np.asarray(inputs["page_table"][sl], dtype=np.int32)),
        cs_p=host_rope_table(np.arange(TP)), cs_s=host_rope_table(16384 + np.arange(4)),
        st_conv=f(inputs["state_conv"][0][sl]), st_gdn=f(inputs["state_gdn"][0][sl]),
    )
    return m


def kernel(**inputs):
    nc = build(dict(TP=4096, NS=64, phases=ALL_PHASES))
    in_maps = [make_inputs(c, inputs) for c in range(8)]
    res = run_bass_kernel_spmd(nc, in_maps, core_ids=list(range(8))).results
    f32 = np.float32
    yp = np.stack([res[b]["yp"] for b in range(4)]).astype(f32)
    ys = np.concatenate([res[c]["ys"].reshape(16, 4, D) for c in range(8)]).astype(f32)
    ckv_p = np.stack([res[b]["ckv_p"] for b in range(4)])[None].astype(f32)
    kr_p = np.stack([res[b]["kr_p"] for b in range(4)])[None].astype(f32)
    conv_p = np.concatenate([res[b]["conv_p"] for b in range(4)])[None].astype(f32)
    gdn_p = np.concatenate([res[b]["gdn_p"] for b in range(4)])[None].astype(f32)
    ckv_s = np.concatenate([res[c]["ckv_s"].reshape(16, 4, 128) for c in range(8)])[None].astype(f32)
    kr_s = np.concatenate([res[c]["kr_s"].reshape(16, 4, 32) for c in range(8)])[None].astype(f32)
    conv_s = np.concatenate([res[c]["conv_s"] for c in range(8)])[None].astype(f32)
    gdn_s = np.concatenate([res[c]["gdn_s"] for c in range(8)])[None].astype(f32)
    return (yp, ys, ckv_p, kr_p, conv_p, gdn_p, ckv_s, kr_s, conv_s, gdn_s)
```
